# Optimizing a Trainium2 kernel written in Bass

```python
import jax, jax.numpy as jnp
from jax import lax
import numpy as np

D_MODEL = 2048
BATCH = 2
SEQ = 8192
DEPTH = 4

HEAD_DIM = 128
ROPE_THETA = 10000.0
GRID_W = 64
BLOCK = 128
EPS = 1e-6
NEG_INF = -1e30
A_HEADS = 4
A_KV = 2
A_WINDOW = 128
B_HEADS = 4
B_KV = 4
B_PATTERNS = ((128, 1), (512, 4), (2048, 16))
C_HEADS = 4
C_KV = 2
D_HEADS = 4
D_KV = 4
NA_ROWS = 8
NA_COLS = 16

MIXER_HEADS = ((A_HEADS, A_KV), (B_HEADS, B_KV), (C_HEADS, C_KV), (D_HEADS, D_KV))
N_BRANCH = 4
BRANCH_W = A_HEADS * HEAD_DIM
IN_COLS = sum((2 * h + 2 * kv) * HEAD_DIM for h, kv in MIXER_HEADS)

kernel_name = "hybrid_gated_parallel_mixers_encoder"


def rms_norm(x, g):
    xf = x.astype(jnp.float32)
    y = xf * lax.rsqrt(jnp.mean(xf * xf, axis=-1, keepdims=True) + EPS)
    return (y * g.astype(jnp.float32)).astype(x.dtype)


def rope_tables(pos, dim):
    inv = ROPE_THETA ** (-jnp.arange(0, dim, 2, dtype=jnp.float32) / dim)
    ang = pos.astype(jnp.float32)[:, None] * inv[None, :]
    ang = jnp.concatenate([ang, ang], axis=-1)
    return (jnp.cos(ang), jnp.sin(ang))


def apply_rope(x, cos, sin):
    xf = x.astype(jnp.float32)
    x1, x2 = jnp.split(xf, 2, axis=-1)
    return (xf * cos + jnp.concatenate([-x2, x1], axis=-1) * sin).astype(x.dtype)


def apply_axial_rope(x, cos_r, sin_r, cos_c, sin_c):
    half = x.shape[-1] // 2
    return jnp.concatenate([apply_rope(x[..., :half], cos_r, sin_r),
                            apply_rope(x[..., half:], cos_c, sin_c)], axis=-1)


def column_split_points():
    points, total = [], 0
    for heads, kv in MIXER_HEADS:
        for width in (heads * HEAD_DIM, kv * HEAD_DIM, kv * HEAD_DIM, heads * HEAD_DIM):
            total += width
            points.append(total)
    return points[:-1]


def to_q_heads(t, n_kv):
    b, s, _ = t.shape
    return t.reshape(b, s, n_kv, -1, HEAD_DIM).transpose(0, 2, 3, 1, 4)


def to_kv_heads(t):
    b, s, _ = t.shape
    return t.reshape(b, s, -1, HEAD_DIM).transpose(0, 2, 1, 3)


def merge_heads(y):
    return jnp.moveaxis(y, -2, 1).reshape(y.shape[0], y.shape[-2], -1)


def banded_attention(q, k, v, reach, sink=None):
    n, hkv, g, length, hd = q.shape
    blk = min(BLOCK, length)
    nb = -(-length // blk)
    lp = nb * blk
    qb = jnp.pad(q, ((0, 0), (0, 0), (0, 0), (0, lp - length), (0, 0))).reshape(n, hkv, g, nb, blk, hd)
    pad_kv = ((0, 0), (0, 0), (reach, lp - length + reach), (0, 0))
    kp = jnp.pad(k, pad_kv)
    vp = jnp.pad(v, pad_kv)
    span = blk + 2 * reach
    idx = jnp.arange(nb)[:, None] * blk + jnp.arange(span)[None, :]
    kband = kp[:, :, idx]
    vband = vp[:, :, idx]
    kpos = (idx - reach)[:, None, :]
    qpos = (jnp.arange(nb)[:, None] * blk + jnp.arange(blk)[None, :])[:, :, None]
    mask = (jnp.abs(qpos - kpos) <= reach) & (kpos >= 0) & (kpos < length)
    s = jnp.einsum('nhgbqd,nhbkd->nhgbqk', qb, kband).astype(jnp.float32) * (hd ** -0.5)
    s = jnp.where(mask, s, NEG_INF)
    m = jnp.max(s, axis=-1)
    if sink is not None:
        sink_b = sink.astype(jnp.float32)[None, :, :, None, None]
        m = jnp.maximum(m, sink_b)
    p = jnp.exp(s - m[..., None])
    den = jnp.sum(p, axis=-1)
    if sink is not None:
        den = den + jnp.exp(sink_b - m)
    o = jnp.einsum('nhgbqk,nhbkd->nhgbqd', p.astype(v.dtype), vband).astype(jnp.float32) / den[..., None]
    o = o.reshape(n, hkv, g, lp, hd)[:, :, :, :length].astype(q.dtype)
    lse = (m + jnp.log(den)).reshape(n, hkv, g, lp)[..., :length]
    return o, lse


def dilated_attention(q, k, v):
    bsz, hkv, g, seq, hd = q.shape
    outs, lses = [], []
    for window, dil in B_PATTERNS:
        reach = (window // 2) // dil
        length = seq // dil
        qd = q.reshape(bsz, hkv, g, length, dil, hd).transpose(0, 4, 1, 2, 3, 5).reshape(bsz * dil, hkv, g, length, hd)
        kd = k.reshape(bsz, hkv, length, dil, hd).transpose(0, 3, 1, 2, 4).reshape(bsz * dil, hkv, length, hd)
        vd = v.reshape(bsz, hkv, length, dil, hd).transpose(0, 3, 1, 2, 4).reshape(bsz * dil, hkv, length, hd)
        o, lse = banded_attention(qd, kd, vd, reach)
        outs.append(o.reshape(bsz, dil, hkv, g, length, hd).transpose(0, 2, 3, 4, 1, 5).reshape(bsz, hkv, g, seq, hd))
        lses.append(lse.reshape(bsz, dil, hkv, g, length).transpose(0, 2, 3, 4, 1).reshape(bsz, hkv, g, seq))
    w = jax.nn.softmax(jnp.stack(lses, axis=0), axis=0)
    return jnp.sum(w[..., None] * jnp.stack(outs, axis=0).astype(jnp.float32), axis=0).astype(q.dtype)


def dense_block_attention(q, k, v):
    bsz, hkv, g, seq, hd = q.shape
    nb = seq // BLOCK
    qb = jnp.moveaxis(q.reshape(bsz, hkv, g, nb, BLOCK, hd), 3, 0)

    def attend(qi):
        s = jnp.einsum('nhgqd,nhkd->nhgqk', qi, k).astype(jnp.float32) * (hd ** -0.5)
        p = jax.nn.softmax(s, axis=-1)
        return jnp.einsum('nhgqk,nhkd->nhgqd', p.astype(v.dtype), v)

    o = lax.map(attend, qb)
    return jnp.moveaxis(o, 0, 3).reshape(bsz, hkv, g, seq, hd)


def neighbourhood_attention(q, k, v, rel_bias, rows):
    bsz, heads, seq, hd = q.shape
    kr = min(NA_ROWS, rows)
    kc = min(NA_COLS, GRID_W)
    qg = q.reshape(bsz, heads, rows, GRID_W, hd)
    kg = k.reshape(bsz, heads, rows, GRID_W, hd)
    vg = v.reshape(bsz, heads, rows, GRID_W, hd)
    r = jnp.arange(rows)
    col = jnp.arange(GRID_W)
    row_start = jnp.clip(r - kr // 2, 0, rows - kr)
    row_idx = row_start[:, None] + jnp.arange(kr)[None, :]
    col_start = jnp.clip(col - kc // 2, 0, GRID_W - kc)
    col_mask = (col[None, :] >= col_start[:, None]) & (col[None, :] < col_start[:, None] + kc)
    kblk = kg[:, :, row_idx].reshape(bsz, heads, rows, kr * GRID_W, hd)
    vblk = vg[:, :, row_idx].reshape(bsz, heads, rows, kr * GRID_W, hd)
    dr = row_idx - r[:, None]
    dc = jnp.clip(col[None, :] - col[:, None], -(NA_COLS - 1), NA_COLS - 1)
    bias = rel_bias[:, (dr + NA_ROWS - 1)[:, None, :, None], (dc + NA_COLS - 1)[None, :, None, :]]
    bias = bias.reshape(heads, rows, GRID_W, kr * GRID_W).astype(jnp.float32)
    mask = jnp.broadcast_to(col_mask[:, None, :], (GRID_W, kr, GRID_W)).reshape(GRID_W, kr * GRID_W)
    s = jnp.einsum('bhrqd,bhrkd->bhrqk', qg, kblk).astype(jnp.float32) * (hd ** -0.5) + bias
    s = jnp.where(mask, s, NEG_INF)
    p = jax.nn.softmax(s, axis=-1)
    o = jnp.einsum('bhrqk,bhrkd->bhrqd', p.astype(v.dtype), vblk)
    return o.reshape(bsz, heads, seq, hd)


def setup_inputs(seed: int = 0) -> dict:
    key = jax.random.key(seed)
    ks = jax.random.split(key, 14)
    nrm = jax.random.normal
    f32 = jnp.float32
    return {
        "x": nrm(ks[0], (BATCH, SEQ, D_MODEL), f32),
        "c": nrm(ks[1], (BATCH, D_MODEL), f32),
        "norm_g": 1.0 + 0.02 * nrm(ks[2], (DEPTH, D_MODEL), f32),
        "w_ada": nrm(ks[3], (DEPTH, D_MODEL, 3 * D_MODEL), f32) * D_MODEL ** -0.5,
        "b_ada": 0.01 * nrm(ks[4], (DEPTH, 3 * D_MODEL), f32),
        "w_in": nrm(ks[5], (DEPTH, D_MODEL, IN_COLS), f32) * D_MODEL ** -0.5,
        "a_sink": nrm(ks[6], (DEPTH, A_HEADS), f32),
        "c_q_norm": 1.0 + 0.02 * nrm(ks[7], (DEPTH, HEAD_DIM), f32),
        "c_k_norm": 1.0 + 0.02 * nrm(ks[8], (DEPTH, HEAD_DIM), f32),
        "d_rel_bias": 0.1 * nrm(ks[9], (DEPTH, D_HEADS, 2 * NA_ROWS - 1, 2 * NA_COLS - 1), f32),
        "w_gate_merge": nrm(ks[10], (DEPTH, D_MODEL, N_BRANCH * D_MODEL), f32) * D_MODEL ** -0.5,
        "w_branch": nrm(ks[11], (DEPTH, N_BRANCH, BRANCH_W, D_MODEL), f32) * BRANCH_W ** -0.5,
        "w_out": nrm(ks[12], (DEPTH, D_MODEL, D_MODEL), f32) * D_MODEL ** -0.5,
        "final_g": 1.0 + 0.02 * nrm(ks[13], (D_MODEL,), f32),
    }


def reference(x, c, norm_g, w_ada, b_ada, w_in, a_sink, c_q_norm, c_k_norm, d_rel_bias, w_gate_merge, w_branch, w_out, final_g):
    bsz, seq, dm = x.shape
    rows = seq // GRID_W
    pos = jnp.arange(seq, dtype=jnp.int32)
    cos1, sin1 = rope_tables(pos, HEAD_DIM)
    axial = rope_tables(pos // GRID_W, HEAD_DIM // 2) + rope_tables(pos % GRID_W, HEAD_DIM // 2)
    split_points = column_split_points()
    cond = jax.nn.silu(c)
    for layer in range(DEPTH):
        shift, scale, gate = jnp.split(cond @ w_ada[layer] + b_ada[layer], 3, axis=-1)
        h = rms_norm(x, norm_g[layer]) * (1.0 + scale[:, None, :]) + shift[:, None, :]
        (qa, ka, va, ga, qb, kb, vb, gb, qc, kc, vc, gc, qd, kd, vd, gd) = jnp.split(h @ w_in[layer], split_points, axis=-1)
        ya, _ = banded_attention(apply_rope(to_q_heads(qa, A_KV), cos1, sin1),
                                 apply_rope(to_kv_heads(ka), cos1, sin1), to_kv_heads(va),
                                 A_WINDOW, a_sink[layer].reshape(A_KV, A_HEADS // A_KV))
        yb = dilated_attention(apply_rope(to_q_heads(qb, B_KV), cos1, sin1),
                               apply_rope(to_kv_heads(kb), cos1, sin1), to_kv_heads(vb))
        yc = dense_block_attention(apply_axial_rope(rms_norm(to_q_heads(qc, C_KV), c_q_norm[layer]), *axial),
                                   apply_axial_rope(rms_norm(to_kv_heads(kc), c_k_norm[layer]), *axial),
                                   to_kv_heads(vc))
        yd = neighbourhood_attention(to_kv_heads(qd), to_kv_heads(kd), to_kv_heads(vd), d_rel_bias[layer], rows)
        branches = jnp.stack([merge_heads(ya) * jax.nn.silu(ga), merge_heads(yb) * jax.nn.silu(gb),
                              merge_heads(yc) * jax.nn.silu(gc), merge_heads(yd) * jax.nn.silu(gd)], axis=2)
        proj = jnp.einsum('bsnw,nwd->bsnd', branches, w_branch[layer])
        merge_gate = jax.nn.sigmoid((h @ w_gate_merge[layer]).reshape(bsz, seq, N_BRANCH, dm))
        x = x + gate[:, None, :] * (jnp.sum(merge_gate * proj, axis=2) @ w_out[layer])
    return rms_norm(x, final_g)
```

```python
import numpy as np
import ml_dtypes
from contextlib import ExitStack
import concourse.bass as bass
import concourse.mybir as mybir
from concourse.bass_utils import run_bass_kernel_spmd

F32 = mybir.dt.float32
BF16 = mybir.dt.bfloat16
AF = mybir.ActivationFunctionType
ALU = mybir.AluOpType
NPBF = ml_dtypes.bfloat16

D = 2048
KC = 16
T = 2048
SEQ = 8192
NTB = 4
HD = 128
WIN = 4096
EPS = 1e-6
SCALE = HD ** -0.5
NEG = -30000.0
DEPTH = 4
NCORES = 8

_MIX_COL0 = {'A': 0, 'B': 1536, 'C': 3584, 'D': 5120}
_MIX_KV = {'A': 2, 'B': 4, 'C': 2, 'D': 4}
FM_CHUNKS = []
QIDX = {}
KIDX = {}
GIDX = {}
_qi = _ki = _gi = 0
for _m in 'ABCD':
    c0 = _MIX_COL0[_m]
    kv = _MIX_KV[_m]
    for h in range(4):
        FM_CHUNKS.append(('q', _m, h, c0 + h * 128)); QIDX[(_m, h)] = _qi; _qi += 1
    for h in range(kv):
        FM_CHUNKS.append(('k', _m, h, c0 + 512 + h * 128)); KIDX[(_m, h)] = _ki; _ki += 1
    for h in range(4):
        FM_CHUNKS.append(('g', _m, h, c0 + 512 + 2 * kv * 128 + h * 128)); GIDX[(_m, h)] = _gi; _gi += 1
NFM = len(FM_CHUNKS)
V_COLS = (list(range(768, 1024)) + list(range(4352, 4608)),
          list(range(2560, 3072)),
          list(range(6144, 6656)))
VOFF = {}
for h in range(2):
    VOFF[('A', h)] = h
    VOFF[('C', h)] = 2 + h
for h in range(4):
    VOFF[('B', h)] = 4 + h
    VOFF[('D', h)] = 8 + h
KW_IDX = {}
_i = 0
for _m, n in (('A', 2), ('B', 4), ('D', 4)):
    for h in range(n):
        KW_IDX[(_m, h)] = _i; _i += 1

D_CLASSES = {'gen': (0, 5, -2), 't0': (5, 6, -2), 't1': (11, 5, -2), 't14': (16, 5, -2), 't15': (21, 6, -3)}
D_NSLOT = 27


class Sched:
    CENG = ('pe', 'act', 'dve', 'pool')
    ALLQ = ('pe', 'act', 'dve', 'pool', 'sp')

    def __init__(self, nc, es, n_dma_sems=48):
        self.nc = nc
        self.es = es
        self.e = {'pe': nc.tensor, 'act': nc.scalar, 'dve': nc.vector, 'pool': nc.gpsimd, 'sp': nc.sync}
        self.dsems = [es.enter_context(nc.semaphore(f"dq{i}")) for i in range(n_dma_sems)]
        self.dcnt = [0] * n_dma_sems
        self.dnext = 0
        self.NP = 16
        self.psems = [es.enter_context(nc.semaphore(f"pq{i}")) for i in range(self.NP)]
        self.pcnt = [0] * self.NP
        self.pnext = 0
        self.epoch = 0
        self.waited = {q: {} for q in self.ALLQ}
        self.ccsem = None
        self.cccnt = 0
        self._new_epoch()

    def _new_epoch(self):
        self.epoch += 1
        self.sems = {e: self.es.enter_context(self.nc.semaphore(f"s{self.epoch}{e}")) for e in self.CENG}
        self.cnt = {e: 0 for e in self.CENG}
        self.lastw = {}
        self.readers = {}

    def _wait(self, q, ev):
        sem, val, key, _src = ev
        w = self.waited[q]
        if w.get(key, 0) >= val:
            return
        self.e[q].wait_ge(sem, val)
        w[key] = val

    def _deps(self, r, w):
        evs = []
        for t in r:
            ev = self.lastw.get(t)
            if ev is not None:
                evs.append(ev)
        for t in w:
            ev = self.lastw.get(t)
            if ev is not None:
                evs.append(ev)
            rd = self.readers.get(t)
            if rd:
                evs.extend(rd.values())
        return evs

    def _record(self, ev, r, w):
        for t in w:
            self.lastw[t] = ev
            self.readers[t] = {}
        for t in r:
            self.readers.setdefault(t, {})[ev[2]] = ev

    def op(self, eng, fn, r=(), w=()):
        pr = [t for t in r if isinstance(t, tuple) and t[0] == 'ps']
        if pr:
            r = [t for t in r if not (isinstance(t, tuple) and t[0] == 'ps')]
            w = list(w) + pr
        for ev in self._deps(r, w):
            if eng == 'pe' and ev[3] == 'pe':
                continue
            self._wait(eng, ev)
        ins = fn(self.e[eng])
        self.cnt[eng] += 1
        ins.then_inc(self.sems[eng], 1)
        ev = (self.sems[eng], self.cnt[eng], ('e', self.epoch, eng), eng)
        self._record(ev, r, w)

    def _dsem(self, q):
        if q == 'pool':
            i = self.pnext
            self.pnext = (i + 1) % self.NP
            self.pcnt[i] += 16
            return self.psems[i], self.pcnt[i], ('p', i)
        i = self.dnext
        self.dnext = (i + 1) % len(self.dsems)
        self.dcnt[i] += 16
        return self.dsems[i], self.dcnt[i], ('d', i)

    def dma(self, q, out, in_, r=(), w=()):
        for ev in self._deps(r, w):
            self._wait(q, ev)
        sem, val, key = self._dsem(q)
        self.e[q].dma_start(out=out, in_=in_).then_inc(sem, 16)
        ev = (sem, val, key, 'dma')
        self._record(ev, r, w)

    def idma(self, out, in_, idx_ap, r=(), w=()):
        q = 'pool'
        for ev in self._deps(r, w):
            self._wait(q, ev)
        sem, val, key = self._dsem(q)
        self.e[q].indirect_dma_start(out=out, out_offset=None, in_=in_,
                                     in_offset=bass.IndirectOffsetOnAxis(ap=idx_ap, axis=0)
                                     ).then_inc(sem, 16)
        ev = (sem, val, key, 'dma')
        self._record(ev, r, w)

    def allgather(self, src_t, dst_t, groups, r=(), w=()):
        q = 'pool'
        if self.ccsem is None:
            self.ccsem = self.es.enter_context(self.nc.semaphore("ccsem"))
        for ev in self._deps(r, w):
            self._wait(q, ev)
        self.cccnt += 1
        self.e[q].collective_compute("AllGather", ALU.bypass, replica_groups=groups,
                                     ins=[src_t.ap().opt()], outs=[dst_t.ap().opt()]).then_inc(self.ccsem, 1)
        ev = (self.ccsem, self.cccnt, ('cc',), 'cc')
        self._record(ev, r, w)

    def barrier(self, partial=False, keep=()):
        evs = [(self.sems[e], self.cnt[e], ('e', self.epoch, e), e) for e in self.CENG if self.cnt[e] > 0]
        evs += [(self.dsems[i], self.dcnt[i], ('d', i), 'dma') for i in range(len(self.dsems)) if self.dcnt[i] > 0]
        if not partial:
            evs += [(self.psems[i], self.pcnt[i], ('p', i), 'dma') for i in range(self.NP) if self.pcnt[i] > 0]
            if self.cccnt:
                evs.append((self.ccsem, self.cccnt, ('cc',), 'cc'))
        for q in self.ALLQ:
            for ev in evs:
                if q == 'pe' and ev[3] == 'pe':
                    continue
                self._wait(q, ev)
        kept = {t: ev for t, ev in self.lastw.items() if isinstance(t, tuple) and t[0] in keep} if partial else {}
        if max(self.cnt.values()) > 10000:
            self._new_epoch()
        else:
            self.lastw = {}
            self.readers = {}
        self.lastw.update(kept)


class Ctx:
    pass


XKEEP = ('KG', 'VG', 'KWd', 'VWd')

_SBT_N = [0]


def _sbt(nc, name, shape, dt):
    _SBT_N[0] += 1
    return nc.sbuf_tensor(f"{name}_u{_SBT_N[0]}", shape, dt)


def _mm(out, lhsT, rhs, start, stop):
    return lambda e: e.matmul(out, lhsT, rhs, start=start, stop=stop)


def phase_ada(cx, dr, ADA):
    nc, S, ps = cx.nc, cx.S, cx.ps
    with ExitStack() as st:
        CT = st.enter_context(_sbt(nc, "ada_ct", [128, 16], F32))
        CS = st.enter_context(_sbt(nc, "ada_cs", [128, 16], F32))
        BA = st.enter_context(_sbt(nc, "ada_b", [128, 48], F32))
        WA = [st.enter_context(_sbt(nc, f"ada_w{i}", [128, 16, 512], F32)) for i in range(2)]
        S.dma('sp', CT[:], dr['cT'], w=['CT'])
        S.dma('sp', BA[:], dr['bada'], w=['BA'])
        S.op('act', lambda e: e.activation(out=CS[:], in_=CT[:], func=AF.Silu), r=['CT'], w=['CS'])
        for s in range(12):
            wb = WA[s % 2]
            S.dma('sp', wb[:], dr['wada'][s], w=[('WA', s % 2)])
            for mm in range(4):
                m = s * 4 + mm
                for kc in range(KC):
                    S.op('pe', _mm(ps[0][:, m:m + 1], wb[:, kc, mm * 128:(mm + 1) * 128], CS[:, kc:kc + 1],
                                   kc == 0, kc == KC - 1),
                         r=[('WA', s % 2), 'CS'], w=[('ps', 0)])
        S.op('dve', lambda e: e.tensor_tensor(out=ADA, in0=ps[0][:, 0:48], in1=BA[:], op=ALU.add),
             r=[('ps', 0), 'BA'], w=['ADA'])
        S.barrier()


def norm_block(cx, XB, SQ, RS, tb, xsrc_tokens):
    nc, S, ps = cx.nc, cx.S, cx.ps
    S.op('act', lambda e: e.activation(out=SQ[:].rearrange("p c t -> p (c t)"),
                                       in_=XB[:].rearrange("p c t -> p (c t)"), func=AF.Square),
         r=['XB'], w=['SQ'])
    for c in range(KC):
        S.op('pe', _mm(ps[7][:, :], cx.ONES[:], SQ[:, c, :], c == 0, c == KC - 1), r=['SQ', 'CONST'], w=[('ps', 7)])
    S.op('act', lambda e: e.activation(out=RS[:], in_=ps[7][:, :], func=AF.Sqrt, bias=cx.EPSC[:, 0:1],
                                       scale=1.0 / D), r=[('ps', 7), 'EPSC'], w=['RS'])
    S.op('dve', lambda e: e.reciprocal(out=RS[:], in_=RS[:]), r=['RS'], w=['RS'])


def phase1(cx, dr, ADA, HT, stop=9):
    nc, S, ps = cx.nc, cx.S, cx.ps
    with ExitStack() as st:
        XB = st.enter_context(_sbt(nc, "p1_xb", [128, 16, 512], F32))
        SQ = st.enter_context(_sbt(nc, "p1_sq", [128, 16, 512], BF16))
        RS = st.enter_context(_sbt(nc, "p1_rs", [128, 512], F32))
        TM = [st.enter_context(_sbt(nc, f"p1_tm{i}", [128, 512], F32)) for i in range(2)]
        NG = st.enter_context(_sbt(nc, "p1_ng", [128, 16], F32))
        GG = st.enter_context(_sbt(nc, "p1_gg", [128, 16], F32))
        S.dma('sp', NG[:], dr['ng'], w=['NG'])
        S.op('dve', lambda e: e.scalar_tensor_tensor(out=GG[:], in0=ADA[:, 16:32], scalar=1.0, in1=NG[:],
                                                     op0=ALU.add, op1=ALU.mult), r=['ADA', 'NG'], w=['GG'])
        for tb in range(NTB):
            S.dma('sp', XB[:], dr['xT'][:, :, tb * 512:(tb + 1) * 512], w=['XB'])
            norm_block(cx, XB, SQ, RS, tb, None)
            for c in range(KC):
                tm = TM[c % 2]
                S.op('dve', lambda e, c=c, tm=tm: e.tensor_tensor(out=tm[:], in0=XB[:, c, :], in1=RS[:], op=ALU.mult),
                     r=['XB', 'RS'], w=[('TM', c % 2)])
                S.op('act', lambda e, c=c, tm=tm: e.activation(out=HT[:, c, tb * 512:(tb + 1) * 512], in_=tm[:],
                                                              func=AF.Identity, bias=ADA[:, c:c + 1],
                                                              scale=GG[:, c:c + 1]),
                     r=[('TM', c % 2), 'GG', 'ADA'], w=[('HT', tb)])
        S.barrier()
    if stop < 3:
        return
    with ExitStack() as st:
        WB = [st.enter_context(_sbt(nc, f"p1_wb{i}", [128, 16, 128], BF16)) for i in range(3)]
        WS = [st.enter_context(_sbt(nc, f"p1_ws{i}", [128, 2048], F32)) for i in range(2)]
        WV = st.enter_context(_sbt(nc, "p1_wv", [128, 16, 512], BF16))
        COS1 = st.enter_context(_sbt(nc, "p1_cos1", [128, T], F32))
        SIN1 = st.enter_context(_sbt(nc, "p1_sin1", [128, T], F32))
        COSC = st.enter_context(_sbt(nc, "p1_cosc", [128, T], F32))
        SINC = st.enter_context(_sbt(nc, "p1_sinc", [128, T], F32))
        QN = st.enter_context(_sbt(nc, "p1_qn", [128, 2], F32))
        OST = [st.enter_context(_sbt(nc, f"p1_ost{i}", [128, T], BF16)) for i in range(2)]
        VST = st.enter_context(_sbt(nc, "p1_vst", [128, 16, 512], BF16))
        QB = [st.enter_context(_sbt(nc, f"p1_qb{i}", [128, 512], BF16)) for i in range(2)]
        QF = [st.enter_context(_sbt(nc, f"p1_qf{i}", [128, 512], F32)) for i in range(2)]
        T1 = [st.enter_context(_sbt(nc, f"p1_t1{i}", [128, 512], F32)) for i in range(2)]
        T2 = [st.enter_context(_sbt(nc, f"p1_t2{i}", [128, 512], F32)) for i in range(2)]
        RS2 = [st.enter_context(_sbt(nc, f"p1_rs2{i}", [128, 512], F32)) for i in range(2)]
        S.dma('sp', COS1[:], dr['rope'][0], w=['ROPE'])
        S.dma('sp', SIN1[:], dr['rope'][1], w=['ROPE'])
        S.dma('sp', COSC[:], dr['rope'][2], w=['ROPE'])
        S.dma('sp', SINC[:], dr['rope'][3], w=['ROPE'])
        S.dma('sp', QN[:], dr['qkn'], w=['QN'])
        st8 = {'blk': 0, 'seq': 0}

        def w_load(seq, ci):
            S.dma('sp', WS[seq % 2][:], dr['wfm'][ci].rearrange("p k n -> p (k n)"), w=[('WS', seq % 2)])

        def w_cast(seq):
            ws, wb = WS[seq % 2], WB[seq % 3]
            S.op('dve', lambda e: e.tensor_copy(out=wb[:].rearrange("p k n -> p (k n)"), in_=ws[:]),
                 r=[('WS', seq % 2)], w=[('WB', seq % 3)])

        def do_chunk(seq, ci):
            kind, mixer, idx, _c0 = FM_CHUNKS[ci]
            wb = WB[seq % 3]
            wbtok = ('WB', seq % 3)
            ost = OST[seq % 2]
            osttok = ('OST', seq % 2)
            for tb in range(NTB):
                blk = st8['blk']
                st8['blk'] += 1
                pb = blk % 3
                b2 = 3 + blk % 2
                b3 = 5 + blk % 2
                k2 = blk % 2
                tsl = slice(tb * 512, (tb + 1) * 512)
                for kc in range(KC):
                    S.op('pe', _mm(ps[pb][:, :], wb[:, kc, :], HT[:, kc, tsl], kc == 0, kc == KC - 1),
                         r=[wbtok, ('HT', tb)], w=[('ps', pb)])
                if kind == 'g':
                    S.op('act', lambda e, pb=pb, tsl=tsl, ost=ost: e.activation(out=ost[:, tsl], in_=ps[pb][:, :],
                                                                               func=AF.Silu),
                         r=[('ps', pb)], w=[osttok])
                elif mixer == 'D':
                    S.op('act', lambda e, pb=pb, tsl=tsl, ost=ost: e.activation(out=ost[:, tsl], in_=ps[pb][:, :],
                                                                               func=AF.Copy),
                         r=[('ps', pb)], w=[osttok])
                elif mixer in 'AB':
                    qb, t1, t2 = QB[k2], T1[k2], T2[k2]
                    S.op('act', lambda e, pb=pb, qb=qb: e.activation(out=qb[:], in_=ps[pb][:, :], func=AF.Copy),
                         r=[('ps', pb)], w=[('QB', k2)])
                    S.op('pe', _mm(ps[b2][:, :], cx.R1[:], qb[:], True, True), r=[('QB', k2), 'CONST'],
                         w=[('ps', b2)])
                    S.op('dve', lambda e, pb=pb, t1=t1, tsl=tsl: e.tensor_tensor(out=t1[:], in0=ps[pb][:, :],
                                                                                in1=COS1[:, tsl], op=ALU.mult),
                         r=[('ps', pb), 'ROPE'], w=[('T1', k2)])
                    S.op('dve', lambda e, b2=b2, t2=t2, tsl=tsl: e.tensor_tensor(out=t2[:], in0=ps[b2][:, :],
                                                                                in1=SIN1[:, tsl], op=ALU.mult),
                         r=[('ps', b2), 'ROPE'], w=[('T2', k2)])
                    S.op('dve', lambda e, t1=t1, t2=t2, tsl=tsl, ost=ost: e.tensor_tensor(out=ost[:, tsl], in0=t1[:],
                                                                                         in1=t2[:], op=ALU.add),
                         r=[('T1', k2), ('T2', k2)], w=[osttok])
                else:
                    qb, qf, t1, t2, rs2 = QB[k2], QF[k2], T1[k2], T2[k2], RS2[k2]
                    qcol = 0 if kind == 'q' else 1
                    S.op('act', lambda e, pb=pb, qb=qb: e.activation(out=qb[:], in_=ps[pb][:, :], func=AF.Square),
                         r=[('ps', pb)], w=[('QB', k2)])
                    S.op('pe', _mm(ps[b3][:, :], cx.ONES[:], qb[:], True, True), r=[('QB', k2), 'CONST'],
                         w=[('ps', b3)])
                    S.op('act', lambda e, b3=b3, rs2=rs2: e.activation(out=rs2[:], in_=ps[b3][:, :], func=AF.Sqrt,
                                                                      bias=cx.EPSC[:, 0:1], scale=1.0 / HD),
                         r=[('ps', b3), 'EPSC'], w=[('RS2', k2)])
                    S.op('dve', lambda e, rs2=rs2: e.reciprocal(out=rs2[:], in_=rs2[:]),
                         r=[('RS2', k2)], w=[('RS2', k2)])
                    S.op('dve', lambda e, pb=pb, qf=qf, rs2=rs2, qcol=qcol: e.scalar_tensor_tensor(
                        out=qf[:], in0=ps[pb][:, :], scalar=QN[:, qcol:qcol + 1], in1=rs2[:],
                        op0=ALU.mult, op1=ALU.mult), r=[('ps', pb), ('RS2', k2), 'QN'], w=[('QF', k2)])
                    S.op('act', lambda e, qb=qb, qf=qf: e.activation(out=qb[:], in_=qf[:], func=AF.Copy),
                         r=[('QF', k2)], w=[('QB', k2)])
                    S.op('pe', _mm(ps[b2][:, :], cx.RC[:], qb[:], True, True), r=[('QB', k2), 'CONST'],
                         w=[('ps', b2)])
                    S.op('dve', lambda e, qf=qf, t1=t1, tsl=tsl: e.tensor_tensor(out=t1[:], in0=qf[:],
                                                                                in1=COSC[:, tsl], op=ALU.mult),
                         r=[('QF', k2), 'ROPE'], w=[('T1', k2)])
                    S.op('dve', lambda e, b2=b2, t2=t2, tsl=tsl: e.tensor_tensor(out=t2[:], in0=ps[b2][:, :],
                                                                                in1=SINC[:, tsl], op=ALU.mult),
                         r=[('ps', b2), 'ROPE'], w=[('T2', k2)])
                    S.op('dve', lambda e, t1=t1, t2=t2, tsl=tsl, ost=ost: e.tensor_tensor(out=ost[:, tsl], in0=t1[:],
                                                                                         in1=t2[:], op=ALU.add),
                         r=[('T1', k2), ('T2', k2)], w=[osttok])
            if kind == 'q':
                dst = dr['QS'][QIDX[(mixer, idx)]]
            elif kind == 'k':
                dst = dr['KS'][KIDX[(mixer, idx)]]
            else:
                dst = dr['GS'][GIDX[(mixer, idx)]]
            S.dma('sp', dst, ost[:], r=[osttok], w=[('DR', kind, mixer, idx)])

        def run_chunks(order):
            base = st8['seq']
            n = len(order)
            for i in range(min(2, n)):
                w_load(base + i, order[i])
            w_cast(base)
            for i, ci in enumerate(order):
                if i + 1 < n:
                    w_cast(base + i + 1)
                if i + 2 < n:
                    w_load(base + i + 2, order[i + 2])
                do_chunk(base + i, ci)
            st8['seq'] = base + n

        order_k = [ci for ci, ch in enumerate(FM_CHUNKS) if ch[0] == 'k']
        order_qg = [ci for ci, ch in enumerate(FM_CHUNKS) if ch[0] != 'k']
        if stop == 3:
            order_k, order_qg = order_k[:2], []
        run_chunks(order_k)
        nld = 0
        for vg in range(3):
            if stop < 5:
                break
            for q4 in range(4):
                ws = WS[nld % 2]
                wstok = ('WS', nld % 2)
                nld += 1
                S.dma('sp', ws[:], dr['wv'][vg][:, q4 * 4:(q4 + 1) * 4, :].rearrange("p k n -> p (k n)"), w=[wstok])
                S.op('dve', lambda e, ws=ws, q4=q4: e.tensor_copy(
                    out=WV[:, q4 * 4:(q4 + 1) * 4, :].rearrange("p k n -> p (k n)"), in_=ws[:]),
                     r=[wstok], w=['WV'])
            for tt in range(16):
                blk = st8['blk']
                st8['blk'] += 1
                pb = blk % 3
                for kc in range(KC):
                    S.op('pe', _mm(ps[pb][:, :], HT[:, kc, tt * 128:(tt + 1) * 128], WV[:, kc, :], kc == 0,
                                   kc == KC - 1),
                         r=['WV', ('HT', tt // 4)], w=[('ps', pb)])
                if tt % 2 == 0:
                    S.op('act', lambda e, pb=pb, tt=tt: e.activation(out=VST[:, tt, :], in_=ps[pb][:, :],
                                                                    func=AF.Copy),
                         r=[('ps', pb)], w=['VST'])
                else:
                    S.op('dve', lambda e, pb=pb, tt=tt: e.tensor_copy(out=VST[:, tt, :], in_=ps[pb][:, :]),
                         r=[('ps', pb)], w=['VST'])
            if isinstance(dr['VS'], list):
                for hf in range(2):
                    S.dma('sp', dr['VS'][vg * 2 + hf].rearrange("(tt p) c -> p tt c", p=128),
                          VST[:, :, hf * 256:(hf + 1) * 256], r=['VST'], w=[('DRV', vg * 2 + hf)])
            else:
                S.dma('sp', dr['VS'][:, vg * 512:(vg + 1) * 512].rearrange("(tt p) c -> p tt c", p=128), VST[:],
                      r=['VST'], w=[('DRV', vg)])
        if 'exch' in dr:
            dr['exch'](cx.KST, cx.VSG)
        run_chunks(order_qg)
        S.barrier(partial=('exch' in dr), keep=XKEEP)


def attn_tile(cx, PT, nq, slots, bias_ap, sbanks, obank, dbank, ocol, toks):
    S, ps = cx.S, cx.ps
    ktoks, qtok, vtoks, btok = toks
    ns = len(slots)
    per_bank = 512 // nq
    nb = (ns + per_bank - 1) // per_bank
    for b in range(nb):
        s0 = b * per_bank
        s1 = min(ns, s0 + per_bank)
        bk = sbanks[b]
        S.op('pe', _mm(ps[bk][:, 0:(s1 - s0) * nq], cx.IDENT[:], bias_ap[:, s0 * nq:s1 * nq], True, False),
             r=['CONST', btok], w=[('ps', bk)])
        for s in range(s0, s1):
            kT, _v, qT = slots[s]
            S.op('pe', _mm(ps[bk][:, (s - s0) * nq:(s - s0 + 1) * nq], kT, qT, False, True),
                 r=ktoks + [qtok], w=[('ps', bk)])
    pt, pttok = PT
    for b in range(nb):
        s0 = b * per_bank
        s1 = min(ns, s0 + per_bank)
        bk = sbanks[b]
        S.op('act', lambda e, bk=bk, s0=s0, s1=s1: e.activation(out=pt[:, s0 * nq:s1 * nq],
                                                               in_=ps[bk][:, 0:(s1 - s0) * nq], func=AF.Exp,
                                                               scale=SCALE),
             r=[('ps', bk)], w=[pttok])
    for s in range(ns):
        _k, v, _q = slots[s]
        S.op('pe', _mm(ps[obank][:, ocol:ocol + nq], v, pt[:, s * nq:(s + 1) * nq], s == 0, s == ns - 1),
             r=[pttok] + vtoks, w=[('ps', obank)])
    for s in range(ns):
        S.op('pe', _mm(ps[dbank][:, ocol:ocol + nq], cx.ONES[:], pt[:, s * nq:(s + 1) * nq], s == 0, s == ns - 1),
             r=[pttok, 'CONST'], w=[('ps', dbank)])


def phase2(cx, dr, BR, l_sink):
    nc, S, ps = cx.nc, cx.S, cx.ps
    with ExitStack() as st:
        KH = st.enter_context(_sbt(nc, "p2_kh", [128, SEQ], BF16))
        VH = st.enter_context(_sbt(nc, "p2_vh", [128, 64, 128], BF16))
        VH4b = [st.enter_context(_sbt(nc, f"p2_vh4{i}", [128, 32, 128], BF16)) for i in range(2)]
        VH16b = [st.enter_context(_sbt(nc, f"p2_vh16{i}", [128, 32, 128], BF16)) for i in range(2)]
        QHb = [st.enter_context(_sbt(nc, f"p2_qh{i}", [128, T], BF16)) for i in range(2)]
        GHb = [st.enter_context(_sbt(nc, f"p2_gh{i}", [128, T], BF16)) for i in range(2)]
        OB = st.enter_context(_sbt(nc, "p2_ob", [128, T], F32))
        DENB = st.enter_context(_sbt(nc, "p2_den", [128, T], F32))
        PTS = [st.enter_context(_sbt(nc, f"p2_pt{i}", [128, 768], BF16)) for i in range(3)]
        MSK = st.enter_context(_sbt(nc, "p2_msk", [128, 7, 384], BF16))
        BDb = [st.enter_context(_sbt(nc, f"p2_bd{i}", [128, D_NSLOT * 128], BF16)) for i in range(2)]
        SK = st.enter_context(_sbt(nc, "p2_sk", [128, 4], F32))
        ES = st.enter_context(_sbt(nc, "p2_es", [128, 4], F32))
        S.dma('pool', MSK[:], dr['masks'], w=['BIAS'])
        S.dma('sp', SK[:], l_sink, w=['SK'])
        S.op('act', lambda e: e.activation(out=ES[:], in_=SK[:], func=AF.Exp), r=['SK'], w=['ES'])
        state = {'pt': 0, 'ob': 0}

        def next_pt():
            i = state['pt']
            state['pt'] = (i + 1) % 3
            return (PTS[i], ('PT', i))

        def obanks():
            i = state['ob']
            state['ob'] = 1 - i
            return 4 + i, 6 + i

        def finalize(job, sink_col=None):
            hidx = QIDX[(job['mixer'], job['h'])]
            GH = GHb[job['qb']]
            if sink_col is not None:
                S.op('dve', lambda e: e.tensor_scalar(out=DENB[:], in0=DENB[:], scalar1=ES[:, sink_col:sink_col + 1],
                                                      scalar2=None, op0=ALU.add), r=['DENB', 'ES'], w=['DENB'])
            S.op('dve', lambda e: e.reciprocal(out=DENB[:], in_=DENB[:]), r=['DENB'], w=['DENB'])
            S.op('dve', lambda e: e.tensor_tensor(out=OB[:], in0=OB[:], in1=DENB[:], op=ALU.mult),
                 r=['OB', 'DENB'], w=['OB'])
            S.op('dve', lambda e: e.tensor_tensor(out=BR[:, hidx, :], in0=OB[:], in1=GH[:], op=ALU.mult),
                 r=['OB', ('GH', job['qb'])], w=[('BR', hidx)])

        def evac(obank, dbank, ov, dv, first):
            pso = ps[obank][:, :]
            psd = ps[dbank][:, :]
            if len(ov.shape) == 3:
                pso = pso.rearrange("p (a b) -> p a b", a=ov.shape[1])
                psd = psd.rearrange("p (a b) -> p a b", a=ov.shape[1])
            if first:
                S.op('dve', lambda e: e.tensor_copy(out=ov, in_=pso), r=[('ps', obank)], w=['OB'])
                S.op('act', lambda e: e.activation(out=dv, in_=psd, func=AF.Copy), r=[('ps', dbank)], w=['DENB'])
            else:
                S.op('dve', lambda e: e.tensor_tensor(out=ov, in0=ov, in1=pso, op=ALU.add),
                     r=[('ps', obank), 'OB'], w=['OB'])
                S.op('dve', lambda e: e.tensor_tensor(out=dv, in0=dv, in1=psd, op=ALU.add),
                     r=[('ps', dbank), 'DENB'], w=['DENB'])

        jobs = []
        for kvh in range(2):
            for g in range(2):
                jobs.append(dict(mixer='C', h=kvh * 2 + g, kvh=kvh, load_kv=(g == 0)))
        for kvh in range(2):
            for g in range(2):
                jobs.append(dict(mixer='A', h=kvh * 2 + g, kvh=kvh, load_kv=(g == 0)))
        for h in range(4):
            jobs.append(dict(mixer='B', h=h, kvh=h, load_kv=True))
        for h in range(4):
            jobs.append(dict(mixer='D', h=h, kvh=h, load_kv=True))
        kvb = 1
        for i, job in enumerate(jobs):
            job['qb'] = i % 2
            if job['load_kv']:
                kvb = 1 - kvb
            job['kvb'] = kvb
            job['kv_late'] = job['mixer'] == 'C' or (i > 0 and jobs[i - 1]['mixer'] == 'C')

        def kv_views(job):
            if job['mixer'] == 'C':
                return KH, VH, [('KH', 0), ('KH', 1)], [('VH', 0), ('VH', 1)]
            b = job['kvb']
            return (KH[:, b * WIN:(b + 1) * WIN], VH[:, b * 32:(b + 1) * 32, :], [('KH', b)], [('VH', b)])

        def emit_load(job, late):
            m, h, q = job['mixer'], job['h'], job['qb']
            is_c = (m == 'C')
            if job['load_kv'] and late == job['kv_late']:
                Kv, Vv, kt, vt = kv_views(job)
                if is_c:
                    kvh = job['kvh']
                    if 'KCr' in dr:
                        for rk in range(4):
                            S.dma('sp', KH[:, rk * T:(rk + 1) * T], dr['KCr'][rk][kvh], r=[('KG', dr['kcc'])], w=kt)
                    else:
                        S.dma('sp', KH[:], dr['KC'][kvh], w=kt)
                    S.dma('sp', VH[:], dr['VC'][:, kvh * 128:(kvh + 1) * 128].rearrange("(t p) c -> p t c", p=128),
                          r=[('VG', dr.get('vcc', 0))], w=vt)
                else:
                    i = KW_IDX[(m, job['kvh'])]
                    vo = VOFF[(m, job['kvh'])]
                    vsrc = dr['VW'][:, vo * 128:(vo + 1) * 128]
                    vwt = [('VWd', wt, vo // 2) for wt in range(32)]
                    S.dma('sp', Kv, dr['KW'][i], r=[('KWd', i, p) for p in range(4)], w=kt)
                    S.dma('sp', Vv, vsrc.rearrange("(t p) c -> p t c", p=128), r=vwt, w=vt)
                    if m == 'B':
                        for r in range(4):
                            S.dma('sp', VH4b[h % 2][:, r * 8:(r + 1) * 8, :],
                                  vsrc.rearrange("(t p r) c -> r p t c", p=128, r=4)[r], r=vwt, w=[('VH4', h % 2)])
                        for r in range(16):
                            S.dma('sp', VH16b[h % 2][:, r * 2:(r + 1) * 2, :],
                                  vsrc.rearrange("(t p r) c -> r p t c", p=128, r=16)[r], r=vwt, w=[('VH16', h % 2)])
                    if m == 'D':
                        S.dma('pool', BDb[h % 2][:], dr['biasD'][h], w=[('BD', h % 2)])
            if not late:
                S.dma('sp', QHb[q][:], dr['QS'][QIDX[(m, h)]], w=[('QH', q)])
                S.dma('sp', GHb[q][:], dr['GS'][GIDX[(m, h)]], w=[('GH', q)])

        def banded(job, mbase, first_pattern):
            Kv, Vv, kt, vt = kv_views(job)
            QH = QHb[job['qb']]
            toks = (kt, ('QH', job['qb']), vt, 'BIAS')
            for g4 in range(4):
                ob, db = obanks()
                for qq in range(4):
                    qt = g4 * 4 + qq
                    cls = 0 if qt == 0 else (2 if qt == 15 else 1)
                    slots = []
                    for s in range(3):
                        wt = 8 + qt - 1 + s
                        slots.append((Kv[:, wt * 128:(wt + 1) * 128], Vv[:, wt, :], QH[:, qt * 128:(qt + 1) * 128]))
                    attn_tile(cx, next_pt(), 128, slots, MSK[:, mbase + cls, :], [qt % 3], ob, db, qq * 128, toks)
                evac(ob, db, OB[:, g4 * 512:(g4 + 1) * 512], DENB[:, g4 * 512:(g4 + 1) * 512], first_pattern)

        def compute(job):
            m, h = job['mixer'], job['h']
            Kv, Vv, kt, vt = kv_views(job)
            QH = QHb[job['qb']]
            qtok = ('QH', job['qb'])
            VH4, VH16, BD = VH4b[h % 2], VH16b[h % 2], BDb[h % 2]
            if m == 'A':
                banded(job, 0, True)
                finalize(job, sink_col=h)
            elif m == 'B':
                banded(job, 3, True)
                for r in range(4):
                    ob, db = obanks()
                    for it in range(4):
                        cls = 0 if it == 0 else (2 if it == 3 else 1)
                        q0 = r + 4 * it * 128
                        qT = QH[:, q0:q0 + 4 * 127 + 1:4]
                        slots = []
                        for s in range(3):
                            wt = 2 + it - 1 + s
                            k0 = r + 512 * wt
                            slots.append((Kv[:, k0:k0 + 4 * 127 + 1:4], VH4[:, r * 8 + wt, :], qT))
                        attn_tile(cx, next_pt(), 128, slots, MSK[:, 3 + cls, :], [it % 3], ob, db, it * 128,
                                  (kt, qtok, [('VH4', h % 2)], 'BIAS'))
                    evac(ob, db, OB[:, r:r + 4 * 511 + 1:4], DENB[:, r:r + 4 * 511 + 1:4], False)
                for r0 in range(0, 16, 4):
                    ob, db = obanks()
                    for rr in range(4):
                        r = r0 + rr
                        qT = QH[:, r:r + 16 * 127 + 1:16]
                        slots = []
                        for s in range(2):
                            k0 = r + 16 * 128 * s
                            slots.append((Kv[:, k0:k0 + 16 * 127 + 1:16], VH16[:, r * 2 + s, :], qT))
                        attn_tile(cx, next_pt(), 128, slots, MSK[:, 6, 0:256], [rr % 3], ob, db, rr * 128,
                                  (kt, qtok, [('VH16', h % 2)], 'BIAS'))
                    ov = OB[:].rearrange("p (i r) -> p r i", r=16)[:, r0:r0 + 4, :]
                    dv = DENB[:].rearrange("p (i r) -> p r i", r=16)[:, r0:r0 + 4, :]
                    evac(ob, db, ov, dv, False)
                finalize(job)
            elif m == 'D':
                for g4 in range(4):
                    ob, db = obanks()
                    for qq in range(4):
                        ul = g4 * 4 + qq
                        cname = {0: 't0', 1: 't1', 14: 't14', 15: 't15'}.get(ul, 'gen')
                        sl0, nsl, da0 = D_CLASSES[cname]
                        slots = []
                        for s in range(nsl):
                            wt = 8 + ul + da0 + s
                            slots.append((Kv[:, wt * 128:(wt + 1) * 128], Vv[:, wt, :],
                                          QH[:, ul * 128:(ul + 1) * 128]))
                        sb = [0, 2] if ul % 2 == 0 else [1, 3]
                        attn_tile(cx, next_pt(), 128, slots, BD[:, sl0 * 128:(sl0 + nsl) * 128], sb, ob, db,
                                  qq * 128, (kt, qtok, vt, ('BD', h % 2)))
                    evac(ob, db, OB[:, g4 * 512:(g4 + 1) * 512], DENB[:, g4 * 512:(g4 + 1) * 512], True)
                finalize(job)
            else:
                for qb in range(4):
                    ob, db = obanks()
                    qT = QH[:, qb * 512:(qb + 1) * 512]
                    NKT = 64
                    pts = {}

                    def qk(kt_):
                        bk = kt_ % 3
                        S.op('pe', _mm(ps[bk][:, :], KH[:, kt_ * 128:(kt_ + 1) * 128], qT, True, True),
                             r=kt + [qtok], w=[('ps', bk)])
                        pt, pttok = next_pt()
                        pts[kt_] = (pt, pttok)
                        S.op('act', lambda e, bk=bk, pt=pt: e.activation(out=pt[:, 0:512], in_=ps[bk][:, :],
                                                                        func=AF.Exp, scale=SCALE),
                             r=[('ps', bk)], w=[pttok])

                    def pv(kt_):
                        pt, pttok = pts.pop(kt_)
                        S.op('pe', _mm(ps[ob][:, :], VH[:, kt_, :], pt[:, 0:512], kt_ == 0, kt_ == NKT - 1),
                             r=[pttok] + vt, w=[('ps', ob)])
                        S.op('pe', _mm(ps[db][:, :], cx.ONES[:], pt[:, 0:512], kt_ == 0, kt_ == NKT - 1),
                             r=[pttok, 'CONST'], w=[('ps', db)])

                    qk(0)
                    qk(1)
                    for kt_ in range(NKT):
                        if kt_ + 2 < NKT:
                            qk(kt_ + 2)
                        pv(kt_)
                    evac(ob, db, OB[:, qb * 512:(qb + 1) * 512], DENB[:, qb * 512:(qb + 1) * 512], True)
                finalize(job)

        emit_load(jobs[0], False)
        for i, job in enumerate(jobs):
            emit_load(job, True)
            if i + 1 < len(jobs):
                emit_load(jobs[i + 1], False)
            compute(job)
        S.barrier()


def phase3(cx, dr, ADA, HT, BR, MX, last_fg=None):
    nc, S, ps = cx.nc, cx.S, cx.ps
    with ExitStack() as st:
        WG = [st.enter_context(_sbt(nc, f"p3_wg{i}", [128, 16, 128], BF16)) for i in range(3)]
        WR = [st.enter_context(_sbt(nc, f"p3_wr{i}", [128, 4, 128], BF16)) for i in range(3)]
        SG = [st.enter_context(_sbt(nc, f"p3_sg{i}", [128, 512], F32)) for i in range(2)]
        TP = [st.enter_context(_sbt(nc, f"p3_tp{i}", [128, 512], F32)) for i in range(2)]
        MIXF = st.enter_context(_sbt(nc, "p3_mixf", [128, 4, 512], F32))
        MIXB = [st.enter_context(_sbt(nc, f"p3_mixb{i}", [128, T], BF16)) for i in range(2)]
        blk = 0
        wi = 0
        for m in range(16):
            mixb = MIXB[m % 2]
            for n in range(4):
                wg, wr = WG[wi % 3], WR[wi % 3]
                wtok = ('W3', wi % 3)
                wi += 1
                S.dma('pool', wg[:], dr['wgm'][m, n], w=[wtok])
                S.dma('pool', wr[:], dr['wbr'][m, n], w=[wtok])
                for tb in range(NTB):
                    pa = blk % 2
                    pg = 2 + blk % 2
                    k2 = blk % 2
                    blk += 1
                    tsl = slice(tb * 512, (tb + 1) * 512)
                    for kc in range(4):
                        S.op('pe', _mm(ps[pa][:, :], wr[:, kc, :], BR[:, n * 4 + kc, tsl], kc == 0, kc == 3),
                             r=[wtok, 'BR'], w=[('ps', pa)])
                    for kc in range(KC):
                        S.op('pe', _mm(ps[pg][:, :], wg[:, kc, :], HT[:, kc, tsl], kc == 0, kc == KC - 1),
                             r=[wtok, 'HT'], w=[('ps', pg)])
                    sg, tp = SG[k2], TP[k2]
                    S.op('act', lambda e, pg=pg, sg=sg: e.activation(out=sg[:], in_=ps[pg][:, :], func=AF.Sigmoid),
                         r=[('ps', pg)], w=[('SG', k2)])
                    if n == 0:
                        S.op('dve', lambda e, pa=pa, sg=sg, tb=tb: e.tensor_tensor(out=MIXF[:, tb, :], in0=ps[pa][:, :],
                                                                                  in1=sg[:], op=ALU.mult),
                             r=[('ps', pa), ('SG', k2)], w=[('MIXF', tb)])
                    else:
                        S.op('dve', lambda e, pa=pa, sg=sg, tp=tp: e.tensor_tensor(out=tp[:], in0=ps[pa][:, :],
                                                                                  in1=sg[:], op=ALU.mult),
                             r=[('ps', pa), ('SG', k2)], w=[('TP', k2)])
                        if n < 3:
                            S.op('dve', lambda e, tp=tp, tb=tb: e.tensor_tensor(out=MIXF[:, tb, :], in0=MIXF[:, tb, :],
                                                                               in1=tp[:], op=ALU.add),
                                 r=[('TP', k2), ('MIXF', tb)], w=[('MIXF', tb)])
                        else:
                            S.op('dve', lambda e, tp=tp, tb=tb, tsl=tsl, mixb=mixb: e.tensor_tensor(
                                out=mixb[:, tsl], in0=MIXF[:, tb, :], in1=tp[:], op=ALU.add),
                                 r=[('TP', k2), ('MIXF', tb)], w=[('MIXB', m % 2)])
            S.dma('sp', dr['MIXS'][m], mixb[:], r=[('MIXB', m % 2)], w=[('DRM', m)])
        S.barrier()
    with ExitStack() as st:
        WO = [st.enter_context(_sbt(nc, f"p3_wo{i}", [128, 16, 128], BF16)) for i in range(3)]
        XR = [st.enter_context(_sbt(nc, f"p3_xr{i}", [128, T], F32)) for i in range(2)]
        for m in range(16):
            S.dma('sp', MX[:, m, :], dr['MIXS'][m], w=[('MX', m)])
        blk = 0
        for mo in range(16):
            wo = WO[mo % 3]
            xr = XR[mo % 2]
            S.dma('pool', wo[:], dr['wout'][mo], w=[('WO', mo % 3)])
            S.dma('sp', xr[:], dr['xT'][:, mo, :], w=[('XR', mo % 2)])
            for tb in range(NTB):
                pb = blk % 4
                blk += 1
                tsl = slice(tb * 512, (tb + 1) * 512)
                for kc in range(KC):
                    S.op('pe', _mm(ps[pb][:, :], wo[:, kc, :], MX[:, kc, tsl], kc == 0, kc == KC - 1),
                         r=[('WO', mo % 3), ('MX', kc)], w=[('ps', pb)])
                S.op('dve', lambda e, pb=pb, xr=xr, tsl=tsl, mo=mo: e.scalar_tensor_tensor(
                    out=xr[:, tsl], in0=ps[pb][:, :], scalar=ADA[:, 32 + mo:33 + mo], in1=xr[:, tsl],
                    op0=ALU.mult, op1=ALU.add), r=[('ps', pb), ('XR', mo % 2), 'ADA'], w=[('XR', mo % 2)])
            S.dma('sp', dr['xTo'][:, mo, :], xr[:], r=[('XR', mo % 2)], w=[('DRX', mo)])
        S.barrier()


def phase_final(cx, dr):
    nc, S, ps = cx.nc, cx.S, cx.ps
    with ExitStack() as st:
        XB = st.enter_context(_sbt(nc, "pf_xb", [128, 16, 512], F32))
        SQ = st.enter_context(_sbt(nc, "pf_sq", [128, 16, 512], BF16))
        RS = st.enter_context(_sbt(nc, "pf_rs", [128, 512], F32))
        FG = st.enter_context(_sbt(nc, "pf_fg", [128, 16], F32))
        YB = [st.enter_context(_sbt(nc, f"pf_yb{i}", [128, 16, 512], F32)) for i in range(1)]
        S.dma('sp', FG[:], dr['fg'], w=['FG'])
        for tb in range(NTB):
            S.dma('sp', XB[:], dr['xT'][:, :, tb * 512:(tb + 1) * 512], w=['XB'])
            norm_block(cx, XB, SQ, RS, tb, None)
            yb = YB[0]
            for c in range(KC):
                S.op('dve', lambda e, c=c, yb=yb: e.scalar_tensor_tensor(out=yb[:, c, :], in0=XB[:, c, :],
                                                                        scalar=FG[:, c:c + 1], in1=RS[:],
                                                                        op0=ALU.mult, op1=ALU.mult),
                     r=['XB', 'RS', 'FG'], w=['YB'])
            S.dma('sp', dr['yT'][:, :, tb * 512:(tb + 1) * 512], yb[:], r=['YB'], w=[('DRY', tb)])
        S.barrier()


def _new_nc():
    return bass.Bass("TRN2", target_bir_lowering=False)


def _setup(nc, es):
    cx = Ctx()
    cx.nc = nc
    cx.S = Sched(nc, es)
    cx.ps = [es.enter_context(nc.psum_tensor(f"psb{i}", [128, 512], F32)) for i in range(8)]
    CM = es.enter_context(_sbt(nc, "cmat_sb", [128, 4, 128], BF16))
    cx.CM = CM
    cx.IDENT = CM[:, 0, :]
    cx.ONES = CM[:, 1, :]
    cx.R1 = CM[:, 2, :]
    cx.RC = CM[:, 3, :]
    cx.EPSC = es.enter_context(_sbt(nc, "epsc_sb", [128, 1], F32))
    cx.S.op('dve', lambda e: e.memset(cx.EPSC[:], EPS), w=['EPSC'])
    return cx


def _din(nc, name, shape, dt=F32):
    return nc.dram_tensor(name, list(shape), dt, kind="ExternalInput").ap()


def _dout(nc, name, shape, dt=F32):
    return nc.dram_tensor(name, list(shape), dt, kind="ExternalOutput").ap()


def _dint(nc, name, shape, dt=F32):
    return nc.dram_tensor(name, list(shape), dt, kind="Internal").ap()


def build_L1(stop=9):
    nc = _new_nc()
    dr = {}
    dr['xT'] = _din(nc, 'xT', [128, 16, T])
    dr['cT'] = _din(nc, 'cT', [128, 16])
    dr['wada'] = _din(nc, 'wada', [12, 128, 16, 512])
    dr['bada'] = _din(nc, 'bada', [128, 48])
    dr['ng'] = _din(nc, 'ng', [128, 16])
    dr['wfm'] = _din(nc, 'wfm', [NFM, 128, 16, 128])
    dr['wv'] = _din(nc, 'wv', [3, 128, 16, 512])
    dr['qkn'] = _din(nc, 'qkn', [128, 2])
    dr['rope'] = _din(nc, 'rope', [4, 128, T])
    cmat = _din(nc, 'cmat', [128, 4, 128])
    dr['QS'] = _dout(nc, 'QS', [16, 128, T], BF16)
    dr['KS'] = _dout(nc, 'KS', [12, 128, T], BF16)
    dr['GS'] = _dout(nc, 'GS', [16, 128, T], BF16)
    dr['VS'] = _dout(nc, 'VS', [T, 1536], BF16)
    HTo = _dout(nc, 'HT', [128, 16, T], BF16)
    ADAo = _dout(nc, 'ADA', [128, 48])
    with ExitStack() as es:
        cx = _setup(nc, es)
        S = cx.S
        S.dma('pool', cx.CM[:], cmat, w=['CONST'])
        ADA = es.enter_context(_sbt(nc, "ADA_sb", [128, 48], F32))
        HT = es.enter_context(_sbt(nc, "HT_sb", [128, 16, T], BF16))
        if stop >= 1:
            phase_ada(cx, dr, ADA[:])
        if stop >= 2:
            phase1(cx, dr, ADA, HT, stop)
        S.dma('sp', HTo, HT[:], r=[('HT', i) for i in range(4)], w=['DHT'])
        S.dma('sp', ADAo, ADA[:], r=['ADA'], w=['DADA'])
        S.barrier()
    return nc


def build_L2():
    nc = _new_nc()
    dr = {}
    dr['xT'] = _din(nc, 'xT', [128, 16, T])
    ADAi = _din(nc, 'ADA', [128, 48])
    HTi = _din(nc, 'HT', [128, 16, T], BF16)
    dr['QS'] = _din(nc, 'QS', [16, 128, T], BF16)
    dr['GS'] = _din(nc, 'GS', [16, 128, T], BF16)
    dr['KW'] = _din(nc, 'KW', [10, 128, WIN], BF16)
    dr['KC'] = _din(nc, 'KC', [2, 128, SEQ], BF16)
    dr['VW'] = _din(nc, 'VW', [WIN, 1536], BF16)
    dr['VC'] = _din(nc, 'VC', [SEQ, 256], BF16)
    dr['masks'] = _din(nc, 'masks', [128, 7, 384])
    dr['biasD'] = _din(nc, 'biasD', [4, 128, D_NSLOT * 128])
    sink = _din(nc, 'sink', [128, 4])
    dr['wgm'] = _din(nc, 'wgm', [16, 4, 128, 16, 128])
    dr['wbr'] = _din(nc, 'wbr', [16, 4, 128, 4, 128])
    dr['wout'] = _din(nc, 'wout', [16, 128, 16, 128])
    cmat = _din(nc, 'cmat', [128, 4, 128])
    dr['MIXS'] = _dint(nc, 'MIXS', [16, 128, T], BF16)
    dr['xTo'] = _dout(nc, 'xTo', [128, 16, T])
    with ExitStack() as es:
        cx = _setup(nc, es)
        S = cx.S
        S.dma('pool', cx.CM[:], cmat, w=['CONST'])
        ADA = es.enter_context(_sbt(nc, "ADA_sb", [128, 48], F32))
        S.dma('sp', ADA[:], ADAi, w=['ADA'])
        BR = es.enter_context(_sbt(nc, "BR_sb", [128, 16, T], BF16))
        phase2(cx, dr, BR, sink)
        HT = es.enter_context(_sbt(nc, "HT_sb", [128, 16, T], BF16))
        S.dma('sp', HT[:], HTi, w=['HT'])
        S.dma('sp', ADA[:], ADAi, w=['ADA'])
        phase3(cx, dr, ADA, HT, BR, BR)
    return nc


def build_L3():
    nc = _new_nc()
    dr = {}
    dr['xT'] = _din(nc, 'xT', [128, 16, T])
    dr['fg'] = _din(nc, 'fg', [128, 16])
    cmat = _din(nc, 'cmat', [128, 4, 128])
    dr['yT'] = _dout(nc, 'yT', [128, 16, T])
    with ExitStack() as es:
        cx = _setup(nc, es)
        cx.S.dma('pool', cx.CM[:], cmat, w=['CONST'])
        phase_final(cx, dr)
    return nc


def fm_vec(v, nchunk):
    return np.ascontiguousarray(np.asarray(v, np.float32).reshape(nchunk, 128).T)


def fm_weight(w, cols):
    K = w.shape[0]
    ws = w[:, cols]
    return np.ascontiguousarray(ws.reshape(K // 128, 128, ws.shape[1]).transpose(1, 0, 2))


def rope_consts(t0):
    pos = np.arange(t0, t0 + T)

    def tables(p, dim):
        inv = (np.float32(10000.0) ** (-np.arange(0, dim, 2, dtype=np.float32) / np.float32(dim))).astype(np.float32)
        ang = p.astype(np.float32)[:, None] * inv[None, :]
        ang = np.concatenate([ang, ang], axis=-1)
        return np.cos(ang).astype(np.float32), np.sin(ang).astype(np.float32)

    c1, s1 = tables(pos, 128)
    cr, sr = tables(pos // 64, 64)
    cc, sc = tables(pos % 64, 64)
    cC = np.concatenate([cr, cc], axis=-1)
    sC = np.concatenate([sr, sc], axis=-1)
    return np.ascontiguousarray(np.stack([c1.T, s1.T, cC.T, sC.T], axis=0))


def const_mats():
    cm = np.zeros((128, 4, 128), np.float32)
    cm[:, 0, :] = np.eye(128, dtype=np.float32)
    cm[:, 1, :] = 1.0
    for m in range(128):
        if m < 64:
            cm[m + 64, 2, m] = -1.0
        else:
            cm[m - 64, 2, m] = 1.0
    for half in (0, 64):
        for mm in range(64):
            m = half + mm
            if mm < 32:
                cm[m + 32, 3, m] = -1.0
            else:
                cm[m - 32, 3, m] = 1.0
    return cm


def band_masks(j):
    m = np.zeros((128, 7, 384), np.float32)
    ki = np.arange(128)[:, None]
    qi = np.arange(128)[None, :]
    for base, reach in ((0, 128), (3, 64)):
        for cls in range(3):
            for s in range(3):
                dk = (s - 1) * 128 + ki - qi
                ok = np.abs(dk) <= reach
                if cls == 0 and s == 0 and j == 0:
                    ok = np.zeros_like(ok)
                if cls == 2 and s == 2 and j == 3:
                    ok = np.zeros_like(ok)
                m[:, base + cls, s * 128:(s + 1) * 128] = np.where(ok, 0.0, NEG)
    for s in range(2):
        kiw = s * 128 + ki
        ok = np.abs(kiw - (64 + qi)) <= 64
        if j == 0:
            ok = ok & (kiw >= 64)
        if j == 3:
            ok = ok & (kiw < 192)
        m[:, 6, s * 128:(s + 1) * 128] = np.where(ok, 0.0, NEG)
    return m


def d_bias_tables(rel_bias, j):
    out = np.full((4, 128, D_NSLOT * 128), NEG, np.float32)
    rows_total = SEQ // 64
    kk = np.arange(128)
    qq = np.arange(128)
    kc = kk % 64
    qc = qq % 64
    cs = np.clip(qc - 8, 0, 48)
    colok = (kc[:, None] >= cs[None, :]) & (kc[:, None] < cs[None, :] + 16)
    dc = np.clip(kc[:, None] - qc[None, :], -15, 15) + 15
    for cname, (sl0, nsl, da0) in D_CLASSES.items():
        ul = {'gen': 4, 't0': 0, 't1': 1, 't14': 14, 't15': 15}[cname]
        u = j * 16 + ul
        qrow = 2 * u + qq // 64
        rs = np.clip(qrow - 4, 0, rows_total - 8)
        for s in range(nsl):
            a = u + da0 + s
            krow = 2 * a + kk // 64
            rowok = (krow[:, None] >= rs[None, :]) & (krow[:, None] < rs[None, :] + 8)
            inseq = (krow >= 0) & (krow < rows_total)
            ok = rowok & colok & inseq[:, None]
            drr = np.clip(krow[:, None] - qrow[None, :], -7, 7) + 7
            for h in range(4):
                b = rel_bias[h][drr, dc]
                out[h, :, (sl0 + s) * 128:(sl0 + s + 1) * 128] = np.where(ok, b, NEG)
    return out


_PROGS = {}


def _prog(name):
    if name not in _PROGS:
        _PROGS[name] = {'L1': build_L1, 'L2': build_L2, 'L3': build_L3}[name]()
    return _PROGS[name]


def kernel_unfused(x, c, norm_g, w_ada, b_ada, w_in, a_sink, c_q_norm, c_k_norm, d_rel_bias, w_gate_merge,
                   w_branch, w_out, final_g, _depth=DEPTH):
    x = np.asarray(x, np.float32)
    cores = list(range(NCORES))
    cm = const_mats()
    xT = []
    for core in cores:
        b, j = core // 4, core % 4
        xs = x[b, j * T:(j + 1) * T, :]
        xT.append(np.ascontiguousarray(xs.reshape(T, 16, 128).transpose(2, 1, 0)))
    ropes = [rope_consts(j * T) for j in range(4)]
    masks = [band_masks(j) for j in range(4)]
    cT = [fm_vec(np.asarray(c)[b], 16) for b in range(2)]
    for l in range(_depth):
        wl = np.asarray(w_in[l], np.float32)
        wfm = np.stack([fm_weight(wl, list(range(c0, c0 + 128))) for (_k, _m, _i, c0) in FM_CHUNKS], axis=0)
        wv = np.stack([fm_weight(wl, cols) for cols in V_COLS], axis=0)
        wa = np.asarray(w_ada[l], np.float32)
        wada = np.stack([fm_weight(wa, list(range(s * 512, (s + 1) * 512))) for s in range(12)], axis=0)
        bada = fm_vec(b_ada[l], 48)
        ng = fm_vec(norm_g[l], 16)
        qkn = np.ascontiguousarray(np.stack([np.asarray(c_q_norm[l], np.float32),
                                             np.asarray(c_k_norm[l], np.float32)], axis=1))
        in1 = [dict(xT=xT[core], cT=cT[core // 4], wada=wada, bada=bada, ng=ng, wfm=wfm, wv=wv, qkn=qkn,
                    rope=ropes[core % 4], cmat=cm) for core in cores]
        r1 = run_bass_kernel_spmd(_prog('L1'), in1, core_ids=cores).results
        del wfm, wv, wada
        in2 = []
        wg = np.asarray(w_gate_merge[l], np.float32)
        wgm = np.ascontiguousarray(
            wg.reshape(16, 128, 4, 16, 128).transpose(3, 2, 1, 0, 4))
        wb = np.asarray(w_branch[l], np.float32)
        wbr = np.ascontiguousarray(wb.reshape(4, 4, 128, 16, 128).transpose(3, 0, 2, 1, 4))
        wo = np.asarray(w_out[l], np.float32)
        wout = np.ascontiguousarray(wo.reshape(16, 128, 16, 128).transpose(2, 1, 0, 3))
        sink = np.ascontiguousarray(np.broadcast_to(np.asarray(a_sink[l], np.float32)[None, :], (128, 4)))
        biasD = [d_bias_tables(np.asarray(d_rel_bias[l], np.float32), j) for j in range(4)]
        for b in range(2):
            KSb = np.concatenate([np.asarray(r1[b * 4 + j]['KS']) for j in range(4)], axis=2)
            VSb = np.concatenate([np.asarray(r1[b * 4 + j]['VS']) for j in range(4)], axis=0)
            kwi = [KIDX[('A', 0)], KIDX[('A', 1)]] + [KIDX[('B', h)] for h in range(4)] + \
                  [KIDX[('D', h)] for h in range(4)]
            vcols = np.arange(1536)
            Kpad = np.zeros((10, 128, SEQ + 2048), KSb.dtype)
            Kpad[:, :, 1024:1024 + SEQ] = KSb[kwi]
            Vpad = np.zeros((SEQ + 2048, 1536), VSb.dtype)
            Vpad[1024:1024 + SEQ] = VSb[:, vcols]
            KCb = np.ascontiguousarray(KSb[[KIDX[('C', 0)], KIDX[('C', 1)]]])
            VCb = np.ascontiguousarray(VSb[:, VOFF[('C', 0)] * 128:(VOFF[('C', 1)] + 1) * 128])
            for j in range(4):
                core = b * 4 + j
                t0 = j * T
                in2.append(dict(xT=xT[core], ADA=np.asarray(r1[core]['ADA']), HT=np.asarray(r1[core]['HT']),
                                QS=np.asarray(r1[core]['QS']), GS=np.asarray(r1[core]['GS']),
                                KW=np.ascontiguousarray(Kpad[:, :, t0:t0 + WIN]), KC=KCb,
                                VW=np.ascontiguousarray(Vpad[t0:t0 + WIN]), VC=VCb,
                                masks=masks[j], biasD=biasD[j], sink=sink, wgm=wgm, wbr=wbr, wout=wout, cmat=cm))
        del r1
        r2 = run_bass_kernel_spmd(_prog('L2'), in2, core_ids=cores).results
        xT = [np.asarray(r2[core]['xTo']) for core in cores]
        del in2, r2
    fg = fm_vec(final_g, 16)
    r3 = run_bass_kernel_spmd(_prog('L3'), [dict(xT=xT[core], fg=fg, cmat=cm) for core in cores],
                              core_ids=cores).results
    out = np.empty((2, SEQ, D), np.float32)
    for core in cores:
        b, j = core // 4, core % 4
        yT = np.asarray(r3[core]['yT'])
        out[b, j * T:(j + 1) * T, :] = yT.transpose(2, 1, 0).reshape(T, D)
    return out


GROUPS = [[0, 1, 2, 3], [4, 5, 6, 7]]
I32 = mybir.dt.int32


def emit_exchange(cx, dr, KS_t, VS_t, KG_t, VG_t, KST, VSG):
    S = cx.S
    kheads = {}
    for (mixer, h), kid in KIDX.items():
        kheads.setdefault(kid // 2, []).append(('DR', 'k', mixer, h))
    kcc, vcc = KIDX[('C', 0)] // 2, VOFF[('C', 0)] // 2
    S.allgather(KS_t[kcc], KG_t[kcc], GROUPS, r=kheads[kcc], w=[('KG', kcc)])
    S.allgather(VS_t[vcc], VG_t[vcc], GROUPS, r=[('DRV', vcc)], w=[('VG', vcc)])
    for c in range(6):
        if c != kcc:
            S.allgather(KS_t[c], KG_t[c], GROUPS, r=kheads[c], w=[('KG', c)])
    for c in range(6):
        if c != vcc:
            S.allgather(VS_t[c], VG_t[c], GROUPS, r=[('DRV', c)], w=[('VG', c)])
    items = []
    n = 0
    for (mixer, h), i in KW_IDX.items():
        kid = KIDX[(mixer, h)]
        KG2 = KG_t[kid // 2].ap().rearrange("r (h t) -> (r h) t", h=2)
        for p in range(4):
            items.append((KST[n % 3], ('KST', n % 3), KG2, cx.KIDXT[:, i * 4 + p:i * 4 + p + 1], ('KG', kid // 2),
                          dr['KW'][i][:, p * 1024:(p + 1) * 1024], ('KWd', i, p)))
            n += 1
    n = 0
    for c in (0, 2, 3, 4, 5):
        VG = VG_t[c].ap()
        for wt in range(32):
            items.append((VSG[n % 4], ('VSG', n % 4), VG, cx.VIDXT[:, wt:wt + 1], ('VG', c),
                          dr['VW'][wt * 128:(wt + 1) * 128, c * 256:(c + 1) * 256], ('VWd', wt, c)))
            n += 1
    pend = []
    for it in items:
        buf, tok, src, idx, srctok, dst, dtok = it
        S.idma(buf[:], src, idx, r=[srctok, 'IDX'], w=[tok])
        pend.append(it)
        if len(pend) > 2:
            b2, t2, _s, _i, _st, d2, dt2 = pend.pop(0)
            S.dma('pool', d2, b2[:], r=[t2], w=[dt2])
    for b2, t2, _s, _i, _st, d2, dt2 in pend:
        S.dma('pool', d2, b2[:], r=[t2], w=[dt2])


def build_fused(depth=DEPTH):
    nc = _new_nc()
    xT_in = _din(nc, 'xT', [128, 16, T])
    cT = _din(nc, 'cT', [128, 16])
    wada = _din(nc, 'wada', [depth, 12, 128, 16, 512])
    bada = _din(nc, 'bada', [depth, 128, 48])
    ng = _din(nc, 'ng', [depth, 128, 16])
    wfm = _din(nc, 'wfm', [depth, NFM, 128, 16, 128])
    wv = _din(nc, 'wv', [depth, 3, 128, 16, 512])
    qkn = _din(nc, 'qkn', [depth, 128, 2])
    rope = _din(nc, 'rope', [4, 128, T])
    cmat = _din(nc, 'cmat', [128, 4, 128])
    masks = _din(nc, 'masks', [128, 7, 384])
    biasD = _din(nc, 'biasD', [depth, 4, 128, D_NSLOT * 128])
    sink = _din(nc, 'sink', [depth, 128, 4])
    wgm = _din(nc, 'wgm', [depth, 16, 4, 128, 16, 128])
    wbr = _din(nc, 'wbr', [depth, 16, 4, 128, 4, 128])
    wout = _din(nc, 'wout', [depth, 16, 128, 16, 128])
    fg = _din(nc, 'fg', [128, 16])
    kidx = _din(nc, 'kidx', [128, 40], I32)
    vidx = _din(nc, 'vidx', [128, 32], I32)
    yT = _dout(nc, 'yT', [128, 16, T])
    XS = [_dint(nc, f'XS{i}', [128, 16, T]) for i in range(2)]
    QS = _dint(nc, 'QS', [16, 128, T], BF16)
    GS = _dint(nc, 'GS', [16, 128, T], BF16)
    HTd = _dint(nc, 'HTd', [128, 16, T], BF16)
    MIXS = _dint(nc, 'MIXS', [16, 128, T], BF16)
    KW = _dint(nc, 'KW', [10, 128, WIN], BF16)
    VW = _dint(nc, 'VW', [WIN, 1536], BF16)
    KS_t = [[nc.dram_tensor(f'KS{i}_{c}', [256, T], BF16) for c in range(6)] for i in range(2)]
    VS_t = [[nc.dram_tensor(f'VS{i}_{c}', [T, 256], BF16) for c in range(6)] for i in range(2)]
    KG_t = [[nc.dram_tensor(f'KG{i}_{c}', [4 * 256, T], BF16) for c in range(6)] for i in range(2)]
    VG_t = [[nc.dram_tensor(f'VG{i}_{c}', [4 * T, 256], BF16) for c in range(6)] for i in range(2)]
    with ExitStack() as es:
        cx = _setup(nc, es)
        S = cx.S
        S.dma('pool', cx.CM[:], cmat, w=['CONST'])
        KIDXT = es.enter_context(_sbt(nc, "kidx_sb", [128, 40], I32))
        VIDXT = es.enter_context(_sbt(nc, "vidx_sb", [128, 32], I32))
        cx.KIDXT, cx.VIDXT = KIDXT, VIDXT
        S.dma('sp', KIDXT[:], kidx, w=['IDX'])
        S.dma('sp', VIDXT[:], vidx, w=['IDX'])
        ADAALL = es.enter_context(_sbt(nc, "ada_all", [128, depth, 48], F32))
        cx.KST = [es.enter_context(_sbt(nc, f"ex_k{i}", [128, 1024], BF16)) for i in range(3)]
        cx.VSG = [es.enter_context(_sbt(nc, f"ex_v{i}", [128, 256], BF16)) for i in range(4)]
        S.barrier()
        for l in range(depth):
            phase_ada(cx, dict(cT=cT, bada=bada[l], wada=wada[l]), ADAALL[:, l, :])
        for l in range(depth):
            par = l % 2
            ADA = ADAALL[:, l, :]
            x_in = xT_in if l == 0 else XS[(l - 1) % 2]
            x_out = XS[l % 2]
            kcc = KIDX[('C', 0)] // 2
            vcc = VOFF[('C', 0)] // 2
            assert KIDX[('C', 0)] % 2 == 0 and VOFF[('C', 0)] % 2 == 0
            KG4 = KG_t[par][kcc].ap().rearrange("(r h p) t -> r h p t", r=4, h=2)
            KSl = [KS_t[par][k // 2].ap().rearrange("(h p) t -> h p t", h=2)[k % 2] for k in range(12)]
            dr = dict(xT=x_in, xTo=x_out, ng=ng[l], wfm=wfm[l], wv=wv[l], qkn=qkn[l], rope=rope,
                      QS=QS, GS=GS, KS=KSl, VS=[t_.ap() for t_ in VS_t[par]],
                      KW=KW, VW=VW, masks=masks, biasD=biasD[l],
                      KCr=[[KG4[rk][kvh] for kvh in range(2)] for rk in range(4)],
                      VC=VG_t[par][VOFF[('C', 0)] // 2].ap(),
                      wgm=wgm[l], wbr=wbr[l], wout=wout[l], MIXS=MIXS, kcc=kcc, vcc=vcc)
            dr['exch'] = (lambda KST, VSG, dr=dr, par=par: emit_exchange(
                cx, dr, KS_t[par], VS_t[par], KG_t[par], VG_t[par], KST, VSG))
            with ExitStack() as ls:
                HT = ls.enter_context(_sbt(nc, f"HT_sb{l}", [128, 16, T], BF16))
                phase1(cx, dr, ADA, HT)
                S.dma('sp', HTd, HT[:], w=['DHT'])
                S.barrier(partial=True, keep=XKEEP)
            with ExitStack() as ls:
                BR = ls.enter_context(_sbt(nc, f"BR_sb{l}", [128, 16, T], BF16))
                phase2(cx, dr, BR, sink[l])
                HT = ls.enter_context(_sbt(nc, f"HT2_sb{l}", [128, 16, T], BF16))
                S.dma('sp', HT[:], HTd, w=['HT'])
                phase3(cx, dr, ADA, HT, BR, BR)
        phase_final(cx, dict(xT=XS[(depth - 1) % 2], fg=fg, yT=yT))
    return nc


def window_index_tables(j):
    kt = np.zeros((128, 40), np.int32)
    p128 = np.arange(128)
    for (mixer, h), i in KW_IDX.items():
        kid = KIDX[(mixer, h)]
        for p in range(4):
            hs = (2 * j - 1 + p) % 8
            rank, half = hs // 2, hs % 2
            assert half == (p + 1) % 2
            kt[:, i * 4 + p] = (rank * 256 + (kid % 2) * 128 + p128) * 2 + half
    vt = np.zeros((128, 32), np.int32)
    for wt in range(32):
        vt[:, wt] = ((j * T - 1024 + wt * 128) % SEQ) + p128
    return kt, vt


_FUSED = {}


def kernel(x, c, norm_g, w_ada, b_ada, w_in, a_sink, c_q_norm, c_k_norm, d_rel_bias, w_gate_merge, w_branch,
           w_out, final_g, _depth=DEPTH):
    x = np.asarray(x, np.float32)
    cores = list(range(NCORES))
    dp = _depth
    cm = const_mats()
    f32 = lambda a: np.asarray(a, np.float32)
    wfm = np.stack([np.stack([fm_weight(f32(w_in[l]), list(range(c0, c0 + 128))) for (_k, _m, _i, c0) in FM_CHUNKS],
                             axis=0) for l in range(dp)], axis=0)
    wv = np.stack([np.stack([fm_weight(f32(w_in[l]), cols) for cols in V_COLS], axis=0) for l in range(dp)], axis=0)
    wada = np.stack([np.stack([fm_weight(f32(w_ada[l]), list(range(s * 512, (s + 1) * 512))) for s in range(12)],
                              axis=0) for l in range(dp)], axis=0)
    bada = np.stack([fm_vec(b_ada[l], 48) for l in range(dp)], axis=0)
    ng = np.stack([fm_vec(norm_g[l], 16) for l in range(dp)], axis=0)
    qkn = np.stack([np.stack([f32(c_q_norm[l]), f32(c_k_norm[l])], axis=1) for l in range(dp)], axis=0)
    wgm = np.stack([f32(w_gate_merge[l]).reshape(16, 128, 4, 16, 128).transpose(3, 2, 1, 0, 4)
                    for l in range(dp)], axis=0)
    wbr = np.stack([f32(w_branch[l]).reshape(4, 4, 128, 16, 128).transpose(3, 0, 2, 1, 4) for l in range(dp)], axis=0)
    wout = np.stack([f32(w_out[l]).reshape(16, 128, 16, 128).transpose(2, 1, 0, 3) for l in range(dp)], axis=0)
    sink = np.stack([np.broadcast_to(f32(a_sink[l])[None, :], (128, 4)) for l in range(dp)], axis=0)
    fg = fm_vec(final_g, 16)
    shared = dict(wada=np.ascontiguousarray(wada), bada=bada, ng=ng, wfm=wfm, wv=wv, qkn=np.ascontiguousarray(qkn),
                  cmat=cm, sink=np.ascontiguousarray(sink), wgm=np.ascontiguousarray(wgm),
                  wbr=np.ascontiguousarray(wbr), wout=np.ascontiguousarray(wout), fg=fg)
    per_j = []
    for j in range(4):
        kt, vt = window_index_tables(j)
        per_j.append(dict(rope=rope_consts(j * T), masks=band_masks(j), kidx=kt, vidx=vt,
                          biasD=np.stack([d_bias_tables(f32(d_rel_bias[l]), j) for l in range(dp)], axis=0)))
    in_maps = []
    for core in cores:
        b, j = core // 4, core % 4
        xs = x[b, j * T:(j + 1) * T, :]
        m = dict(shared)
        m.update(per_j[j])
        m['xT'] = np.ascontiguousarray(xs.reshape(T, 16, 128).transpose(2, 1, 0))
        m['cT'] = fm_vec(np.asarray(c)[b], 16)
        in_maps.append(m)
    if dp not in _FUSED:
        _FUSED[dp] = build_fused(dp)
    res = run_bass_kernel_spmd(_FUSED[dp], in_maps, core_ids=cores).results
    out = np.empty((2, SEQ, D), np.float32)
    for core in cores:
        b, j = core // 4, core % 4
        yT = np.asarray(res[core]['yT'])
        out[b, j * T:(j + 1) * T, :] = yT.transpose(2, 1, 0).reshape(T, D)
    return out
```

```python
import numpy as np
import ml_dtypes
from contextlib import ExitStack
import concourse.bass as bass
import concourse.mybir as mybir
from concourse.bass_utils import run_bass_kernel_spmd

F32 = mybir.dt.float32
BF16 = mybir.dt.bfloat16
AF = mybir.ActivationFunctionType
ALU = mybir.AluOpType
NPBF = ml_dtypes.bfloat16

D = 2048
KC = 16
T = 2048
SEQ = 8192
NTB = 4
HD = 128
WIN = 4096
EPS = 1e-6
SCALE = HD ** -0.5
NEG = -30000.0
DEPTH = 4
NCORES = 8

_MIX_COL0 = {'A': 0, 'B': 1536, 'C': 3584, 'D': 5120}
_MIX_KV = {'A': 2, 'B': 4, 'C': 2, 'D': 4}
FM_CHUNKS = []
QIDX = {}
KIDX = {}
GIDX = {}
_qi = _ki = _gi = 0
for _m in 'ABCD':
    c0 = _MIX_COL0[_m]
    kv = _MIX_KV[_m]
    for h in range(4):
        FM_CHUNKS.append(('q', _m, h, c0 + h * 128)); QIDX[(_m, h)] = _qi; _qi += 1
    for h in range(kv):
        FM_CHUNKS.append(('k', _m, h, c0 + 512 + h * 128)); KIDX[(_m, h)] = _ki; _ki += 1
    for h in range(4):
        FM_CHUNKS.append(('g', _m, h, c0 + 512 + 2 * kv * 128 + h * 128)); GIDX[(_m, h)] = _gi; _gi += 1
NFM = len(FM_CHUNKS)
V_COLS = (list(range(768, 1024)) + list(range(4352, 4608)),
          list(range(2560, 3072)),
          list(range(6144, 6656)))
VOFF = {}
for h in range(2):
    VOFF[('A', h)] = h
    VOFF[('C', h)] = 2 + h
for h in range(4):
    VOFF[('B', h)] = 4 + h
    VOFF[('D', h)] = 8 + h
KW_IDX = {}
_i = 0
for _m, n in (('A', 2), ('B', 4), ('D', 4)):
    for h in range(n):
        KW_IDX[(_m, h)] = _i; _i += 1

D_CLASSES = {'gen': (0, 5, -2), 't0': (5, 6, -2), 't1': (11, 5, -2), 't14': (16, 5, -2), 't15': (21, 6, -3)}
D_NSLOT = 27


class Sched:
    CENG = ('pe', 'act', 'dve', 'pool')
    ALLQ = ('pe', 'act', 'dve', 'pool', 'sp')

    def __init__(self, nc, es, n_dma_sems=48):
        self.nc = nc
        self.es = es
        self.e = {'pe': nc.tensor, 'act': nc.scalar, 'dve': nc.vector, 'pool': nc.gpsimd, 'sp': nc.sync}
        self.dsems = [es.enter_context(nc.semaphore(f"dq{i}")) for i in range(n_dma_sems)]
        self.dcnt = [0] * n_dma_sems
        self.dnext = 0
        self.NP = 16
        self.psems = [es.enter_context(nc.semaphore(f"pq{i}")) for i in range(self.NP)]
        self.pcnt = [0] * self.NP
        self.pnext = 0
        self.epoch = 0
        self.waited = {q: {} for q in self.ALLQ}
        self.ccsem = None
        self.cccnt = 0
        self._new_epoch()

    def _new_epoch(self):
        self.epoch += 1
        self.sems = {e: self.es.enter_context(self.nc.semaphore(f"s{self.epoch}{e}")) for e in self.CENG}
        self.cnt = {e: 0 for e in self.CENG}
        self.lastw = {}
        self.readers = {}

    def _wait(self, q, ev):
        sem, val, key, _src = ev
        w = self.waited[q]
        if w.get(key, 0) >= val:
            return
        self.e[q].wait_ge(sem, val)
        w[key] = val

    def _deps(self, r, w):
        evs = []
        for t in r:
            ev = self.lastw.get(t)
            if ev is not None:
                evs.append(ev)
        for t in w:
            ev = self.lastw.get(t)
            if ev is not None:
                evs.append(ev)
            rd = self.readers.get(t)
            if rd:
                evs.extend(rd.values())
        return evs

    def _record(self, ev, r, w):
        for t in w:
            self.lastw[t] = ev
            self.readers[t] = {}
        for t in r:
            self.readers.setdefault(t, {})[ev[2]] = ev

    def op(self, eng, fn, r=(), w=()):
        pr = [t for t in r if isinstance(t, tuple) and t[0] == 'ps']
        if pr:
            r = [t for t in r if not (isinstance(t, tuple) and t[0] == 'ps')]
            w = list(w) + pr
        for ev in self._deps(r, w):
            if eng == 'pe' and ev[3] == 'pe':
                continue
            self._wait(eng, ev)
        ins = fn(self.e[eng])
        self.cnt[eng] += 1
        ins.then_inc(self.sems[eng], 1)
        ev = (self.sems[eng], self.cnt[eng], ('e', self.epoch, eng), eng)
        self._record(ev, r, w)

    def _dsem(self, q):
        if q == 'pool':
            i = self.pnext
            self.pnext = (i + 1) % self.NP
            self.pcnt[i] += 16
            return self.psems[i], self.pcnt[i], ('p', i)
        i = self.dnext
        self.dnext = (i + 1) % len(self.dsems)
        self.dcnt[i] += 16
        return self.dsems[i], self.dcnt[i], ('d', i)

    def dma(self, q, out, in_, r=(), w=()):
        for ev in self._deps(r, w):
            self._wait(q, ev)
        sem, val, key = self._dsem(q)
        self.e[q].dma_start(out=out, in_=in_).then_inc(sem, 16)
        ev = (sem, val, key, 'dma')
        self._record(ev, r, w)

    def idma(self, out, in_, idx_ap, r=(), w=()):
        q = 'pool'
        for ev in self._deps(r, w):
            self._wait(q, ev)
        sem, val, key = self._dsem(q)
        self.e[q].indirect_dma_start(out=out, out_offset=None, in_=in_,
                                     in_offset=bass.IndirectOffsetOnAxis(ap=idx_ap, axis=0)
                                     ).then_inc(sem, 16)
        ev = (sem, val, key, 'dma')
        self._record(ev, r, w)

    def allgather(self, src_t, dst_t, groups, r=(), w=()):
        q = 'pool'
        if self.ccsem is None:
            self.ccsem = self.es.enter_context(self.nc.semaphore("ccsem"))
        for ev in self._deps(r, w):
            self._wait(q, ev)
        self.cccnt += 1
        self.e[q].collective_compute("AllGather", ALU.bypass, replica_groups=groups,
                                     ins=[src_t.ap().opt()], outs=[dst_t.ap().opt()]).then_inc(self.ccsem, 1)
        ev = (self.ccsem, self.cccnt, ('cc',), 'cc')
        self._record(ev, r, w)

    def barrier(self, partial=False, keep=()):
        evs = [(self.sems[e], self.cnt[e], ('e', self.epoch, e), e) for e in self.CENG if self.cnt[e] > 0]
        evs += [(self.dsems[i], self.dcnt[i], ('d', i), 'dma') for i in range(len(self.dsems)) if self.dcnt[i] > 0]
        if not partial:
            evs += [(self.psems[i], self.pcnt[i], ('p', i), 'dma') for i in range(self.NP) if self.pcnt[i] > 0]
            if self.cccnt:
                evs.append((self.ccsem, self.cccnt, ('cc',), 'cc'))
        for q in self.ALLQ:
            for ev in evs:
                if q == 'pe' and ev[3] == 'pe':
                    continue
                self._wait(q, ev)
        kept = {t: ev for t, ev in self.lastw.items() if isinstance(t, tuple) and t[0] in keep} if partial else {}
        if max(self.cnt.values()) > 10000:
            self._new_epoch()
        else:
            self.lastw = {}
            self.readers = {}
        self.lastw.update(kept)


class Ctx:
    pass


XKEEP = ('KG', 'VG', 'KWd', 'VWd')

_SBT_N = [0]


def _sbt(nc, name, shape, dt):
    _SBT_N[0] += 1
    return nc.sbuf_tensor(f"{name}_u{_SBT_N[0]}", shape, dt)


def _mm(out, lhsT, rhs, start, stop):
    return lambda e: e.matmul(out, lhsT, rhs, start=start, stop=stop)


def phase_ada(cx, dr, ADA):
    nc, S, ps = cx.nc, cx.S, cx.ps
    with ExitStack() as st:
        CT = st.enter_context(_sbt(nc, "ada_ct", [128, 16], F32))
        CS = st.enter_context(_sbt(nc, "ada_cs", [128, 16], F32))
        BA = st.enter_context(_sbt(nc, "ada_b", [128, 48], F32))
        WA = [st.enter_context(_sbt(nc, f"ada_w{i}", [128, 16, 512], F32)) for i in range(2)]
        S.dma('sp', CT[:], dr['cT'], w=['CT'])
        S.dma('sp', BA[:], dr['bada'], w=['BA'])
        S.op('act', lambda e: e.activation(out=CS[:], in_=CT[:], func=AF.Silu), r=['CT'], w=['CS'])
        for s in range(12):
            wb = WA[s % 2]
            S.dma('sp', wb[:], dr['wada'][s], w=[('WA', s % 2)])
            for mm in range(4):
                m = s * 4 + mm
                for kc in range(KC):
                    S.op('pe', _mm(ps[0][:, m:m + 1], wb[:, kc, mm * 128:(mm + 1) * 128], CS[:, kc:kc + 1],
                                   kc == 0, kc == KC - 1),
                         r=[('WA', s % 2), 'CS'], w=[('ps', 0)])
        S.op('dve', lambda e: e.tensor_tensor(out=ADA, in0=ps[0][:, 0:48], in1=BA[:], op=ALU.add),
             r=[('ps', 0), 'BA'], w=['ADA'])
        S.barrier()


def norm_block(cx, XB, SQ, RS, tb, xsrc_tokens):
    nc, S, ps = cx.nc, cx.S, cx.ps
    S.op('act', lambda e: e.activation(out=SQ[:].rearrange("p c t -> p (c t)"),
                                       in_=XB[:].rearrange("p c t -> p (c t)"), func=AF.Square),
         r=['XB'], w=['SQ'])
    for c in range(KC):
        S.op('pe', _mm(ps[7][:, :], cx.ONES[:], SQ[:, c, :], c == 0, c == KC - 1), r=['SQ', 'CONST'], w=[('ps', 7)])
    S.op('act', lambda e: e.activation(out=RS[:], in_=ps[7][:, :], func=AF.Sqrt, bias=cx.EPSC[:, 0:1],
                                       scale=1.0 / D), r=[('ps', 7), 'EPSC'], w=['RS'])
    S.op('dve', lambda e: e.reciprocal(out=RS[:], in_=RS[:]), r=['RS'], w=['RS'])


def phase1(cx, dr, ADA, HT, stop=9):
    nc, S, ps = cx.nc, cx.S, cx.ps
    with ExitStack() as st:
        XB = st.enter_context(_sbt(nc, "p1_xb", [128, 16, 512], F32))
        SQ = st.enter_context(_sbt(nc, "p1_sq", [128, 16, 512], BF16))
        RS = st.enter_context(_sbt(nc, "p1_rs", [128, 512], F32))
        TM = [st.enter_context(_sbt(nc, f"p1_tm{i}", [128, 512], F32)) for i in range(2)]
        NG = st.enter_context(_sbt(nc, "p1_ng", [128, 16], F32))
        GG = st.enter_context(_sbt(nc, "p1_gg", [128, 16], F32))
        S.dma('sp', NG[:], dr['ng'], w=['NG'])
        S.op('dve', lambda e: e.scalar_tensor_tensor(out=GG[:], in0=ADA[:, 16:32], scalar=1.0, in1=NG[:],
                                                     op0=ALU.add, op1=ALU.mult), r=['ADA', 'NG'], w=['GG'])
        for tb in range(NTB):
            S.dma('sp', XB[:], dr['xT'][:, :, tb * 512:(tb + 1) * 512], w=['XB'])
            norm_block(cx, XB, SQ, RS, tb, None)
            for c in range(KC):
                tm = TM[c % 2]
                S.op('dve', lambda e, c=c, tm=tm: e.tensor_tensor(out=tm[:], in0=XB[:, c, :], in1=RS[:], op=ALU.mult),
                     r=['XB', 'RS'], w=[('TM', c % 2)])
                S.op('act', lambda e, c=c, tm=tm: e.activation(out=HT[:, c, tb * 512:(tb + 1) * 512], in_=tm[:],
                                                              func=AF.Identity, bias=ADA[:, c:c + 1],
                                                              scale=GG[:, c:c + 1]),
                     r=[('TM', c % 2), 'GG', 'ADA'], w=[('HT', tb)])
        S.barrier()
    if stop < 3:
        return
    with ExitStack() as st:
        WB = [st.enter_context(_sbt(nc, f"p1_wb{i}", [128, 16, 128], BF16)) for i in range(3)]
        WS = [st.enter_context(_sbt(nc, f"p1_ws{i}", [128, 2048], F32)) for i in range(2)]
        WV = st.enter_context(_sbt(nc, "p1_wv", [128, 16, 512], BF16))
        COS1 = st.enter_context(_sbt(nc, "p1_cos1", [128, T], F32))
        SIN1 = st.enter_context(_sbt(nc, "p1_sin1", [128, T], F32))
        COSC = st.enter_context(_sbt(nc, "p1_cosc", [128, T], F32))
        SINC = st.enter_context(_sbt(nc, "p1_sinc", [128, T], F32))
        QN = st.enter_context(_sbt(nc, "p1_qn", [128, 2], F32))
        OST = [st.enter_context(_sbt(nc, f"p1_ost{i}", [128, T], BF16)) for i in range(2)]
        VST = st.enter_context(_sbt(nc, "p1_vst", [128, 16, 512], BF16))
        QB = [st.enter_context(_sbt(nc, f"p1_qb{i}", [128, 512], BF16)) for i in range(2)]
        QF = [st.enter_context(_sbt(nc, f"p1_qf{i}", [128, 512], F32)) for i in range(2)]
        T1 = [st.enter_context(_sbt(nc, f"p1_t1{i}", [128, 512], F32)) for i in range(2)]
        T2 = [st.enter_context(_sbt(nc, f"p1_t2{i}", [128, 512], F32)) for i in range(2)]
        RS2 = [st.enter_context(_sbt(nc, f"p1_rs2{i}", [128, 512], F32)) for i in range(2)]
        S.dma('sp', COS1[:], dr['rope'][0], w=['ROPE'])
        S.dma('sp', SIN1[:], dr['rope'][1], w=['ROPE'])
        S.dma('sp', COSC[:], dr['rope'][2], w=['ROPE'])
        S.dma('sp', SINC[:], dr['rope'][3], w=['ROPE'])
        S.dma('sp', QN[:], dr['qkn'], w=['QN'])
        st8 = {'blk': 0, 'seq': 0}

        def w_load(seq, ci):
            S.dma('sp', WS[seq % 2][:], dr['wfm'][ci].rearrange("p k n -> p (k n)"), w=[('WS', seq % 2)])

        def w_cast(seq):
            ws, wb = WS[seq % 2], WB[seq % 3]
            S.op('dve', lambda e: e.tensor_copy(out=wb[:].rearrange("p k n -> p (k n)"), in_=ws[:]),
                 r=[('WS', seq % 2)], w=[('WB', seq % 3)])

        def do_chunk(seq, ci):
            kind, mixer, idx, _c0 = FM_CHUNKS[ci]
            wb = WB[seq % 3]
            wbtok = ('WB', seq % 3)
            ost = OST[seq % 2]
            osttok = ('OST', seq % 2)
            for tb in range(NTB):
                blk = st8['blk']
                st8['blk'] += 1
                pb = blk % 3
                b2 = 3 + blk % 2
                b3 = 5 + blk % 2
                k2 = blk % 2
                tsl = slice(tb * 512, (tb + 1) * 512)
                for kc in range(KC):
                    S.op('pe', _mm(ps[pb][:, :], wb[:, kc, :], HT[:, kc, tsl], kc == 0, kc == KC - 1),
                         r=[wbtok, ('HT', tb)], w=[('ps', pb)])
                if kind == 'g':
                    S.op('act', lambda e, pb=pb, tsl=tsl, ost=ost: e.activation(out=ost[:, tsl], in_=ps[pb][:, :],
                                                                               func=AF.Silu),
                         r=[('ps', pb)], w=[osttok])
                elif mixer == 'D':
                    S.op('act', lambda e, pb=pb, tsl=tsl, ost=ost: e.activation(out=ost[:, tsl], in_=ps[pb][:, :],
                                                                               func=AF.Copy),
                         r=[('ps', pb)], w=[osttok])
                elif mixer in 'AB':
                    qb, t1, t2 = QB[k2], T1[k2], T2[k2]
                    S.op('act', lambda e, pb=pb, qb=qb: e.activation(out=qb[:], in_=ps[pb][:, :], func=AF.Copy),
                         r=[('ps', pb)], w=[('QB', k2)])
                    S.op('pe', _mm(ps[b2][:, :], cx.R1[:], qb[:], True, True), r=[('QB', k2), 'CONST'],
                         w=[('ps', b2)])
                    S.op('dve', lambda e, pb=pb, t1=t1, tsl=tsl: e.tensor_tensor(out=t1[:], in0=ps[pb][:, :],
                                                                                in1=COS1[:, tsl], op=ALU.mult),
                         r=[('ps', pb), 'ROPE'], w=[('T1', k2)])
                    S.op('dve', lambda e, b2=b2, t2=t2, tsl=tsl: e.tensor_tensor(out=t2[:], in0=ps[b2][:, :],
                                                                                in1=SIN1[:, tsl], op=ALU.mult),
                         r=[('ps', b2), 'ROPE'], w=[('T2', k2)])
                    S.op('dve', lambda e, t1=t1, t2=t2, tsl=tsl, ost=ost: e.tensor_tensor(out=ost[:, tsl], in0=t1[:],
                                                                                         in1=t2[:], op=ALU.add),
                         r=[('T1', k2), ('T2', k2)], w=[osttok])
                else:
                    qb, qf, t1, t2, rs2 = QB[k2], QF[k2], T1[k2], T2[k2], RS2[k2]
                    qcol = 0 if kind == 'q' else 1
                    S.op('act', lambda e, pb=pb, qb=qb: e.activation(out=qb[:], in_=ps[pb][:, :], func=AF.Square),
                         r=[('ps', pb)], w=[('QB', k2)])
                    S.op('pe', _mm(ps[b3][:, :], cx.ONES[:], qb[:], True, True), r=[('QB', k2), 'CONST'],
                         w=[('ps', b3)])
                    S.op('act', lambda e, b3=b3, rs2=rs2: e.activation(out=rs2[:], in_=ps[b3][:, :], func=AF.Sqrt,
                                                                      bias=cx.EPSC[:, 0:1], scale=1.0 / HD),
                         r=[('ps', b3), 'EPSC'], w=[('RS2', k2)])
                    S.op('dve', lambda e, rs2=rs2: e.reciprocal(out=rs2[:], in_=rs2[:]),
                         r=[('RS2', k2)], w=[('RS2', k2)])
                    S.op('dve', lambda e, pb=pb, qf=qf, rs2=rs2, qcol=qcol: e.scalar_tensor_tensor(
                        out=qf[:], in0=ps[pb][:, :], scalar=QN[:, qcol:qcol + 1], in1=rs2[:],
                        op0=ALU.mult, op1=ALU.mult), r=[('ps', pb), ('RS2', k2), 'QN'], w=[('QF', k2)])
                    S.op('act', lambda e, qb=qb, qf=qf: e.activation(out=qb[:], in_=qf[:], func=AF.Copy),
                         r=[('QF', k2)], w=[('QB', k2)])
                    S.op('pe', _mm(ps[b2][:, :], cx.RC[:], qb[:], True, True), r=[('QB', k2), 'CONST'],
                         w=[('ps', b2)])
                    S.op('dve', lambda e, qf=qf, t1=t1, tsl=tsl: e.tensor_tensor(out=t1[:], in0=qf[:],
                                                                                in1=COSC[:, tsl], op=ALU.mult),
                         r=[('QF', k2), 'ROPE'], w=[('T1', k2)])
                    S.op('dve', lambda e, b2=b2, t2=t2, tsl=tsl: e.tensor_tensor(out=t2[:], in0=ps[b2][:, :],
                                                                                in1=SINC[:, tsl], op=ALU.mult),
                         r=[('ps', b2), 'ROPE'], w=[('T2', k2)])
                    S.op('dve', lambda e, t1=t1, t2=t2, tsl=tsl, ost=ost: e.tensor_tensor(out=ost[:, tsl], in0=t1[:],
                                                                                         in1=t2[:], op=ALU.add),
                         r=[('T1', k2), ('T2', k2)], w=[osttok])
            if kind == 'q':
                dst = dr['QS'][QIDX[(mixer, idx)]]
            elif kind == 'k':
                dst = dr['KS'][KIDX[(mixer, idx)]]
            else:
                dst = dr['GS'][GIDX[(mixer, idx)]]
            S.dma('sp', dst, ost[:], r=[osttok], w=[('DR', kind, mixer, idx)])

        def run_chunks(order):
            base = st8['seq']
            n = len(order)
            for i in range(min(2, n)):
                w_load(base + i, order[i])
            w_cast(base)
            for i, ci in enumerate(order):
                if i + 1 < n:
                    w_cast(base + i + 1)
                if i + 2 < n:
                    w_load(base + i + 2, order[i + 2])
                do_chunk(base + i, ci)
            st8['seq'] = base + n

        order_k = [ci for ci, ch in enumerate(FM_CHUNKS) if ch[0] == 'k']
        order_qg = [ci for ci, ch in enumerate(FM_CHUNKS) if ch[0] != 'k']
        if stop == 3:
            order_k, order_qg = order_k[:2], []
        run_chunks(order_k)
        nld = 0
        for vg in range(3):
            if stop < 5:
                break
            for q4 in range(4):
                ws = WS[nld % 2]
                wstok = ('WS', nld % 2)
                nld += 1
                S.dma('sp', ws[:], dr['wv'][vg][:, q4 * 4:(q4 + 1) * 4, :].rearrange("p k n -> p (k n)"), w=[wstok])
                S.op('dve', lambda e, ws=ws, q4=q4: e.tensor_copy(
                    out=WV[:, q4 * 4:(q4 + 1) * 4, :].rearrange("p k n -> p (k n)"), in_=ws[:]),
                     r=[wstok], w=['WV'])
            for tt in range(16):
                blk = st8['blk']
                st8['blk'] += 1
                pb = blk % 3
                for kc in range(KC):
                    S.op('pe', _mm(ps[pb][:, :], HT[:, kc, tt * 128:(tt + 1) * 128], WV[:, kc, :], kc == 0,
                                   kc == KC - 1),
                         r=['WV', ('HT', tt // 4)], w=[('ps', pb)])
                if tt % 2 == 0:
                    S.op('act', lambda e, pb=pb, tt=tt: e.activation(out=VST[:, tt, :], in_=ps[pb][:, :],
                                                                    func=AF.Copy),
                         r=[('ps', pb)], w=['VST'])
                else:
                    S.op('dve', lambda e, pb=pb, tt=tt: e.tensor_copy(out=VST[:, tt, :], in_=ps[pb][:, :]),
                         r=[('ps', pb)], w=['VST'])
            if isinstance(dr['VS'], list):
                for hf in range(2):
                    S.dma('sp', dr['VS'][vg * 2 + hf].rearrange("(tt p) c -> p tt c", p=128),
                          VST[:, :, hf * 256:(hf + 1) * 256], r=['VST'], w=[('DRV', vg * 2 + hf)])
            else:
                S.dma('sp', dr['VS'][:, vg * 512:(vg + 1) * 512].rearrange("(tt p) c -> p tt c", p=128), VST[:],
                      r=['VST'], w=[('DRV', vg)])
        if 'exch' in dr:
            dr['exch'](cx.KST, cx.VSG)
        run_chunks(order_qg)
        S.barrier(partial=('exch' in dr), keep=XKEEP)


def attn_qk(cx, PT, nq, slots, bias_ap, sbanks, toks):
    S, ps = cx.S, cx.ps
    ktoks, qtok, vtoks, btok = toks
    ns = len(slots)
    per_bank = 512 // nq
    nb = (ns + per_bank - 1) // per_bank
    for b in range(nb):
        s0 = b * per_bank
        s1 = min(ns, s0 + per_bank)
        bk = sbanks[b]
        S.op('pe', _mm(ps[bk][:, 0:(s1 - s0) * nq], cx.IDENT[:], bias_ap[:, s0 * nq:s1 * nq], True, False),
             r=['CONST', btok], w=[('ps', bk)])
        for s in range(s0, s1):
            kT, _v, qT = slots[s]
            S.op('pe', _mm(ps[bk][:, (s - s0) * nq:(s - s0 + 1) * nq], kT, qT, False, True),
                 r=ktoks + [qtok], w=[('ps', bk)])
    pt, pttok = PT
    for b in range(nb):
        s0 = b * per_bank
        s1 = min(ns, s0 + per_bank)
        bk = sbanks[b]
        S.op('act', lambda e, bk=bk, s0=s0, s1=s1: e.activation(out=pt[:, s0 * nq:s1 * nq],
                                                               in_=ps[bk][:, 0:(s1 - s0) * nq], func=AF.Exp,
                                                               scale=SCALE),
             r=[('ps', bk)], w=[pttok])


def attn_pv(cx, PT, nq, slots, obank, dbank, ocol, toks):
    S, ps = cx.S, cx.ps
    ktoks, qtok, vtoks, btok = toks
    ns = len(slots)
    pt, pttok = PT
    for s in range(ns):
        _k, v, _q = slots[s]
        S.op('pe', _mm(ps[obank][:, ocol:ocol + nq], v, pt[:, s * nq:(s + 1) * nq], s == 0, s == ns - 1),
             r=[pttok] + vtoks, w=[('ps', obank)])
    for s in range(ns):
        S.op('pe', _mm(ps[dbank][:, ocol:ocol + nq], cx.ONES[:], pt[:, s * nq:(s + 1) * nq], s == 0, s == ns - 1),
             r=[pttok, 'CONST'], w=[('ps', dbank)])


def phase2(cx, dr, BR, l_sink):
    nc, S, ps = cx.nc, cx.S, cx.ps
    with ExitStack() as st:
        KH = st.enter_context(_sbt(nc, "p2_kh", [128, SEQ], BF16))
        VH = st.enter_context(_sbt(nc, "p2_vh", [128, 64, 128], BF16))
        VH4b = [st.enter_context(_sbt(nc, f"p2_vh4{i}", [128, 32, 128], BF16)) for i in range(2)]
        VH16b = [st.enter_context(_sbt(nc, f"p2_vh16{i}", [128, 32, 128], BF16)) for i in range(2)]
        QHb = [st.enter_context(_sbt(nc, f"p2_qh{i}", [128, T], BF16)) for i in range(2)]
        GHb = [st.enter_context(_sbt(nc, f"p2_gh{i}", [128, T], BF16)) for i in range(2)]
        OB = st.enter_context(_sbt(nc, "p2_ob", [128, T], F32))
        DENB = st.enter_context(_sbt(nc, "p2_den", [128, T], F32))
        PTS = [st.enter_context(_sbt(nc, f"p2_pt{i}", [128, 768], BF16)) for i in range(3)]
        MSK = st.enter_context(_sbt(nc, "p2_msk", [128, 7, 384], BF16))
        BDb = [st.enter_context(_sbt(nc, f"p2_bd{i}", [128, D_NSLOT * 128], BF16)) for i in range(2)]
        SK = st.enter_context(_sbt(nc, "p2_sk", [128, 4], F32))
        ES = st.enter_context(_sbt(nc, "p2_es", [128, 4], F32))
        S.dma('pool', MSK[:], dr['masks'], w=['BIAS'])
        S.dma('sp', SK[:], l_sink, w=['SK'])
        S.op('act', lambda e: e.activation(out=ES[:], in_=SK[:], func=AF.Exp), r=['SK'], w=['ES'])
        state = {'pt': 0, 'ob': 0}

        def next_pt():
            i = state['pt']
            state['pt'] = (i + 1) % 3
            return (PTS[i], ('PT', i))

        def obanks():
            i = state['ob']
            state['ob'] = 1 - i
            return 4 + i, 6 + i

        def finalize(job, sink_col=None):
            hidx = QIDX[(job['mixer'], job['h'])]
            GH = GHb[job['qb']]
            if sink_col is not None:
                S.op('dve', lambda e: e.tensor_scalar(out=DENB[:], in0=DENB[:], scalar1=ES[:, sink_col:sink_col + 1],
                                                      scalar2=None, op0=ALU.add), r=['DENB', 'ES'], w=['DENB'])
            S.op('dve', lambda e: e.reciprocal(out=DENB[:], in_=DENB[:]), r=['DENB'], w=['DENB'])
            S.op('dve', lambda e: e.tensor_tensor(out=OB[:], in0=OB[:], in1=DENB[:], op=ALU.mult),
                 r=['OB', 'DENB'], w=['OB'])
            S.op('dve', lambda e: e.tensor_tensor(out=BR[:, hidx, :], in0=OB[:], in1=GH[:], op=ALU.mult),
                 r=['OB', ('GH', job['qb'])], w=[('BR', hidx)])

        def evac(obank, dbank, ov, dv, first):
            pso = ps[obank][:, :]
            psd = ps[dbank][:, :]
            if len(ov.shape) == 3:
                pso = pso.rearrange("p (a b) -> p a b", a=ov.shape[1])
                psd = psd.rearrange("p (a b) -> p a b", a=ov.shape[1])
            if first:
                S.op('dve', lambda e: e.tensor_copy(out=ov, in_=pso), r=[('ps', obank)], w=['OB'])
                S.op('act', lambda e: e.activation(out=dv, in_=psd, func=AF.Copy), r=[('ps', dbank)], w=['DENB'])
            else:
                S.op('dve', lambda e: e.tensor_tensor(out=ov, in0=ov, in1=pso, op=ALU.add),
                     r=[('ps', obank), 'OB'], w=['OB'])
                S.op('dve', lambda e: e.tensor_tensor(out=dv, in0=dv, in1=psd, op=ALU.add),
                     r=[('ps', dbank), 'DENB'], w=['DENB'])

        jobs = []
        for kvh in range(2):
            for g in range(2):
                jobs.append(dict(mixer='C', h=kvh * 2 + g, kvh=kvh, load_kv=(g == 0)))
        for kvh in range(2):
            for g in range(2):
                jobs.append(dict(mixer='A', h=kvh * 2 + g, kvh=kvh, load_kv=(g == 0)))
        for h in range(4):
            jobs.append(dict(mixer='B', h=h, kvh=h, load_kv=True))
        for h in range(4):
            jobs.append(dict(mixer='D', h=h, kvh=h, load_kv=True))
        kvb = 1
        for i, job in enumerate(jobs):
            job['qb'] = i % 2
            if job['load_kv']:
                kvb = 1 - kvb
            job['kvb'] = kvb
            job['kv_late'] = job['mixer'] == 'C' or (i > 0 and jobs[i - 1]['mixer'] == 'C')

        def kv_views(job):
            if job['mixer'] == 'C':
                return KH, VH, [('KH', 0), ('KH', 1)], [('VH', 0), ('VH', 1)]
            b = job['kvb']
            return (KH[:, b * WIN:(b + 1) * WIN], VH[:, b * 32:(b + 1) * 32, :], [('KH', b)], [('VH', b)])

        def emit_load(job, late):
            m, h, q = job['mixer'], job['h'], job['qb']
            is_c = (m == 'C')
            if job['load_kv'] and late == job['kv_late']:
                Kv, Vv, kt, vt = kv_views(job)
                if is_c:
                    kvh = job['kvh']
                    if 'KCr' in dr:
                        for rk in range(4):
                            S.dma('sp', KH[:, rk * T:(rk + 1) * T], dr['KCr'][rk][kvh], r=[('KG', dr['kcc'])], w=kt)
                    else:
                        S.dma('sp', KH[:], dr['KC'][kvh], w=kt)
                    S.dma('sp', VH[:], dr['VC'][:, kvh * 128:(kvh + 1) * 128].rearrange("(t p) c -> p t c", p=128),
                          r=[('VG', dr.get('vcc', 0))], w=vt)
                else:
                    i = KW_IDX[(m, job['kvh'])]
                    vo = VOFF[(m, job['kvh'])]
                    vsrc = dr['VW'][:, vo * 128:(vo + 1) * 128]
                    vwt = [('VWd', wt, vo // 2) for wt in range(32)]
                    S.dma('sp', Kv, dr['KW'][i], r=[('KWd', i, p) for p in range(4)], w=kt)
                    S.dma('sp', Vv, vsrc.rearrange("(t p) c -> p t c", p=128), r=vwt, w=vt)
                    if m == 'B':
                        for r in range(4):
                            S.dma('sp', VH4b[h % 2][:, r * 8:(r + 1) * 8, :],
                                  vsrc.rearrange("(t p r) c -> r p t c", p=128, r=4)[r], r=vwt, w=[('VH4', h % 2)])
                        for r in range(16):
                            S.dma('sp', VH16b[h % 2][:, r * 2:(r + 1) * 2, :],
                                  vsrc.rearrange("(t p r) c -> r p t c", p=128, r=16)[r], r=vwt, w=[('VH16', h % 2)])
                    if m == 'D':
                        S.dma('pool', BDb[h % 2][:], dr['biasD'][h], w=[('BD', h % 2)])
            if not late:
                S.dma('sp', QHb[q][:], dr['QS'][QIDX[(m, h)]], w=[('QH', q)])
                S.dma('sp', GHb[q][:], dr['GS'][GIDX[(m, h)]], w=[('GH', q)])

        def run_tiles(tiles):
            for t in tiles:
                t['pt'] = None
            n = len(tiles)

            def stage1(t):
                t['pt'] = next_pt()
                attn_qk(cx, t['pt'], 128, t['slots'], t['bias'], t['sb'], t['toks'])

            stage1(tiles[0])
            for i, t in enumerate(tiles):
                if i + 1 < n:
                    stage1(tiles[i + 1])
                attn_pv(cx, t['pt'], 128, t['slots'], t['ob'], t['db'], t['ocol'], t['toks'])
                if t.get('after') is not None:
                    t['after']()

        def banded_tiles(job, mbase, first_pattern):
            Kv, Vv, kt, vt = kv_views(job)
            QH = QHb[job['qb']]
            toks = (kt, ('QH', job['qb']), vt, 'BIAS')
            tiles = []
            for g4 in range(4):
                ob, db = obanks()
                for qq in range(4):
                    qt = g4 * 4 + qq
                    cls = 0 if qt == 0 else (2 if qt == 15 else 1)
                    slots = []
                    for s in range(3):
                        wt = 8 + qt - 1 + s
                        slots.append((Kv[:, wt * 128:(wt + 1) * 128], Vv[:, wt, :], QH[:, qt * 128:(qt + 1) * 128]))
                    tiles.append(dict(slots=slots, bias=MSK[:, mbase + cls, :], sb=[qt % 3], ob=ob, db=db,
                                      ocol=qq * 128, toks=toks))
                tiles[-1]['after'] = (lambda ob=ob, db=db, g4=g4: evac(
                    ob, db, OB[:, g4 * 512:(g4 + 1) * 512], DENB[:, g4 * 512:(g4 + 1) * 512], first_pattern))
            return tiles

        def compute(job):
            m, h = job['mixer'], job['h']
            Kv, Vv, kt, vt = kv_views(job)
            QH = QHb[job['qb']]
            qtok = ('QH', job['qb'])
            VH4, VH16, BD = VH4b[h % 2], VH16b[h % 2], BDb[h % 2]
            if m == 'A':
                run_tiles(banded_tiles(job, 0, True))
                finalize(job, sink_col=h)
            elif m == 'B':
                tiles = banded_tiles(job, 3, True)
                for r in range(4):
                    ob, db = obanks()
                    for it in range(4):
                        cls = 0 if it == 0 else (2 if it == 3 else 1)
                        q0 = r + 4 * it * 128
                        qT = QH[:, q0:q0 + 4 * 127 + 1:4]
                        slots = []
                        for s in range(3):
                            wt = 2 + it - 1 + s
                            k0 = r + 512 * wt
                            slots.append((Kv[:, k0:k0 + 4 * 127 + 1:4], VH4[:, r * 8 + wt, :], qT))
                        tiles.append(dict(slots=slots, bias=MSK[:, 3 + cls, :], sb=[it % 3], ob=ob, db=db,
                                          ocol=it * 128, toks=(kt, qtok, [('VH4', h % 2)], 'BIAS')))
                    tiles[-1]['after'] = (lambda ob=ob, db=db, r=r: evac(
                        ob, db, OB[:, r:r + 4 * 511 + 1:4], DENB[:, r:r + 4 * 511 + 1:4], False))
                for r0 in range(0, 16, 4):
                    ob, db = obanks()
                    for rr in range(4):
                        r = r0 + rr
                        qT = QH[:, r:r + 16 * 127 + 1:16]
                        slots = []
                        for s in range(2):
                            k0 = r + 16 * 128 * s
                            slots.append((Kv[:, k0:k0 + 16 * 127 + 1:16], VH16[:, r * 2 + s, :], qT))
                        tiles.append(dict(slots=slots, bias=MSK[:, 6, 0:256], sb=[rr % 3], ob=ob, db=db,
                                          ocol=rr * 128, toks=(kt, qtok, [('VH16', h % 2)], 'BIAS')))

                    def after16(ob=ob, db=db, r0=r0):
                        ov = OB[:].rearrange("p (i r) -> p r i", r=16)[:, r0:r0 + 4, :]
                        dv = DENB[:].rearrange("p (i r) -> p r i", r=16)[:, r0:r0 + 4, :]
                        evac(ob, db, ov, dv, False)
                    tiles[-1]['after'] = after16
                run_tiles(tiles)
                finalize(job)
            elif m == 'D':
                tiles = []
                for g4 in range(4):
                    ob, db = obanks()
                    for qq in range(4):
                        ul = g4 * 4 + qq
                        cname = {0: 't0', 1: 't1', 14: 't14', 15: 't15'}.get(ul, 'gen')
                        sl0, nsl, da0 = D_CLASSES[cname]
                        slots = []
                        for s in range(nsl):
                            wt = 8 + ul + da0 + s
                            slots.append((Kv[:, wt * 128:(wt + 1) * 128], Vv[:, wt, :],
                                          QH[:, ul * 128:(ul + 1) * 128]))
                        sb = [0, 2] if ul % 2 == 0 else [1, 3]
                        tiles.append(dict(slots=slots, bias=BD[:, sl0 * 128:(sl0 + nsl) * 128], sb=sb, ob=ob, db=db,
                                          ocol=qq * 128, toks=(kt, qtok, vt, ('BD', h % 2))))
                    tiles[-1]['after'] = (lambda ob=ob, db=db, g4=g4: evac(
                        ob, db, OB[:, g4 * 512:(g4 + 1) * 512], DENB[:, g4 * 512:(g4 + 1) * 512], True))
                run_tiles(tiles)
                finalize(job)
            else:
                for qb in range(4):
                    ob, db = obanks()
                    qT = QH[:, qb * 512:(qb + 1) * 512]
                    NKT = 64
                    pts = {}

                    def qk(kt_):
                        bk = kt_ % 3
                        S.op('pe', _mm(ps[bk][:, :], KH[:, kt_ * 128:(kt_ + 1) * 128], qT, True, True),
                             r=kt + [qtok], w=[('ps', bk)])
                        pt, pttok = next_pt()
                        pts[kt_] = (pt, pttok)
                        S.op('act', lambda e, bk=bk, pt=pt: e.activation(out=pt[:, 0:512], in_=ps[bk][:, :],
                                                                        func=AF.Exp, scale=SCALE),
                             r=[('ps', bk)], w=[pttok])

                    def pv(kt_):
                        pt, pttok = pts.pop(kt_)
                        S.op('pe', _mm(ps[ob][:, :], VH[:, kt_, :], pt[:, 0:512], kt_ == 0, kt_ == NKT - 1),
                             r=[pttok] + vt, w=[('ps', ob)])
                        S.op('pe', _mm(ps[db][:, :], cx.ONES[:], pt[:, 0:512], kt_ == 0, kt_ == NKT - 1),
                             r=[pttok, 'CONST'], w=[('ps', db)])

                    qk(0)
                    qk(1)
                    for kt_ in range(NKT):
                        if kt_ + 2 < NKT:
                            qk(kt_ + 2)
                        pv(kt_)
                    evac(ob, db, OB[:, qb * 512:(qb + 1) * 512], DENB[:, qb * 512:(qb + 1) * 512], True)
                finalize(job)

        emit_load(jobs[0], False)
        for i, job in enumerate(jobs):
            emit_load(job, True)
            if i + 1 < len(jobs):
                emit_load(jobs[i + 1], False)
            compute(job)
        S.barrier()


def phase3(cx, dr, ADA, HT, BR, MX, last_fg=None):
    nc, S, ps = cx.nc, cx.S, cx.ps
    with ExitStack() as st:
        WG = [st.enter_context(_sbt(nc, f"p3_wg{i}", [128, 16, 128], BF16)) for i in range(3)]
        WR = [st.enter_context(_sbt(nc, f"p3_wr{i}", [128, 4, 128], BF16)) for i in range(3)]
        SG = [st.enter_context(_sbt(nc, f"p3_sg{i}", [128, 512], F32)) for i in range(2)]
        TP = [st.enter_context(_sbt(nc, f"p3_tp{i}", [128, 512], F32)) for i in range(2)]
        MIXF = st.enter_context(_sbt(nc, "p3_mixf", [128, 4, 512], F32))
        MIXB = [st.enter_context(_sbt(nc, f"p3_mixb{i}", [128, T], BF16)) for i in range(2)]
        blk = 0
        wi = 0
        for m in range(16):
            mixb = MIXB[m % 2]
            for n in range(4):
                wg, wr = WG[wi % 3], WR[wi % 3]
                wtok = ('W3', wi % 3)
                wi += 1
                S.dma('pool', wg[:], dr['wgm'][m, n], w=[wtok])
                S.dma('pool', wr[:], dr['wbr'][m, n], w=[wtok])
                for tb in range(NTB):
                    pa = blk % 2
                    pg = 2 + blk % 2
                    k2 = blk % 2
                    blk += 1
                    tsl = slice(tb * 512, (tb + 1) * 512)
                    for kc in range(4):
                        S.op('pe', _mm(ps[pa][:, :], wr[:, kc, :], BR[:, n * 4 + kc, tsl], kc == 0, kc == 3),
                             r=[wtok, 'BR'], w=[('ps', pa)])
                    for kc in range(KC):
                        S.op('pe', _mm(ps[pg][:, :], wg[:, kc, :], HT[:, kc, tsl], kc == 0, kc == KC - 1),
                             r=[wtok, 'HT'], w=[('ps', pg)])
                    sg, tp = SG[k2], TP[k2]
                    S.op('act', lambda e, pg=pg, sg=sg: e.activation(out=sg[:], in_=ps[pg][:, :], func=AF.Sigmoid),
                         r=[('ps', pg)], w=[('SG', k2)])
                    if n == 0:
                        S.op('dve', lambda e, pa=pa, sg=sg, tb=tb: e.tensor_tensor(out=MIXF[:, tb, :], in0=ps[pa][:, :],
                                                                                  in1=sg[:], op=ALU.mult),
                             r=[('ps', pa), ('SG', k2)], w=[('MIXF', tb)])
                    else:
                        S.op('dve', lambda e, pa=pa, sg=sg, tp=tp: e.tensor_tensor(out=tp[:], in0=ps[pa][:, :],
                                                                                  in1=sg[:], op=ALU.mult),
                             r=[('ps', pa), ('SG', k2)], w=[('TP', k2)])
                        if n < 3:
                            S.op('dve', lambda e, tp=tp, tb=tb: e.tensor_tensor(out=MIXF[:, tb, :], in0=MIXF[:, tb, :],
                                                                               in1=tp[:], op=ALU.add),
                                 r=[('TP', k2), ('MIXF', tb)], w=[('MIXF', tb)])
                        else:
                            S.op('dve', lambda e, tp=tp, tb=tb, tsl=tsl, mixb=mixb: e.tensor_tensor(
                                out=mixb[:, tsl], in0=MIXF[:, tb, :], in1=tp[:], op=ALU.add),
                                 r=[('TP', k2), ('MIXF', tb)], w=[('MIXB', m % 2)])
            S.dma('sp', dr['MIXS'][m], mixb[:], r=[('MIXB', m % 2)], w=[('DRM', m)])
        S.barrier()
    with ExitStack() as st:
        WO = [st.enter_context(_sbt(nc, f"p3_wo{i}", [128, 16, 128], BF16)) for i in range(3)]
        XR = [st.enter_context(_sbt(nc, f"p3_xr{i}", [128, T], F32)) for i in range(2)]
        for m in range(16):
            S.dma('sp', MX[:, m, :], dr['MIXS'][m], w=[('MX', m)])
        blk = 0
        for mo in range(16):
            wo = WO[mo % 3]
            xr = XR[mo % 2]
            S.dma('pool', wo[:], dr['wout'][mo], w=[('WO', mo % 3)])
            S.dma('sp', xr[:], dr['xT'][:, mo, :], w=[('XR', mo % 2)])
            for tb in range(NTB):
                pb = blk % 4
                blk += 1
                tsl = slice(tb * 512, (tb + 1) * 512)
                for kc in range(KC):
                    S.op('pe', _mm(ps[pb][:, :], wo[:, kc, :], MX[:, kc, tsl], kc == 0, kc == KC - 1),
                         r=[('WO', mo % 3), ('MX', kc)], w=[('ps', pb)])
                S.op('dve', lambda e, pb=pb, xr=xr, tsl=tsl, mo=mo: e.scalar_tensor_tensor(
                    out=xr[:, tsl], in0=ps[pb][:, :], scalar=ADA[:, 32 + mo:33 + mo], in1=xr[:, tsl],
                    op0=ALU.mult, op1=ALU.add), r=[('ps', pb), ('XR', mo % 2), 'ADA'], w=[('XR', mo % 2)])
            S.dma('sp', dr['xTo'][:, mo, :], xr[:], r=[('XR', mo % 2)], w=[('DRX', mo)])
        S.barrier()


def phase_final(cx, dr):
    nc, S, ps = cx.nc, cx.S, cx.ps
    with ExitStack() as st:
        XB = st.enter_context(_sbt(nc, "pf_xb", [128, 16, 512], F32))
        SQ = st.enter_context(_sbt(nc, "pf_sq", [128, 16, 512], BF16))
        RS = st.enter_context(_sbt(nc, "pf_rs", [128, 512], F32))
        FG = st.enter_context(_sbt(nc, "pf_fg", [128, 16], F32))
        YB = [st.enter_context(_sbt(nc, f"pf_yb{i}", [128, 16, 512], F32)) for i in range(1)]
        S.dma('sp', FG[:], dr['fg'], w=['FG'])
        for tb in range(NTB):
            S.dma('sp', XB[:], dr['xT'][:, :, tb * 512:(tb + 1) * 512], w=['XB'])
            norm_block(cx, XB, SQ, RS, tb, None)
            yb = YB[0]
            for c in range(KC):
                S.op('dve', lambda e, c=c, yb=yb: e.scalar_tensor_tensor(out=yb[:, c, :], in0=XB[:, c, :],
                                                                        scalar=FG[:, c:c + 1], in1=RS[:],
                                                                        op0=ALU.mult, op1=ALU.mult),
                     r=['XB', 'RS', 'FG'], w=['YB'])
            S.dma('sp', dr['yT'][:, :, tb * 512:(tb + 1) * 512], yb[:], r=['YB'], w=[('DRY', tb)])
        S.barrier()


def _new_nc():
    return bass.Bass("TRN2", target_bir_lowering=False)


def _setup(nc, es):
    cx = Ctx()
    cx.nc = nc
    cx.S = Sched(nc, es)
    cx.ps = [es.enter_context(nc.psum_tensor(f"psb{i}", [128, 512], F32)) for i in range(8)]
    CM = es.enter_context(_sbt(nc, "cmat_sb", [128, 4, 128], BF16))
    cx.CM = CM
    cx.IDENT = CM[:, 0, :]
    cx.ONES = CM[:, 1, :]
    cx.R1 = CM[:, 2, :]
    cx.RC = CM[:, 3, :]
    cx.EPSC = es.enter_context(_sbt(nc, "epsc_sb", [128, 1], F32))
    cx.S.op('dve', lambda e: e.memset(cx.EPSC[:], EPS), w=['EPSC'])
    return cx


def _din(nc, name, shape, dt=F32):
    return nc.dram_tensor(name, list(shape), dt, kind="ExternalInput").ap()


def _dout(nc, name, shape, dt=F32):
    return nc.dram_tensor(name, list(shape), dt, kind="ExternalOutput").ap()


def _dint(nc, name, shape, dt=F32):
    return nc.dram_tensor(name, list(shape), dt, kind="Internal").ap()


def build_L1(stop=9):
    nc = _new_nc()
    dr = {}
    dr['xT'] = _din(nc, 'xT', [128, 16, T])
    dr['cT'] = _din(nc, 'cT', [128, 16])
    dr['wada'] = _din(nc, 'wada', [12, 128, 16, 512])
    dr['bada'] = _din(nc, 'bada', [128, 48])
    dr['ng'] = _din(nc, 'ng', [128, 16])
    dr['wfm'] = _din(nc, 'wfm', [NFM, 128, 16, 128])
    dr['wv'] = _din(nc, 'wv', [3, 128, 16, 512])
    dr['qkn'] = _din(nc, 'qkn', [128, 2])
    dr['rope'] = _din(nc, 'rope', [4, 128, T])
    cmat = _din(nc, 'cmat', [128, 4, 128])
    dr['QS'] = _dout(nc, 'QS', [16, 128, T], BF16)
    dr['KS'] = _dout(nc, 'KS', [12, 128, T], BF16)
    dr['GS'] = _dout(nc, 'GS', [16, 128, T], BF16)
    dr['VS'] = _dout(nc, 'VS', [T, 1536], BF16)
    HTo = _dout(nc, 'HT', [128, 16, T], BF16)
    ADAo = _dout(nc, 'ADA', [128, 48])
    with ExitStack() as es:
        cx = _setup(nc, es)
        S = cx.S
        S.dma('pool', cx.CM[:], cmat, w=['CONST'])
        ADA = es.enter_context(_sbt(nc, "ADA_sb", [128, 48], F32))
        HT = es.enter_context(_sbt(nc, "HT_sb", [128, 16, T], BF16))
        if stop >= 1:
            phase_ada(cx, dr, ADA[:])
        if stop >= 2:
            phase1(cx, dr, ADA, HT, stop)
        S.dma('sp', HTo, HT[:], r=[('HT', i) for i in range(4)], w=['DHT'])
        S.dma('sp', ADAo, ADA[:], r=['ADA'], w=['DADA'])
        S.barrier()
    return nc


def build_L2():
    nc = _new_nc()
    dr = {}
    dr['xT'] = _din(nc, 'xT', [128, 16, T])
    ADAi = _din(nc, 'ADA', [128, 48])
    HTi = _din(nc, 'HT', [128, 16, T], BF16)
    dr['QS'] = _din(nc, 'QS', [16, 128, T], BF16)
    dr['GS'] = _din(nc, 'GS', [16, 128, T], BF16)
    dr['KW'] = _din(nc, 'KW', [10, 128, WIN], BF16)
    dr['KC'] = _din(nc, 'KC', [2, 128, SEQ], BF16)
    dr['VW'] = _din(nc, 'VW', [WIN, 1536], BF16)
    dr['VC'] = _din(nc, 'VC', [SEQ, 256], BF16)
    dr['masks'] = _din(nc, 'masks', [128, 7, 384])
    dr['biasD'] = _din(nc, 'biasD', [4, 128, D_NSLOT * 128])
    sink = _din(nc, 'sink', [128, 4])
    dr['wgm'] = _din(nc, 'wgm', [16, 4, 128, 16, 128])
    dr['wbr'] = _din(nc, 'wbr', [16, 4, 128, 4, 128])
    dr['wout'] = _din(nc, 'wout', [16, 128, 16, 128])
    cmat = _din(nc, 'cmat', [128, 4, 128])
    dr['MIXS'] = _dint(nc, 'MIXS', [16, 128, T], BF16)
    dr['xTo'] = _dout(nc, 'xTo', [128, 16, T])
    with ExitStack() as es:
        cx = _setup(nc, es)
        S = cx.S
        S.dma('pool', cx.CM[:], cmat, w=['CONST'])
        ADA = es.enter_context(_sbt(nc, "ADA_sb", [128, 48], F32))
        S.dma('sp', ADA[:], ADAi, w=['ADA'])
        BR = es.enter_context(_sbt(nc, "BR_sb", [128, 16, T], BF16))
        phase2(cx, dr, BR, sink)
        HT = es.enter_context(_sbt(nc, "HT_sb", [128, 16, T], BF16))
        S.dma('sp', HT[:], HTi, w=['HT'])
        S.dma('sp', ADA[:], ADAi, w=['ADA'])
        phase3(cx, dr, ADA, HT, BR, BR)
    return nc


def build_L3():
    nc = _new_nc()
    dr = {}
    dr['xT'] = _din(nc, 'xT', [128, 16, T])
    dr['fg'] = _din(nc, 'fg', [128, 16])
    cmat = _din(nc, 'cmat', [128, 4, 128])
    dr['yT'] = _dout(nc, 'yT', [128, 16, T])
    with ExitStack() as es:
        cx = _setup(nc, es)
        cx.S.dma('pool', cx.CM[:], cmat, w=['CONST'])
        phase_final(cx, dr)
    return nc


def fm_vec(v, nchunk):
    return np.ascontiguousarray(np.asarray(v, np.float32).reshape(nchunk, 128).T)


def fm_weight(w, cols):
    K = w.shape[0]
    ws = w[:, cols]
    return np.ascontiguousarray(ws.reshape(K // 128, 128, ws.shape[1]).transpose(1, 0, 2))


def rope_consts(t0):
    pos = np.arange(t0, t0 + T)

    def tables(p, dim):
        inv = (np.float32(10000.0) ** (-np.arange(0, dim, 2, dtype=np.float32) / np.float32(dim))).astype(np.float32)
        ang = p.astype(np.float32)[:, None] * inv[None, :]
        ang = np.concatenate([ang, ang], axis=-1)
        return np.cos(ang).astype(np.float32), np.sin(ang).astype(np.float32)

    c1, s1 = tables(pos, 128)
    cr, sr = tables(pos // 64, 64)
    cc, sc = tables(pos % 64, 64)
    cC = np.concatenate([cr, cc], axis=-1)
    sC = np.concatenate([sr, sc], axis=-1)
    return np.ascontiguousarray(np.stack([c1.T, s1.T, cC.T, sC.T], axis=0))


def const_mats():
    cm = np.zeros((128, 4, 128), np.float32)
    cm[:, 0, :] = np.eye(128, dtype=np.float32)
    cm[:, 1, :] = 1.0
    for m in range(128):
        if m < 64:
            cm[m + 64, 2, m] = -1.0
        else:
            cm[m - 64, 2, m] = 1.0
    for half in (0, 64):
        for mm in range(64):
            m = half + mm
            if mm < 32:
                cm[m + 32, 3, m] = -1.0
            else:
                cm[m - 32, 3, m] = 1.0
    return cm


def band_masks(j):
    m = np.zeros((128, 7, 384), np.float32)
    ki = np.arange(128)[:, None]
    qi = np.arange(128)[None, :]
    for base, reach in ((0, 128), (3, 64)):
        for cls in range(3):
            for s in range(3):
                dk = (s - 1) * 128 + ki - qi
                ok = np.abs(dk) <= reach
                if cls == 0 and s == 0 and j == 0:
                    ok = np.zeros_like(ok)
                if cls == 2 and s == 2 and j == 3:
                    ok = np.zeros_like(ok)
                m[:, base + cls, s * 128:(s + 1) * 128] = np.where(ok, 0.0, NEG)
    for s in range(2):
        kiw = s * 128 + ki
        ok = np.abs(kiw - (64 + qi)) <= 64
        if j == 0:
            ok = ok & (kiw >= 64)
        if j == 3:
            ok = ok & (kiw < 192)
        m[:, 6, s * 128:(s + 1) * 128] = np.where(ok, 0.0, NEG)
    return m


def d_bias_tables(rel_bias, j):
    out = np.full((4, 128, D_NSLOT * 128), NEG, np.float32)
    rows_total = SEQ // 64
    kk = np.arange(128)
    qq = np.arange(128)
    kc = kk % 64
    qc = qq % 64
    cs = np.clip(qc - 8, 0, 48)
    colok = (kc[:, None] >= cs[None, :]) & (kc[:, None] < cs[None, :] + 16)
    dc = np.clip(kc[:, None] - qc[None, :], -15, 15) + 15
    for cname, (sl0, nsl, da0) in D_CLASSES.items():
        ul = {'gen': 4, 't0': 0, 't1': 1, 't14': 14, 't15': 15}[cname]
        u = j * 16 + ul
        qrow = 2 * u + qq // 64
        rs = np.clip(qrow - 4, 0, rows_total - 8)
        for s in range(nsl):
            a = u + da0 + s
            krow = 2 * a + kk // 64
            rowok = (krow[:, None] >= rs[None, :]) & (krow[:, None] < rs[None, :] + 8)
            inseq = (krow >= 0) & (krow < rows_total)
            ok = rowok & colok & inseq[:, None]
            drr = np.clip(krow[:, None] - qrow[None, :], -7, 7) + 7
            for h in range(4):
                b = rel_bias[h][drr, dc]
                out[h, :, (sl0 + s) * 128:(sl0 + s + 1) * 128] = np.where(ok, b, NEG)
    return out


_PROGS = {}


def _prog(name):
    if name not in _PROGS:
        _PROGS[name] = {'L1': build_L1, 'L2': build_L2, 'L3': build_L3}[name]()
    return _PROGS[name]


def kernel_unfused(x, c, norm_g, w_ada, b_ada, w_in, a_sink, c_q_norm, c_k_norm, d_rel_bias, w_gate_merge,
                   w_branch, w_out, final_g, _depth=DEPTH):
    x = np.asarray(x, np.float32)
    cores = list(range(NCORES))
    cm = const_mats()
    xT = []
    for core in cores:
        b, j = core // 4, core % 4
        xs = x[b, j * T:(j + 1) * T, :]
        xT.append(np.ascontiguousarray(xs.reshape(T, 16, 128).transpose(2, 1, 0)))
    ropes = [rope_consts(j * T) for j in range(4)]
    masks = [band_masks(j) for j in range(4)]
    cT = [fm_vec(np.asarray(c)[b], 16) for b in range(2)]
    for l in range(_depth):
        wl = np.asarray(w_in[l], np.float32)
        wfm = np.stack([fm_weight(wl, list(range(c0, c0 + 128))) for (_k, _m, _i, c0) in FM_CHUNKS], axis=0)
        wv = np.stack([fm_weight(wl, cols) for cols in V_COLS], axis=0)
        wa = np.asarray(w_ada[l], np.float32)
        wada = np.stack([fm_weight(wa, list(range(s * 512, (s + 1) * 512))) for s in range(12)], axis=0)
        bada = fm_vec(b_ada[l], 48)
        ng = fm_vec(norm_g[l], 16)
        qkn = np.ascontiguousarray(np.stack([np.asarray(c_q_norm[l], np.float32),
                                             np.asarray(c_k_norm[l], np.float32)], axis=1))
        in1 = [dict(xT=xT[core], cT=cT[core // 4], wada=wada, bada=bada, ng=ng, wfm=wfm, wv=wv, qkn=qkn,
                    rope=ropes[core % 4], cmat=cm) for core in cores]
        r1 = run_bass_kernel_spmd(_prog('L1'), in1, core_ids=cores).results
        del wfm, wv, wada
        in2 = []
        wg = np.asarray(w_gate_merge[l], np.float32)
        wgm = np.ascontiguousarray(
            wg.reshape(16, 128, 4, 16, 128).transpose(3, 2, 1, 0, 4))
        wb = np.asarray(w_branch[l], np.float32)
        wbr = np.ascontiguousarray(wb.reshape(4, 4, 128, 16, 128).transpose(3, 0, 2, 1, 4))
        wo = np.asarray(w_out[l], np.float32)
        wout = np.ascontiguousarray(wo.reshape(16, 128, 16, 128).transpose(2, 1, 0, 3))
        sink = np.ascontiguousarray(np.broadcast_to(np.asarray(a_sink[l], np.float32)[None, :], (128, 4)))
        biasD = [d_bias_tables(np.asarray(d_rel_bias[l], np.float32), j) for j in range(4)]
        for b in range(2):
            KSb = np.concatenate([np.asarray(r1[b * 4 + j]['KS']) for j in range(4)], axis=2)
            VSb = np.concatenate([np.asarray(r1[b * 4 + j]['VS']) for j in range(4)], axis=0)
            kwi = [KIDX[('A', 0)], KIDX[('A', 1)]] + [KIDX[('B', h)] for h in range(4)] + \
                  [KIDX[('D', h)] for h in range(4)]
            vcols = np.arange(1536)
            Kpad = np.zeros((10, 128, SEQ + 2048), KSb.dtype)
            Kpad[:, :, 1024:1024 + SEQ] = KSb[kwi]
            Vpad = np.zeros((SEQ + 2048, 1536), VSb.dtype)
            Vpad[1024:1024 + SEQ] = VSb[:, vcols]
            KCb = np.ascontiguousarray(KSb[[KIDX[('C', 0)], KIDX[('C', 1)]]])
            VCb = np.ascontiguousarray(VSb[:, VOFF[('C', 0)] * 128:(VOFF[('C', 1)] + 1) * 128])
            for j in range(4):
                core = b * 4 + j
                t0 = j * T
                in2.append(dict(xT=xT[core], ADA=np.asarray(r1[core]['ADA']), HT=np.asarray(r1[core]['HT']),
                                QS=np.asarray(r1[core]['QS']), GS=np.asarray(r1[core]['GS']),
                                KW=np.ascontiguousarray(Kpad[:, :, t0:t0 + WIN]), KC=KCb,
                                VW=np.ascontiguousarray(Vpad[t0:t0 + WIN]), VC=VCb,
                                masks=masks[j], biasD=biasD[j], sink=sink, wgm=wgm, wbr=wbr, wout=wout, cmat=cm))
        del r1
        r2 = run_bass_kernel_spmd(_prog('L2'), in2, core_ids=cores).results
        xT = [np.asarray(r2[core]['xTo']) for core in cores]
        del in2, r2
    fg = fm_vec(final_g, 16)
    r3 = run_bass_kernel_spmd(_prog('L3'), [dict(xT=xT[core], fg=fg, cmat=cm) for core in cores],
                              core_ids=cores).results
    out = np.empty((2, SEQ, D), np.float32)
    for core in cores:
        b, j = core // 4, core % 4
        yT = np.asarray(r3[core]['yT'])
        out[b, j * T:(j + 1) * T, :] = yT.transpose(2, 1, 0).reshape(T, D)
    return out


GROUPS = [[0, 1, 2, 3], [4, 5, 6, 7]]
I32 = mybir.dt.int32


def emit_exchange(cx, dr, KS_t, VS_t, KG_t, VG_t, KST, VSG):
    S = cx.S
    kheads = {}
    for (mixer, h), kid in KIDX.items():
        kheads.setdefault(kid // 2, []).append(('DR', 'k', mixer, h))
    kcc, vcc = KIDX[('C', 0)] // 2, VOFF[('C', 0)] // 2
    S.allgather(KS_t[kcc], KG_t[kcc], GROUPS, r=kheads[kcc], w=[('KG', kcc)])
    S.allgather(VS_t[vcc], VG_t[vcc], GROUPS, r=[('DRV', vcc)], w=[('VG', vcc)])
    for c in range(6):
        if c != kcc:
            S.allgather(KS_t[c], KG_t[c], GROUPS, r=kheads[c], w=[('KG', c)])
    for c in range(6):
        if c != vcc:
            S.allgather(VS_t[c], VG_t[c], GROUPS, r=[('DRV', c)], w=[('VG', c)])
    items = []
    n = 0
    for (mixer, h), i in KW_IDX.items():
        kid = KIDX[(mixer, h)]
        KG2 = KG_t[kid // 2].ap().rearrange("r (h t) -> (r h) t", h=2)
        for p in range(4):
            items.append((KST[n % 3], ('KST', n % 3), KG2, cx.KIDXT[:, i * 4 + p:i * 4 + p + 1], ('KG', kid // 2),
                          dr['KW'][i][:, p * 1024:(p + 1) * 1024], ('KWd', i, p)))
            n += 1
    n = 0
    for c in (0, 2, 3, 4, 5):
        VG = VG_t[c].ap()
        for wt in range(32):
            items.append((VSG[n % 4], ('VSG', n % 4), VG, cx.VIDXT[:, wt:wt + 1], ('VG', c),
                          dr['VW'][wt * 128:(wt + 1) * 128, c * 256:(c + 1) * 256], ('VWd', wt, c)))
            n += 1
    pend = []
    for it in items:
        buf, tok, src, idx, srctok, dst, dtok = it
        S.idma(buf[:], src, idx, r=[srctok, 'IDX'], w=[tok])
        pend.append(it)
        if len(pend) > 2:
            b2, t2, _s, _i, _st, d2, dt2 = pend.pop(0)
            S.dma('pool', d2, b2[:], r=[t2], w=[dt2])
    for b2, t2, _s, _i, _st, d2, dt2 in pend:
        S.dma('pool', d2, b2[:], r=[t2], w=[dt2])


def build_fused(depth=DEPTH):
    nc = _new_nc()
    xT_in = _din(nc, 'xT', [128, 16, T])
    cT = _din(nc, 'cT', [128, 16])
    wada = _din(nc, 'wada', [depth, 12, 128, 16, 512])
    bada = _din(nc, 'bada', [depth, 128, 48])
    ng = _din(nc, 'ng', [depth, 128, 16])
    wfm = _din(nc, 'wfm', [depth, NFM, 128, 16, 128])
    wv = _din(nc, 'wv', [depth, 3, 128, 16, 512])
    qkn = _din(nc, 'qkn', [depth, 128, 2])
    rope = _din(nc, 'rope', [4, 128, T])
    cmat = _din(nc, 'cmat', [128, 4, 128])
    masks = _din(nc, 'masks', [128, 7, 384])
    biasD = _din(nc, 'biasD', [depth, 4, 128, D_NSLOT * 128])
    sink = _din(nc, 'sink', [depth, 128, 4])
    wgm = _din(nc, 'wgm', [depth, 16, 4, 128, 16, 128])
    wbr = _din(nc, 'wbr', [depth, 16, 4, 128, 4, 128])
    wout = _din(nc, 'wout', [depth, 16, 128, 16, 128])
    fg = _din(nc, 'fg', [128, 16])
    kidx = _din(nc, 'kidx', [128, 40], I32)
    vidx = _din(nc, 'vidx', [128, 32], I32)
    yT = _dout(nc, 'yT', [128, 16, T])
    XS = [_dint(nc, f'XS{i}', [128, 16, T]) for i in range(2)]
    QS = _dint(nc, 'QS', [16, 128, T], BF16)
    GS = _dint(nc, 'GS', [16, 128, T], BF16)
    HTd = _dint(nc, 'HTd', [128, 16, T], BF16)
    MIXS = _dint(nc, 'MIXS', [16, 128, T], BF16)
    KW = _dint(nc, 'KW', [10, 128, WIN], BF16)
    VW = _dint(nc, 'VW', [WIN, 1536], BF16)
    KS_t = [[nc.dram_tensor(f'KS{i}_{c}', [256, T], BF16) for c in range(6)] for i in range(2)]
    VS_t = [[nc.dram_tensor(f'VS{i}_{c}', [T, 256], BF16) for c in range(6)] for i in range(2)]
    KG_t = [[nc.dram_tensor(f'KG{i}_{c}', [4 * 256, T], BF16) for c in range(6)] for i in range(2)]
    VG_t = [[nc.dram_tensor(f'VG{i}_{c}', [4 * T, 256], BF16) for c in range(6)] for i in range(2)]
    with ExitStack() as es:
        cx = _setup(nc, es)
        S = cx.S
        S.dma('pool', cx.CM[:], cmat, w=['CONST'])
        KIDXT = es.enter_context(_sbt(nc, "kidx_sb", [128, 40], I32))
        VIDXT = es.enter_context(_sbt(nc, "vidx_sb", [128, 32], I32))
        cx.KIDXT, cx.VIDXT = KIDXT, VIDXT
        S.dma('sp', KIDXT[:], kidx, w=['IDX'])
        S.dma('sp', VIDXT[:], vidx, w=['IDX'])
        ADAALL = es.enter_context(_sbt(nc, "ada_all", [128, depth, 48], F32))
        cx.KST = [es.enter_context(_sbt(nc, f"ex_k{i}", [128, 1024], BF16)) for i in range(3)]
        cx.VSG = [es.enter_context(_sbt(nc, f"ex_v{i}", [128, 256], BF16)) for i in range(4)]
        S.barrier()
        for l in range(depth):
            phase_ada(cx, dict(cT=cT, bada=bada[l], wada=wada[l]), ADAALL[:, l, :])
        for l in range(depth):
            par = l % 2
            ADA = ADAALL[:, l, :]
            x_in = xT_in if l == 0 else XS[(l - 1) % 2]
            x_out = XS[l % 2]
            kcc = KIDX[('C', 0)] // 2
            vcc = VOFF[('C', 0)] // 2
            assert KIDX[('C', 0)] % 2 == 0 and VOFF[('C', 0)] % 2 == 0
            KG4 = KG_t[par][kcc].ap().rearrange("(r h p) t -> r h p t", r=4, h=2)
            KSl = [KS_t[par][k // 2].ap().rearrange("(h p) t -> h p t", h=2)[k % 2] for k in range(12)]
            dr = dict(xT=x_in, xTo=x_out, ng=ng[l], wfm=wfm[l], wv=wv[l], qkn=qkn[l], rope=rope,
                      QS=QS, GS=GS, KS=KSl, VS=[t_.ap() for t_ in VS_t[par]],
                      KW=KW, VW=VW, masks=masks, biasD=biasD[l],
                      KCr=[[KG4[rk][kvh] for kvh in range(2)] for rk in range(4)],
                      VC=VG_t[par][VOFF[('C', 0)] // 2].ap(),
                      wgm=wgm[l], wbr=wbr[l], wout=wout[l], MIXS=MIXS, kcc=kcc, vcc=vcc)
            dr['exch'] = (lambda KST, VSG, dr=dr, par=par: emit_exchange(
                cx, dr, KS_t[par], VS_t[par], KG_t[par], VG_t[par], KST, VSG))
            with ExitStack() as ls:
                HT = ls.enter_context(_sbt(nc, f"HT_sb{l}", [128, 16, T], BF16))
                phase1(cx, dr, ADA, HT)
                S.dma('sp', HTd, HT[:], w=['DHT'])
                S.barrier(partial=True, keep=XKEEP)
            with ExitStack() as ls:
                BR = ls.enter_context(_sbt(nc, f"BR_sb{l}", [128, 16, T], BF16))
                phase2(cx, dr, BR, sink[l])
                HT = ls.enter_context(_sbt(nc, f"HT2_sb{l}", [128, 16, T], BF16))
                S.dma('sp', HT[:], HTd, w=['HT'])
                phase3(cx, dr, ADA, HT, BR, BR)
        phase_final(cx, dict(xT=XS[(depth - 1) % 2], fg=fg, yT=yT))
    return nc


def window_index_tables(j):
    kt = np.zeros((128, 40), np.int32)
    p128 = np.arange(128)
    for (mixer, h), i in KW_IDX.items():
        kid = KIDX[(mixer, h)]
        for p in range(4):
            hs = (2 * j - 1 + p) % 8
            rank, half = hs // 2, hs % 2
            assert half == (p + 1) % 2
            kt[:, i * 4 + p] = (rank * 256 + (kid % 2) * 128 + p128) * 2 + half
    vt = np.zeros((128, 32), np.int32)
    for wt in range(32):
        vt[:, wt] = ((j * T - 1024 + wt * 128) % SEQ) + p128
    return kt, vt


_FUSED = {}


def kernel(x, c, norm_g, w_ada, b_ada, w_in, a_sink, c_q_norm, c_k_norm, d_rel_bias, w_gate_merge, w_branch,
           w_out, final_g, _depth=DEPTH):
    x = np.asarray(x, np.float32)
    cores = list(range(NCORES))
    dp = _depth
    cm = const_mats()
    f32 = lambda a: np.asarray(a, np.float32)
    wfm = np.stack([np.stack([fm_weight(f32(w_in[l]), list(range(c0, c0 + 128))) for (_k, _m, _i, c0) in FM_CHUNKS],
                             axis=0) for l in range(dp)], axis=0)
    wv = np.stack([np.stack([fm_weight(f32(w_in[l]), cols) for cols in V_COLS], axis=0) for l in range(dp)], axis=0)
    wada = np.stack([np.stack([fm_weight(f32(w_ada[l]), list(range(s * 512, (s + 1) * 512))) for s in range(12)],
                              axis=0) for l in range(dp)], axis=0)
    bada = np.stack([fm_vec(b_ada[l], 48) for l in range(dp)], axis=0)
    ng = np.stack([fm_vec(norm_g[l], 16) for l in range(dp)], axis=0)
    qkn = np.stack([np.stack([f32(c_q_norm[l]), f32(c_k_norm[l])], axis=1) for l in range(dp)], axis=0)
    wgm = np.stack([f32(w_gate_merge[l]).reshape(16, 128, 4, 16, 128).transpose(3, 2, 1, 0, 4)
                    for l in range(dp)], axis=0)
    wbr = np.stack([f32(w_branch[l]).reshape(4, 4, 128, 16, 128).transpose(3, 0, 2, 1, 4) for l in range(dp)], axis=0)
    wout = np.stack([f32(w_out[l]).reshape(16, 128, 16, 128).transpose(2, 1, 0, 3) for l in range(dp)], axis=0)
    sink = np.stack([np.broadcast_to(f32(a_sink[l])[None, :], (128, 4)) for l in range(dp)], axis=0)
    fg = fm_vec(final_g, 16)
    shared = dict(wada=np.ascontiguousarray(wada), bada=bada, ng=ng, wfm=wfm, wv=wv, qkn=np.ascontiguousarray(qkn),
                  cmat=cm, sink=np.ascontiguousarray(sink), wgm=np.ascontiguousarray(wgm),
                  wbr=np.ascontiguousarray(wbr), wout=np.ascontiguousarray(wout), fg=fg)
    per_j = []
    for j in range(4):
        kt, vt = window_index_tables(j)
        per_j.append(dict(rope=rope_consts(j * T), masks=band_masks(j), kidx=kt, vidx=vt,
                          biasD=np.stack([d_bias_tables(f32(d_rel_bias[l]), j) for l in range(dp)], axis=0)))
    in_maps = []
    for core in cores:
        b, j = core // 4, core % 4
        xs = x[b, j * T:(j + 1) * T, :]
        m = dict(shared)
        m.update(per_j[j])
        m['xT'] = np.ascontiguousarray(xs.reshape(T, 16, 128).transpose(2, 1, 0))
        m['cT'] = fm_vec(np.asarray(c)[b], 16)
        in_maps.append(m)
    if dp not in _FUSED:
        _FUSED[dp] = build_fused(dp)
    res = run_bass_kernel_spmd(_FUSED[dp], in_maps, core_ids=cores).results
    out = np.empty((2, SEQ, D), np.float32)
    for core in cores:
        b, j = core // 4, core % 4
        yT = np.asarray(res[core]['yT'])
        out[b, j * T:(j + 1) * T, :] = yT.transpose(2, 1, 0).reshape(T, D)
    return out
```

```python
import numpy as np
import ml_dtypes
from contextlib import ExitStack
import concourse.bass as bass
import concourse.mybir as mybir
from concourse.bass_utils import run_bass_kernel_spmd

F32 = mybir.dt.float32
BF16 = mybir.dt.bfloat16
AF = mybir.ActivationFunctionType
ALU = mybir.AluOpType
NPBF = ml_dtypes.bfloat16

D = 2048
KC = 16
T = 2048
SEQ = 8192
NTB = 4
HD = 128
WIN = 4096
EPS = 1e-6
SCALE = HD ** -0.5
NEG = -30000.0
DEPTH = 4
NCORES = 8

_MIX_COL0 = {'A': 0, 'B': 1536, 'C': 3584, 'D': 5120}
_MIX_KV = {'A': 2, 'B': 4, 'C': 2, 'D': 4}
FM_CHUNKS = []
QIDX = {}
KIDX = {}
GIDX = {}
_qi = _ki = _gi = 0
for _m in 'ABCD':
    c0 = _MIX_COL0[_m]
    kv = _MIX_KV[_m]
    for h in range(4):
        FM_CHUNKS.append(('q', _m, h, c0 + h * 128)); QIDX[(_m, h)] = _qi; _qi += 1
    for h in range(kv):
        FM_CHUNKS.append(('k', _m, h, c0 + 512 + h * 128)); KIDX[(_m, h)] = _ki; _ki += 1
    for h in range(4):
        FM_CHUNKS.append(('g', _m, h, c0 + 512 + 2 * kv * 128 + h * 128)); GIDX[(_m, h)] = _gi; _gi += 1
NFM = len(FM_CHUNKS)
V_COLS = (list(range(768, 1024)) + list(range(4352, 4608)),
          list(range(2560, 3072)),
          list(range(6144, 6656)))
VOFF = {}
for h in range(2):
    VOFF[('A', h)] = h
    VOFF[('C', h)] = 2 + h
for h in range(4):
    VOFF[('B', h)] = 4 + h
    VOFF[('D', h)] = 8 + h
KW_IDX = {}
_i = 0
for _m, n in (('A', 2), ('B', 4), ('D', 4)):
    for h in range(n):
        KW_IDX[(_m, h)] = _i; _i += 1

D_CLASSES = {'gen': (0, 5, -2), 't0': (5, 6, -2), 't1': (11, 5, -2), 't14': (16, 5, -2), 't15': (21, 6, -3)}
D_NSLOT = 27


class Sched:
    CENG = ('pe', 'act', 'dve', 'pool')
    ALLQ = ('pe', 'act', 'dve', 'pool', 'sp')

    def __init__(self, nc, es, n_dma_sems=48):
        self.nc = nc
        self.es = es
        self.e = {'pe': nc.tensor, 'act': nc.scalar, 'dve': nc.vector, 'pool': nc.gpsimd, 'sp': nc.sync}
        self.dsems = [es.enter_context(nc.semaphore(f"dq{i}")) for i in range(n_dma_sems)]
        self.dcnt = [0] * n_dma_sems
        self.dnext = 0
        self.NP = 16
        self.psems = [es.enter_context(nc.semaphore(f"pq{i}")) for i in range(self.NP)]
        self.pcnt = [0] * self.NP
        self.pnext = 0
        self.epoch = 0
        self.waited = {q: {} for q in self.ALLQ}
        self.ccsem = None
        self.cccnt = 0
        self._new_epoch()

    def _new_epoch(self):
        self.epoch += 1
        self.sems = {e: self.es.enter_context(self.nc.semaphore(f"s{self.epoch}{e}")) for e in self.CENG}
        self.cnt = {e: 0 for e in self.CENG}
        self.lastw = {}
        self.readers = {}

    def _wait(self, q, ev):
        sem, val, key, _src = ev
        w = self.waited[q]
        if w.get(key, 0) >= val:
            return
        self.e[q].wait_ge(sem, val)
        w[key] = val

    def _deps(self, r, w):
        evs = []
        for t in r:
            ev = self.lastw.get(t)
            if ev is not None:
                evs.append(ev)
        for t in w:
            ev = self.lastw.get(t)
            if ev is not None:
                evs.append(ev)
            rd = self.readers.get(t)
            if rd:
                evs.extend(rd.values())
        return evs

    def _record(self, ev, r, w):
        for t in w:
            self.lastw[t] = ev
            self.readers[t] = {}
        for t in r:
            self.readers.setdefault(t, {})[ev[2]] = ev

    def op(self, eng, fn, r=(), w=()):
        pr = [t for t in r if isinstance(t, tuple) and t[0] == 'ps']
        if pr:
            r = [t for t in r if not (isinstance(t, tuple) and t[0] == 'ps')]
            w = list(w) + pr
        for ev in self._deps(r, w):
            if eng == 'pe' and ev[3] == 'pe':
                continue
            self._wait(eng, ev)
        ins = fn(self.e[eng])
        self.cnt[eng] += 1
        ins.then_inc(self.sems[eng], 1)
        ev = (self.sems[eng], self.cnt[eng], ('e', self.epoch, eng), eng)
        self._record(ev, r, w)

    def _dsem(self, q):
        if q == 'pool':
            i = self.pnext
            self.pnext = (i + 1) % self.NP
            self.pcnt[i] += 16
            return self.psems[i], self.pcnt[i], ('p', i)
        i = self.dnext
        self.dnext = (i + 1) % len(self.dsems)
        self.dcnt[i] += 16
        return self.dsems[i], self.dcnt[i], ('d', i)

    def dma(self, q, out, in_, r=(), w=()):
        for ev in self._deps(r, w):
            self._wait(q, ev)
        sem, val, key = self._dsem(q)
        self.e[q].dma_start(out=out, in_=in_).then_inc(sem, 16)
        ev = (sem, val, key, 'dma')
        self._record(ev, r, w)

    def idma(self, out, in_, idx_ap, r=(), w=()):
        q = 'pool'
        for ev in self._deps(r, w):
            self._wait(q, ev)
        sem, val, key = self._dsem(q)
        self.e[q].indirect_dma_start(out=out, out_offset=None, in_=in_,
                                     in_offset=bass.IndirectOffsetOnAxis(ap=idx_ap, axis=0)
                                     ).then_inc(sem, 16)
        ev = (sem, val, key, 'dma')
        self._record(ev, r, w)

    def allgather(self, src_t, dst_t, groups, r=(), w=()):
        q = 'pool'
        if self.ccsem is None:
            self.ccsem = self.es.enter_context(self.nc.semaphore("ccsem"))
        for ev in self._deps(r, w):
            self._wait(q, ev)
        self.cccnt += 1
        self.e[q].collective_compute("AllGather", ALU.bypass, replica_groups=groups,
                                     ins=[src_t.ap().opt()], outs=[dst_t.ap().opt()]).then_inc(self.ccsem, 1)
        ev = (self.ccsem, self.cccnt, ('cc',), 'cc')
        self._record(ev, r, w)

    def barrier(self, partial=False, keep=()):
        evs = [(self.sems[e], self.cnt[e], ('e', self.epoch, e), e) for e in self.CENG if self.cnt[e] > 0]
        evs += [(self.dsems[i], self.dcnt[i], ('d', i), 'dma') for i in range(len(self.dsems)) if self.dcnt[i] > 0]
        if not partial:
            evs += [(self.psems[i], self.pcnt[i], ('p', i), 'dma') for i in range(self.NP) if self.pcnt[i] > 0]
            if self.cccnt:
                evs.append((self.ccsem, self.cccnt, ('cc',), 'cc'))
        for q in self.ALLQ:
            for ev in evs:
                if q == 'pe' and ev[3] == 'pe':
                    continue
                self._wait(q, ev)
        kept = {t: ev for t, ev in self.lastw.items() if isinstance(t, tuple) and t[0] in keep} if partial else {}
        if max(self.cnt.values()) > 10000:
            self._new_epoch()
        else:
            self.lastw = {}
            self.readers = {}
        self.lastw.update(kept)


class Ctx:
    pass


XKEEP = ('KG', 'VG', 'KWd', 'VWd')

_SBT_N = [0]


def _sbt(nc, name, shape, dt):
    _SBT_N[0] += 1
    return nc.sbuf_tensor(f"{name}_u{_SBT_N[0]}", shape, dt)


def _mm(out, lhsT, rhs, start, stop):
    return lambda e: e.matmul(out, lhsT, rhs, start=start, stop=stop)


def phase_ada(cx, dr, ADA):
    nc, S, ps = cx.nc, cx.S, cx.ps
    with ExitStack() as st:
        CT = st.enter_context(_sbt(nc, "ada_ct", [128, 16], F32))
        CS = st.enter_context(_sbt(nc, "ada_cs", [128, 16], F32))
        BA = st.enter_context(_sbt(nc, "ada_b", [128, 48], F32))
        WA = [st.enter_context(_sbt(nc, f"ada_w{i}", [128, 16, 512], F32)) for i in range(2)]
        S.dma('sp', CT[:], dr['cT'], w=['CT'])
        S.dma('sp', BA[:], dr['bada'], w=['BA'])
        S.op('act', lambda e: e.activation(out=CS[:], in_=CT[:], func=AF.Silu), r=['CT'], w=['CS'])
        for s in range(12):
            wb = WA[s % 2]
            S.dma('sp', wb[:], dr['wada'][s], w=[('WA', s % 2)])
            for mm in range(4):
                m = s * 4 + mm
                for kc in range(KC):
                    S.op('pe', _mm(ps[0][:, m:m + 1], wb[:, kc, mm * 128:(mm + 1) * 128], CS[:, kc:kc + 1],
                                   kc == 0, kc == KC - 1),
                         r=[('WA', s % 2), 'CS'], w=[('ps', 0)])
        S.op('dve', lambda e: e.tensor_tensor(out=ADA, in0=ps[0][:, 0:48], in1=BA[:], op=ALU.add),
             r=[('ps', 0), 'BA'], w=['ADA'])
        S.barrier()


def phase_ada_dist(cx, cT, wada, bada, ADAALL, depth, AIN_t, AOUT_t):
    nc, S, ps = cx.nc, cx.S, cx.ps
    with ExitStack() as st:
        CT = st.enter_context(_sbt(nc, "ada_ct", [128, 16], F32))
        CS = st.enter_context(_sbt(nc, "ada_cs", [128, 16], F32))
        BA = st.enter_context(_sbt(nc, "ada_b", [128, depth * 12], F32))
        AP_ = st.enter_context(_sbt(nc, "ada_p", [128, depth * 12], F32))
        WA = [st.enter_context(_sbt(nc, f"ada_w{i}", [128, 16, 512], F32)) for i in range(2)]
        S.dma('sp', CT[:], cT, w=['CT'])
        S.dma('sp', BA[:], bada, w=['BA'])
        S.op('act', lambda e: e.activation(out=CS[:], in_=CT[:], func=AF.Silu), r=['CT'], w=['CS'])
        n = 0
        for l in range(depth):
            for s3 in range(3):
                wb = WA[n % 2]
                wtok = ('WA', n % 2)
                n += 1
                S.dma('sp', wb[:], wada[l][s3], w=[wtok])
                for mm in range(4):
                    col = l * 12 + s3 * 4 + mm
                    for kc in range(KC):
                        S.op('pe', _mm(ps[0][:, col:col + 1], wb[:, kc, mm * 128:(mm + 1) * 128], CS[:, kc:kc + 1],
                                       kc == 0, kc == KC - 1),
                             r=[wtok, 'CS'], w=[('ps', 0)])
        S.op('dve', lambda e: e.tensor_tensor(out=AP_[:], in0=ps[0][:, 0:depth * 12], in1=BA[:], op=ALU.add),
             r=[('ps', 0), 'BA'], w=['AP'])
        S.dma('sp', AIN_t.ap(), AP_[:], r=['AP'], w=['AIN'])
        S.allgather(AIN_t, AOUT_t, GROUPS, r=['AIN'], w=['AOUT'])
        AO = AOUT_t.ap()
        for rk in range(4):
            S.dma('sp', ADAALL[:, :, rk * 12:(rk + 1) * 12],
                  AO[rk * 128:(rk + 1) * 128, :].rearrange("p (l m) -> p l m", l=depth), r=['AOUT'], w=['ADA'])
        S.barrier()


def norm_block(cx, XB, SQ, RS, tb, xsrc_tokens):
    nc, S, ps = cx.nc, cx.S, cx.ps
    S.op('act', lambda e: e.activation(out=SQ[:].rearrange("p c t -> p (c t)"),
                                       in_=XB[:].rearrange("p c t -> p (c t)"), func=AF.Square),
         r=['XB'], w=['SQ'])
    for c in range(KC):
        S.op('pe', _mm(ps[7][:, :], cx.ONES[:], SQ[:, c, :], c == 0, c == KC - 1), r=['SQ', 'CONST'], w=[('ps', 7)])
    S.op('act', lambda e: e.activation(out=RS[:], in_=ps[7][:, :], func=AF.Sqrt, bias=cx.EPSC[:, 0:1],
                                       scale=1.0 / D), r=[('ps', 7), 'EPSC'], w=['RS'])
    S.op('dve', lambda e: e.reciprocal(out=RS[:], in_=RS[:]), r=['RS'], w=['RS'])


def phase1(cx, dr, ADA, HT, stop=9):
    nc, S, ps = cx.nc, cx.S, cx.ps
    with ExitStack() as st:
        XB = st.enter_context(_sbt(nc, "p1_xb", [128, 16, 512], F32))
        SQ = st.enter_context(_sbt(nc, "p1_sq", [128, 16, 512], BF16))
        RS = st.enter_context(_sbt(nc, "p1_rs", [128, 512], F32))
        TM = [st.enter_context(_sbt(nc, f"p1_tm{i}", [128, 512], F32)) for i in range(2)]
        NG = st.enter_context(_sbt(nc, "p1_ng", [128, 16], F32))
        GG = st.enter_context(_sbt(nc, "p1_gg", [128, 16], F32))
        S.dma('sp', NG[:], dr['ng'], w=['NG'])
        S.op('dve', lambda e: e.scalar_tensor_tensor(out=GG[:], in0=ADA[:, 16:32], scalar=1.0, in1=NG[:],
                                                     op0=ALU.add, op1=ALU.mult), r=['ADA', 'NG'], w=['GG'])
        for tb in range(NTB):
            S.dma('sp', XB[:], dr['xT'][:, :, tb * 512:(tb + 1) * 512], w=['XB'])
            norm_block(cx, XB, SQ, RS, tb, None)
            for c in range(KC):
                tm = TM[c % 2]
                S.op('dve', lambda e, c=c, tm=tm: e.tensor_tensor(out=tm[:], in0=XB[:, c, :], in1=RS[:], op=ALU.mult),
                     r=['XB', 'RS'], w=[('TM', c % 2)])
                S.op('act', lambda e, c=c, tm=tm: e.activation(out=HT[:, c, tb * 512:(tb + 1) * 512], in_=tm[:],
                                                              func=AF.Identity, bias=ADA[:, c:c + 1],
                                                              scale=GG[:, c:c + 1]),
                     r=[('TM', c % 2), 'GG', 'ADA'], w=[('HT', tb)])
        S.barrier()
    if stop < 3:
        return
    with ExitStack() as st:
        WB = [st.enter_context(_sbt(nc, f"p1_wb{i}", [128, 16, 128], BF16)) for i in range(3)]
        WS = [st.enter_context(_sbt(nc, f"p1_ws{i}", [128, 2048], F32)) for i in range(2)]
        WV = st.enter_context(_sbt(nc, "p1_wv", [128, 16, 512], BF16))
        COS1 = st.enter_context(_sbt(nc, "p1_cos1", [128, T], F32))
        SIN1 = st.enter_context(_sbt(nc, "p1_sin1", [128, T], F32))
        COSC = st.enter_context(_sbt(nc, "p1_cosc", [128, T], F32))
        SINC = st.enter_context(_sbt(nc, "p1_sinc", [128, T], F32))
        QN = st.enter_context(_sbt(nc, "p1_qn", [128, 2], F32))
        OST = [st.enter_context(_sbt(nc, f"p1_ost{i}", [128, T], BF16)) for i in range(2)]
        VST = st.enter_context(_sbt(nc, "p1_vst", [128, 16, 512], BF16))
        QB = [st.enter_context(_sbt(nc, f"p1_qb{i}", [128, 512], BF16)) for i in range(2)]
        QF = [st.enter_context(_sbt(nc, f"p1_qf{i}", [128, 512], F32)) for i in range(2)]
        T1 = [st.enter_context(_sbt(nc, f"p1_t1{i}", [128, 512], F32)) for i in range(2)]
        T2 = [st.enter_context(_sbt(nc, f"p1_t2{i}", [128, 512], F32)) for i in range(2)]
        RS2 = [st.enter_context(_sbt(nc, f"p1_rs2{i}", [128, 512], F32)) for i in range(2)]
        S.dma('sp', COS1[:], dr['rope'][0], w=['ROPE'])
        S.dma('sp', SIN1[:], dr['rope'][1], w=['ROPE'])
        S.dma('sp', COSC[:], dr['rope'][2], w=['ROPE'])
        S.dma('sp', SINC[:], dr['rope'][3], w=['ROPE'])
        S.dma('sp', QN[:], dr['qkn'], w=['QN'])
        st8 = {'blk': 0, 'seq': 0}

        def w_load(seq, ci):
            S.dma('sp', WS[seq % 2][:], dr['wfm'][ci].rearrange("p k n -> p (k n)"), w=[('WS', seq % 2)])

        def w_cast(seq):
            ws, wb = WS[seq % 2], WB[seq % 3]
            S.op('dve', lambda e: e.tensor_copy(out=wb[:].rearrange("p k n -> p (k n)"), in_=ws[:]),
                 r=[('WS', seq % 2)], w=[('WB', seq % 3)])

        def do_chunk(seq, ci):
            kind, mixer, idx, _c0 = FM_CHUNKS[ci]
            wb = WB[seq % 3]
            wbtok = ('WB', seq % 3)
            ost = OST[seq % 2]
            osttok = ('OST', seq % 2)
            for tb in range(NTB):
                blk = st8['blk']
                st8['blk'] += 1
                pb = blk % 3
                b2 = 3 + blk % 2
                b3 = 5 + blk % 2
                k2 = blk % 2
                tsl = slice(tb * 512, (tb + 1) * 512)
                for kc in range(KC):
                    S.op('pe', _mm(ps[pb][:, :], wb[:, kc, :], HT[:, kc, tsl], kc == 0, kc == KC - 1),
                         r=[wbtok, ('HT', tb)], w=[('ps', pb)])
                if kind == 'g':
                    S.op('act', lambda e, pb=pb, tsl=tsl, ost=ost: e.activation(out=ost[:, tsl], in_=ps[pb][:, :],
                                                                               func=AF.Silu),
                         r=[('ps', pb)], w=[osttok])
                elif mixer == 'D':
                    S.op('act', lambda e, pb=pb, tsl=tsl, ost=ost: e.activation(out=ost[:, tsl], in_=ps[pb][:, :],
                                                                               func=AF.Copy),
                         r=[('ps', pb)], w=[osttok])
                elif mixer in 'AB':
                    qb, t1, t2 = QB[k2], T1[k2], T2[k2]
                    S.op('act', lambda e, pb=pb, qb=qb: e.activation(out=qb[:], in_=ps[pb][:, :], func=AF.Copy),
                         r=[('ps', pb)], w=[('QB', k2)])
                    S.op('pe', _mm(ps[b2][:, :], cx.R1[:], qb[:], True, True), r=[('QB', k2), 'CONST'],
                         w=[('ps', b2)])
                    S.op('dve', lambda e, pb=pb, t1=t1, tsl=tsl: e.tensor_tensor(out=t1[:], in0=ps[pb][:, :],
                                                                                in1=COS1[:, tsl], op=ALU.mult),
                         r=[('ps', pb), 'ROPE'], w=[('T1', k2)])
                    S.op('dve', lambda e, b2=b2, t2=t2, tsl=tsl: e.tensor_tensor(out=t2[:], in0=ps[b2][:, :],
                                                                                in1=SIN1[:, tsl], op=ALU.mult),
                         r=[('ps', b2), 'ROPE'], w=[('T2', k2)])
                    S.op('dve', lambda e, t1=t1, t2=t2, tsl=tsl, ost=ost: e.tensor_tensor(out=ost[:, tsl], in0=t1[:],
                                                                                         in1=t2[:], op=ALU.add),
                         r=[('T1', k2), ('T2', k2)], w=[osttok])
                else:
                    qb, qf, t1, t2, rs2 = QB[k2], QF[k2], T1[k2], T2[k2], RS2[k2]
                    qcol = 0 if kind == 'q' else 1
                    S.op('act', lambda e, pb=pb, qb=qb: e.activation(out=qb[:], in_=ps[pb][:, :], func=AF.Square),
                         r=[('ps', pb)], w=[('QB', k2)])
                    S.op('pe', _mm(ps[b3][:, :], cx.ONES[:], qb[:], True, True), r=[('QB', k2), 'CONST'],
                         w=[('ps', b3)])
                    S.op('act', lambda e, b3=b3, rs2=rs2: e.activation(out=rs2[:], in_=ps[b3][:, :], func=AF.Sqrt,
                                                                      bias=cx.EPSC[:, 0:1], scale=1.0 / HD),
                         r=[('ps', b3), 'EPSC'], w=[('RS2', k2)])
                    S.op('dve', lambda e, rs2=rs2: e.reciprocal(out=rs2[:], in_=rs2[:]),
                         r=[('RS2', k2)], w=[('RS2', k2)])
                    S.op('dve', lambda e, pb=pb, qf=qf, rs2=rs2, qcol=qcol: e.scalar_tensor_tensor(
                        out=qf[:], in0=ps[pb][:, :], scalar=QN[:, qcol:qcol + 1], in1=rs2[:],
                        op0=ALU.mult, op1=ALU.mult), r=[('ps', pb), ('RS2', k2), 'QN'], w=[('QF', k2)])
                    S.op('act', lambda e, qb=qb, qf=qf: e.activation(out=qb[:], in_=qf[:], func=AF.Copy),
                         r=[('QF', k2)], w=[('QB', k2)])
                    S.op('pe', _mm(ps[b2][:, :], cx.RC[:], qb[:], True, True), r=[('QB', k2), 'CONST'],
                         w=[('ps', b2)])
                    S.op('dve', lambda e, qf=qf, t1=t1, tsl=tsl: e.tensor_tensor(out=t1[:], in0=qf[:],
                                                                                in1=COSC[:, tsl], op=ALU.mult),
                         r=[('QF', k2), 'ROPE'], w=[('T1', k2)])
                    S.op('dve', lambda e, b2=b2, t2=t2, tsl=tsl: e.tensor_tensor(out=t2[:], in0=ps[b2][:, :],
                                                                                in1=SINC[:, tsl], op=ALU.mult),
                         r=[('ps', b2), 'ROPE'], w=[('T2', k2)])
                    S.op('dve', lambda e, t1=t1, t2=t2, tsl=tsl, ost=ost: e.tensor_tensor(out=ost[:, tsl], in0=t1[:],
                                                                                         in1=t2[:], op=ALU.add),
                         r=[('T1', k2), ('T2', k2)], w=[osttok])
            if kind == 'q':
                dst = dr['QS'][QIDX[(mixer, idx)]]
            elif kind == 'k':
                dst = dr['KS'][KIDX[(mixer, idx)]]
            else:
                dst = dr['GS'][GIDX[(mixer, idx)]]
            S.dma('sp', dst, ost[:], r=[osttok], w=[('DR', kind, mixer, idx)])

        def run_chunks(order):
            base = st8['seq']
            n = len(order)
            for i in range(min(2, n)):
                w_load(base + i, order[i])
            w_cast(base)
            for i, ci in enumerate(order):
                if i + 1 < n:
                    w_cast(base + i + 1)
                if i + 2 < n:
                    w_load(base + i + 2, order[i + 2])
                do_chunk(base + i, ci)
            st8['seq'] = base + n

        order_k = [ci for ci, ch in enumerate(FM_CHUNKS) if ch[0] == 'k']
        order_qg = [ci for ci, ch in enumerate(FM_CHUNKS) if ch[0] != 'k']
        if stop == 3:
            order_k, order_qg = order_k[:2], []
        run_chunks(order_k)
        nld = 0
        for vg in range(3):
            if stop < 5:
                break
            for q4 in range(4):
                ws = WS[nld % 2]
                wstok = ('WS', nld % 2)
                nld += 1
                S.dma('sp', ws[:], dr['wv'][vg][:, q4 * 4:(q4 + 1) * 4, :].rearrange("p k n -> p (k n)"), w=[wstok])
                S.op('dve', lambda e, ws=ws, q4=q4: e.tensor_copy(
                    out=WV[:, q4 * 4:(q4 + 1) * 4, :].rearrange("p k n -> p (k n)"), in_=ws[:]),
                     r=[wstok], w=['WV'])
            for tt in range(16):
                blk = st8['blk']
                st8['blk'] += 1
                pb = blk % 3
                for kc in range(KC):
                    S.op('pe', _mm(ps[pb][:, :], HT[:, kc, tt * 128:(tt + 1) * 128], WV[:, kc, :], kc == 0,
                                   kc == KC - 1),
                         r=['WV', ('HT', tt // 4)], w=[('ps', pb)])
                if tt % 2 == 0:
                    S.op('act', lambda e, pb=pb, tt=tt: e.activation(out=VST[:, tt, :], in_=ps[pb][:, :],
                                                                    func=AF.Copy),
                         r=[('ps', pb)], w=['VST'])
                else:
                    S.op('dve', lambda e, pb=pb, tt=tt: e.tensor_copy(out=VST[:, tt, :], in_=ps[pb][:, :]),
                         r=[('ps', pb)], w=['VST'])
            if isinstance(dr['VS'], list):
                for hf in range(2):
                    S.dma('sp', dr['VS'][vg * 2 + hf].rearrange("(tt p) c -> p tt c", p=128),
                          VST[:, :, hf * 256:(hf + 1) * 256], r=['VST'], w=[('DRV', vg * 2 + hf)])
            else:
                S.dma('sp', dr['VS'][:, vg * 512:(vg + 1) * 512].rearrange("(tt p) c -> p tt c", p=128), VST[:],
                      r=['VST'], w=[('DRV', vg)])
        if 'exch' in dr:
            dr['exch'](cx.KST, cx.VSG)
        run_chunks(order_qg)
        S.barrier(partial=('exch' in dr), keep=XKEEP)


def attn_qk(cx, PT, nq, slots, bias_ap, sbanks, toks):
    S, ps = cx.S, cx.ps
    ktoks, qtok, vtoks, btok = toks
    ns = len(slots)
    per_bank = 512 // nq
    nb = (ns + per_bank - 1) // per_bank
    for b in range(nb):
        s0 = b * per_bank
        s1 = min(ns, s0 + per_bank)
        bk = sbanks[b]
        S.op('pe', _mm(ps[bk][:, 0:(s1 - s0) * nq], cx.IDENT[:], bias_ap[:, s0 * nq:s1 * nq], True, False),
             r=['CONST', btok], w=[('ps', bk)])
        for s in range(s0, s1):
            kT, _v, qT = slots[s]
            S.op('pe', _mm(ps[bk][:, (s - s0) * nq:(s - s0 + 1) * nq], kT, qT, False, True),
                 r=ktoks + [qtok], w=[('ps', bk)])
    pt, pttok = PT
    for b in range(nb):
        s0 = b * per_bank
        s1 = min(ns, s0 + per_bank)
        bk = sbanks[b]
        S.op('act', lambda e, bk=bk, s0=s0, s1=s1: e.activation(out=pt[:, s0 * nq:s1 * nq],
                                                               in_=ps[bk][:, 0:(s1 - s0) * nq], func=AF.Exp,
                                                               scale=SCALE),
             r=[('ps', bk)], w=[pttok])


def attn_pv(cx, PT, nq, slots, obank, dbank, ocol, toks):
    S, ps = cx.S, cx.ps
    ktoks, qtok, vtoks, btok = toks
    ns = len(slots)
    pt, pttok = PT
    for s in range(ns):
        _k, v, _q = slots[s]
        S.op('pe', _mm(ps[obank][:, ocol:ocol + nq], v, pt[:, s * nq:(s + 1) * nq], s == 0, s == ns - 1),
             r=[pttok] + vtoks, w=[('ps', obank)])
    for s in range(ns):
        S.op('pe', _mm(ps[dbank][:, ocol:ocol + nq], cx.ONES[:], pt[:, s * nq:(s + 1) * nq], s == 0, s == ns - 1),
             r=[pttok, 'CONST'], w=[('ps', dbank)])


def phase2(cx, dr, BR, l_sink):
    nc, S, ps = cx.nc, cx.S, cx.ps
    with ExitStack() as st:
        KH = st.enter_context(_sbt(nc, "p2_kh", [128, SEQ], BF16))
        VH = st.enter_context(_sbt(nc, "p2_vh", [128, 64, 128], BF16))
        VH4b = [st.enter_context(_sbt(nc, f"p2_vh4{i}", [128, 32, 128], BF16)) for i in range(2)]
        VH16b = [st.enter_context(_sbt(nc, f"p2_vh16{i}", [128, 32, 128], BF16)) for i in range(2)]
        QHb = [st.enter_context(_sbt(nc, f"p2_qh{i}", [128, T], BF16)) for i in range(2)]
        GHb = [st.enter_context(_sbt(nc, f"p2_gh{i}", [128, T], BF16)) for i in range(2)]
        OB = st.enter_context(_sbt(nc, "p2_ob", [128, T], F32))
        DENB = st.enter_context(_sbt(nc, "p2_den", [128, T], F32))
        PTS = [st.enter_context(_sbt(nc, f"p2_pt{i}", [128, 768], BF16)) for i in range(3)]
        MSK = st.enter_context(_sbt(nc, "p2_msk", [128, 7, 384], BF16))
        BDb = [st.enter_context(_sbt(nc, f"p2_bd{i}", [128, D_NSLOT * 128], BF16)) for i in range(2)]
        SK = st.enter_context(_sbt(nc, "p2_sk", [128, 4], F32))
        ES = st.enter_context(_sbt(nc, "p2_es", [128, 4], F32))
        S.dma('pool', MSK[:], dr['masks'], w=['BIAS'])
        S.dma('sp', SK[:], l_sink, w=['SK'])
        S.op('act', lambda e: e.activation(out=ES[:], in_=SK[:], func=AF.Exp), r=['SK'], w=['ES'])
        state = {'pt': 0, 'ob': 0}

        def next_pt():
            i = state['pt']
            state['pt'] = (i + 1) % 3
            return (PTS[i], ('PT', i))

        def obanks():
            i = state['ob']
            state['ob'] = 1 - i
            return 4 + i, 6 + i

        def finalize(job, sink_col=None):
            hidx = QIDX[(job['mixer'], job['h'])]
            GH = GHb[job['qb']]
            if sink_col is not None:
                S.op('dve', lambda e: e.tensor_scalar(out=DENB[:], in0=DENB[:], scalar1=ES[:, sink_col:sink_col + 1],
                                                      scalar2=None, op0=ALU.add), r=['DENB', 'ES'], w=['DENB'])
            S.op('dve', lambda e: e.reciprocal(out=DENB[:], in_=DENB[:]), r=['DENB'], w=['DENB'])
            S.op('dve', lambda e: e.tensor_tensor(out=OB[:], in0=OB[:], in1=DENB[:], op=ALU.mult),
                 r=['OB', 'DENB'], w=['OB'])
            S.op('dve', lambda e: e.tensor_tensor(out=BR[:, hidx, :], in0=OB[:], in1=GH[:], op=ALU.mult),
                 r=['OB', ('GH', job['qb'])], w=[('BR', hidx)])

        def evac(obank, dbank, ov, dv, first):
            pso = ps[obank][:, :]
            psd = ps[dbank][:, :]
            if len(ov.shape) == 3:
                pso = pso.rearrange("p (a b) -> p a b", a=ov.shape[1])
                psd = psd.rearrange("p (a b) -> p a b", a=ov.shape[1])
            if first:
                S.op('dve', lambda e: e.tensor_copy(out=ov, in_=pso), r=[('ps', obank)], w=['OB'])
                S.op('act', lambda e: e.activation(out=dv, in_=psd, func=AF.Copy), r=[('ps', dbank)], w=['DENB'])
            else:
                S.op('dve', lambda e: e.tensor_tensor(out=ov, in0=ov, in1=pso, op=ALU.add),
                     r=[('ps', obank), 'OB'], w=['OB'])
                S.op('dve', lambda e: e.tensor_tensor(out=dv, in0=dv, in1=psd, op=ALU.add),
                     r=[('ps', dbank), 'DENB'], w=['DENB'])

        jobs = []
        for kvh in range(2):
            for g in range(2):
                jobs.append(dict(mixer='C', h=kvh * 2 + g, kvh=kvh, load_kv=(g == 0)))
        for kvh in range(2):
            for g in range(2):
                jobs.append(dict(mixer='A', h=kvh * 2 + g, kvh=kvh, load_kv=(g == 0)))
        for h in range(4):
            jobs.append(dict(mixer='B', h=h, kvh=h, load_kv=True))
        for h in range(4):
            jobs.append(dict(mixer='D', h=h, kvh=h, load_kv=True))
        kvb = 1
        for i, job in enumerate(jobs):
            job['qb'] = i % 2
            if job['load_kv']:
                kvb = 1 - kvb
            job['kvb'] = kvb
            job['kv_late'] = job['mixer'] == 'C' or (i > 0 and jobs[i - 1]['mixer'] == 'C')

        def kv_views(job):
            if job['mixer'] == 'C':
                return KH, VH, [('KH', 0), ('KH', 1)], [('VH', 0), ('VH', 1)]
            b = job['kvb']
            return (KH[:, b * WIN:(b + 1) * WIN], VH[:, b * 32:(b + 1) * 32, :], [('KH', b)], [('VH', b)])

        def emit_load(job, late):
            m, h, q = job['mixer'], job['h'], job['qb']
            is_c = (m == 'C')
            if job['load_kv'] and late == job['kv_late']:
                Kv, Vv, kt, vt = kv_views(job)
                if is_c:
                    kvh = job['kvh']
                    if 'KCr' in dr:
                        for rk in range(4):
                            S.dma('sp', KH[:, rk * T:(rk + 1) * T], dr['KCr'][rk][kvh], r=[('KG', dr['kcc'])], w=kt)
                    else:
                        S.dma('sp', KH[:], dr['KC'][kvh], w=kt)
                    S.dma('sp', VH[:], dr['VC'][:, kvh * 128:(kvh + 1) * 128].rearrange("(t p) c -> p t c", p=128),
                          r=[('VG', dr.get('vcc', 0))], w=vt)
                else:
                    i = KW_IDX[(m, job['kvh'])]
                    vo = VOFF[(m, job['kvh'])]
                    vsrc = dr['VW'][:, vo * 128:(vo + 1) * 128]
                    vwt = [('VWd', wt, vo // 2) for wt in range(32)]
                    S.dma('sp', Kv, dr['KW'][i], r=[('KWd', i, p) for p in range(4)], w=kt)
                    S.dma('sp', Vv, vsrc.rearrange("(t p) c -> p t c", p=128), r=vwt, w=vt)
                    if m == 'B':
                        for r in range(4):
                            S.dma('sp', VH4b[h % 2][:, r * 8:(r + 1) * 8, :],
                                  vsrc.rearrange("(t p r) c -> r p t c", p=128, r=4)[r], r=vwt, w=[('VH4', h % 2)])
                        for r in range(16):
                            S.dma('sp', VH16b[h % 2][:, r * 2:(r + 1) * 2, :],
                                  vsrc.rearrange("(t p r) c -> r p t c", p=128, r=16)[r], r=vwt, w=[('VH16', h % 2)])
                    if m == 'D':
                        S.dma('pool', BDb[h % 2][:], dr['biasD'][h], w=[('BD', h % 2)])
            if not late:
                S.dma('sp', QHb[q][:], dr['QS'][QIDX[(m, h)]], w=[('QH', q)])
                S.dma('sp', GHb[q][:], dr['GS'][GIDX[(m, h)]], w=[('GH', q)])

        def run_tiles(tiles):
            for t in tiles:
                t['pt'] = None
            n = len(tiles)

            def stage1(t):
                t['pt'] = next_pt()
                attn_qk(cx, t['pt'], 128, t['slots'], t['bias'], t['sb'], t['toks'])

            stage1(tiles[0])
            for i, t in enumerate(tiles):
                if i + 1 < n:
                    stage1(tiles[i + 1])
                attn_pv(cx, t['pt'], 128, t['slots'], t['ob'], t['db'], t['ocol'], t['toks'])
                if t.get('after') is not None:
                    t['after']()

        def banded_tiles(job, mbase, first_pattern):
            Kv, Vv, kt, vt = kv_views(job)
            QH = QHb[job['qb']]
            toks = (kt, ('QH', job['qb']), vt, 'BIAS')
            tiles = []
            for g4 in range(4):
                ob, db = obanks()
                for qq in range(4):
                    qt = g4 * 4 + qq
                    cls = 0 if qt == 0 else (2 if qt == 15 else 1)
                    slots = []
                    for s in range(3):
                        wt = 8 + qt - 1 + s
                        slots.append((Kv[:, wt * 128:(wt + 1) * 128], Vv[:, wt, :], QH[:, qt * 128:(qt + 1) * 128]))
                    tiles.append(dict(slots=slots, bias=MSK[:, mbase + cls, :], sb=[qt % 3], ob=ob, db=db,
                                      ocol=qq * 128, toks=toks))
                tiles[-1]['after'] = (lambda ob=ob, db=db, g4=g4: evac(
                    ob, db, OB[:, g4 * 512:(g4 + 1) * 512], DENB[:, g4 * 512:(g4 + 1) * 512], first_pattern))
            return tiles

        def compute(job):
            m, h = job['mixer'], job['h']
            Kv, Vv, kt, vt = kv_views(job)
            QH = QHb[job['qb']]
            qtok = ('QH', job['qb'])
            VH4, VH16, BD = VH4b[h % 2], VH16b[h % 2], BDb[h % 2]
            if m == 'A':
                run_tiles(banded_tiles(job, 0, True))
                finalize(job, sink_col=h)
            elif m == 'B':
                tiles = banded_tiles(job, 3, True)
                for r in range(4):
                    ob, db = obanks()
                    for it in range(4):
                        cls = 0 if it == 0 else (2 if it == 3 else 1)
                        q0 = r + 4 * it * 128
                        qT = QH[:, q0:q0 + 4 * 127 + 1:4]
                        slots = []
                        for s in range(3):
                            wt = 2 + it - 1 + s
                            k0 = r + 512 * wt
                            slots.append((Kv[:, k0:k0 + 4 * 127 + 1:4], VH4[:, r * 8 + wt, :], qT))
                        tiles.append(dict(slots=slots, bias=MSK[:, 3 + cls, :], sb=[it % 3], ob=ob, db=db,
                                          ocol=it * 128, toks=(kt, qtok, [('VH4', h % 2)], 'BIAS')))
                    tiles[-1]['after'] = (lambda ob=ob, db=db, r=r: evac(
                        ob, db, OB[:, r:r + 4 * 511 + 1:4], DENB[:, r:r + 4 * 511 + 1:4], False))
                for r0 in range(0, 16, 4):
                    ob, db = obanks()
                    for rr in range(4):
                        r = r0 + rr
                        qT = QH[:, r:r + 16 * 127 + 1:16]
                        slots = []
                        for s in range(2):
                            k0 = r + 16 * 128 * s
                            slots.append((Kv[:, k0:k0 + 16 * 127 + 1:16], VH16[:, r * 2 + s, :], qT))
                        tiles.append(dict(slots=slots, bias=MSK[:, 6, 0:256], sb=[rr % 3], ob=ob, db=db,
                                          ocol=rr * 128, toks=(kt, qtok, [('VH16', h % 2)], 'BIAS')))

                    def after16(ob=ob, db=db, r0=r0):
                        ov = OB[:].rearrange("p (i r) -> p r i", r=16)[:, r0:r0 + 4, :]
                        dv = DENB[:].rearrange("p (i r) -> p r i", r=16)[:, r0:r0 + 4, :]
                        evac(ob, db, ov, dv, False)
                    tiles[-1]['after'] = after16
                run_tiles(tiles)
                finalize(job)
            elif m == 'D':
                tiles = []
                for g4 in range(4):
                    ob, db = obanks()
                    for qq in range(4):
                        ul = g4 * 4 + qq
                        cname = {0: 't0', 1: 't1', 14: 't14', 15: 't15'}.get(ul, 'gen')
                        sl0, nsl, da0 = D_CLASSES[cname]
                        slots = []
                        for s in range(nsl):
                            wt = 8 + ul + da0 + s
                            slots.append((Kv[:, wt * 128:(wt + 1) * 128], Vv[:, wt, :],
                                          QH[:, ul * 128:(ul + 1) * 128]))
                        sb = [0, 2] if ul % 2 == 0 else [1, 3]
                        tiles.append(dict(slots=slots, bias=BD[:, sl0 * 128:(sl0 + nsl) * 128], sb=sb, ob=ob, db=db,
                                          ocol=qq * 128, toks=(kt, qtok, vt, ('BD', h % 2))))
                    tiles[-1]['after'] = (lambda ob=ob, db=db, g4=g4: evac(
                        ob, db, OB[:, g4 * 512:(g4 + 1) * 512], DENB[:, g4 * 512:(g4 + 1) * 512], True))
                run_tiles(tiles)
                finalize(job)
            else:
                for qb in range(4):
                    ob, db = obanks()
                    qT = QH[:, qb * 512:(qb + 1) * 512]
                    NKT = 64
                    pts = {}

                    def qk(kt_):
                        bk = kt_ % 3
                        S.op('pe', _mm(ps[bk][:, :], KH[:, kt_ * 128:(kt_ + 1) * 128], qT, True, True),
                             r=kt + [qtok], w=[('ps', bk)])
                        pt, pttok = next_pt()
                        pts[kt_] = (pt, pttok)
                        S.op('act', lambda e, bk=bk, pt=pt: e.activation(out=pt[:, 0:512], in_=ps[bk][:, :],
                                                                        func=AF.Exp, scale=SCALE),
                             r=[('ps', bk)], w=[pttok])

                    def pv(kt_):
                        pt, pttok = pts.pop(kt_)
                        S.op('pe', _mm(ps[ob][:, :], VH[:, kt_, :], pt[:, 0:512], kt_ == 0, kt_ == NKT - 1),
                             r=[pttok] + vt, w=[('ps', ob)])
                        S.op('pe', _mm(ps[db][:, :], cx.ONES[:], pt[:, 0:512], kt_ == 0, kt_ == NKT - 1),
                             r=[pttok, 'CONST'], w=[('ps', db)])

                    qk(0)
                    qk(1)
                    for kt_ in range(NKT):
                        if kt_ + 2 < NKT:
                            qk(kt_ + 2)
                        pv(kt_)
                    evac(ob, db, OB[:, qb * 512:(qb + 1) * 512], DENB[:, qb * 512:(qb + 1) * 512], True)
                finalize(job)

        emit_load(jobs[0], False)
        for i, job in enumerate(jobs):
            emit_load(job, True)
            if i + 1 < len(jobs):
                emit_load(jobs[i + 1], False)
            compute(job)
        S.barrier()


def phase3(cx, dr, ADA, HT, BR, MX, last_fg=None):
    nc, S, ps = cx.nc, cx.S, cx.ps
    with ExitStack() as st:
        WG = [st.enter_context(_sbt(nc, f"p3_wg{i}", [128, 16, 128], BF16)) for i in range(3)]
        WR = [st.enter_context(_sbt(nc, f"p3_wr{i}", [128, 4, 128], BF16)) for i in range(3)]
        SG = [st.enter_context(_sbt(nc, f"p3_sg{i}", [128, 512], F32)) for i in range(2)]
        TP = [st.enter_context(_sbt(nc, f"p3_tp{i}", [128, 512], F32)) for i in range(2)]
        MIXF = st.enter_context(_sbt(nc, "p3_mixf", [128, 4, 512], F32))
        MIXB = [st.enter_context(_sbt(nc, f"p3_mixb{i}", [128, T], BF16)) for i in range(2)]
        blk = 0
        wi = 0
        for m in range(16):
            mixb = MIXB[m % 2]
            for n in range(4):
                wg, wr = WG[wi % 3], WR[wi % 3]
                wtok = ('W3', wi % 3)
                wi += 1
                S.dma('pool', wg[:], dr['wgm'][m, n], w=[wtok])
                S.dma('pool', wr[:], dr['wbr'][m, n], w=[wtok])
                for tb in range(NTB):
                    pa = blk % 2
                    pg = 2 + blk % 2
                    k2 = blk % 2
                    blk += 1
                    tsl = slice(tb * 512, (tb + 1) * 512)
                    for kc in range(4):
                        S.op('pe', _mm(ps[pa][:, :], wr[:, kc, :], BR[:, n * 4 + kc, tsl], kc == 0, kc == 3),
                             r=[wtok, 'BR'], w=[('ps', pa)])
                    for kc in range(KC):
                        S.op('pe', _mm(ps[pg][:, :], wg[:, kc, :], HT[:, kc, tsl], kc == 0, kc == KC - 1),
                             r=[wtok, 'HT'], w=[('ps', pg)])
                    sg, tp = SG[k2], TP[k2]
                    S.op('act', lambda e, pg=pg, sg=sg: e.activation(out=sg[:], in_=ps[pg][:, :], func=AF.Sigmoid),
                         r=[('ps', pg)], w=[('SG', k2)])
                    if n == 0:
                        S.op('dve', lambda e, pa=pa, sg=sg, tb=tb: e.tensor_tensor(out=MIXF[:, tb, :], in0=ps[pa][:, :],
                                                                                  in1=sg[:], op=ALU.mult),
                             r=[('ps', pa), ('SG', k2)], w=[('MIXF', tb)])
                    else:
                        S.op('dve', lambda e, pa=pa, sg=sg, tp=tp: e.tensor_tensor(out=tp[:], in0=ps[pa][:, :],
                                                                                  in1=sg[:], op=ALU.mult),
                             r=[('ps', pa), ('SG', k2)], w=[('TP', k2)])
                        if n < 3:
                            S.op('dve', lambda e, tp=tp, tb=tb: e.tensor_tensor(out=MIXF[:, tb, :], in0=MIXF[:, tb, :],
                                                                               in1=tp[:], op=ALU.add),
                                 r=[('TP', k2), ('MIXF', tb)], w=[('MIXF', tb)])
                        else:
                            S.op('dve', lambda e, tp=tp, tb=tb, tsl=tsl, mixb=mixb: e.tensor_tensor(
                                out=mixb[:, tsl], in0=MIXF[:, tb, :], in1=tp[:], op=ALU.add),
                                 r=[('TP', k2), ('MIXF', tb)], w=[('MIXB', m % 2)])
            S.dma('sp', dr['MIXS'][m], mixb[:], r=[('MIXB', m % 2)], w=[('DRM', m)])
        S.barrier()
    with ExitStack() as st:
        WO = [st.enter_context(_sbt(nc, f"p3_wo{i}", [128, 16, 128], BF16)) for i in range(3)]
        XR = [st.enter_context(_sbt(nc, f"p3_xr{i}", [128, T], F32)) for i in range(2)]
        for m in range(16):
            S.dma('sp', MX[:, m, :], dr['MIXS'][m], w=[('MX', m)])
        blk = 0
        for mo in range(16):
            wo = WO[mo % 3]
            xr = XR[mo % 2]
            S.dma('pool', wo[:], dr['wout'][mo], w=[('WO', mo % 3)])
            S.dma('sp', xr[:], dr['xT'][:, mo, :], w=[('XR', mo % 2)])
            for tb in range(NTB):
                pb = blk % 4
                blk += 1
                tsl = slice(tb * 512, (tb + 1) * 512)
                for kc in range(KC):
                    S.op('pe', _mm(ps[pb][:, :], wo[:, kc, :], MX[:, kc, tsl], kc == 0, kc == KC - 1),
                         r=[('WO', mo % 3), ('MX', kc)], w=[('ps', pb)])
                S.op('dve', lambda e, pb=pb, xr=xr, tsl=tsl, mo=mo: e.scalar_tensor_tensor(
                    out=xr[:, tsl], in0=ps[pb][:, :], scalar=ADA[:, 32 + mo:33 + mo], in1=xr[:, tsl],
                    op0=ALU.mult, op1=ALU.add), r=[('ps', pb), ('XR', mo % 2), 'ADA'], w=[('XR', mo % 2)])
            S.dma('sp', dr['xTo'][:, mo, :], xr[:], r=[('XR', mo % 2)], w=[('DRX', mo)])
        S.barrier()


def phase_final(cx, dr):
    nc, S, ps = cx.nc, cx.S, cx.ps
    with ExitStack() as st:
        XB = st.enter_context(_sbt(nc, "pf_xb", [128, 16, 512], F32))
        SQ = st.enter_context(_sbt(nc, "pf_sq", [128, 16, 512], BF16))
        RS = st.enter_context(_sbt(nc, "pf_rs", [128, 512], F32))
        FG = st.enter_context(_sbt(nc, "pf_fg", [128, 16], F32))
        YB = [st.enter_context(_sbt(nc, f"pf_yb{i}", [128, 16, 512], F32)) for i in range(1)]
        S.dma('sp', FG[:], dr['fg'], w=['FG'])
        for tb in range(NTB):
            S.dma('sp', XB[:], dr['xT'][:, :, tb * 512:(tb + 1) * 512], w=['XB'])
            norm_block(cx, XB, SQ, RS, tb, None)
            yb = YB[0]
            for c in range(KC):
                S.op('dve', lambda e, c=c, yb=yb: e.scalar_tensor_tensor(out=yb[:, c, :], in0=XB[:, c, :],
                                                                        scalar=FG[:, c:c + 1], in1=RS[:],
                                                                        op0=ALU.mult, op1=ALU.mult),
                     r=['XB', 'RS', 'FG'], w=['YB'])
            S.dma('sp', dr['yT'][:, :, tb * 512:(tb + 1) * 512], yb[:], r=['YB'], w=[('DRY', tb)])
        S.barrier()


def _new_nc():
    return bass.Bass("TRN2", target_bir_lowering=False)


def _setup(nc, es):
    cx = Ctx()
    cx.nc = nc
    cx.S = Sched(nc, es)
    cx.ps = [es.enter_context(nc.psum_tensor(f"psb{i}", [128, 512], F32)) for i in range(8)]
    CM = es.enter_context(_sbt(nc, "cmat_sb", [128, 4, 128], BF16))
    cx.CM = CM
    cx.IDENT = CM[:, 0, :]
    cx.ONES = CM[:, 1, :]
    cx.R1 = CM[:, 2, :]
    cx.RC = CM[:, 3, :]
    cx.EPSC = es.enter_context(_sbt(nc, "epsc_sb", [128, 1], F32))
    cx.S.op('dve', lambda e: e.memset(cx.EPSC[:], EPS), w=['EPSC'])
    return cx


def _din(nc, name, shape, dt=F32):
    return nc.dram_tensor(name, list(shape), dt, kind="ExternalInput").ap()


def _dout(nc, name, shape, dt=F32):
    return nc.dram_tensor(name, list(shape), dt, kind="ExternalOutput").ap()


def _dint(nc, name, shape, dt=F32):
    return nc.dram_tensor(name, list(shape), dt, kind="Internal").ap()


def build_L1(stop=9):
    nc = _new_nc()
    dr = {}
    dr['xT'] = _din(nc, 'xT', [128, 16, T])
    dr['cT'] = _din(nc, 'cT', [128, 16])
    dr['wada'] = _din(nc, 'wada', [12, 128, 16, 512])
    dr['bada'] = _din(nc, 'bada', [128, 48])
    dr['ng'] = _din(nc, 'ng', [128, 16])
    dr['wfm'] = _din(nc, 'wfm', [NFM, 128, 16, 128])
    dr['wv'] = _din(nc, 'wv', [3, 128, 16, 512])
    dr['qkn'] = _din(nc, 'qkn', [128, 2])
    dr['rope'] = _din(nc, 'rope', [4, 128, T])
    cmat = _din(nc, 'cmat', [128, 4, 128])
    dr['QS'] = _dout(nc, 'QS', [16, 128, T], BF16)
    dr['KS'] = _dout(nc, 'KS', [12, 128, T], BF16)
    dr['GS'] = _dout(nc, 'GS', [16, 128, T], BF16)
    dr['VS'] = _dout(nc, 'VS', [T, 1536], BF16)
    HTo = _dout(nc, 'HT', [128, 16, T], BF16)
    ADAo = _dout(nc, 'ADA', [128, 48])
    with ExitStack() as es:
        cx = _setup(nc, es)
        S = cx.S
        S.dma('pool', cx.CM[:], cmat, w=['CONST'])
        ADA = es.enter_context(_sbt(nc, "ADA_sb", [128, 48], F32))
        HT = es.enter_context(_sbt(nc, "HT_sb", [128, 16, T], BF16))
        if stop >= 1:
            phase_ada(cx, dr, ADA[:])
        if stop >= 2:
            phase1(cx, dr, ADA, HT, stop)
        S.dma('sp', HTo, HT[:], r=[('HT', i) for i in range(4)], w=['DHT'])
        S.dma('sp', ADAo, ADA[:], r=['ADA'], w=['DADA'])
        S.barrier()
    return nc


def build_L2():
    nc = _new_nc()
    dr = {}
    dr['xT'] = _din(nc, 'xT', [128, 16, T])
    ADAi = _din(nc, 'ADA', [128, 48])
    HTi = _din(nc, 'HT', [128, 16, T], BF16)
    dr['QS'] = _din(nc, 'QS', [16, 128, T], BF16)
    dr['GS'] = _din(nc, 'GS', [16, 128, T], BF16)
    dr['KW'] = _din(nc, 'KW', [10, 128, WIN], BF16)
    dr['KC'] = _din(nc, 'KC', [2, 128, SEQ], BF16)
    dr['VW'] = _din(nc, 'VW', [WIN, 1536], BF16)
    dr['VC'] = _din(nc, 'VC', [SEQ, 256], BF16)
    dr['masks'] = _din(nc, 'masks', [128, 7, 384])
    dr['biasD'] = _din(nc, 'biasD', [4, 128, D_NSLOT * 128])
    sink = _din(nc, 'sink', [128, 4])
    dr['wgm'] = _din(nc, 'wgm', [16, 4, 128, 16, 128])
    dr['wbr'] = _din(nc, 'wbr', [16, 4, 128, 4, 128])
    dr['wout'] = _din(nc, 'wout', [16, 128, 16, 128])
    cmat = _din(nc, 'cmat', [128, 4, 128])
    dr['MIXS'] = _dint(nc, 'MIXS', [16, 128, T], BF16)
    dr['xTo'] = _dout(nc, 'xTo', [128, 16, T])
    with ExitStack() as es:
        cx = _setup(nc, es)
        S = cx.S
        S.dma('pool', cx.CM[:], cmat, w=['CONST'])
        ADA = es.enter_context(_sbt(nc, "ADA_sb", [128, 48], F32))
        S.dma('sp', ADA[:], ADAi, w=['ADA'])
        BR = es.enter_context(_sbt(nc, "BR_sb", [128, 16, T], BF16))
        phase2(cx, dr, BR, sink)
        HT = es.enter_context(_sbt(nc, "HT_sb", [128, 16, T], BF16))
        S.dma('sp', HT[:], HTi, w=['HT'])
        S.dma('sp', ADA[:], ADAi, w=['ADA'])
        phase3(cx, dr, ADA, HT, BR, BR)
    return nc


def build_L3():
    nc = _new_nc()
    dr = {}
    dr['xT'] = _din(nc, 'xT', [128, 16, T])
    dr['fg'] = _din(nc, 'fg', [128, 16])
    cmat = _din(nc, 'cmat', [128, 4, 128])
    dr['yT'] = _dout(nc, 'yT', [128, 16, T])
    with ExitStack() as es:
        cx = _setup(nc, es)
        cx.S.dma('pool', cx.CM[:], cmat, w=['CONST'])
        phase_final(cx, dr)
    return nc


def fm_vec(v, nchunk):
    return np.ascontiguousarray(np.asarray(v, np.float32).reshape(nchunk, 128).T)


def fm_weight(w, cols):
    K = w.shape[0]
    ws = w[:, cols]
    return np.ascontiguousarray(ws.reshape(K // 128, 128, ws.shape[1]).transpose(1, 0, 2))


def rope_consts(t0):
    pos = np.arange(t0, t0 + T)

    def tables(p, dim):
        inv = (np.float32(10000.0) ** (-np.arange(0, dim, 2, dtype=np.float32) / np.float32(dim))).astype(np.float32)
        ang = p.astype(np.float32)[:, None] * inv[None, :]
        ang = np.concatenate([ang, ang], axis=-1)
        return np.cos(ang).astype(np.float32), np.sin(ang).astype(np.float32)

    c1, s1 = tables(pos, 128)
    cr, sr = tables(pos // 64, 64)
    cc, sc = tables(pos % 64, 64)
    cC = np.concatenate([cr, cc], axis=-1)
    sC = np.concatenate([sr, sc], axis=-1)
    return np.ascontiguousarray(np.stack([c1.T, s1.T, cC.T, sC.T], axis=0))


def const_mats():
    cm = np.zeros((128, 4, 128), np.float32)
    cm[:, 0, :] = np.eye(128, dtype=np.float32)
    cm[:, 1, :] = 1.0
    for m in range(128):
        if m < 64:
            cm[m + 64, 2, m] = -1.0
        else:
            cm[m - 64, 2, m] = 1.0
    for half in (0, 64):
        for mm in range(64):
            m = half + mm
            if mm < 32:
                cm[m + 32, 3, m] = -1.0
            else:
                cm[m - 32, 3, m] = 1.0
    return cm


def band_masks(j):
    m = np.zeros((128, 7, 384), np.float32)
    ki = np.arange(128)[:, None]
    qi = np.arange(128)[None, :]
    for base, reach in ((0, 128), (3, 64)):
        for cls in range(3):
            for s in range(3):
                dk = (s - 1) * 128 + ki - qi
                ok = np.abs(dk) <= reach
                if cls == 0 and s == 0 and j == 0:
                    ok = np.zeros_like(ok)
                if cls == 2 and s == 2 and j == 3:
                    ok = np.zeros_like(ok)
                m[:, base + cls, s * 128:(s + 1) * 128] = np.where(ok, 0.0, NEG)
    for s in range(2):
        kiw = s * 128 + ki
        ok = np.abs(kiw - (64 + qi)) <= 64
        if j == 0:
            ok = ok & (kiw >= 64)
        if j == 3:
            ok = ok & (kiw < 192)
        m[:, 6, s * 128:(s + 1) * 128] = np.where(ok, 0.0, NEG)
    return m


def d_bias_tables(rel_bias, j):
    out = np.full((4, 128, D_NSLOT * 128), NEG, np.float32)
    rows_total = SEQ // 64
    kk = np.arange(128)
    qq = np.arange(128)
    kc = kk % 64
    qc = qq % 64
    cs = np.clip(qc - 8, 0, 48)
    colok = (kc[:, None] >= cs[None, :]) & (kc[:, None] < cs[None, :] + 16)
    dc = np.clip(kc[:, None] - qc[None, :], -15, 15) + 15
    for cname, (sl0, nsl, da0) in D_CLASSES.items():
        ul = {'gen': 4, 't0': 0, 't1': 1, 't14': 14, 't15': 15}[cname]
        u = j * 16 + ul
        qrow = 2 * u + qq // 64
        rs = np.clip(qrow - 4, 0, rows_total - 8)
        for s in range(nsl):
            a = u + da0 + s
            krow = 2 * a + kk // 64
            rowok = (krow[:, None] >= rs[None, :]) & (krow[:, None] < rs[None, :] + 8)
            inseq = (krow >= 0) & (krow < rows_total)
            ok = rowok & colok & inseq[:, None]
            drr = np.clip(krow[:, None] - qrow[None, :], -7, 7) + 7
            for h in range(4):
                b = rel_bias[h][drr, dc]
                out[h, :, (sl0 + s) * 128:(sl0 + s + 1) * 128] = np.where(ok, b, NEG)
    return out


_PROGS = {}


def _prog(name):
    if name not in _PROGS:
        _PROGS[name] = {'L1': build_L1, 'L2': build_L2, 'L3': build_L3}[name]()
    return _PROGS[name]


def kernel_unfused(x, c, norm_g, w_ada, b_ada, w_in, a_sink, c_q_norm, c_k_norm, d_rel_bias, w_gate_merge,
                   w_branch, w_out, final_g, _depth=DEPTH):
    x = np.asarray(x, np.float32)
    cores = list(range(NCORES))
    cm = const_mats()
    xT = []
    for core in cores:
        b, j = core // 4, core % 4
        xs = x[b, j * T:(j + 1) * T, :]
        xT.append(np.ascontiguousarray(xs.reshape(T, 16, 128).transpose(2, 1, 0)))
    ropes = [rope_consts(j * T) for j in range(4)]
    masks = [band_masks(j) for j in range(4)]
    cT = [fm_vec(np.asarray(c)[b], 16) for b in range(2)]
    for l in range(_depth):
        wl = np.asarray(w_in[l], np.float32)
        wfm = np.stack([fm_weight(wl, list(range(c0, c0 + 128))) for (_k, _m, _i, c0) in FM_CHUNKS], axis=0)
        wv = np.stack([fm_weight(wl, cols) for cols in V_COLS], axis=0)
        wa = np.asarray(w_ada[l], np.float32)
        wada = np.stack([fm_weight(wa, list(range(s * 512, (s + 1) * 512))) for s in range(12)], axis=0)
        bada = fm_vec(b_ada[l], 48)
        ng = fm_vec(norm_g[l], 16)
        qkn = np.ascontiguousarray(np.stack([np.asarray(c_q_norm[l], np.float32),
                                             np.asarray(c_k_norm[l], np.float32)], axis=1))
        in1 = [dict(xT=xT[core], cT=cT[core // 4], wada=wada, bada=bada, ng=ng, wfm=wfm, wv=wv, qkn=qkn,
                    rope=ropes[core % 4], cmat=cm) for core in cores]
        r1 = run_bass_kernel_spmd(_prog('L1'), in1, core_ids=cores).results
        del wfm, wv, wada
        in2 = []
        wg = np.asarray(w_gate_merge[l], np.float32)
        wgm = np.ascontiguousarray(
            wg.reshape(16, 128, 4, 16, 128).transpose(3, 2, 1, 0, 4))
        wb = np.asarray(w_branch[l], np.float32)
        wbr = np.ascontiguousarray(wb.reshape(4, 4, 128, 16, 128).transpose(3, 0, 2, 1, 4))
        wo = np.asarray(w_out[l], np.float32)
        wout = np.ascontiguousarray(wo.reshape(16, 128, 16, 128).transpose(2, 1, 0, 3))
        sink = np.ascontiguousarray(np.broadcast_to(np.asarray(a_sink[l], np.float32)[None, :], (128, 4)))
        biasD = [d_bias_tables(np.asarray(d_rel_bias[l], np.float32), j) for j in range(4)]
        for b in range(2):
            KSb = np.concatenate([np.asarray(r1[b * 4 + j]['KS']) for j in range(4)], axis=2)
            VSb = np.concatenate([np.asarray(r1[b * 4 + j]['VS']) for j in range(4)], axis=0)
            kwi = [KIDX[('A', 0)], KIDX[('A', 1)]] + [KIDX[('B', h)] for h in range(4)] + \
                  [KIDX[('D', h)] for h in range(4)]
            vcols = np.arange(1536)
            Kpad = np.zeros((10, 128, SEQ + 2048), KSb.dtype)
            Kpad[:, :, 1024:1024 + SEQ] = KSb[kwi]
            Vpad = np.zeros((SEQ + 2048, 1536), VSb.dtype)
            Vpad[1024:1024 + SEQ] = VSb[:, vcols]
            KCb = np.ascontiguousarray(KSb[[KIDX[('C', 0)], KIDX[('C', 1)]]])
            VCb = np.ascontiguousarray(VSb[:, VOFF[('C', 0)] * 128:(VOFF[('C', 1)] + 1) * 128])
            for j in range(4):
                core = b * 4 + j
                t0 = j * T
                in2.append(dict(xT=xT[core], ADA=np.asarray(r1[core]['ADA']), HT=np.asarray(r1[core]['HT']),
                                QS=np.asarray(r1[core]['QS']), GS=np.asarray(r1[core]['GS']),
                                KW=np.ascontiguousarray(Kpad[:, :, t0:t0 + WIN]), KC=KCb,
                                VW=np.ascontiguousarray(Vpad[t0:t0 + WIN]), VC=VCb,
                                masks=masks[j], biasD=biasD[j], sink=sink, wgm=wgm, wbr=wbr, wout=wout, cmat=cm))
        del r1
        r2 = run_bass_kernel_spmd(_prog('L2'), in2, core_ids=cores).results
        xT = [np.asarray(r2[core]['xTo']) for core in cores]
        del in2, r2
    fg = fm_vec(final_g, 16)
    r3 = run_bass_kernel_spmd(_prog('L3'), [dict(xT=xT[core], fg=fg, cmat=cm) for core in cores],
                              core_ids=cores).results
    out = np.empty((2, SEQ, D), np.float32)
    for core in cores:
        b, j = core // 4, core % 4
        yT = np.asarray(r3[core]['yT'])
        out[b, j * T:(j + 1) * T, :] = yT.transpose(2, 1, 0).reshape(T, D)
    return out


GROUPS = [[0, 1, 2, 3], [4, 5, 6, 7]]
I32 = mybir.dt.int32


def emit_exchange(cx, dr, KS_t, VS_t, KG_t, VG_t, KST, VSG):
    S = cx.S
    kheads = {}
    for (mixer, h), kid in KIDX.items():
        kheads.setdefault(kid // 2, []).append(('DR', 'k', mixer, h))
    kcc, vcc = KIDX[('C', 0)] // 2, VOFF[('C', 0)] // 2
    S.allgather(KS_t[kcc], KG_t[kcc], GROUPS, r=kheads[kcc], w=[('KG', kcc)])
    S.allgather(VS_t[vcc], VG_t[vcc], GROUPS, r=[('DRV', vcc)], w=[('VG', vcc)])
    for c in range(6):
        if c != kcc:
            S.allgather(KS_t[c], KG_t[c], GROUPS, r=kheads[c], w=[('KG', c)])
    for c in range(6):
        if c != vcc:
            S.allgather(VS_t[c], VG_t[c], GROUPS, r=[('DRV', c)], w=[('VG', c)])
    items = []
    n = 0
    for (mixer, h), i in KW_IDX.items():
        kid = KIDX[(mixer, h)]
        KG2 = KG_t[kid // 2].ap().rearrange("r (h t) -> (r h) t", h=2)
        for p in range(4):
            items.append((KST[n % 3], ('KST', n % 3), KG2, cx.KIDXT[:, i * 4 + p:i * 4 + p + 1], ('KG', kid // 2),
                          dr['KW'][i][:, p * 1024:(p + 1) * 1024], ('KWd', i, p)))
            n += 1
    n = 0
    for c in (0, 2, 3, 4, 5):
        VG = VG_t[c].ap()
        for wt in range(32):
            items.append((VSG[n % 4], ('VSG', n % 4), VG, cx.VIDXT[:, wt:wt + 1], ('VG', c),
                          dr['VW'][wt * 128:(wt + 1) * 128, c * 256:(c + 1) * 256], ('VWd', wt, c)))
            n += 1
    pend = []
    for it in items:
        buf, tok, src, idx, srctok, dst, dtok = it
        S.idma(buf[:], src, idx, r=[srctok, 'IDX'], w=[tok])
        pend.append(it)
        if len(pend) > 2:
            b2, t2, _s, _i, _st, d2, dt2 = pend.pop(0)
            S.dma('pool', d2, b2[:], r=[t2], w=[dt2])
    for b2, t2, _s, _i, _st, d2, dt2 in pend:
        S.dma('pool', d2, b2[:], r=[t2], w=[dt2])


def build_fused(depth=DEPTH):
    nc = _new_nc()
    xT_in = _din(nc, 'xT', [128, 16, T])
    cT = _din(nc, 'cT', [128, 16])
    wada = _din(nc, 'wada', [depth, 3, 128, 16, 512])
    bada = _din(nc, 'bada', [128, depth * 12])
    AIN_t = nc.dram_tensor('AIN', [128, depth * 12], F32)
    AOUT_t = nc.dram_tensor('AOUT', [4 * 128, depth * 12], F32)
    ng = _din(nc, 'ng', [depth, 128, 16])
    wfm = _din(nc, 'wfm', [depth, NFM, 128, 16, 128])
    wv = _din(nc, 'wv', [depth, 3, 128, 16, 512])
    qkn = _din(nc, 'qkn', [depth, 128, 2])
    rope = _din(nc, 'rope', [4, 128, T])
    cmat = _din(nc, 'cmat', [128, 4, 128])
    masks = _din(nc, 'masks', [128, 7, 384])
    biasD = _din(nc, 'biasD', [depth, 4, 128, D_NSLOT * 128])
    sink = _din(nc, 'sink', [depth, 128, 4])
    wgm = _din(nc, 'wgm', [depth, 16, 4, 128, 16, 128])
    wbr = _din(nc, 'wbr', [depth, 16, 4, 128, 4, 128])
    wout = _din(nc, 'wout', [depth, 16, 128, 16, 128])
    fg = _din(nc, 'fg', [128, 16])
    kidx = _din(nc, 'kidx', [128, 40], I32)
    vidx = _din(nc, 'vidx', [128, 32], I32)
    yT = _dout(nc, 'yT', [128, 16, T])
    XS = [_dint(nc, f'XS{i}', [128, 16, T]) for i in range(2)]
    QS = _dint(nc, 'QS', [16, 128, T], BF16)
    GS = _dint(nc, 'GS', [16, 128, T], BF16)
    HTd = _dint(nc, 'HTd', [128, 16, T], BF16)
    MIXS = _dint(nc, 'MIXS', [16, 128, T], BF16)
    KW = _dint(nc, 'KW', [10, 128, WIN], BF16)
    VW = _dint(nc, 'VW', [WIN, 1536], BF16)
    KS_t = [[nc.dram_tensor(f'KS{i}_{c}', [256, T], BF16) for c in range(6)] for i in range(2)]
    VS_t = [[nc.dram_tensor(f'VS{i}_{c}', [T, 256], BF16) for c in range(6)] for i in range(2)]
    KG_t = [[nc.dram_tensor(f'KG{i}_{c}', [4 * 256, T], BF16) for c in range(6)] for i in range(2)]
    VG_t = [[nc.dram_tensor(f'VG{i}_{c}', [4 * T, 256], BF16) for c in range(6)] for i in range(2)]
    with ExitStack() as es:
        cx = _setup(nc, es)
        S = cx.S
        S.dma('pool', cx.CM[:], cmat, w=['CONST'])
        KIDXT = es.enter_context(_sbt(nc, "kidx_sb", [128, 40], I32))
        VIDXT = es.enter_context(_sbt(nc, "vidx_sb", [128, 32], I32))
        cx.KIDXT, cx.VIDXT = KIDXT, VIDXT
        S.dma('sp', KIDXT[:], kidx, w=['IDX'])
        S.dma('sp', VIDXT[:], vidx, w=['IDX'])
        ADAALL = es.enter_context(_sbt(nc, "ada_all", [128, depth, 48], F32))
        cx.KST = [es.enter_context(_sbt(nc, f"ex_k{i}", [128, 1024], BF16)) for i in range(3)]
        cx.VSG = [es.enter_context(_sbt(nc, f"ex_v{i}", [128, 256], BF16)) for i in range(4)]
        S.barrier()
        phase_ada_dist(cx, cT, wada, bada, ADAALL, depth, AIN_t, AOUT_t)
        for l in range(depth):
            par = l % 2
            ADA = ADAALL[:, l, :]
            x_in = xT_in if l == 0 else XS[(l - 1) % 2]
            x_out = XS[l % 2]
            kcc = KIDX[('C', 0)] // 2
            vcc = VOFF[('C', 0)] // 2
            assert KIDX[('C', 0)] % 2 == 0 and VOFF[('C', 0)] % 2 == 0
            KG4 = KG_t[par][kcc].ap().rearrange("(r h p) t -> r h p t", r=4, h=2)
            KSl = [KS_t[par][k // 2].ap().rearrange("(h p) t -> h p t", h=2)[k % 2] for k in range(12)]
            dr = dict(xT=x_in, xTo=x_out, ng=ng[l], wfm=wfm[l], wv=wv[l], qkn=qkn[l], rope=rope,
                      QS=QS, GS=GS, KS=KSl, VS=[t_.ap() for t_ in VS_t[par]],
                      KW=KW, VW=VW, masks=masks, biasD=biasD[l],
                      KCr=[[KG4[rk][kvh] for kvh in range(2)] for rk in range(4)],
                      VC=VG_t[par][VOFF[('C', 0)] // 2].ap(),
                      wgm=wgm[l], wbr=wbr[l], wout=wout[l], MIXS=MIXS, kcc=kcc, vcc=vcc)
            dr['exch'] = (lambda KST, VSG, dr=dr, par=par: emit_exchange(
                cx, dr, KS_t[par], VS_t[par], KG_t[par], VG_t[par], KST, VSG))
            with ExitStack() as ls:
                HT = ls.enter_context(_sbt(nc, f"HT_sb{l}", [128, 16, T], BF16))
                phase1(cx, dr, ADA, HT)
                S.dma('sp', HTd, HT[:], w=['DHT'])
                S.barrier(partial=True, keep=XKEEP)
            with ExitStack() as ls:
                BR = ls.enter_context(_sbt(nc, f"BR_sb{l}", [128, 16, T], BF16))
                phase2(cx, dr, BR, sink[l])
                HT = ls.enter_context(_sbt(nc, f"HT2_sb{l}", [128, 16, T], BF16))
                S.dma('sp', HT[:], HTd, w=['HT'])
                phase3(cx, dr, ADA, HT, BR, BR)
        phase_final(cx, dict(xT=XS[(depth - 1) % 2], fg=fg, yT=yT))
    return nc


def window_index_tables(j):
    kt = np.zeros((128, 40), np.int32)
    p128 = np.arange(128)
    for (mixer, h), i in KW_IDX.items():
        kid = KIDX[(mixer, h)]
        for p in range(4):
            hs = (2 * j - 1 + p) % 8
            rank, half = hs // 2, hs % 2
            assert half == (p + 1) % 2
            kt[:, i * 4 + p] = (rank * 256 + (kid % 2) * 128 + p128) * 2 + half
    vt = np.zeros((128, 32), np.int32)
    for wt in range(32):
        vt[:, wt] = ((j * T - 1024 + wt * 128) % SEQ) + p128
    return kt, vt


_FUSED = {}


def kernel(x, c, norm_g, w_ada, b_ada, w_in, a_sink, c_q_norm, c_k_norm, d_rel_bias, w_gate_merge, w_branch,
           w_out, final_g, _depth=DEPTH):
    x = np.asarray(x, np.float32)
    cores = list(range(NCORES))
    dp = _depth
    cm = const_mats()
    f32 = lambda a: np.asarray(a, np.float32)
    wfm = np.stack([np.stack([fm_weight(f32(w_in[l]), list(range(c0, c0 + 128))) for (_k, _m, _i, c0) in FM_CHUNKS],
                             axis=0) for l in range(dp)], axis=0)
    wv = np.stack([np.stack([fm_weight(f32(w_in[l]), cols) for cols in V_COLS], axis=0) for l in range(dp)], axis=0)
    wada_j = [np.ascontiguousarray(np.stack([np.stack(
        [fm_weight(f32(w_ada[l]), list(range(j * 1536 + s * 512, j * 1536 + (s + 1) * 512))) for s in range(3)],
        axis=0) for l in range(dp)], axis=0)) for j in range(4)]
    bada_j = [np.ascontiguousarray(np.concatenate([fm_vec(b_ada[l], 48)[:, j * 12:(j + 1) * 12] for l in range(dp)],
                                                  axis=1)) for j in range(4)]
    ng = np.stack([fm_vec(norm_g[l], 16) for l in range(dp)], axis=0)
    qkn = np.stack([np.stack([f32(c_q_norm[l]), f32(c_k_norm[l])], axis=1) for l in range(dp)], axis=0)
    wgm = np.stack([f32(w_gate_merge[l]).reshape(16, 128, 4, 16, 128).transpose(3, 2, 1, 0, 4)
                    for l in range(dp)], axis=0)
    wbr = np.stack([f32(w_branch[l]).reshape(4, 4, 128, 16, 128).transpose(3, 0, 2, 1, 4) for l in range(dp)], axis=0)
    wout = np.stack([f32(w_out[l]).reshape(16, 128, 16, 128).transpose(2, 1, 0, 3) for l in range(dp)], axis=0)
    sink = np.stack([np.broadcast_to(f32(a_sink[l])[None, :], (128, 4)) for l in range(dp)], axis=0)
    fg = fm_vec(final_g, 16)
    shared = dict(ng=ng, wfm=wfm, wv=wv, qkn=np.ascontiguousarray(qkn),
                  cmat=cm, sink=np.ascontiguousarray(sink), wgm=np.ascontiguousarray(wgm),
                  wbr=np.ascontiguousarray(wbr), wout=np.ascontiguousarray(wout), fg=fg)
    per_j = []
    for j in range(4):
        kt, vt = window_index_tables(j)
        per_j.append(dict(rope=rope_consts(j * T), masks=band_masks(j), kidx=kt, vidx=vt, wada=wada_j[j],
                          bada=bada_j[j],
                          biasD=np.stack([d_bias_tables(f32(d_rel_bias[l]), j) for l in range(dp)], axis=0)))
    in_maps = []
    for core in cores:
        b, j = core // 4, core % 4
        xs = x[b, j * T:(j + 1) * T, :]
        m = dict(shared)
        m.update(per_j[j])
        m['xT'] = np.ascontiguousarray(xs.reshape(T, 16, 128).transpose(2, 1, 0))
        m['cT'] = fm_vec(np.asarray(c)[b], 16)
        in_maps.append(m)
    if dp not in _FUSED:
        _FUSED[dp] = build_fused(dp)
    res = run_bass_kernel_spmd(_FUSED[dp], in_maps, core_ids=cores).results
    out = np.empty((2, SEQ, D), np.float32)
    for core in cores:
        b, j = core // 4, core % 4
        yT = np.asarray(res[core]['yT'])
        out[b, j * T:(j + 1) * T, :] = yT.transpose(2, 1, 0).reshape(T, D)
    return out
```

```python
import numpy as np
import ml_dtypes
from contextlib import ExitStack
import concourse.bass as bass
import concourse.mybir as mybir
from concourse.bass_utils import run_bass_kernel_spmd

F32 = mybir.dt.float32
BF16 = mybir.dt.bfloat16
AF = mybir.ActivationFunctionType
ALU = mybir.AluOpType
NPBF = ml_dtypes.bfloat16

D = 2048
KC = 16
T = 2048
SEQ = 8192
NTB = 4
HD = 128
WIN = 4096
EPS = 1e-6
SCALE = HD ** -0.5
NEG = -30000.0
DEPTH = 4
NCORES = 8

_MIX_COL0 = {'A': 0, 'B': 1536, 'C': 3584, 'D': 5120}
_MIX_KV = {'A': 2, 'B': 4, 'C': 2, 'D': 4}
FM_CHUNKS = []
QIDX = {}
KIDX = {}
GIDX = {}
_qi = _ki = _gi = 0
for _m in 'ABCD':
    c0 = _MIX_COL0[_m]
    kv = _MIX_KV[_m]
    for h in range(4):
        FM_CHUNKS.append(('q', _m, h, c0 + h * 128)); QIDX[(_m, h)] = _qi; _qi += 1
    for h in range(kv):
        FM_CHUNKS.append(('k', _m, h, c0 + 512 + h * 128)); KIDX[(_m, h)] = _ki; _ki += 1
    for h in range(4):
        FM_CHUNKS.append(('g', _m, h, c0 + 512 + 2 * kv * 128 + h * 128)); GIDX[(_m, h)] = _gi; _gi += 1
NFM = len(FM_CHUNKS)
V_COLS = (list(range(768, 1024)) + list(range(4352, 4608)),
          list(range(2560, 3072)),
          list(range(6144, 6656)))
VOFF = {}
for h in range(2):
    VOFF[('A', h)] = h
    VOFF[('C', h)] = 2 + h
for h in range(4):
    VOFF[('B', h)] = 4 + h
    VOFF[('D', h)] = 8 + h
KW_IDX = {}
_i = 0
for _m, n in (('A', 2), ('B', 4), ('D', 4)):
    for h in range(n):
        KW_IDX[(_m, h)] = _i; _i += 1

D_CLASSES = {'gen': (0, 5, -2), 't0': (5, 6, -2), 't1': (11, 5, -2), 't14': (16, 5, -2), 't15': (21, 6, -3)}
D_NSLOT = 27


class Sched:
    CENG = ('pe', 'act', 'dve', 'pool')
    ALLQ = ('pe', 'act', 'dve', 'pool', 'sp')

    def __init__(self, nc, es, n_dma_sems=48):
        self.nc = nc
        self.es = es
        self.e = {'pe': nc.tensor, 'act': nc.scalar, 'dve': nc.vector, 'pool': nc.gpsimd, 'sp': nc.sync}
        self.dsems = [es.enter_context(nc.semaphore(f"dq{i}")) for i in range(n_dma_sems)]
        self.dcnt = [0] * n_dma_sems
        self.dnext = 0
        self.NP = 16
        self.psems = [es.enter_context(nc.semaphore(f"pq{i}")) for i in range(self.NP)]
        self.pcnt = [0] * self.NP
        self.pnext = 0
        self.epoch = 0
        self.waited = {q: {} for q in self.ALLQ}
        self.ccsem = None
        self.cccnt = 0
        self._new_epoch()

    def _new_epoch(self):
        self.epoch += 1
        self.sems = {e: self.es.enter_context(self.nc.semaphore(f"s{self.epoch}{e}")) for e in self.CENG}
        self.cnt = {e: 0 for e in self.CENG}
        self.lastw = {}
        self.readers = {}

    def _wait(self, q, ev):
        sem, val, key, _src = ev
        w = self.waited[q]
        if w.get(key, 0) >= val:
            return
        self.e[q].wait_ge(sem, val)
        w[key] = val

    def _deps(self, r, w):
        evs = []
        for t in r:
            ev = self.lastw.get(t)
            if ev is not None:
                evs.append(ev)
        for t in w:
            ev = self.lastw.get(t)
            if ev is not None:
                evs.append(ev)
            rd = self.readers.get(t)
            if rd:
                evs.extend(rd.values())
        return evs

    def _record(self, ev, r, w):
        for t in w:
            self.lastw[t] = ev
            self.readers[t] = {}
        for t in r:
            self.readers.setdefault(t, {})[ev[2]] = ev

    def op(self, eng, fn, r=(), w=()):
        pr = [t for t in r if isinstance(t, tuple) and t[0] == 'ps']
        if pr:
            r = [t for t in r if not (isinstance(t, tuple) and t[0] == 'ps')]
            w = list(w) + pr
        for ev in self._deps(r, w):
            if eng == 'pe' and ev[3] == 'pe':
                continue
            self._wait(eng, ev)
        ins = fn(self.e[eng])
        self.cnt[eng] += 1
        ins.then_inc(self.sems[eng], 1)
        ev = (self.sems[eng], self.cnt[eng], ('e', self.epoch, eng), eng)
        self._record(ev, r, w)

    def _dsem(self, q):
        if q == 'pool':
            i = self.pnext
            self.pnext = (i + 1) % self.NP
            self.pcnt[i] += 16
            return self.psems[i], self.pcnt[i], ('p', i)
        i = self.dnext
        self.dnext = (i + 1) % len(self.dsems)
        self.dcnt[i] += 16
        return self.dsems[i], self.dcnt[i], ('d', i)

    def dma(self, q, out, in_, r=(), w=()):
        for ev in self._deps(r, w):
            self._wait(q, ev)
        sem, val, key = self._dsem(q)
        self.e[q].dma_start(out=out, in_=in_).then_inc(sem, 16)
        ev = (sem, val, key, 'dma')
        self._record(ev, r, w)

    def idma(self, out, in_, idx_ap, r=(), w=()):
        q = 'pool'
        for ev in self._deps(r, w):
            self._wait(q, ev)
        sem, val, key = self._dsem(q)
        self.e[q].indirect_dma_start(out=out, out_offset=None, in_=in_,
                                     in_offset=bass.IndirectOffsetOnAxis(ap=idx_ap, axis=0)
                                     ).then_inc(sem, 16)
        ev = (sem, val, key, 'dma')
        self._record(ev, r, w)

    def allgather(self, src_t, dst_t, groups, r=(), w=()):
        q = 'pool'
        if self.ccsem is None:
            self.ccsem = self.es.enter_context(self.nc.semaphore("ccsem"))
        for ev in self._deps(r, w):
            self._wait(q, ev)
        self.cccnt += 1
        self.e[q].collective_compute("AllGather", ALU.bypass, replica_groups=groups,
                                     ins=[src_t.ap().opt()], outs=[dst_t.ap().opt()]).then_inc(self.ccsem, 1)
        ev = (self.ccsem, self.cccnt, ('cc',), 'cc')
        self._record(ev, r, w)

    def barrier(self, partial=False, keep=()):
        evs = [(self.sems[e], self.cnt[e], ('e', self.epoch, e), e) for e in self.CENG if self.cnt[e] > 0]
        evs += [(self.dsems[i], self.dcnt[i], ('d', i), 'dma') for i in range(len(self.dsems)) if self.dcnt[i] > 0]
        if not partial:
            evs += [(self.psems[i], self.pcnt[i], ('p', i), 'dma') for i in range(self.NP) if self.pcnt[i] > 0]
            if self.cccnt:
                evs.append((self.ccsem, self.cccnt, ('cc',), 'cc'))
        for q in self.ALLQ:
            for ev in evs:
                if q == 'pe' and ev[3] == 'pe':
                    continue
                self._wait(q, ev)
        kept = {t: ev for t, ev in self.lastw.items() if isinstance(t, tuple) and t[0] in keep} if partial else {}
        if max(self.cnt.values()) > 10000:
            self._new_epoch()
        else:
            self.lastw = {}
            self.readers = {}
        self.lastw.update(kept)


class Ctx:
    pass


XKEEP = ('KG', 'VG', 'KWd', 'VWd')

_SBT_N = [0]


def _sbt(nc, name, shape, dt):
    _SBT_N[0] += 1
    return nc.sbuf_tensor(f"{name}_u{_SBT_N[0]}", shape, dt)


def _mm(out, lhsT, rhs, start, stop):
    return lambda e: e.matmul(out, lhsT, rhs, start=start, stop=stop)


def phase_ada(cx, dr, ADA):
    nc, S, ps = cx.nc, cx.S, cx.ps
    with ExitStack() as st:
        CT = st.enter_context(_sbt(nc, "ada_ct", [128, 16], F32))
        CS = st.enter_context(_sbt(nc, "ada_cs", [128, 16], F32))
        BA = st.enter_context(_sbt(nc, "ada_b", [128, 48], F32))
        WA = [st.enter_context(_sbt(nc, f"ada_w{i}", [128, 16, 512], F32)) for i in range(2)]
        S.dma('sp', CT[:], dr['cT'], w=['CT'])
        S.dma('sp', BA[:], dr['bada'], w=['BA'])
        S.op('act', lambda e: e.activation(out=CS[:], in_=CT[:], func=AF.Silu), r=['CT'], w=['CS'])
        for s in range(12):
            wb = WA[s % 2]
            S.dma('sp', wb[:], dr['wada'][s], w=[('WA', s % 2)])
            for mm in range(4):
                m = s * 4 + mm
                for kc in range(KC):
                    S.op('pe', _mm(ps[0][:, m:m + 1], wb[:, kc, mm * 128:(mm + 1) * 128], CS[:, kc:kc + 1],
                                   kc == 0, kc == KC - 1),
                         r=[('WA', s % 2), 'CS'], w=[('ps', 0)])
        S.op('dve', lambda e: e.tensor_tensor(out=ADA, in0=ps[0][:, 0:48], in1=BA[:], op=ALU.add),
             r=[('ps', 0), 'BA'], w=['ADA'])
        S.barrier()


def phase_ada_dist(cx, cT, wada, bada, ADAALL, depth, AIN_t, AOUT_t):
    nc, S, ps = cx.nc, cx.S, cx.ps
    with ExitStack() as st:
        CT = st.enter_context(_sbt(nc, "ada_ct", [128, 16], F32))
        CS = st.enter_context(_sbt(nc, "ada_cs", [128, 16], F32))
        BA = st.enter_context(_sbt(nc, "ada_b", [128, depth * 12], F32))
        AP_ = st.enter_context(_sbt(nc, "ada_p", [128, depth * 12], F32))
        WA = [st.enter_context(_sbt(nc, f"ada_w{i}", [128, 16, 512], F32)) for i in range(2)]
        S.dma('sp', CT[:], cT, w=['CT'])
        S.dma('sp', BA[:], bada, w=['BA'])
        S.op('act', lambda e: e.activation(out=CS[:], in_=CT[:], func=AF.Silu), r=['CT'], w=['CS'])
        n = 0
        for l in range(depth):
            for s3 in range(3):
                wb = WA[n % 2]
                wtok = ('WA', n % 2)
                n += 1
                S.dma('sp', wb[:], wada[l][s3], w=[wtok])
                for mm in range(4):
                    col = l * 12 + s3 * 4 + mm
                    for kc in range(KC):
                        S.op('pe', _mm(ps[0][:, col:col + 1], wb[:, kc, mm * 128:(mm + 1) * 128], CS[:, kc:kc + 1],
                                       kc == 0, kc == KC - 1),
                             r=[wtok, 'CS'], w=[('ps', 0)])
        S.op('dve', lambda e: e.tensor_tensor(out=AP_[:], in0=ps[0][:, 0:depth * 12], in1=BA[:], op=ALU.add),
             r=[('ps', 0), 'BA'], w=['AP'])
        S.dma('sp', AIN_t.ap(), AP_[:], r=['AP'], w=['AIN'])
        S.allgather(AIN_t, AOUT_t, GROUPS, r=['AIN'], w=['AOUT'])
        AO = AOUT_t.ap()
        for rk in range(4):
            S.dma('sp', ADAALL[:, :, rk * 12:(rk + 1) * 12],
                  AO[rk * 128:(rk + 1) * 128, :].rearrange("p (l m) -> p l m", l=depth), r=['AOUT'], w=['ADA'])
        S.barrier()


def norm_block(cx, XB, SQ, RS, tb, xsrc_tokens):
    nc, S, ps = cx.nc, cx.S, cx.ps
    xtok = xsrc_tokens if xsrc_tokens is not None else 'XB'
    S.op('act', lambda e: e.activation(out=SQ[:].rearrange("p c t -> p (c t)"),
                                       in_=XB[:].rearrange("p c t -> p (c t)"), func=AF.Square),
         r=[xtok], w=['SQ'])
    for c in range(KC):
        S.op('pe', _mm(ps[7][:, :], cx.ONES[:], SQ[:, c, :], c == 0, c == KC - 1), r=['SQ', 'CONST'], w=[('ps', 7)])
    S.op('act', lambda e: e.activation(out=RS[:], in_=ps[7][:, :], func=AF.Sqrt, bias=cx.EPSC[:, 0:1],
                                       scale=1.0 / D), r=[('ps', 7), 'EPSC'], w=['RS'])
    S.op('dve', lambda e: e.reciprocal(out=RS[:], in_=RS[:]), r=['RS'], w=['RS'])


def phase1(cx, dr, ADA, HT, stop=9):
    nc, S, ps = cx.nc, cx.S, cx.ps
    with ExitStack() as st:
        XBb = [st.enter_context(_sbt(nc, f"p1_xb{i}", [128, 16, 512], F32)) for i in range(2)]
        SQ = st.enter_context(_sbt(nc, "p1_sq", [128, 16, 512], BF16))
        RS = st.enter_context(_sbt(nc, "p1_rs", [128, 512], F32))
        TM = [st.enter_context(_sbt(nc, f"p1_tm{i}", [128, 512], F32)) for i in range(2)]
        NG = st.enter_context(_sbt(nc, "p1_ng", [128, 16], F32))
        GG = st.enter_context(_sbt(nc, "p1_gg", [128, 16], F32))
        S.dma('sp', NG[:], dr['ng'], w=['NG'])
        S.op('dve', lambda e: e.scalar_tensor_tensor(out=GG[:], in0=ADA[:, 16:32], scalar=1.0, in1=NG[:],
                                                     op0=ALU.add, op1=ALU.mult), r=['ADA', 'NG'], w=['GG'])
        S.dma('sp', XBb[0][:], dr['xT'][:, :, 0:512], w=[('XB', 0)])
        for tb in range(NTB):
            XB = XBb[tb % 2]
            xtok = ('XB', tb % 2)
            if tb + 1 < NTB:
                S.dma('sp', XBb[(tb + 1) % 2][:], dr['xT'][:, :, (tb + 1) * 512:(tb + 2) * 512],
                      w=[('XB', (tb + 1) % 2)])
            norm_block(cx, XB, SQ, RS, tb, xtok)
            for c in range(KC):
                tm = TM[c % 2]
                S.op('dve', lambda e, c=c, tm=tm, XB=XB: e.tensor_tensor(out=tm[:], in0=XB[:, c, :], in1=RS[:],
                                                                        op=ALU.mult),
                     r=[xtok, 'RS'], w=[('TM', c % 2)])
                S.op('act', lambda e, c=c, tm=tm: e.activation(out=HT[:, c, tb * 512:(tb + 1) * 512], in_=tm[:],
                                                              func=AF.Identity, bias=ADA[:, c:c + 1],
                                                              scale=GG[:, c:c + 1]),
                     r=[('TM', c % 2), 'GG', 'ADA'], w=[('HT', tb)])
        S.barrier()
    if stop < 3:
        return
    with ExitStack() as st:
        WB = [st.enter_context(_sbt(nc, f"p1_wb{i}", [128, 16, 128], BF16)) for i in range(3)]
        WS = [st.enter_context(_sbt(nc, f"p1_ws{i}", [128, 2048], F32)) for i in range(2)]
        WVb = [st.enter_context(_sbt(nc, f"p1_wv{i}", [128, 16, 512], BF16)) for i in range(2)]
        COS1 = st.enter_context(_sbt(nc, "p1_cos1", [128, T], F32))
        SIN1 = st.enter_context(_sbt(nc, "p1_sin1", [128, T], F32))
        COSC = st.enter_context(_sbt(nc, "p1_cosc", [128, T], F32))
        SINC = st.enter_context(_sbt(nc, "p1_sinc", [128, T], F32))
        QN = st.enter_context(_sbt(nc, "p1_qn", [128, 2], F32))
        OST = [st.enter_context(_sbt(nc, f"p1_ost{i}", [128, T], BF16)) for i in range(2)]
        VST = st.enter_context(_sbt(nc, "p1_vst", [128, 8, 512], BF16))
        QB = [st.enter_context(_sbt(nc, f"p1_qb{i}", [128, 512], BF16)) for i in range(2)]
        QF = [st.enter_context(_sbt(nc, f"p1_qf{i}", [128, 512], F32)) for i in range(2)]
        T1 = [st.enter_context(_sbt(nc, f"p1_t1{i}", [128, 512], F32)) for i in range(2)]
        T2 = [st.enter_context(_sbt(nc, f"p1_t2{i}", [128, 512], F32)) for i in range(2)]
        RS2 = [st.enter_context(_sbt(nc, f"p1_rs2{i}", [128, 512], F32)) for i in range(2)]
        S.dma('sp', COS1[:], dr['rope'][0], w=['ROPE'])
        S.dma('sp', SIN1[:], dr['rope'][1], w=['ROPE'])
        S.dma('sp', COSC[:], dr['rope'][2], w=['ROPE'])
        S.dma('sp', SINC[:], dr['rope'][3], w=['ROPE'])
        S.dma('sp', QN[:], dr['qkn'], w=['QN'])
        st8 = {'blk': 0, 'seq': 0}

        def w_load(seq, ci):
            S.dma('sp', WS[seq % 2][:], dr['wfm'][ci].rearrange("p k n -> p (k n)"), w=[('WS', seq % 2)])

        def w_cast(seq, on_act=False):
            ws, wb = WS[seq % 2], WB[seq % 3]
            if on_act:
                S.op('act', lambda e: e.activation(out=wb[:].rearrange("p k n -> p (k n)"), in_=ws[:], func=AF.Copy),
                     r=[('WS', seq % 2)], w=[('WB', seq % 3)])
            else:
                S.op('dve', lambda e: e.tensor_copy(out=wb[:].rearrange("p k n -> p (k n)"), in_=ws[:]),
                     r=[('WS', seq % 2)], w=[('WB', seq % 3)])

        def do_chunk(seq, ci):
            kind, mixer, idx, _c0 = FM_CHUNKS[ci]
            wb = WB[seq % 3]
            wbtok = ('WB', seq % 3)
            ost = OST[seq % 2]
            osttok = ('OST', seq % 2)
            pend = []
            for tb in range(NTB):
                blk = st8['blk']
                st8['blk'] += 1
                pb = blk % 3
                b2 = 3 + blk % 2
                b3 = 5 + blk % 2
                k2 = blk % 2
                tsl = slice(tb * 512, (tb + 1) * 512)
                for kc in range(KC):
                    S.op('pe', _mm(ps[pb][:, :], wb[:, kc, :], HT[:, kc, tsl], kc == 0, kc == KC - 1),
                         r=[wbtok, ('HT', tb)], w=[('ps', pb)])
                if kind == 'g':
                    S.op('act', lambda e, pb=pb, tsl=tsl, ost=ost: e.activation(out=ost[:, tsl], in_=ps[pb][:, :],
                                                                               func=AF.Silu),
                         r=[('ps', pb)], w=[osttok])
                elif mixer == 'D':
                    S.op('act', lambda e, pb=pb, tsl=tsl, ost=ost: e.activation(out=ost[:, tsl], in_=ps[pb][:, :],
                                                                               func=AF.Copy),
                         r=[('ps', pb)], w=[osttok])
                elif mixer in 'AB':
                    qb, t1, t2 = QB[k2], T1[k2], T2[k2]
                    S.op('act', lambda e, pb=pb, qb=qb: e.activation(out=qb[:], in_=ps[pb][:, :], func=AF.Copy),
                         r=[('ps', pb)], w=[('QB', k2)])
                    S.op('dve', lambda e, pb=pb, t1=t1, tsl=tsl: e.tensor_tensor(out=t1[:], in0=ps[pb][:, :],
                                                                                in1=COS1[:, tsl], op=ALU.mult),
                         r=[('ps', pb), 'ROPE'], w=[('T1', k2)])

                    def stage1(qb=qb, t1=t1, t2=t2, b2=b2, k2=k2, tsl=tsl):
                        S.op('pe', _mm(ps[b2][:, :], cx.R1[:], qb[:], True, True), r=[('QB', k2), 'CONST'],
                             w=[('ps', b2)])
                        S.op('dve', lambda e: e.tensor_tensor(out=t2[:], in0=ps[b2][:, :], in1=SIN1[:, tsl],
                                                              op=ALU.mult),
                             r=[('ps', b2), 'ROPE'], w=[('T2', k2)])
                        S.op('dve', lambda e: e.tensor_tensor(out=ost[:, tsl], in0=t1[:], in1=t2[:], op=ALU.add),
                             r=[('T1', k2), ('T2', k2)], w=[osttok])
                    pend.append(stage1)
                else:
                    qb, qf, t1, t2, rs2 = QB[k2], QF[k2], T1[k2], T2[k2], RS2[k2]
                    qcol = 0 if kind == 'q' else 1
                    S.op('act', lambda e, pb=pb, qb=qb: e.activation(out=qb[:], in_=ps[pb][:, :], func=AF.Square),
                         r=[('ps', pb)], w=[('QB', k2)])
                    S.op('dve', lambda e, pb=pb, qf=qf, qcol=qcol: e.tensor_scalar(
                        out=qf[:], in0=ps[pb][:, :], scalar1=QN[:, qcol:qcol + 1], scalar2=None, op0=ALU.mult),
                         r=[('ps', pb), 'QN'], w=[('QF', k2)])

                    def stage1(qb=qb, qf=qf, t1=t1, t2=t2, rs2=rs2, b2=b2, b3=b3, k2=k2, tsl=tsl):
                        S.op('pe', _mm(ps[b3][:, :], cx.ONES[:], qb[:], True, True), r=[('QB', k2), 'CONST'],
                             w=[('ps', b3)])
                        S.op('act', lambda e: e.activation(out=rs2[:], in_=ps[b3][:, :], func=AF.Sqrt,
                                                           bias=cx.EPSC[:, 0:1], scale=1.0 / HD),
                             r=[('ps', b3), 'EPSC'], w=[('RS2', k2)])
                        S.op('dve', lambda e: e.reciprocal(out=rs2[:], in_=rs2[:]),
                             r=[('RS2', k2)], w=[('RS2', k2)])
                        S.op('dve', lambda e: e.tensor_tensor(out=qf[:], in0=qf[:], in1=rs2[:], op=ALU.mult),
                             r=[('QF', k2), ('RS2', k2)], w=[('QF', k2)])
                        S.op('act', lambda e: e.activation(out=qb[:], in_=qf[:], func=AF.Copy),
                             r=[('QF', k2)], w=[('QB', k2)])
                        S.op('pe', _mm(ps[b2][:, :], cx.RC[:], qb[:], True, True), r=[('QB', k2), 'CONST'],
                             w=[('ps', b2)])
                        S.op('dve', lambda e: e.tensor_tensor(out=t1[:], in0=qf[:], in1=COSC[:, tsl], op=ALU.mult),
                             r=[('QF', k2), 'ROPE'], w=[('T1', k2)])
                        S.op('dve', lambda e: e.tensor_tensor(out=t2[:], in0=ps[b2][:, :], in1=SINC[:, tsl],
                                                              op=ALU.mult),
                             r=[('ps', b2), 'ROPE'], w=[('T2', k2)])
                        S.op('dve', lambda e: e.tensor_tensor(out=ost[:, tsl], in0=t1[:], in1=t2[:], op=ALU.add),
                             r=[('T1', k2), ('T2', k2)], w=[osttok])
                    pend.append(stage1)
                while len(pend) > 1:
                    pend.pop(0)()
            while pend:
                pend.pop(0)()
            if kind == 'q':
                dst = dr['QS'][QIDX[(mixer, idx)]]
            elif kind == 'k':
                dst = dr['KS'][KIDX[(mixer, idx)]]
            else:
                dst = dr['GS'][GIDX[(mixer, idx)]]
            S.dma('sp', dst, ost[:], r=[osttok], w=[('DR', kind, mixer, idx)])

        def run_chunks(order, hook=None):
            base = st8['seq']
            n = len(order)
            for i in range(min(2, n)):
                w_load(base + i, order[i])
            w_cast(base)
            for i, ci in enumerate(order):
                if i + 1 < n:
                    kind_, mixer_ = FM_CHUNKS[ci][0], FM_CHUNKS[ci][1]
                    w_cast(base + i + 1, on_act=(kind_ != 'g' and mixer_ != 'D'))
                if i + 2 < n:
                    w_load(base + i + 2, order[i + 2])
                if hook is not None:
                    hook(i, n, (base + i + 1) % 2)
                do_chunk(base + i, ci)
            st8['seq'] = base + n

        vst8 = {'n': 0}

        def v_wload(vg, q4, buf=None):
            n = vst8['n']
            vst8['n'] += 1
            if buf is None:
                buf = n % 2
            ws = WS[buf]
            wstok = ('WS', buf)
            wv = WVb[vg % 2]
            S.dma('sp', ws[:], dr['wv'][vg][:, q4 * 4:(q4 + 1) * 4, :].rearrange("p k n -> p (k n)"), w=[wstok])
            S.op('dve', lambda e: e.tensor_copy(out=wv[:, q4 * 4:(q4 + 1) * 4, :].rearrange("p k n -> p (k n)"),
                                                in_=ws[:]), r=[wstok], w=[('WV', vg % 2)])

        def k_hook(i, n, freebuf):
            if stop >= 5 and i >= n - 4:
                v_wload(0, i - (n - 4), freebuf)

        order_k = [ci for ci, ch in enumerate(FM_CHUNKS) if ch[0] == 'k']
        order_qg = [ci for ci, ch in enumerate(FM_CHUNKS) if ch[0] != 'k']
        if stop == 3:
            order_k, order_qg = order_k[:2], []
        run_chunks(order_k, k_hook)
        for vg in range(3):
            if stop < 5:
                break
            WV = WVb[vg % 2]
            wvtok = ('WV', vg % 2)
            for tt in range(16):
                if vg + 1 < 3 and tt % 4 == 0:
                    v_wload(vg + 1, tt // 4)
                blk = st8['blk']
                st8['blk'] += 1
                pb = blk % 3
                for kc in range(KC):
                    S.op('pe', _mm(ps[pb][:, :], HT[:, kc, tt * 128:(tt + 1) * 128], WV[:, kc, :], kc == 0,
                                   kc == KC - 1),
                         r=[wvtok, ('HT', tt // 4)], w=[('ps', pb)])
                if tt % 2 == 0:
                    S.op('act', lambda e, pb=pb, tt=tt: e.activation(out=VST[:, tt % 8, :], in_=ps[pb][:, :],
                                                                    func=AF.Copy),
                         r=[('ps', pb)], w=['VST'])
                else:
                    S.op('dve', lambda e, pb=pb, tt=tt: e.tensor_copy(out=VST[:, tt % 8, :], in_=ps[pb][:, :]),
                         r=[('ps', pb)], w=['VST'])
                if tt % 8 == 7:
                    h8 = tt // 8
                    if isinstance(dr['VS'], list):
                        for hf in range(2):
                            dst = dr['VS'][vg * 2 + hf].rearrange("(tt p) c -> p tt c", p=128)
                            S.dma('sp', dst[:, h8 * 8:(h8 + 1) * 8, :], VST[:, :, hf * 256:(hf + 1) * 256],
                                  r=['VST'], w=[('DRV', vg * 2 + hf)])
                    else:
                        dst = dr['VS'][:, vg * 512:(vg + 1) * 512].rearrange("(tt p) c -> p tt c", p=128)
                        S.dma('sp', dst[:, h8 * 8:(h8 + 1) * 8, :], VST[:], r=['VST'], w=[('DRV', vg)])
        if 'exch' in dr:
            dr['exch'](cx.KST, cx.VSG)
        run_chunks(order_qg)
        S.barrier(partial=('exch' in dr), keep=XKEEP)


def attn_qk(cx, PT, nq, slots, bias_ap, sbanks, toks):
    S, ps = cx.S, cx.ps
    ktoks, qtok, vtoks, btok = toks
    ns = len(slots)
    per_bank = 512 // nq
    nb = (ns + per_bank - 1) // per_bank
    for b in range(nb):
        s0 = b * per_bank
        s1 = min(ns, s0 + per_bank)
        bk = sbanks[b]
        S.op('pe', _mm(ps[bk][:, 0:(s1 - s0) * nq], cx.IDENT[:], bias_ap[:, s0 * nq:s1 * nq], True, False),
             r=['CONST', btok], w=[('ps', bk)])
        for s in range(s0, s1):
            kT, _v, qT = slots[s]
            S.op('pe', _mm(ps[bk][:, (s - s0) * nq:(s - s0 + 1) * nq], kT, qT, False, True),
                 r=ktoks + [qtok], w=[('ps', bk)])
    pt, pttok = PT
    for b in range(nb):
        s0 = b * per_bank
        s1 = min(ns, s0 + per_bank)
        bk = sbanks[b]
        S.op('act', lambda e, bk=bk, s0=s0, s1=s1: e.activation(out=pt[:, s0 * nq:s1 * nq],
                                                               in_=ps[bk][:, 0:(s1 - s0) * nq], func=AF.Exp,
                                                               scale=SCALE),
             r=[('ps', bk)], w=[pttok])


def attn_pv(cx, PT, nq, slots, obank, dbank, ocol, toks):
    S, ps = cx.S, cx.ps
    ktoks, qtok, vtoks, btok = toks
    ns = len(slots)
    pt, pttok = PT
    for s in range(ns):
        _k, v, _q = slots[s]
        S.op('pe', _mm(ps[obank][:, ocol:ocol + nq], v, pt[:, s * nq:(s + 1) * nq], s == 0, s == ns - 1),
             r=[pttok] + vtoks, w=[('ps', obank)])
    for s in range(ns):
        S.op('pe', _mm(ps[dbank][:, ocol:ocol + nq], cx.ONES[:], pt[:, s * nq:(s + 1) * nq], s == 0, s == ns - 1),
             r=[pttok, 'CONST'], w=[('ps', dbank)])


def phase2(cx, dr, BR, l_sink):
    nc, S, ps = cx.nc, cx.S, cx.ps
    with ExitStack() as st:
        KH = st.enter_context(_sbt(nc, "p2_kh", [128, SEQ], BF16))
        VH = st.enter_context(_sbt(nc, "p2_vh", [128, 64, 128], BF16))
        VH4b = [st.enter_context(_sbt(nc, f"p2_vh4{i}", [128, 32, 128], BF16)) for i in range(2)]
        VH16b = [st.enter_context(_sbt(nc, f"p2_vh16{i}", [128, 32, 128], BF16)) for i in range(2)]
        QHb = [st.enter_context(_sbt(nc, f"p2_qh{i}", [128, T], BF16)) for i in range(2)]
        GHb = [st.enter_context(_sbt(nc, f"p2_gh{i}", [128, T], BF16)) for i in range(2)]
        OB = st.enter_context(_sbt(nc, "p2_ob", [128, T], F32))
        DENB = st.enter_context(_sbt(nc, "p2_den", [128, T], F32))
        PTS = [st.enter_context(_sbt(nc, f"p2_pt{i}", [128, 768], BF16)) for i in range(3)]
        MSK = st.enter_context(_sbt(nc, "p2_msk", [128, 7, 384], BF16))
        BDb = [st.enter_context(_sbt(nc, f"p2_bd{i}", [128, D_NSLOT * 128], BF16)) for i in range(2)]
        SK = st.enter_context(_sbt(nc, "p2_sk", [128, 4], F32))
        ES = st.enter_context(_sbt(nc, "p2_es", [128, 4], F32))
        S.dma('pool', MSK[:], dr['masks'], w=['BIAS'])
        S.dma('sp', SK[:], l_sink, w=['SK'])
        S.op('act', lambda e: e.activation(out=ES[:], in_=SK[:], func=AF.Exp), r=['SK'], w=['ES'])
        state = {'pt': 0, 'ob': 0}

        def next_pt():
            i = state['pt']
            state['pt'] = (i + 1) % 3
            return (PTS[i], ('PT', i))

        def obanks():
            i = state['ob']
            state['ob'] = 1 - i
            return 4 + i, 6 + i

        def finalize(job, sink_col=None):
            hidx = QIDX[(job['mixer'], job['h'])]
            GH = GHb[job['qb']]
            if sink_col is not None:
                S.op('dve', lambda e: e.tensor_scalar(out=DENB[:], in0=DENB[:], scalar1=ES[:, sink_col:sink_col + 1],
                                                      scalar2=None, op0=ALU.add), r=['DENB', 'ES'], w=['DENB'])
            S.op('dve', lambda e: e.reciprocal(out=DENB[:], in_=DENB[:]), r=['DENB'], w=['DENB'])
            S.op('dve', lambda e: e.tensor_tensor(out=OB[:], in0=OB[:], in1=DENB[:], op=ALU.mult),
                 r=['OB', 'DENB'], w=['OB'])
            S.op('dve', lambda e: e.tensor_tensor(out=BR[:, hidx, :], in0=OB[:], in1=GH[:], op=ALU.mult),
                 r=['OB', ('GH', job['qb'])], w=[('BR', hidx)])

        def evac(obank, dbank, ov, dv, first):
            pso = ps[obank][:, :]
            psd = ps[dbank][:, :]
            if len(ov.shape) == 3:
                pso = pso.rearrange("p (a b) -> p a b", a=ov.shape[1])
                psd = psd.rearrange("p (a b) -> p a b", a=ov.shape[1])
            if first:
                S.op('dve', lambda e: e.tensor_copy(out=ov, in_=pso), r=[('ps', obank)], w=['OB'])
                S.op('act', lambda e: e.activation(out=dv, in_=psd, func=AF.Copy), r=[('ps', dbank)], w=['DENB'])
            else:
                S.op('dve', lambda e: e.tensor_tensor(out=ov, in0=ov, in1=pso, op=ALU.add),
                     r=[('ps', obank), 'OB'], w=['OB'])
                S.op('dve', lambda e: e.tensor_tensor(out=dv, in0=dv, in1=psd, op=ALU.add),
                     r=[('ps', dbank), 'DENB'], w=['DENB'])

        jobs = []
        for kvh in range(2):
            for g in range(2):
                jobs.append(dict(mixer='C', h=kvh * 2 + g, kvh=kvh, load_kv=(g == 0)))
        for kvh in range(2):
            for g in range(2):
                jobs.append(dict(mixer='A', h=kvh * 2 + g, kvh=kvh, load_kv=(g == 0)))
        for h in range(4):
            jobs.append(dict(mixer='B', h=h, kvh=h, load_kv=True))
        for h in range(4):
            jobs.append(dict(mixer='D', h=h, kvh=h, load_kv=True))
        kvb = 1
        for i, job in enumerate(jobs):
            job['qb'] = i % 2
            if job['load_kv']:
                kvb = 1 - kvb
            job['kvb'] = kvb
            job['kv_late'] = job['mixer'] == 'C' or (i > 0 and jobs[i - 1]['mixer'] == 'C')

        def kv_views(job):
            if job['mixer'] == 'C':
                return KH, VH, [('KH', 0), ('KH', 1)], [('VH', 0), ('VH', 1)]
            b = job['kvb']
            return (KH[:, b * WIN:(b + 1) * WIN], VH[:, b * 32:(b + 1) * 32, :], [('KH', b)], [('VH', b)])

        def emit_load(job, late):
            m, h, q = job['mixer'], job['h'], job['qb']
            is_c = (m == 'C')
            if job['load_kv'] and late == job['kv_late']:
                Kv, Vv, kt, vt = kv_views(job)
                if is_c:
                    kvh = job['kvh']
                    if 'KCr' in dr:
                        for rk in range(4):
                            S.dma('sp', KH[:, rk * T:(rk + 1) * T], dr['KCr'][rk][kvh], r=[('KG', dr['kcc'])], w=kt)
                    else:
                        S.dma('sp', KH[:], dr['KC'][kvh], w=kt)
                    S.dma('sp', VH[:], dr['VC'][:, kvh * 128:(kvh + 1) * 128].rearrange("(t p) c -> p t c", p=128),
                          r=[('VG', dr.get('vcc', 0))], w=vt)
                else:
                    i = KW_IDX[(m, job['kvh'])]
                    vo = VOFF[(m, job['kvh'])]
                    vsrc = dr['VW'][:, vo * 128:(vo + 1) * 128]
                    vwt = [('VWd', wt, vo // 2) for wt in range(32)]
                    S.dma('sp', Kv, dr['KW'][i], r=[('KWd', i, p) for p in range(4)], w=kt)
                    S.dma('sp', Vv, vsrc.rearrange("(t p) c -> p t c", p=128), r=vwt, w=vt)
                    if m == 'B':
                        for r in range(4):
                            S.dma('sp', VH4b[h % 2][:, r * 8:(r + 1) * 8, :],
                                  vsrc.rearrange("(t p r) c -> r p t c", p=128, r=4)[r], r=vwt, w=[('VH4', h % 2)])
                        for r in range(16):
                            S.dma('sp', VH16b[h % 2][:, r * 2:(r + 1) * 2, :],
                                  vsrc.rearrange("(t p r) c -> r p t c", p=128, r=16)[r], r=vwt, w=[('VH16', h % 2)])
                    if m == 'D':
                        S.dma('pool', BDb[h % 2][:], dr['biasD'][h], w=[('BD', h % 2)])
            if not late:
                S.dma('sp', QHb[q][:], dr['QS'][QIDX[(m, h)]], w=[('QH', q)])
                S.dma('sp', GHb[q][:], dr['GS'][GIDX[(m, h)]], w=[('GH', q)])

        def run_tiles(tiles):
            for t in tiles:
                t['pt'] = None
            n = len(tiles)

            def stage1(t):
                t['pt'] = next_pt()
                attn_qk(cx, t['pt'], 128, t['slots'], t['bias'], t['sb'], t['toks'])

            stage1(tiles[0])
            for i, t in enumerate(tiles):
                if i + 1 < n:
                    stage1(tiles[i + 1])
                attn_pv(cx, t['pt'], 128, t['slots'], t['ob'], t['db'], t['ocol'], t['toks'])
                if t.get('after') is not None:
                    t['after']()

        def banded_tiles(job, mbase, first_pattern):
            Kv, Vv, kt, vt = kv_views(job)
            QH = QHb[job['qb']]
            toks = (kt, ('QH', job['qb']), vt, 'BIAS')
            tiles = []
            for g4 in range(4):
                ob, db = obanks()
                for qq in range(4):
                    qt = g4 * 4 + qq
                    cls = 0 if qt == 0 else (2 if qt == 15 else 1)
                    slots = []
                    for s in range(3):
                        wt = 8 + qt - 1 + s
                        slots.append((Kv[:, wt * 128:(wt + 1) * 128], Vv[:, wt, :], QH[:, qt * 128:(qt + 1) * 128]))
                    tiles.append(dict(slots=slots, bias=MSK[:, mbase + cls, :], sb=[qt % 3], ob=ob, db=db,
                                      ocol=qq * 128, toks=toks))
                tiles[-1]['after'] = (lambda ob=ob, db=db, g4=g4: evac(
                    ob, db, OB[:, g4 * 512:(g4 + 1) * 512], DENB[:, g4 * 512:(g4 + 1) * 512], first_pattern))
            return tiles

        def compute(job):
            m, h = job['mixer'], job['h']
            Kv, Vv, kt, vt = kv_views(job)
            QH = QHb[job['qb']]
            qtok = ('QH', job['qb'])
            VH4, VH16, BD = VH4b[h % 2], VH16b[h % 2], BDb[h % 2]
            if m == 'A':
                run_tiles(banded_tiles(job, 0, True))
                finalize(job, sink_col=h)
            elif m == 'B':
                tiles = banded_tiles(job, 3, True)
                for r in range(4):
                    ob, db = obanks()
                    for it in range(4):
                        cls = 0 if it == 0 else (2 if it == 3 else 1)
                        q0 = r + 4 * it * 128
                        qT = QH[:, q0:q0 + 4 * 127 + 1:4]
                        slots = []
                        for s in range(3):
                            wt = 2 + it - 1 + s
                            k0 = r + 512 * wt
                            slots.append((Kv[:, k0:k0 + 4 * 127 + 1:4], VH4[:, r * 8 + wt, :], qT))
                        tiles.append(dict(slots=slots, bias=MSK[:, 3 + cls, :], sb=[it % 3], ob=ob, db=db,
                                          ocol=it * 128, toks=(kt, qtok, [('VH4', h % 2)], 'BIAS')))
                    tiles[-1]['after'] = (lambda ob=ob, db=db, r=r: evac(
                        ob, db, OB[:, r:r + 4 * 511 + 1:4], DENB[:, r:r + 4 * 511 + 1:4], False))
                for r0 in range(0, 16, 4):
                    ob, db = obanks()
                    for rr in range(4):
                        r = r0 + rr
                        qT = QH[:, r:r + 16 * 127 + 1:16]
                        slots = []
                        for s in range(2):
                            k0 = r + 16 * 128 * s
                            slots.append((Kv[:, k0:k0 + 16 * 127 + 1:16], VH16[:, r * 2 + s, :], qT))
                        tiles.append(dict(slots=slots, bias=MSK[:, 6, 0:256], sb=[rr % 3], ob=ob, db=db,
                                          ocol=rr * 128, toks=(kt, qtok, [('VH16', h % 2)], 'BIAS')))

                    def after16(ob=ob, db=db, r0=r0):
                        ov = OB[:].rearrange("p (i r) -> p r i", r=16)[:, r0:r0 + 4, :]
                        dv = DENB[:].rearrange("p (i r) -> p r i", r=16)[:, r0:r0 + 4, :]
                        evac(ob, db, ov, dv, False)
                    tiles[-1]['after'] = after16
                run_tiles(tiles)
                finalize(job)
            elif m == 'D':
                tiles = []
                for g4 in range(4):
                    ob, db = obanks()
                    for qq in range(4):
                        ul = g4 * 4 + qq
                        cname = {0: 't0', 1: 't1', 14: 't14', 15: 't15'}.get(ul, 'gen')
                        sl0, nsl, da0 = D_CLASSES[cname]
                        slots = []
                        for s in range(nsl):
                            wt = 8 + ul + da0 + s
                            slots.append((Kv[:, wt * 128:(wt + 1) * 128], Vv[:, wt, :],
                                          QH[:, ul * 128:(ul + 1) * 128]))
                        sb = [0, 2] if ul % 2 == 0 else [1, 3]
                        tiles.append(dict(slots=slots, bias=BD[:, sl0 * 128:(sl0 + nsl) * 128], sb=sb, ob=ob, db=db,
                                          ocol=qq * 128, toks=(kt, qtok, vt, ('BD', h % 2))))
                    tiles[-1]['after'] = (lambda ob=ob, db=db, g4=g4: evac(
                        ob, db, OB[:, g4 * 512:(g4 + 1) * 512], DENB[:, g4 * 512:(g4 + 1) * 512], True))
                run_tiles(tiles)
                finalize(job)
            else:
                for qb in range(4):
                    ob, db = obanks()
                    qT = QH[:, qb * 512:(qb + 1) * 512]
                    NKT = 64
                    pts = {}

                    def qk(kt_):
                        bk = kt_ % 3
                        S.op('pe', _mm(ps[bk][:, :], KH[:, kt_ * 128:(kt_ + 1) * 128], qT, True, True),
                             r=kt + [qtok], w=[('ps', bk)])
                        pt, pttok = next_pt()
                        pts[kt_] = (pt, pttok)
                        S.op('act', lambda e, bk=bk, pt=pt: e.activation(out=pt[:, 0:512], in_=ps[bk][:, :],
                                                                        func=AF.Exp, scale=SCALE),
                             r=[('ps', bk)], w=[pttok])

                    def pv(kt_):
                        pt, pttok = pts.pop(kt_)
                        S.op('pe', _mm(ps[ob][:, :], VH[:, kt_, :], pt[:, 0:512], kt_ == 0, kt_ == NKT - 1),
                             r=[pttok] + vt, w=[('ps', ob)])
                        S.op('pe', _mm(ps[db][:, :], cx.ONES[:], pt[:, 0:512], kt_ == 0, kt_ == NKT - 1),
                             r=[pttok, 'CONST'], w=[('ps', db)])

                    qk(0)
                    qk(1)
                    for kt_ in range(NKT):
                        if kt_ + 2 < NKT:
                            qk(kt_ + 2)
                        pv(kt_)
                    evac(ob, db, OB[:, qb * 512:(qb + 1) * 512], DENB[:, qb * 512:(qb + 1) * 512], True)
                finalize(job)

        emit_load(jobs[0], False)
        for i, job in enumerate(jobs):
            emit_load(job, True)
            if i + 1 < len(jobs):
                emit_load(jobs[i + 1], False)
            compute(job)
        S.barrier()


def phase3(cx, dr, ADA, HT, BR, MX, last_fg=None):
    nc, S, ps = cx.nc, cx.S, cx.ps
    with ExitStack() as st:
        WG = [st.enter_context(_sbt(nc, f"p3_wg{i}", [128, 16, 128], BF16)) for i in range(3)]
        WR = [st.enter_context(_sbt(nc, f"p3_wr{i}", [128, 4, 128], BF16)) for i in range(3)]
        SG = [st.enter_context(_sbt(nc, f"p3_sg{i}", [128, 512], F32)) for i in range(2)]
        TP = [st.enter_context(_sbt(nc, f"p3_tp{i}", [128, 512], F32)) for i in range(2)]
        MIXF = st.enter_context(_sbt(nc, "p3_mixf", [128, 4, 512], F32))
        MIXB = [st.enter_context(_sbt(nc, f"p3_mixb{i}", [128, T], BF16)) for i in range(2)]
        blk = 0
        wi = 0
        for m in range(16):
            mixb = MIXB[m % 2]
            for n in range(4):
                wg, wr = WG[wi % 3], WR[wi % 3]
                wtok = ('W3', wi % 3)
                wi += 1
                S.dma('pool', wg[:], dr['wgm'][m, n], w=[wtok])
                S.dma('pool', wr[:], dr['wbr'][m, n], w=[wtok])
                for tb in range(NTB):
                    pa = blk % 2
                    pg = 2 + blk % 2
                    k2 = blk % 2
                    blk += 1
                    tsl = slice(tb * 512, (tb + 1) * 512)
                    for kc in range(4):
                        S.op('pe', _mm(ps[pa][:, :], wr[:, kc, :], BR[:, n * 4 + kc, tsl], kc == 0, kc == 3),
                             r=[wtok, 'BR'], w=[('ps', pa)])
                    for kc in range(KC):
                        S.op('pe', _mm(ps[pg][:, :], wg[:, kc, :], HT[:, kc, tsl], kc == 0, kc == KC - 1),
                             r=[wtok, 'HT', ('HT', tb)], w=[('ps', pg)])
                    sg, tp = SG[k2], TP[k2]
                    S.op('act', lambda e, pg=pg, sg=sg: e.activation(out=sg[:], in_=ps[pg][:, :], func=AF.Sigmoid),
                         r=[('ps', pg)], w=[('SG', k2)])
                    if n == 0:
                        S.op('dve', lambda e, pa=pa, sg=sg, tb=tb: e.tensor_tensor(out=MIXF[:, tb, :], in0=ps[pa][:, :],
                                                                                  in1=sg[:], op=ALU.mult),
                             r=[('ps', pa), ('SG', k2)], w=[('MIXF', tb)])
                    else:
                        S.op('dve', lambda e, pa=pa, sg=sg, tp=tp: e.tensor_tensor(out=tp[:], in0=ps[pa][:, :],
                                                                                  in1=sg[:], op=ALU.mult),
                             r=[('ps', pa), ('SG', k2)], w=[('TP', k2)])
                        if n < 3:
                            S.op('dve', lambda e, tp=tp, tb=tb: e.tensor_tensor(out=MIXF[:, tb, :], in0=MIXF[:, tb, :],
                                                                               in1=tp[:], op=ALU.add),
                                 r=[('TP', k2), ('MIXF', tb)], w=[('MIXF', tb)])
                        else:
                            S.op('dve', lambda e, tp=tp, tb=tb, tsl=tsl, mixb=mixb: e.tensor_tensor(
                                out=mixb[:, tsl], in0=MIXF[:, tb, :], in1=tp[:], op=ALU.add),
                                 r=[('TP', k2), ('MIXF', tb)], w=[('MIXB', m % 2)])
            S.dma('sp', dr['MIXS'][m], mixb[:], r=[('MIXB', m % 2)], w=[('DRM', m)])
        S.barrier()
    with ExitStack() as st:
        WO = [st.enter_context(_sbt(nc, f"p3_wo{i}", [128, 16, 128], BF16)) for i in range(3)]
        XR = [st.enter_context(_sbt(nc, f"p3_xr{i}", [128, T], F32)) for i in range(2)]
        for tb in range(NTB):
            S.dma('sp', MX[:, :, tb * 512:(tb + 1) * 512],
                  dr['MIXS'][:, :, tb * 512:(tb + 1) * 512].rearrange("m p t -> p m t"), w=[('MX', tb)])
        blk = 0
        for mo in range(16):
            wo = WO[mo % 3]
            xr = XR[mo % 2]
            S.dma('pool', wo[:], dr['wout'][mo], w=[('WO', mo % 3)])
            S.dma('sp', xr[:], dr['xT'][:, mo, :], w=[('XR', mo % 2)])
            for tb in range(NTB):
                pb = blk % 4
                blk += 1
                tsl = slice(tb * 512, (tb + 1) * 512)
                for kc in range(KC):
                    S.op('pe', _mm(ps[pb][:, :], wo[:, kc, :], MX[:, kc, tsl], kc == 0, kc == KC - 1),
                         r=[('WO', mo % 3), ('MX', tb)], w=[('ps', pb)])
                S.op('dve', lambda e, pb=pb, xr=xr, tsl=tsl, mo=mo: e.scalar_tensor_tensor(
                    out=xr[:, tsl], in0=ps[pb][:, :], scalar=ADA[:, 32 + mo:33 + mo], in1=xr[:, tsl],
                    op0=ALU.mult, op1=ALU.add), r=[('ps', pb), ('XR', mo % 2), 'ADA'], w=[('XR', mo % 2)])
            S.dma('sp', dr['xTo'][:, mo, :], xr[:], r=[('XR', mo % 2)], w=[('DRX', mo)])
        S.barrier()


def phase_final(cx, dr):
    nc, S, ps = cx.nc, cx.S, cx.ps
    with ExitStack() as st:
        XB = st.enter_context(_sbt(nc, "pf_xb", [128, 16, 512], F32))
        SQ = st.enter_context(_sbt(nc, "pf_sq", [128, 16, 512], BF16))
        RS = st.enter_context(_sbt(nc, "pf_rs", [128, 512], F32))
        FG = st.enter_context(_sbt(nc, "pf_fg", [128, 16], F32))
        YB = [st.enter_context(_sbt(nc, f"pf_yb{i}", [128, 16, 512], F32)) for i in range(1)]
        S.dma('sp', FG[:], dr['fg'], w=['FG'])
        for tb in range(NTB):
            S.dma('sp', XB[:], dr['xT'][:, :, tb * 512:(tb + 1) * 512], w=['XB'])
            norm_block(cx, XB, SQ, RS, tb, None)
            yb = YB[0]
            for c in range(KC):
                S.op('dve', lambda e, c=c, yb=yb: e.scalar_tensor_tensor(out=yb[:, c, :], in0=XB[:, c, :],
                                                                        scalar=FG[:, c:c + 1], in1=RS[:],
                                                                        op0=ALU.mult, op1=ALU.mult),
                     r=['XB', 'RS', 'FG'], w=['YB'])
            S.dma('sp', dr['yT'][:, :, tb * 512:(tb + 1) * 512], yb[:], r=['YB'], w=[('DRY', tb)])
        S.barrier()


def _new_nc():
    return bass.Bass("TRN2", target_bir_lowering=False)


def _setup(nc, es):
    cx = Ctx()
    cx.nc = nc
    cx.S = Sched(nc, es)
    cx.ps = [es.enter_context(nc.psum_tensor(f"psb{i}", [128, 512], F32)) for i in range(8)]
    CM = es.enter_context(_sbt(nc, "cmat_sb", [128, 4, 128], BF16))
    cx.CM = CM
    cx.IDENT = CM[:, 0, :]
    cx.ONES = CM[:, 1, :]
    cx.R1 = CM[:, 2, :]
    cx.RC = CM[:, 3, :]
    cx.EPSC = es.enter_context(_sbt(nc, "epsc_sb", [128, 1], F32))
    cx.S.op('dve', lambda e: e.memset(cx.EPSC[:], EPS), w=['EPSC'])
    return cx


def _din(nc, name, shape, dt=F32):
    return nc.dram_tensor(name, list(shape), dt, kind="ExternalInput").ap()


def _dout(nc, name, shape, dt=F32):
    return nc.dram_tensor(name, list(shape), dt, kind="ExternalOutput").ap()


def _dint(nc, name, shape, dt=F32):
    return nc.dram_tensor(name, list(shape), dt, kind="Internal").ap()


def build_L1(stop=9):
    nc = _new_nc()
    dr = {}
    dr['xT'] = _din(nc, 'xT', [128, 16, T])
    dr['cT'] = _din(nc, 'cT', [128, 16])
    dr['wada'] = _din(nc, 'wada', [12, 128, 16, 512])
    dr['bada'] = _din(nc, 'bada', [128, 48])
    dr['ng'] = _din(nc, 'ng', [128, 16])
    dr['wfm'] = _din(nc, 'wfm', [NFM, 128, 16, 128])
    dr['wv'] = _din(nc, 'wv', [3, 128, 16, 512])
    dr['qkn'] = _din(nc, 'qkn', [128, 2])
    dr['rope'] = _din(nc, 'rope', [4, 128, T])
    cmat = _din(nc, 'cmat', [128, 4, 128])
    dr['QS'] = _dout(nc, 'QS', [16, 128, T], BF16)
    dr['KS'] = _dout(nc, 'KS', [12, 128, T], BF16)
    dr['GS'] = _dout(nc, 'GS', [16, 128, T], BF16)
    dr['VS'] = _dout(nc, 'VS', [T, 1536], BF16)
    HTo = _dout(nc, 'HT', [128, 16, T], BF16)
    ADAo = _dout(nc, 'ADA', [128, 48])
    with ExitStack() as es:
        cx = _setup(nc, es)
        S = cx.S
        S.dma('pool', cx.CM[:], cmat, w=['CONST'])
        ADA = es.enter_context(_sbt(nc, "ADA_sb", [128, 48], F32))
        HT = es.enter_context(_sbt(nc, "HT_sb", [128, 16, T], BF16))
        if stop >= 1:
            phase_ada(cx, dr, ADA[:])
        if stop >= 2:
            phase1(cx, dr, ADA, HT, stop)
        S.dma('sp', HTo, HT[:], r=[('HT', i) for i in range(4)], w=['DHT'])
        S.dma('sp', ADAo, ADA[:], r=['ADA'], w=['DADA'])
        S.barrier()
    return nc


def build_L2():
    nc = _new_nc()
    dr = {}
    dr['xT'] = _din(nc, 'xT', [128, 16, T])
    ADAi = _din(nc, 'ADA', [128, 48])
    HTi = _din(nc, 'HT', [128, 16, T], BF16)
    dr['QS'] = _din(nc, 'QS', [16, 128, T], BF16)
    dr['GS'] = _din(nc, 'GS', [16, 128, T], BF16)
    dr['KW'] = _din(nc, 'KW', [10, 128, WIN], BF16)
    dr['KC'] = _din(nc, 'KC', [2, 128, SEQ], BF16)
    dr['VW'] = _din(nc, 'VW', [WIN, 1536], BF16)
    dr['VC'] = _din(nc, 'VC', [SEQ, 256], BF16)
    dr['masks'] = _din(nc, 'masks', [128, 7, 384])
    dr['biasD'] = _din(nc, 'biasD', [4, 128, D_NSLOT * 128])
    sink = _din(nc, 'sink', [128, 4])
    dr['wgm'] = _din(nc, 'wgm', [16, 4, 128, 16, 128])
    dr['wbr'] = _din(nc, 'wbr', [16, 4, 128, 4, 128])
    dr['wout'] = _din(nc, 'wout', [16, 128, 16, 128])
    cmat = _din(nc, 'cmat', [128, 4, 128])
    dr['MIXS'] = _dint(nc, 'MIXS', [16, 128, T], BF16)
    dr['xTo'] = _dout(nc, 'xTo', [128, 16, T])
    with ExitStack() as es:
        cx = _setup(nc, es)
        S = cx.S
        S.dma('pool', cx.CM[:], cmat, w=['CONST'])
        ADA = es.enter_context(_sbt(nc, "ADA_sb", [128, 48], F32))
        S.dma('sp', ADA[:], ADAi, w=['ADA'])
        BR = es.enter_context(_sbt(nc, "BR_sb", [128, 16, T], BF16))
        phase2(cx, dr, BR, sink)
        HT = es.enter_context(_sbt(nc, "HT_sb", [128, 16, T], BF16))
        S.dma('sp', HT[:], HTi, w=['HT'])
        S.dma('sp', ADA[:], ADAi, w=['ADA'])
        phase3(cx, dr, ADA, HT, BR, BR)
    return nc


def build_L3():
    nc = _new_nc()
    dr = {}
    dr['xT'] = _din(nc, 'xT', [128, 16, T])
    dr['fg'] = _din(nc, 'fg', [128, 16])
    cmat = _din(nc, 'cmat', [128, 4, 128])
    dr['yT'] = _dout(nc, 'yT', [128, 16, T])
    with ExitStack() as es:
        cx = _setup(nc, es)
        cx.S.dma('pool', cx.CM[:], cmat, w=['CONST'])
        phase_final(cx, dr)
    return nc


def fm_vec(v, nchunk):
    return np.ascontiguousarray(np.asarray(v, np.float32).reshape(nchunk, 128).T)


def fm_weight(w, cols):
    K = w.shape[0]
    ws = w[:, cols]
    return np.ascontiguousarray(ws.reshape(K // 128, 128, ws.shape[1]).transpose(1, 0, 2))


def rope_consts(t0):
    pos = np.arange(t0, t0 + T)

    def tables(p, dim):
        inv = (np.float32(10000.0) ** (-np.arange(0, dim, 2, dtype=np.float32) / np.float32(dim))).astype(np.float32)
        ang = p.astype(np.float32)[:, None] * inv[None, :]
        ang = np.concatenate([ang, ang], axis=-1)
        return np.cos(ang).astype(np.float32), np.sin(ang).astype(np.float32)

    c1, s1 = tables(pos, 128)
    cr, sr = tables(pos // 64, 64)
    cc, sc = tables(pos % 64, 64)
    cC = np.concatenate([cr, cc], axis=-1)
    sC = np.concatenate([sr, sc], axis=-1)
    return np.ascontiguousarray(np.stack([c1.T, s1.T, cC.T, sC.T], axis=0))


def const_mats():
    cm = np.zeros((128, 4, 128), np.float32)
    cm[:, 0, :] = np.eye(128, dtype=np.float32)
    cm[:, 1, :] = 1.0
    for m in range(128):
        if m < 64:
            cm[m + 64, 2, m] = -1.0
        else:
            cm[m - 64, 2, m] = 1.0
    for half in (0, 64):
        for mm in range(64):
            m = half + mm
            if mm < 32:
                cm[m + 32, 3, m] = -1.0
            else:
                cm[m - 32, 3, m] = 1.0
    return cm


def band_masks(j):
    m = np.zeros((128, 7, 384), np.float32)
    ki = np.arange(128)[:, None]
    qi = np.arange(128)[None, :]
    for base, reach in ((0, 128), (3, 64)):
        for cls in range(3):
            for s in range(3):
                dk = (s - 1) * 128 + ki - qi
                ok = np.abs(dk) <= reach
                if cls == 0 and s == 0 and j == 0:
                    ok = np.zeros_like(ok)
                if cls == 2 and s == 2 and j == 3:
                    ok = np.zeros_like(ok)
                m[:, base + cls, s * 128:(s + 1) * 128] = np.where(ok, 0.0, NEG)
    for s in range(2):
        kiw = s * 128 + ki
        ok = np.abs(kiw - (64 + qi)) <= 64
        if j == 0:
            ok = ok & (kiw >= 64)
        if j == 3:
            ok = ok & (kiw < 192)
        m[:, 6, s * 128:(s + 1) * 128] = np.where(ok, 0.0, NEG)
    return m


def d_bias_tables(rel_bias, j):
    out = np.full((4, 128, D_NSLOT * 128), NEG, np.float32)
    rows_total = SEQ // 64
    kk = np.arange(128)
    qq = np.arange(128)
    kc = kk % 64
    qc = qq % 64
    cs = np.clip(qc - 8, 0, 48)
    colok = (kc[:, None] >= cs[None, :]) & (kc[:, None] < cs[None, :] + 16)
    dc = np.clip(kc[:, None] - qc[None, :], -15, 15) + 15
    for cname, (sl0, nsl, da0) in D_CLASSES.items():
        ul = {'gen': 4, 't0': 0, 't1': 1, 't14': 14, 't15': 15}[cname]
        u = j * 16 + ul
        qrow = 2 * u + qq // 64
        rs = np.clip(qrow - 4, 0, rows_total - 8)
        for s in range(nsl):
            a = u + da0 + s
            krow = 2 * a + kk // 64
            rowok = (krow[:, None] >= rs[None, :]) & (krow[:, None] < rs[None, :] + 8)
            inseq = (krow >= 0) & (krow < rows_total)
            ok = rowok & colok & inseq[:, None]
            drr = np.clip(krow[:, None] - qrow[None, :], -7, 7) + 7
            for h in range(4):
                b = rel_bias[h][drr, dc]
                out[h, :, (sl0 + s) * 128:(sl0 + s + 1) * 128] = np.where(ok, b, NEG)
    return out


_PROGS = {}


def _prog(name):
    if name not in _PROGS:
        _PROGS[name] = {'L1': build_L1, 'L2': build_L2, 'L3': build_L3}[name]()
    return _PROGS[name]


def kernel_unfused(x, c, norm_g, w_ada, b_ada, w_in, a_sink, c_q_norm, c_k_norm, d_rel_bias, w_gate_merge,
                   w_branch, w_out, final_g, _depth=DEPTH):
    x = np.asarray(x, np.float32)
    cores = list(range(NCORES))
    cm = const_mats()
    xT = []
    for core in cores:
        b, j = core // 4, core % 4
        xs = x[b, j * T:(j + 1) * T, :]
        xT.append(np.ascontiguousarray(xs.reshape(T, 16, 128).transpose(2, 1, 0)))
    ropes = [rope_consts(j * T) for j in range(4)]
    masks = [band_masks(j) for j in range(4)]
    cT = [fm_vec(np.asarray(c)[b], 16) for b in range(2)]
    for l in range(_depth):
        wl = np.asarray(w_in[l], np.float32)
        wfm = np.stack([fm_weight(wl, list(range(c0, c0 + 128))) for (_k, _m, _i, c0) in FM_CHUNKS], axis=0)
        wv = np.stack([fm_weight(wl, cols) for cols in V_COLS], axis=0)
        wa = np.asarray(w_ada[l], np.float32)
        wada = np.stack([fm_weight(wa, list(range(s * 512, (s + 1) * 512))) for s in range(12)], axis=0)
        bada = fm_vec(b_ada[l], 48)
        ng = fm_vec(norm_g[l], 16)
        qkn = np.ascontiguousarray(np.stack([np.asarray(c_q_norm[l], np.float32),
                                             np.asarray(c_k_norm[l], np.float32)], axis=1))
        in1 = [dict(xT=xT[core], cT=cT[core // 4], wada=wada, bada=bada, ng=ng, wfm=wfm, wv=wv, qkn=qkn,
                    rope=ropes[core % 4], cmat=cm) for core in cores]
        r1 = run_bass_kernel_spmd(_prog('L1'), in1, core_ids=cores).results
        del wfm, wv, wada
        in2 = []
        wg = np.asarray(w_gate_merge[l], np.float32)
        wgm = np.ascontiguousarray(
            wg.reshape(16, 128, 4, 16, 128).transpose(3, 2, 1, 0, 4))
        wb = np.asarray(w_branch[l], np.float32)
        wbr = np.ascontiguousarray(wb.reshape(4, 4, 128, 16, 128).transpose(3, 0, 2, 1, 4))
        wo = np.asarray(w_out[l], np.float32)
        wout = np.ascontiguousarray(wo.reshape(16, 128, 16, 128).transpose(2, 1, 0, 3))
        sink = np.ascontiguousarray(np.broadcast_to(np.asarray(a_sink[l], np.float32)[None, :], (128, 4)))
        biasD = [d_bias_tables(np.asarray(d_rel_bias[l], np.float32), j) for j in range(4)]
        for b in range(2):
            KSb = np.concatenate([np.asarray(r1[b * 4 + j]['KS']) for j in range(4)], axis=2)
            VSb = np.concatenate([np.asarray(r1[b * 4 + j]['VS']) for j in range(4)], axis=0)
            kwi = [KIDX[('A', 0)], KIDX[('A', 1)]] + [KIDX[('B', h)] for h in range(4)] + \
                  [KIDX[('D', h)] for h in range(4)]
            vcols = np.arange(1536)
            Kpad = np.zeros((10, 128, SEQ + 2048), KSb.dtype)
            Kpad[:, :, 1024:1024 + SEQ] = KSb[kwi]
            Vpad = np.zeros((SEQ + 2048, 1536), VSb.dtype)
            Vpad[1024:1024 + SEQ] = VSb[:, vcols]
            KCb = np.ascontiguousarray(KSb[[KIDX[('C', 0)], KIDX[('C', 1)]]])
            VCb = np.ascontiguousarray(VSb[:, VOFF[('C', 0)] * 128:(VOFF[('C', 1)] + 1) * 128])
            for j in range(4):
                core = b * 4 + j
                t0 = j * T
                in2.append(dict(xT=xT[core], ADA=np.asarray(r1[core]['ADA']), HT=np.asarray(r1[core]['HT']),
                                QS=np.asarray(r1[core]['QS']), GS=np.asarray(r1[core]['GS']),
                                KW=np.ascontiguousarray(Kpad[:, :, t0:t0 + WIN]), KC=KCb,
                                VW=np.ascontiguousarray(Vpad[t0:t0 + WIN]), VC=VCb,
                                masks=masks[j], biasD=biasD[j], sink=sink, wgm=wgm, wbr=wbr, wout=wout, cmat=cm))
        del r1
        r2 = run_bass_kernel_spmd(_prog('L2'), in2, core_ids=cores).results
        xT = [np.asarray(r2[core]['xTo']) for core in cores]
        del in2, r2
    fg = fm_vec(final_g, 16)
    r3 = run_bass_kernel_spmd(_prog('L3'), [dict(xT=xT[core], fg=fg, cmat=cm) for core in cores],
                              core_ids=cores).results
    out = np.empty((2, SEQ, D), np.float32)
    for core in cores:
        b, j = core // 4, core % 4
        yT = np.asarray(r3[core]['yT'])
        out[b, j * T:(j + 1) * T, :] = yT.transpose(2, 1, 0).reshape(T, D)
    return out


GROUPS = [[0, 1, 2, 3], [4, 5, 6, 7]]
I32 = mybir.dt.int32


def emit_exchange(cx, dr, KS_t, VS_t, KG_t, VG_t, KST, VSG):
    S = cx.S
    kheads = {}
    for (mixer, h), kid in KIDX.items():
        kheads.setdefault(kid // 2, []).append(('DR', 'k', mixer, h))
    kcc, vcc = KIDX[('C', 0)] // 2, VOFF[('C', 0)] // 2
    S.allgather(KS_t[kcc], KG_t[kcc], GROUPS, r=kheads[kcc], w=[('KG', kcc)])
    S.allgather(VS_t[vcc], VG_t[vcc], GROUPS, r=[('DRV', vcc)], w=[('VG', vcc)])
    for c in range(6):
        if c != kcc:
            S.allgather(KS_t[c], KG_t[c], GROUPS, r=kheads[c], w=[('KG', c)])
    for c in range(6):
        if c != vcc:
            S.allgather(VS_t[c], VG_t[c], GROUPS, r=[('DRV', c)], w=[('VG', c)])
    items = []
    n = 0
    for (mixer, h), i in KW_IDX.items():
        kid = KIDX[(mixer, h)]
        KG2 = KG_t[kid // 2].ap().rearrange("r (h t) -> (r h) t", h=2)
        for p in range(4):
            items.append((KST[n % 3], ('KST', n % 3), KG2, cx.KIDXT[:, i * 4 + p:i * 4 + p + 1], ('KG', kid // 2),
                          dr['KW'][i][:, p * 1024:(p + 1) * 1024], ('KWd', i, p)))
            n += 1
    n = 0
    for c in (0, 2, 3, 4, 5):
        VG = VG_t[c].ap()
        for wt in range(32):
            items.append((VSG[n % 4], ('VSG', n % 4), VG, cx.VIDXT[:, wt:wt + 1], ('VG', c),
                          dr['VW'][wt * 128:(wt + 1) * 128, c * 256:(c + 1) * 256], ('VWd', wt, c)))
            n += 1
    pend = []
    for it in items:
        buf, tok, src, idx, srctok, dst, dtok = it
        S.idma(buf[:], src, idx, r=[srctok, 'IDX'], w=[tok])
        pend.append(it)
        if len(pend) > 2:
            b2, t2, _s, _i, _st, d2, dt2 = pend.pop(0)
            S.dma('pool', d2, b2[:], r=[t2], w=[dt2])
    for b2, t2, _s, _i, _st, d2, dt2 in pend:
        S.dma('pool', d2, b2[:], r=[t2], w=[dt2])


def build_fused(depth=DEPTH):
    nc = _new_nc()
    xT_in = _din(nc, 'xT', [128, 16, T])
    cT = _din(nc, 'cT', [128, 16])
    wada = _din(nc, 'wada', [depth, 3, 128, 16, 512])
    bada = _din(nc, 'bada', [128, depth * 12])
    AIN_t = nc.dram_tensor('AIN', [128, depth * 12], F32)
    AOUT_t = nc.dram_tensor('AOUT', [4 * 128, depth * 12], F32)
    ng = _din(nc, 'ng', [depth, 128, 16])
    wfm = _din(nc, 'wfm', [depth, NFM, 128, 16, 128])
    wv = _din(nc, 'wv', [depth, 3, 128, 16, 512])
    qkn = _din(nc, 'qkn', [depth, 128, 2])
    rope = _din(nc, 'rope', [4, 128, T])
    cmat = _din(nc, 'cmat', [128, 4, 128])
    masks = _din(nc, 'masks', [128, 7, 384])
    biasD = _din(nc, 'biasD', [depth, 4, 128, D_NSLOT * 128])
    sink = _din(nc, 'sink', [depth, 128, 4])
    wgm = _din(nc, 'wgm', [depth, 16, 4, 128, 16, 128])
    wbr = _din(nc, 'wbr', [depth, 16, 4, 128, 4, 128])
    wout = _din(nc, 'wout', [depth, 16, 128, 16, 128])
    fg = _din(nc, 'fg', [128, 16])
    kidx = _din(nc, 'kidx', [128, 40], I32)
    vidx = _din(nc, 'vidx', [128, 32], I32)
    yT = _dout(nc, 'yT', [128, 16, T])
    XS = [_dint(nc, f'XS{i}', [128, 16, T]) for i in range(2)]
    QS = _dint(nc, 'QS', [16, 128, T], BF16)
    GS = _dint(nc, 'GS', [16, 128, T], BF16)
    HTd = _dint(nc, 'HTd', [128, 16, T], BF16)
    MIXS = _dint(nc, 'MIXS', [16, 128, T], BF16)
    KW = _dint(nc, 'KW', [10, 128, WIN], BF16)
    VW = _dint(nc, 'VW', [WIN, 1536], BF16)
    KS_t = [[nc.dram_tensor(f'KS{i}_{c}', [256, T], BF16) for c in range(6)] for i in range(2)]
    VS_t = [[nc.dram_tensor(f'VS{i}_{c}', [T, 256], BF16) for c in range(6)] for i in range(2)]
    KG_t = [[nc.dram_tensor(f'KG{i}_{c}', [4 * 256, T], BF16) for c in range(6)] for i in range(2)]
    VG_t = [[nc.dram_tensor(f'VG{i}_{c}', [4 * T, 256], BF16) for c in range(6)] for i in range(2)]
    with ExitStack() as es:
        cx = _setup(nc, es)
        S = cx.S
        S.dma('pool', cx.CM[:], cmat, w=['CONST'])
        KIDXT = es.enter_context(_sbt(nc, "kidx_sb", [128, 40], I32))
        VIDXT = es.enter_context(_sbt(nc, "vidx_sb", [128, 32], I32))
        cx.KIDXT, cx.VIDXT = KIDXT, VIDXT
        S.dma('sp', KIDXT[:], kidx, w=['IDX'])
        S.dma('sp', VIDXT[:], vidx, w=['IDX'])
        ADAALL = es.enter_context(_sbt(nc, "ada_all", [128, depth, 48], F32))
        cx.KST = [es.enter_context(_sbt(nc, f"ex_k{i}", [128, 1024], BF16)) for i in range(3)]
        cx.VSG = [es.enter_context(_sbt(nc, f"ex_v{i}", [128, 256], BF16)) for i in range(4)]
        S.barrier()
        phase_ada_dist(cx, cT, wada, bada, ADAALL, depth, AIN_t, AOUT_t)
        for l in range(depth):
            par = l % 2
            ADA = ADAALL[:, l, :]
            x_in = xT_in if l == 0 else XS[(l - 1) % 2]
            x_out = XS[l % 2]
            kcc = KIDX[('C', 0)] // 2
            vcc = VOFF[('C', 0)] // 2
            assert KIDX[('C', 0)] % 2 == 0 and VOFF[('C', 0)] % 2 == 0
            KG4 = KG_t[par][kcc].ap().rearrange("(r h p) t -> r h p t", r=4, h=2)
            KSl = [KS_t[par][k // 2].ap().rearrange("(h p) t -> h p t", h=2)[k % 2] for k in range(12)]
            dr = dict(xT=x_in, xTo=x_out, ng=ng[l], wfm=wfm[l], wv=wv[l], qkn=qkn[l], rope=rope,
                      QS=QS, GS=GS, KS=KSl, VS=[t_.ap() for t_ in VS_t[par]],
                      KW=KW, VW=VW, masks=masks, biasD=biasD[l],
                      KCr=[[KG4[rk][kvh] for kvh in range(2)] for rk in range(4)],
                      VC=VG_t[par][VOFF[('C', 0)] // 2].ap(),
                      wgm=wgm[l], wbr=wbr[l], wout=wout[l], MIXS=MIXS, kcc=kcc, vcc=vcc)
            dr['exch'] = (lambda KST, VSG, dr=dr, par=par: emit_exchange(
                cx, dr, KS_t[par], VS_t[par], KG_t[par], VG_t[par], KST, VSG))
            with ExitStack() as ls:
                HT = ls.enter_context(_sbt(nc, f"HT_sb{l}", [128, 16, T], BF16))
                phase1(cx, dr, ADA, HT)
                S.dma('sp', HTd, HT[:], w=['DHT'])
                S.barrier(partial=True, keep=XKEEP)
            with ExitStack() as ls:
                BR = ls.enter_context(_sbt(nc, f"BR_sb{l}", [128, 16, T], BF16))
                phase2(cx, dr, BR, sink[l])
                HT = ls.enter_context(_sbt(nc, f"HT2_sb{l}", [128, 16, T], BF16))
                for tb in range(NTB):
                    S.dma('sp', HT[:, :, tb * 512:(tb + 1) * 512], HTd[:, :, tb * 512:(tb + 1) * 512],
                          w=[('HT', tb)])
                phase3(cx, dr, ADA, HT, BR, BR)
        phase_final(cx, dict(xT=XS[(depth - 1) % 2], fg=fg, yT=yT))
    return nc


def window_index_tables(j):
    kt = np.zeros((128, 40), np.int32)
    p128 = np.arange(128)
    for (mixer, h), i in KW_IDX.items():
        kid = KIDX[(mixer, h)]
        for p in range(4):
            hs = (2 * j - 1 + p) % 8
            rank, half = hs // 2, hs % 2
            assert half == (p + 1) % 2
            kt[:, i * 4 + p] = (rank * 256 + (kid % 2) * 128 + p128) * 2 + half
    vt = np.zeros((128, 32), np.int32)
    for wt in range(32):
        vt[:, wt] = ((j * T - 1024 + wt * 128) % SEQ) + p128
    return kt, vt


_FUSED = {}


def kernel(x, c, norm_g, w_ada, b_ada, w_in, a_sink, c_q_norm, c_k_norm, d_rel_bias, w_gate_merge, w_branch,
           w_out, final_g, _depth=DEPTH):
    x = np.asarray(x, np.float32)
    cores = list(range(NCORES))
    dp = _depth
    cm = const_mats()
    f32 = lambda a: np.asarray(a, np.float32)
    wfm = np.stack([np.stack([fm_weight(f32(w_in[l]), list(range(c0, c0 + 128))) for (_k, _m, _i, c0) in FM_CHUNKS],
                             axis=0) for l in range(dp)], axis=0)
    wv = np.stack([np.stack([fm_weight(f32(w_in[l]), cols) for cols in V_COLS], axis=0) for l in range(dp)], axis=0)
    wada_j = [np.ascontiguousarray(np.stack([np.stack(
        [fm_weight(f32(w_ada[l]), list(range(j * 1536 + s * 512, j * 1536 + (s + 1) * 512))) for s in range(3)],
        axis=0) for l in range(dp)], axis=0)) for j in range(4)]
    bada_j = [np.ascontiguousarray(np.concatenate([fm_vec(b_ada[l], 48)[:, j * 12:(j + 1) * 12] for l in range(dp)],
                                                  axis=1)) for j in range(4)]
    ng = np.stack([fm_vec(norm_g[l], 16) for l in range(dp)], axis=0)
    qkn = np.stack([np.stack([f32(c_q_norm[l]), f32(c_k_norm[l])], axis=1) for l in range(dp)], axis=0)
    wgm = np.stack([f32(w_gate_merge[l]).reshape(16, 128, 4, 16, 128).transpose(3, 2, 1, 0, 4)
                    for l in range(dp)], axis=0)
    wbr = np.stack([f32(w_branch[l]).reshape(4, 4, 128, 16, 128).transpose(3, 0, 2, 1, 4) for l in range(dp)], axis=0)
    wout = np.stack([f32(w_out[l]).reshape(16, 128, 16, 128).transpose(2, 1, 0, 3) for l in range(dp)], axis=0)
    sink = np.stack([np.broadcast_to(f32(a_sink[l])[None, :], (128, 4)) for l in range(dp)], axis=0)
    fg = fm_vec(final_g, 16)
    shared = dict(ng=ng, wfm=wfm, wv=wv, qkn=np.ascontiguousarray(qkn),
                  cmat=cm, sink=np.ascontiguousarray(sink), wgm=np.ascontiguousarray(wgm),
                  wbr=np.ascontiguousarray(wbr), wout=np.ascontiguousarray(wout), fg=fg)
    per_j = []
    for j in range(4):
        kt, vt = window_index_tables(j)
        per_j.append(dict(rope=rope_consts(j * T), masks=band_masks(j), kidx=kt, vidx=vt, wada=wada_j[j],
                          bada=bada_j[j],
                          biasD=np.stack([d_bias_tables(f32(d_rel_bias[l]), j) for l in range(dp)], axis=0)))
    in_maps = []
    for core in cores:
        b, j = core // 4, core % 4
        xs = x[b, j * T:(j + 1) * T, :]
        m = dict(shared)
        m.update(per_j[j])
        m['xT'] = np.ascontiguousarray(xs.reshape(T, 16, 128).transpose(2, 1, 0))
        m['cT'] = fm_vec(np.asarray(c)[b], 16)
        in_maps.append(m)
    if dp not in _FUSED:
        _FUSED[dp] = build_fused(dp)
    res = run_bass_kernel_spmd(_FUSED[dp], in_maps, core_ids=cores).results
    out = np.empty((2, SEQ, D), np.float32)
    for core in cores:
        b, j = core // 4, core % 4
        yT = np.asarray(res[core]['yT'])
        out[b, j * T:(j + 1) * T, :] = yT.transpose(2, 1, 0).reshape(T, D)
    return out
```

```python
import numpy as np
import ml_dtypes
from contextlib import ExitStack
import concourse.bass as bass
import concourse.mybir as mybir
from concourse.bass_utils import run_bass_kernel_spmd

F32 = mybir.dt.float32
BF16 = mybir.dt.bfloat16
AF = mybir.ActivationFunctionType
ALU = mybir.AluOpType
NPBF = ml_dtypes.bfloat16

D = 2048
KC = 16
T = 2048
SEQ = 8192
NTB = 4
HD = 128
WIN = 4096
EPS = 1e-6
SCALE = HD ** -0.5
NEG = -30000.0
DEPTH = 4
NCORES = 8

_MIX_COL0 = {'A': 0, 'B': 1536, 'C': 3584, 'D': 5120}
_MIX_KV = {'A': 2, 'B': 4, 'C': 2, 'D': 4}
FM_CHUNKS = []
QIDX = {}
KIDX = {}
GIDX = {}
_qi = _ki = _gi = 0
for _m in 'ABCD':
    c0 = _MIX_COL0[_m]
    kv = _MIX_KV[_m]
    for h in range(4):
        FM_CHUNKS.append(('q', _m, h, c0 + h * 128)); QIDX[(_m, h)] = _qi; _qi += 1
    for h in range(kv):
        FM_CHUNKS.append(('k', _m, h, c0 + 512 + h * 128)); KIDX[(_m, h)] = _ki; _ki += 1
    for h in range(4):
        FM_CHUNKS.append(('g', _m, h, c0 + 512 + 2 * kv * 128 + h * 128)); GIDX[(_m, h)] = _gi; _gi += 1
NFM = len(FM_CHUNKS)
V_COLS = (list(range(768, 1024)) + list(range(4352, 4608)),
          list(range(2560, 3072)),
          list(range(6144, 6656)))
VOFF = {}
for h in range(2):
    VOFF[('A', h)] = h
    VOFF[('C', h)] = 2 + h
for h in range(4):
    VOFF[('B', h)] = 4 + h
    VOFF[('D', h)] = 8 + h
KW_IDX = {}
_i = 0
for _m, n in (('A', 2), ('B', 4), ('D', 4)):
    for h in range(n):
        KW_IDX[(_m, h)] = _i; _i += 1

D_CLASSES = {'gen': (0, 5, -2), 't0': (5, 6, -2), 't1': (11, 5, -2), 't14': (16, 5, -2), 't15': (21, 6, -3)}
D_NSLOT = 27


class Sched:
    CENG = ('pe', 'act', 'dve', 'pool')
    ALLQ = ('pe', 'act', 'dve', 'pool', 'sp')

    def __init__(self, nc, es, n_dma_sems=48):
        self.nc = nc
        self.es = es
        self.e = {'pe': nc.tensor, 'act': nc.scalar, 'dve': nc.vector, 'pool': nc.gpsimd, 'sp': nc.sync}
        self.dsems = [es.enter_context(nc.semaphore(f"dq{i}")) for i in range(n_dma_sems)]
        self.dcnt = [0] * n_dma_sems
        self.dnext = 0
        self.NP = 16
        self.psems = [es.enter_context(nc.semaphore(f"pq{i}")) for i in range(self.NP)]
        self.pcnt = [0] * self.NP
        self.pnext = 0
        self.epoch = 0
        self.waited = {q: {} for q in self.ALLQ}
        self.ccsem = None
        self.cccnt = 0
        self._new_epoch()

    def _new_epoch(self):
        self.epoch += 1
        self.sems = {e: self.es.enter_context(self.nc.semaphore(f"s{self.epoch}{e}")) for e in self.CENG}
        self.cnt = {e: 0 for e in self.CENG}
        self.lastw = {}
        self.readers = {}

    def _wait(self, q, ev):
        sem, val, key, _src = ev
        w = self.waited[q]
        if w.get(key, 0) >= val:
            return
        self.e[q].wait_ge(sem, val)
        w[key] = val

    def _deps(self, r, w):
        evs = []
        for t in r:
            ev = self.lastw.get(t)
            if ev is not None:
                evs.append(ev)
        for t in w:
            ev = self.lastw.get(t)
            if ev is not None:
                evs.append(ev)
            rd = self.readers.get(t)
            if rd:
                evs.extend(rd.values())
        return evs

    def _record(self, ev, r, w):
        for t in w:
            self.lastw[t] = ev
            self.readers[t] = {}
        for t in r:
            self.readers.setdefault(t, {})[ev[2]] = ev

    def op(self, eng, fn, r=(), w=()):
        pr = [t for t in r if isinstance(t, tuple) and t[0] == 'ps']
        if pr:
            r = [t for t in r if not (isinstance(t, tuple) and t[0] == 'ps')]
            w = list(w) + pr
        for ev in self._deps(r, w):
            if eng == 'pe' and ev[3] == 'pe':
                continue
            self._wait(eng, ev)
        ins = fn(self.e[eng])
        self.cnt[eng] += 1
        ins.then_inc(self.sems[eng], 1)
        ev = (self.sems[eng], self.cnt[eng], ('e', self.epoch, eng), eng)
        self._record(ev, r, w)

    def _dsem(self, q):
        if q == 'pool':
            i = self.pnext
            self.pnext = (i + 1) % self.NP
            self.pcnt[i] += 16
            return self.psems[i], self.pcnt[i], ('p', i)
        i = self.dnext
        self.dnext = (i + 1) % len(self.dsems)
        self.dcnt[i] += 16
        return self.dsems[i], self.dcnt[i], ('d', i)

    def dma(self, q, out, in_, r=(), w=()):
        for ev in self._deps(r, w):
            self._wait(q, ev)
        sem, val, key = self._dsem(q)
        self.e[q].dma_start(out=out, in_=in_).then_inc(sem, 16)
        ev = (sem, val, key, 'dma')
        self._record(ev, r, w)

    def idma(self, out, in_, idx_ap, r=(), w=()):
        q = 'pool'
        for ev in self._deps(r, w):
            self._wait(q, ev)
        sem, val, key = self._dsem(q)
        self.e[q].indirect_dma_start(out=out, out_offset=None, in_=in_,
                                     in_offset=bass.IndirectOffsetOnAxis(ap=idx_ap, axis=0)
                                     ).then_inc(sem, 16)
        ev = (sem, val, key, 'dma')
        self._record(ev, r, w)

    def allgather(self, src_t, dst_t, groups, r=(), w=()):
        q = 'pool'
        if self.ccsem is None:
            self.ccsem = self.es.enter_context(self.nc.semaphore("ccsem"))
        for ev in self._deps(r, w):
            self._wait(q, ev)
        self.cccnt += 1
        self.e[q].collective_compute("AllGather", ALU.bypass, replica_groups=groups,
                                     ins=[src_t.ap().opt()], outs=[dst_t.ap().opt()]).then_inc(self.ccsem, 1)
        ev = (self.ccsem, self.cccnt, ('cc',), 'cc')
        self._record(ev, r, w)

    def barrier(self, partial=False, keep=()):
        evs = [(self.sems[e], self.cnt[e], ('e', self.epoch, e), e) for e in self.CENG if self.cnt[e] > 0]
        evs += [(self.dsems[i], self.dcnt[i], ('d', i), 'dma') for i in range(len(self.dsems)) if self.dcnt[i] > 0]
        if not partial:
            evs += [(self.psems[i], self.pcnt[i], ('p', i), 'dma') for i in range(self.NP) if self.pcnt[i] > 0]
            if self.cccnt:
                evs.append((self.ccsem, self.cccnt, ('cc',), 'cc'))
        for q in self.ALLQ:
            for ev in evs:
                if q == 'pe' and ev[3] == 'pe':
                    continue
                self._wait(q, ev)
        kept = {t: ev for t, ev in self.lastw.items() if isinstance(t, tuple) and t[0] in keep} if partial else {}
        if max(self.cnt.values()) > 10000:
            self._new_epoch()
        else:
            self.lastw = {}
            self.readers = {}
        self.lastw.update(kept)


class Ctx:
    pass


XKEEP = ('KG', 'VG', 'KWd', 'VWd')

_SBT_N = [0]


def _sbt(nc, name, shape, dt):
    _SBT_N[0] += 1
    return nc.sbuf_tensor(f"{name}_u{_SBT_N[0]}", shape, dt)


def _mm(out, lhsT, rhs, start, stop):
    return lambda e: e.matmul(out, lhsT, rhs, start=start, stop=stop)


def phase_ada(cx, dr, ADA):
    nc, S, ps = cx.nc, cx.S, cx.ps
    with ExitStack() as st:
        CT = st.enter_context(_sbt(nc, "ada_ct", [128, 16], F32))
        CS = st.enter_context(_sbt(nc, "ada_cs", [128, 16], F32))
        BA = st.enter_context(_sbt(nc, "ada_b", [128, 48], F32))
        WA = [st.enter_context(_sbt(nc, f"ada_w{i}", [128, 16, 512], F32)) for i in range(2)]
        S.dma('sp', CT[:], dr['cT'], w=['CT'])
        S.dma('sp', BA[:], dr['bada'], w=['BA'])
        S.op('act', lambda e: e.activation(out=CS[:], in_=CT[:], func=AF.Silu), r=['CT'], w=['CS'])
        for s in range(12):
            wb = WA[s % 2]
            S.dma('sp', wb[:], dr['wada'][s], w=[('WA', s % 2)])
            for mm in range(4):
                m = s * 4 + mm
                for kc in range(KC):
                    S.op('pe', _mm(ps[0][:, m:m + 1], wb[:, kc, mm * 128:(mm + 1) * 128], CS[:, kc:kc + 1],
                                   kc == 0, kc == KC - 1),
                         r=[('WA', s % 2), 'CS'], w=[('ps', 0)])
        S.op('dve', lambda e: e.tensor_tensor(out=ADA, in0=ps[0][:, 0:48], in1=BA[:], op=ALU.add),
             r=[('ps', 0), 'BA'], w=['ADA'])
        S.barrier()


def phase_ada_dist(cx, cT, wada, bada, ADAALL, depth, AIN_t, AOUT_t):
    nc, S, ps = cx.nc, cx.S, cx.ps
    with ExitStack() as st:
        CT = st.enter_context(_sbt(nc, "ada_ct", [128, 16], F32))
        CS = st.enter_context(_sbt(nc, "ada_cs", [128, 16], F32))
        BA = st.enter_context(_sbt(nc, "ada_b", [128, depth * 12], F32))
        AP_ = st.enter_context(_sbt(nc, "ada_p", [128, depth * 12], F32))
        WA = [st.enter_context(_sbt(nc, f"ada_w{i}", [128, 16, 512], F32)) for i in range(2)]
        S.dma('sp', CT[:], cT, w=['CT'])
        S.dma('sp', BA[:], bada, w=['BA'])
        S.op('act', lambda e: e.activation(out=CS[:], in_=CT[:], func=AF.Silu), r=['CT'], w=['CS'])
        n = 0
        for l in range(depth):
            for s3 in range(3):
                wb = WA[n % 2]
                wtok = ('WA', n % 2)
                n += 1
                S.dma('sp', wb[:], wada[l][s3], w=[wtok])
                for mm in range(4):
                    col = l * 12 + s3 * 4 + mm
                    for kc in range(KC):
                        S.op('pe', _mm(ps[0][:, col:col + 1], wb[:, kc, mm * 128:(mm + 1) * 128], CS[:, kc:kc + 1],
                                       kc == 0, kc == KC - 1),
                             r=[wtok, 'CS'], w=[('ps', 0)])
        S.op('dve', lambda e: e.tensor_tensor(out=AP_[:], in0=ps[0][:, 0:depth * 12], in1=BA[:], op=ALU.add),
             r=[('ps', 0), 'BA'], w=['AP'])
        S.dma('sp', AIN_t.ap(), AP_[:], r=['AP'], w=['AIN'])
        S.allgather(AIN_t, AOUT_t, GROUPS, r=['AIN'], w=['AOUT'])
        AO = AOUT_t.ap()
        for rk in range(4):
            S.dma('sp', ADAALL[:, :, rk * 12:(rk + 1) * 12],
                  AO[rk * 128:(rk + 1) * 128, :].rearrange("p (l m) -> p l m", l=depth), r=['AOUT'], w=['ADA'])
        S.barrier()


def norm_block(cx, XB, SQ, RS, tb, xsrc_tokens):
    nc, S, ps = cx.nc, cx.S, cx.ps
    xtok = xsrc_tokens if xsrc_tokens is not None else 'XB'
    S.op('act', lambda e: e.activation(out=SQ[:].rearrange("p c t -> p (c t)"),
                                       in_=XB[:].rearrange("p c t -> p (c t)"), func=AF.Square),
         r=[xtok], w=['SQ'])
    for c in range(KC):
        S.op('pe', _mm(ps[7][:, :], cx.ONES[:], SQ[:, c, :], c == 0, c == KC - 1), r=['SQ', 'CONST'], w=[('ps', 7)])
    S.op('act', lambda e: e.activation(out=RS[:], in_=ps[7][:, :], func=AF.Sqrt, bias=cx.EPSC[:, 0:1],
                                       scale=1.0 / D), r=[('ps', 7), 'EPSC'], w=['RS'])
    S.op('dve', lambda e: e.reciprocal(out=RS[:], in_=RS[:]), r=['RS'], w=['RS'])


def phase1(cx, dr, ADA, HT, stop=9):
    nc, S, ps = cx.nc, cx.S, cx.ps
    with ExitStack() as st:
        XBb = [st.enter_context(_sbt(nc, f"p1_xb{i}", [128, 16, 512], F32)) for i in range(2)]
        SQ = st.enter_context(_sbt(nc, "p1_sq", [128, 16, 512], BF16))
        RS = st.enter_context(_sbt(nc, "p1_rs", [128, 512], F32))
        TM = [st.enter_context(_sbt(nc, f"p1_tm{i}", [128, 512], F32)) for i in range(2)]
        NG = st.enter_context(_sbt(nc, "p1_ng", [128, 16], F32))
        GG = st.enter_context(_sbt(nc, "p1_gg", [128, 16], F32))
        S.dma('sp', NG[:], dr['ng'], w=['NG'])
        S.op('dve', lambda e: e.scalar_tensor_tensor(out=GG[:], in0=ADA[:, 16:32], scalar=1.0, in1=NG[:],
                                                     op0=ALU.add, op1=ALU.mult), r=['ADA', 'NG'], w=['GG'])
        S.dma('sp', XBb[0][:], dr['xT'][:, :, 0:512], w=[('XB', 0)])
        for tb in range(NTB):
            XB = XBb[tb % 2]
            xtok = ('XB', tb % 2)
            if tb + 1 < NTB:
                S.dma('sp', XBb[(tb + 1) % 2][:], dr['xT'][:, :, (tb + 1) * 512:(tb + 2) * 512],
                      w=[('XB', (tb + 1) % 2)])
            norm_block(cx, XB, SQ, RS, tb, xtok)
            for c in range(KC):
                tm = TM[c % 2]
                S.op('dve', lambda e, c=c, tm=tm, XB=XB: e.tensor_tensor(out=tm[:], in0=XB[:, c, :], in1=RS[:],
                                                                        op=ALU.mult),
                     r=[xtok, 'RS'], w=[('TM', c % 2)])
                S.op('act', lambda e, c=c, tm=tm: e.activation(out=HT[:, c, tb * 512:(tb + 1) * 512], in_=tm[:],
                                                              func=AF.Identity, bias=ADA[:, c:c + 1],
                                                              scale=GG[:, c:c + 1]),
                     r=[('TM', c % 2), 'GG', 'ADA'], w=[('HT', tb)])
        S.barrier()
    if stop < 3:
        return
    with ExitStack() as st:
        WB = [st.enter_context(_sbt(nc, f"p1_wb{i}", [128, 16, 128], BF16)) for i in range(3)]
        WS = [st.enter_context(_sbt(nc, f"p1_ws{i}", [128, 2048], F32)) for i in range(2)]
        WVb = [st.enter_context(_sbt(nc, f"p1_wv{i}", [128, 16, 512], BF16)) for i in range(2)]
        COS1 = st.enter_context(_sbt(nc, "p1_cos1", [128, T], F32))
        SIN1 = st.enter_context(_sbt(nc, "p1_sin1", [128, T], F32))
        COSC = st.enter_context(_sbt(nc, "p1_cosc", [128, T], F32))
        SINC = st.enter_context(_sbt(nc, "p1_sinc", [128, T], F32))
        QN = st.enter_context(_sbt(nc, "p1_qn", [128, 2], F32))
        OST = [st.enter_context(_sbt(nc, f"p1_ost{i}", [128, T], BF16)) for i in range(2)]
        VST = st.enter_context(_sbt(nc, "p1_vst", [128, 8, 512], BF16))
        QB = [st.enter_context(_sbt(nc, f"p1_qb{i}", [128, 512], BF16)) for i in range(2)]
        QF = [st.enter_context(_sbt(nc, f"p1_qf{i}", [128, 512], F32)) for i in range(2)]
        T1 = [st.enter_context(_sbt(nc, f"p1_t1{i}", [128, 512], F32)) for i in range(2)]
        T2 = [st.enter_context(_sbt(nc, f"p1_t2{i}", [128, 512], F32)) for i in range(2)]
        RS2 = [st.enter_context(_sbt(nc, f"p1_rs2{i}", [128, 512], F32)) for i in range(2)]
        S.dma('sp', COS1[:], dr['rope'][0], w=['ROPE'])
        S.dma('sp', SIN1[:], dr['rope'][1], w=['ROPE'])
        S.dma('sp', COSC[:], dr['rope'][2], w=['ROPE'])
        S.dma('sp', SINC[:], dr['rope'][3], w=['ROPE'])
        S.dma('sp', QN[:], dr['qkn'], w=['QN'])
        st8 = {'blk': 0, 'seq': 0}

        def w_load(seq, ci):
            S.dma('sp', WS[seq % 2][:], dr['wfm'][ci].rearrange("p k n -> p (k n)"), w=[('WS', seq % 2)])

        def w_cast(seq, on_act=False):
            ws, wb = WS[seq % 2], WB[seq % 3]
            if on_act:
                S.op('act', lambda e: e.activation(out=wb[:].rearrange("p k n -> p (k n)"), in_=ws[:], func=AF.Copy),
                     r=[('WS', seq % 2)], w=[('WB', seq % 3)])
            else:
                S.op('dve', lambda e: e.tensor_copy(out=wb[:].rearrange("p k n -> p (k n)"), in_=ws[:]),
                     r=[('WS', seq % 2)], w=[('WB', seq % 3)])

        def do_chunk(seq, ci):
            kind, mixer, idx, _c0 = FM_CHUNKS[ci]
            wb = WB[seq % 3]
            wbtok = ('WB', seq % 3)
            ost = OST[seq % 2]
            osttok = ('OST', seq % 2)
            pend = []
            for tb in range(NTB):
                blk = st8['blk']
                st8['blk'] += 1
                pb = blk % 3
                b2 = 3 + blk % 2
                b3 = 5 + blk % 2
                k2 = blk % 2
                tsl = slice(tb * 512, (tb + 1) * 512)
                for kc in range(KC):
                    S.op('pe', _mm(ps[pb][:, :], wb[:, kc, :], HT[:, kc, tsl], kc == 0, kc == KC - 1),
                         r=[wbtok, ('HT', tb)], w=[('ps', pb)])
                if kind == 'g':
                    S.op('act', lambda e, pb=pb, tsl=tsl, ost=ost: e.activation(out=ost[:, tsl], in_=ps[pb][:, :],
                                                                               func=AF.Silu),
                         r=[('ps', pb)], w=[osttok])
                elif mixer == 'D':
                    S.op('act', lambda e, pb=pb, tsl=tsl, ost=ost: e.activation(out=ost[:, tsl], in_=ps[pb][:, :],
                                                                               func=AF.Copy),
                         r=[('ps', pb)], w=[osttok])
                elif mixer in 'AB':
                    qb, t1, t2 = QB[k2], T1[k2], T2[k2]
                    S.op('act', lambda e, pb=pb, qb=qb: e.activation(out=qb[:], in_=ps[pb][:, :], func=AF.Copy),
                         r=[('ps', pb)], w=[('QB', k2)])
                    S.op('dve', lambda e, pb=pb, t1=t1, tsl=tsl: e.tensor_tensor(out=t1[:], in0=ps[pb][:, :],
                                                                                in1=COS1[:, tsl], op=ALU.mult),
                         r=[('ps', pb), 'ROPE'], w=[('T1', k2)])

                    def stage1(qb=qb, t1=t1, t2=t2, b2=b2, k2=k2, tsl=tsl):
                        S.op('pe', _mm(ps[b2][:, :], cx.R1[:], qb[:], True, True), r=[('QB', k2), 'CONST'],
                             w=[('ps', b2)])
                        S.op('dve', lambda e: e.tensor_tensor(out=t2[:], in0=ps[b2][:, :], in1=SIN1[:, tsl],
                                                              op=ALU.mult),
                             r=[('ps', b2), 'ROPE'], w=[('T2', k2)])
                        S.op('dve', lambda e: e.tensor_tensor(out=ost[:, tsl], in0=t1[:], in1=t2[:], op=ALU.add),
                             r=[('T1', k2), ('T2', k2)], w=[osttok])
                    pend.append(stage1)
                else:
                    qb, qf, t1, t2, rs2 = QB[k2], QF[k2], T1[k2], T2[k2], RS2[k2]
                    qcol = 0 if kind == 'q' else 1
                    S.op('act', lambda e, pb=pb, qb=qb: e.activation(out=qb[:], in_=ps[pb][:, :], func=AF.Square),
                         r=[('ps', pb)], w=[('QB', k2)])
                    S.op('dve', lambda e, pb=pb, qf=qf, qcol=qcol: e.tensor_scalar(
                        out=qf[:], in0=ps[pb][:, :], scalar1=QN[:, qcol:qcol + 1], scalar2=None, op0=ALU.mult),
                         r=[('ps', pb), 'QN'], w=[('QF', k2)])

                    def stage1(qb=qb, qf=qf, t1=t1, t2=t2, rs2=rs2, b2=b2, b3=b3, k2=k2, tsl=tsl):
                        S.op('pe', _mm(ps[b3][:, :], cx.ONES[:], qb[:], True, True), r=[('QB', k2), 'CONST'],
                             w=[('ps', b3)])
                        S.op('act', lambda e: e.activation(out=rs2[:], in_=ps[b3][:, :], func=AF.Sqrt,
                                                           bias=cx.EPSC[:, 0:1], scale=1.0 / HD),
                             r=[('ps', b3), 'EPSC'], w=[('RS2', k2)])
                        S.op('dve', lambda e: e.reciprocal(out=rs2[:], in_=rs2[:]),
                             r=[('RS2', k2)], w=[('RS2', k2)])
                        S.op('dve', lambda e: e.tensor_tensor(out=qf[:], in0=qf[:], in1=rs2[:], op=ALU.mult),
                             r=[('QF', k2), ('RS2', k2)], w=[('QF', k2)])
                        S.op('act', lambda e: e.activation(out=qb[:], in_=qf[:], func=AF.Copy),
                             r=[('QF', k2)], w=[('QB', k2)])
                        S.op('pe', _mm(ps[b2][:, :], cx.RC[:], qb[:], True, True), r=[('QB', k2), 'CONST'],
                             w=[('ps', b2)])
                        S.op('dve', lambda e: e.tensor_tensor(out=t1[:], in0=qf[:], in1=COSC[:, tsl], op=ALU.mult),
                             r=[('QF', k2), 'ROPE'], w=[('T1', k2)])
                        S.op('dve', lambda e: e.tensor_tensor(out=t2[:], in0=ps[b2][:, :], in1=SINC[:, tsl],
                                                              op=ALU.mult),
                             r=[('ps', b2), 'ROPE'], w=[('T2', k2)])
                        S.op('dve', lambda e: e.tensor_tensor(out=ost[:, tsl], in0=t1[:], in1=t2[:], op=ALU.add),
                             r=[('T1', k2), ('T2', k2)], w=[osttok])
                    pend.append(stage1)
                while len(pend) > 1:
                    pend.pop(0)()
            while pend:
                pend.pop(0)()
            if kind == 'q':
                dst = dr['QS'][QIDX[(mixer, idx)]]
            elif kind == 'k':
                dst = dr['KS'][KIDX[(mixer, idx)]]
            else:
                dst = dr['GS'][GIDX[(mixer, idx)]]
            S.dma('sp', dst, ost[:], r=[osttok], w=[('DR', kind, mixer, idx)])

        def run_chunks(order, hook=None):
            base = st8['seq']
            n = len(order)
            for i in range(min(2, n)):
                w_load(base + i, order[i])
            w_cast(base)
            for i, ci in enumerate(order):
                if i + 1 < n:
                    kind_, mixer_ = FM_CHUNKS[ci][0], FM_CHUNKS[ci][1]
                    w_cast(base + i + 1, on_act=(kind_ != 'g' and mixer_ != 'D'))
                if i + 2 < n:
                    w_load(base + i + 2, order[i + 2])
                if hook is not None:
                    hook(i, n, (base + i + 1) % 2)
                do_chunk(base + i, ci)
            st8['seq'] = base + n

        vst8 = {'n': 0}

        def v_wload(vg, q4, buf=None):
            n = vst8['n']
            vst8['n'] += 1
            if buf is None:
                buf = n % 2
            ws = WS[buf]
            wstok = ('WS', buf)
            wv = WVb[vg % 2]
            S.dma('sp', ws[:], dr['wv'][vg][:, q4 * 4:(q4 + 1) * 4, :].rearrange("p k n -> p (k n)"), w=[wstok])
            S.op('dve', lambda e: e.tensor_copy(out=wv[:, q4 * 4:(q4 + 1) * 4, :].rearrange("p k n -> p (k n)"),
                                                in_=ws[:]), r=[wstok], w=[('WV', vg % 2)])

        def k_hook(i, n, freebuf):
            if stop >= 5 and i >= n - 4:
                v_wload(0, i - (n - 4), freebuf)

        order_k = [ci for ci, ch in enumerate(FM_CHUNKS) if ch[0] == 'k']
        order_qg = [ci for ci, ch in enumerate(FM_CHUNKS) if ch[0] == 'q'] + \
                   [ci for ci, ch in enumerate(FM_CHUNKS) if ch[0] == 'g']
        if stop == 3:
            order_k, order_qg = order_k[:2], []
        run_chunks(order_k, k_hook)
        for vg in range(3):
            if stop < 5:
                break
            WV = WVb[vg % 2]
            wvtok = ('WV', vg % 2)
            for tt in range(16):
                if vg + 1 < 3 and tt % 4 == 0:
                    v_wload(vg + 1, tt // 4)
                blk = st8['blk']
                st8['blk'] += 1
                pb = blk % 3
                for kc in range(KC):
                    S.op('pe', _mm(ps[pb][:, :], HT[:, kc, tt * 128:(tt + 1) * 128], WV[:, kc, :], kc == 0,
                                   kc == KC - 1),
                         r=[wvtok, ('HT', tt // 4)], w=[('ps', pb)])
                if tt % 2 == 0:
                    S.op('act', lambda e, pb=pb, tt=tt: e.activation(out=VST[:, tt % 8, :], in_=ps[pb][:, :],
                                                                    func=AF.Copy),
                         r=[('ps', pb)], w=['VST'])
                else:
                    S.op('dve', lambda e, pb=pb, tt=tt: e.tensor_copy(out=VST[:, tt % 8, :], in_=ps[pb][:, :]),
                         r=[('ps', pb)], w=['VST'])
                if tt % 8 == 7:
                    h8 = tt // 8
                    if isinstance(dr['VS'], list):
                        for hf in range(2):
                            dst = dr['VS'][vg * 2 + hf].rearrange("(tt p) c -> p tt c", p=128)
                            S.dma('sp', dst[:, h8 * 8:(h8 + 1) * 8, :], VST[:, :, hf * 256:(hf + 1) * 256],
                                  r=['VST'], w=[('DRV', vg * 2 + hf)])
                    else:
                        dst = dr['VS'][:, vg * 512:(vg + 1) * 512].rearrange("(tt p) c -> p tt c", p=128)
                        S.dma('sp', dst[:, h8 * 8:(h8 + 1) * 8, :], VST[:], r=['VST'], w=[('DRV', vg)])
        if 'exch' in dr:
            dr['exch'](cx.KST, cx.VSG)
        run_chunks(order_qg)
        S.barrier(partial=('exch' in dr), keep=XKEEP)


def attn_qk(cx, PT, nq, slots, bias_ap, sbanks, toks):
    S, ps = cx.S, cx.ps
    ktoks, qtok, vtoks, btok = toks
    ns = len(slots)
    per_bank = 512 // nq
    nb = (ns + per_bank - 1) // per_bank
    for b in range(nb):
        s0 = b * per_bank
        s1 = min(ns, s0 + per_bank)
        bk = sbanks[b]
        S.op('pe', _mm(ps[bk][:, 0:(s1 - s0) * nq], cx.IDENT[:], bias_ap[:, s0 * nq:s1 * nq], True, False),
             r=['CONST', btok], w=[('ps', bk)])
        for s in range(s0, s1):
            kT, _v, qT = slots[s]
            S.op('pe', _mm(ps[bk][:, (s - s0) * nq:(s - s0 + 1) * nq], kT, qT, False, True),
                 r=ktoks + [qtok], w=[('ps', bk)])
    pt, pttok = PT
    for b in range(nb):
        s0 = b * per_bank
        s1 = min(ns, s0 + per_bank)
        bk = sbanks[b]
        S.op('act', lambda e, bk=bk, s0=s0, s1=s1: e.activation(out=pt[:, s0 * nq:s1 * nq],
                                                               in_=ps[bk][:, 0:(s1 - s0) * nq], func=AF.Exp,
                                                               scale=SCALE),
             r=[('ps', bk)], w=[pttok])


def attn_pv(cx, PT, nq, slots, obank, dbank, ocol, toks):
    S, ps = cx.S, cx.ps
    ktoks, qtok, vtoks, btok = toks
    ns = len(slots)
    pt, pttok = PT
    for s in range(ns):
        _k, v, _q = slots[s]
        S.op('pe', _mm(ps[obank][:, ocol:ocol + nq], v, pt[:, s * nq:(s + 1) * nq], s == 0, s == ns - 1),
             r=[pttok] + vtoks, w=[('ps', obank)])
    for s in range(ns):
        S.op('pe', _mm(ps[dbank][:, ocol:ocol + nq], cx.ONES[:], pt[:, s * nq:(s + 1) * nq], s == 0, s == ns - 1),
             r=[pttok, 'CONST'], w=[('ps', dbank)])


def phase2(cx, dr, BR, l_sink):
    nc, S, ps = cx.nc, cx.S, cx.ps
    with ExitStack() as st:
        KH = st.enter_context(_sbt(nc, "p2_kh", [128, SEQ], BF16))
        VH = st.enter_context(_sbt(nc, "p2_vh", [128, 64, 128], BF16))
        VH4b = [st.enter_context(_sbt(nc, f"p2_vh4{i}", [128, 32, 128], BF16)) for i in range(2)]
        VH16b = [st.enter_context(_sbt(nc, f"p2_vh16{i}", [128, 32, 128], BF16)) for i in range(2)]
        QHb = [st.enter_context(_sbt(nc, f"p2_qh{i}", [128, T], BF16)) for i in range(2)]
        GHb = [st.enter_context(_sbt(nc, f"p2_gh{i}", [128, T], BF16)) for i in range(2)]
        OB = st.enter_context(_sbt(nc, "p2_ob", [128, T], F32))
        DENB = st.enter_context(_sbt(nc, "p2_den", [128, T], F32))
        PTS = [st.enter_context(_sbt(nc, f"p2_pt{i}", [128, 768], BF16)) for i in range(3)]
        MSK = st.enter_context(_sbt(nc, "p2_msk", [128, 7, 384], BF16))
        BDb = [st.enter_context(_sbt(nc, f"p2_bd{i}", [128, D_NSLOT * 128], BF16)) for i in range(2)]
        SK = st.enter_context(_sbt(nc, "p2_sk", [128, 4], F32))
        ES = st.enter_context(_sbt(nc, "p2_es", [128, 4], F32))
        S.dma('pool', MSK[:], dr['masks'], w=['BIAS'])
        S.dma('sp', SK[:], l_sink, w=['SK'])
        S.op('act', lambda e: e.activation(out=ES[:], in_=SK[:], func=AF.Exp), r=['SK'], w=['ES'])
        state = {'pt': 0, 'ob': 0}

        def next_pt():
            i = state['pt']
            state['pt'] = (i + 1) % 3
            return (PTS[i], ('PT', i))

        def obanks():
            i = state['ob']
            state['ob'] = 1 - i
            return 4 + i, 6 + i

        def finalize(job, sink_col=None):
            hidx = QIDX[(job['mixer'], job['h'])]
            GH = GHb[job['qb']]
            if sink_col is not None:
                S.op('dve', lambda e: e.tensor_scalar(out=DENB[:], in0=DENB[:], scalar1=ES[:, sink_col:sink_col + 1],
                                                      scalar2=None, op0=ALU.add), r=['DENB', 'ES'], w=['DENB'])
            S.op('dve', lambda e: e.reciprocal(out=DENB[:], in_=DENB[:]), r=['DENB'], w=['DENB'])
            S.op('dve', lambda e: e.tensor_tensor(out=OB[:], in0=OB[:], in1=DENB[:], op=ALU.mult),
                 r=['OB', 'DENB'], w=['OB'])
            S.op('pool', lambda e: e.tensor_tensor(out=BR[:, hidx, :], in0=OB[:], in1=GH[:], op=ALU.mult),
                 r=['OB', ('GH', job['qb'])], w=[('BR', hidx)])

        def evac(obank, dbank, ov, dv, first):
            pso = ps[obank][:, :]
            psd = ps[dbank][:, :]
            if len(ov.shape) == 3:
                pso = pso.rearrange("p (a b) -> p a b", a=ov.shape[1])
                psd = psd.rearrange("p (a b) -> p a b", a=ov.shape[1])
            if first:
                S.op('dve', lambda e: e.tensor_copy(out=ov, in_=pso), r=[('ps', obank)], w=['OB'])
                S.op('act', lambda e: e.activation(out=dv, in_=psd, func=AF.Copy), r=[('ps', dbank)], w=['DENB'])
            else:
                S.op('dve', lambda e: e.tensor_tensor(out=ov, in0=ov, in1=pso, op=ALU.add),
                     r=[('ps', obank), 'OB'], w=['OB'])
                S.op('dve', lambda e: e.tensor_tensor(out=dv, in0=dv, in1=psd, op=ALU.add),
                     r=[('ps', dbank), 'DENB'], w=['DENB'])

        jobs = []
        for kvh in range(2):
            for g in range(2):
                jobs.append(dict(mixer='C', h=kvh * 2 + g, kvh=kvh, load_kv=(g == 0)))
        for kvh in range(2):
            for g in range(2):
                jobs.append(dict(mixer='A', h=kvh * 2 + g, kvh=kvh, load_kv=(g == 0)))
        for h in range(4):
            jobs.append(dict(mixer='B', h=h, kvh=h, load_kv=True))
        for h in range(4):
            jobs.append(dict(mixer='D', h=h, kvh=h, load_kv=True))
        kvb = 1
        for i, job in enumerate(jobs):
            job['qb'] = i % 2
            if job['load_kv']:
                kvb = 1 - kvb
            job['kvb'] = kvb
            job['kv_late'] = job['mixer'] == 'C' or (i > 0 and jobs[i - 1]['mixer'] == 'C')

        def kv_views(job):
            if job['mixer'] == 'C':
                return KH, VH, [('KH', 0), ('KH', 1)], [('VH', 0), ('VH', 1)]
            b = job['kvb']
            return (KH[:, b * WIN:(b + 1) * WIN], VH[:, b * 32:(b + 1) * 32, :], [('KH', b)], [('VH', b)])

        def emit_load(job, late):
            m, h, q = job['mixer'], job['h'], job['qb']
            is_c = (m == 'C')
            if job['load_kv'] and late == job['kv_late']:
                Kv, Vv, kt, vt = kv_views(job)
                if is_c:
                    kvh = job['kvh']
                    if 'KCr' in dr:
                        for rk in range(4):
                            S.dma('sp', KH[:, rk * T:(rk + 1) * T], dr['KCr'][rk][kvh], r=[('KG', dr['kcc'])], w=kt)
                    else:
                        S.dma('sp', KH[:], dr['KC'][kvh], w=kt)
                    S.dma('sp', VH[:], dr['VC'][:, kvh * 128:(kvh + 1) * 128].rearrange("(t p) c -> p t c", p=128),
                          r=[('VG', dr.get('vcc', 0))], w=vt)
                else:
                    i = KW_IDX[(m, job['kvh'])]
                    vo = VOFF[(m, job['kvh'])]
                    vsrc = dr['VW'][:, vo * 128:(vo + 1) * 128]
                    vwt = [('VWd', wt, vo // 2) for wt in range(32)]
                    S.dma('sp', Kv, dr['KW'][i], r=[('KWd', i, p) for p in range(4)], w=kt)
                    S.dma('sp', Vv, vsrc.rearrange("(t p) c -> p t c", p=128), r=vwt, w=vt)
                    if m == 'B':
                        for r in range(4):
                            S.dma('sp', VH4b[h % 2][:, r * 8:(r + 1) * 8, :],
                                  vsrc.rearrange("(t p r) c -> r p t c", p=128, r=4)[r], r=vwt, w=[('VH4', h % 2)])
                        for r in range(16):
                            S.dma('sp', VH16b[h % 2][:, r * 2:(r + 1) * 2, :],
                                  vsrc.rearrange("(t p r) c -> r p t c", p=128, r=16)[r], r=vwt, w=[('VH16', h % 2)])
                    if m == 'D':
                        S.dma('pool', BDb[h % 2][:], dr['biasD'][h], w=[('BD', h % 2)])
            if not late:
                S.dma('sp', QHb[q][:], dr['QS'][QIDX[(m, h)]], w=[('QH', q)])
                S.dma('sp', GHb[q][:], dr['GS'][GIDX[(m, h)]], w=[('GH', q)])

        def run_tiles(tiles):
            for t in tiles:
                t['pt'] = None
            n = len(tiles)

            def stage1(t):
                t['pt'] = next_pt()
                attn_qk(cx, t['pt'], 128, t['slots'], t['bias'], t['sb'], t['toks'])

            stage1(tiles[0])
            for i, t in enumerate(tiles):
                if i + 1 < n:
                    stage1(tiles[i + 1])
                attn_pv(cx, t['pt'], 128, t['slots'], t['ob'], t['db'], t['ocol'], t['toks'])
                if t.get('after') is not None:
                    t['after']()

        def banded_tiles(job, mbase, first_pattern):
            Kv, Vv, kt, vt = kv_views(job)
            QH = QHb[job['qb']]
            toks = (kt, ('QH', job['qb']), vt, 'BIAS')
            tiles = []
            for g4 in range(4):
                ob, db = obanks()
                for qq in range(4):
                    qt = g4 * 4 + qq
                    cls = 0 if qt == 0 else (2 if qt == 15 else 1)
                    slots = []
                    for s in range(3):
                        wt = 8 + qt - 1 + s
                        slots.append((Kv[:, wt * 128:(wt + 1) * 128], Vv[:, wt, :], QH[:, qt * 128:(qt + 1) * 128]))
                    tiles.append(dict(slots=slots, bias=MSK[:, mbase + cls, :], sb=[qt % 3], ob=ob, db=db,
                                      ocol=qq * 128, toks=toks))
                tiles[-1]['after'] = (lambda ob=ob, db=db, g4=g4: evac(
                    ob, db, OB[:, g4 * 512:(g4 + 1) * 512], DENB[:, g4 * 512:(g4 + 1) * 512], first_pattern))
            return tiles

        def compute(job):
            m, h = job['mixer'], job['h']
            Kv, Vv, kt, vt = kv_views(job)
            QH = QHb[job['qb']]
            qtok = ('QH', job['qb'])
            VH4, VH16, BD = VH4b[h % 2], VH16b[h % 2], BDb[h % 2]
            if m == 'A':
                run_tiles(banded_tiles(job, 0, True))
                finalize(job, sink_col=h)
            elif m == 'B':
                tiles = banded_tiles(job, 3, True)
                for r in range(4):
                    ob, db = obanks()
                    for it in range(4):
                        cls = 0 if it == 0 else (2 if it == 3 else 1)
                        q0 = r + 4 * it * 128
                        qT = QH[:, q0:q0 + 4 * 127 + 1:4]
                        slots = []
                        for s in range(3):
                            wt = 2 + it - 1 + s
                            k0 = r + 512 * wt
                            slots.append((Kv[:, k0:k0 + 4 * 127 + 1:4], VH4[:, r * 8 + wt, :], qT))
                        tiles.append(dict(slots=slots, bias=MSK[:, 3 + cls, :], sb=[it % 3], ob=ob, db=db,
                                          ocol=it * 128, toks=(kt, qtok, [('VH4', h % 2)], 'BIAS')))
                    tiles[-1]['after'] = (lambda ob=ob, db=db, r=r: evac(
                        ob, db, OB[:, r:r + 4 * 511 + 1:4], DENB[:, r:r + 4 * 511 + 1:4], False))
                for r0 in range(0, 16, 4):
                    ob, db = obanks()
                    for rr in range(4):
                        r = r0 + rr
                        qT = QH[:, r:r + 16 * 127 + 1:16]
                        slots = []
                        for s in range(2):
                            k0 = r + 16 * 128 * s
                            slots.append((Kv[:, k0:k0 + 16 * 127 + 1:16], VH16[:, r * 2 + s, :], qT))
                        tiles.append(dict(slots=slots, bias=MSK[:, 6, 0:256], sb=[rr % 3], ob=ob, db=db,
                                          ocol=rr * 128, toks=(kt, qtok, [('VH16', h % 2)], 'BIAS')))

                    def after16(ob=ob, db=db, r0=r0):
                        ov = OB[:].rearrange("p (i r) -> p r i", r=16)[:, r0:r0 + 4, :]
                        dv = DENB[:].rearrange("p (i r) -> p r i", r=16)[:, r0:r0 + 4, :]
                        evac(ob, db, ov, dv, False)
                    tiles[-1]['after'] = after16
                run_tiles(tiles)
                finalize(job)
            elif m == 'D':
                tiles = []
                for g4 in range(4):
                    ob, db = obanks()
                    for qq in range(4):
                        ul = g4 * 4 + qq
                        cname = {0: 't0', 1: 't1', 14: 't14', 15: 't15'}.get(ul, 'gen')
                        sl0, nsl, da0 = D_CLASSES[cname]
                        slots = []
                        for s in range(nsl):
                            wt = 8 + ul + da0 + s
                            slots.append((Kv[:, wt * 128:(wt + 1) * 128], Vv[:, wt, :],
                                          QH[:, ul * 128:(ul + 1) * 128]))
                        sb = [0, 2] if ul % 2 == 0 else [1, 3]
                        tiles.append(dict(slots=slots, bias=BD[:, sl0 * 128:(sl0 + nsl) * 128], sb=sb, ob=ob, db=db,
                                          ocol=qq * 128, toks=(kt, qtok, vt, ('BD', h % 2))))
                    tiles[-1]['after'] = (lambda ob=ob, db=db, g4=g4: evac(
                        ob, db, OB[:, g4 * 512:(g4 + 1) * 512], DENB[:, g4 * 512:(g4 + 1) * 512], True))
                run_tiles(tiles)
                finalize(job)
            else:
                for qb in range(4):
                    ob, db = obanks()
                    qT = QH[:, qb * 512:(qb + 1) * 512]
                    NKT = 64
                    pts = {}

                    def qk(kt_):
                        bk = kt_ % 3
                        S.op('pe', _mm(ps[bk][:, :], KH[:, kt_ * 128:(kt_ + 1) * 128], qT, True, True),
                             r=kt + [qtok], w=[('ps', bk)])
                        pt, pttok = next_pt()
                        pts[kt_] = (pt, pttok)
                        S.op('act', lambda e, bk=bk, pt=pt: e.activation(out=pt[:, 0:512], in_=ps[bk][:, :],
                                                                        func=AF.Exp, scale=SCALE),
                             r=[('ps', bk)], w=[pttok])

                    def pv(kt_):
                        pt, pttok = pts.pop(kt_)
                        S.op('pe', _mm(ps[ob][:, :], VH[:, kt_, :], pt[:, 0:512], kt_ == 0, kt_ == NKT - 1),
                             r=[pttok] + vt, w=[('ps', ob)])
                        S.op('pe', _mm(ps[db][:, :], cx.ONES[:], pt[:, 0:512], kt_ == 0, kt_ == NKT - 1),
                             r=[pttok, 'CONST'], w=[('ps', db)])

                    qk(0)
                    qk(1)
                    for kt_ in range(NKT):
                        if kt_ + 2 < NKT:
                            qk(kt_ + 2)
                        pv(kt_)
                    evac(ob, db, OB[:, qb * 512:(qb + 1) * 512], DENB[:, qb * 512:(qb + 1) * 512], True)
                finalize(job)

        emit_load(jobs[0], False)
        for i, job in enumerate(jobs):
            emit_load(job, True)
            if i + 1 < len(jobs):
                emit_load(jobs[i + 1], False)
            compute(job)
        S.barrier()


def phase3(cx, dr, ADA, HT, BR, MX, last_fg=None):
    nc, S, ps = cx.nc, cx.S, cx.ps
    with ExitStack() as st:
        WG = [st.enter_context(_sbt(nc, f"p3_wg{i}", [128, 16, 128], BF16)) for i in range(3)]
        WR = [st.enter_context(_sbt(nc, f"p3_wr{i}", [128, 4, 128], BF16)) for i in range(3)]
        SG = [st.enter_context(_sbt(nc, f"p3_sg{i}", [128, 512], F32)) for i in range(2)]
        TP = [st.enter_context(_sbt(nc, f"p3_tp{i}", [128, 512], F32)) for i in range(2)]
        MIXF = st.enter_context(_sbt(nc, "p3_mixf", [128, 4, 512], F32))
        MIXB = [st.enter_context(_sbt(nc, f"p3_mixb{i}", [128, T], BF16)) for i in range(2)]
        blk = 0
        wi = 0
        for m in range(16):
            mixb = MIXB[m % 2]
            for n in range(4):
                wg, wr = WG[wi % 3], WR[wi % 3]
                wtok = ('W3', wi % 3)
                wi += 1
                S.dma('pool', wg[:], dr['wgm'][m, n], w=[wtok])
                S.dma('pool', wr[:], dr['wbr'][m, n], w=[wtok])
                for tb in range(NTB):
                    pa = blk % 2
                    pg = 2 + blk % 2
                    k2 = blk % 2
                    blk += 1
                    tsl = slice(tb * 512, (tb + 1) * 512)
                    for kc in range(4):
                        S.op('pe', _mm(ps[pa][:, :], wr[:, kc, :], BR[:, n * 4 + kc, tsl], kc == 0, kc == 3),
                             r=[wtok, 'BR'], w=[('ps', pa)])
                    for kc in range(KC):
                        S.op('pe', _mm(ps[pg][:, :], wg[:, kc, :], HT[:, kc, tsl], kc == 0, kc == KC - 1),
                             r=[wtok, 'HT', ('HT', tb)], w=[('ps', pg)])
                    sg, tp = SG[k2], TP[k2]
                    S.op('act', lambda e, pg=pg, sg=sg: e.activation(out=sg[:], in_=ps[pg][:, :], func=AF.Sigmoid),
                         r=[('ps', pg)], w=[('SG', k2)])
                    if n == 0:
                        S.op('dve', lambda e, pa=pa, sg=sg, tb=tb: e.tensor_tensor(out=MIXF[:, tb, :], in0=ps[pa][:, :],
                                                                                  in1=sg[:], op=ALU.mult),
                             r=[('ps', pa), ('SG', k2)], w=[('MIXF', tb)])
                    else:
                        S.op('dve', lambda e, pa=pa, sg=sg, tp=tp: e.tensor_tensor(out=tp[:], in0=ps[pa][:, :],
                                                                                  in1=sg[:], op=ALU.mult),
                             r=[('ps', pa), ('SG', k2)], w=[('TP', k2)])
                        if n < 3:
                            S.op('dve', lambda e, tp=tp, tb=tb: e.tensor_tensor(out=MIXF[:, tb, :], in0=MIXF[:, tb, :],
                                                                               in1=tp[:], op=ALU.add),
                                 r=[('TP', k2), ('MIXF', tb)], w=[('MIXF', tb)])
                        else:
                            S.op('dve', lambda e, tp=tp, tb=tb, tsl=tsl, mixb=mixb: e.tensor_tensor(
                                out=mixb[:, tsl], in0=MIXF[:, tb, :], in1=tp[:], op=ALU.add),
                                 r=[('TP', k2), ('MIXF', tb)], w=[('MIXB', m % 2)])
            S.dma('sp', dr['MIXS'][m], mixb[:], r=[('MIXB', m % 2)], w=[('DRM', m)])
        S.barrier()
    with ExitStack() as st:
        WO = [st.enter_context(_sbt(nc, f"p3_wo{i}", [128, 16, 128], BF16)) for i in range(3)]
        XR = [st.enter_context(_sbt(nc, f"p3_xr{i}", [128, T], F32)) for i in range(2)]
        for tb in range(NTB):
            S.dma('sp', MX[:, :, tb * 512:(tb + 1) * 512],
                  dr['MIXS'][:, :, tb * 512:(tb + 1) * 512].rearrange("m p t -> p m t"), w=[('MX', tb)])
        blk = 0
        for mo in range(16):
            wo = WO[mo % 3]
            xr = XR[mo % 2]
            S.dma('pool', wo[:], dr['wout'][mo], w=[('WO', mo % 3)])
            S.dma('sp', xr[:], dr['xT'][:, mo, :], w=[('XR', mo % 2)])
            for tb in range(NTB):
                pb = blk % 4
                blk += 1
                tsl = slice(tb * 512, (tb + 1) * 512)
                for kc in range(KC):
                    S.op('pe', _mm(ps[pb][:, :], wo[:, kc, :], MX[:, kc, tsl], kc == 0, kc == KC - 1),
                         r=[('WO', mo % 3), ('MX', tb)], w=[('ps', pb)])
                S.op('dve', lambda e, pb=pb, xr=xr, tsl=tsl, mo=mo: e.scalar_tensor_tensor(
                    out=xr[:, tsl], in0=ps[pb][:, :], scalar=ADA[:, 32 + mo:33 + mo], in1=xr[:, tsl],
                    op0=ALU.mult, op1=ALU.add), r=[('ps', pb), ('XR', mo % 2), 'ADA'], w=[('XR', mo % 2)])
            S.dma('sp', dr['xTo'][:, mo, :], xr[:], r=[('XR', mo % 2)], w=[('DRX', mo)])
        S.barrier()


def phase_final(cx, dr):
    nc, S, ps = cx.nc, cx.S, cx.ps
    with ExitStack() as st:
        XB = st.enter_context(_sbt(nc, "pf_xb", [128, 16, 512], F32))
        SQ = st.enter_context(_sbt(nc, "pf_sq", [128, 16, 512], BF16))
        RS = st.enter_context(_sbt(nc, "pf_rs", [128, 512], F32))
        FG = st.enter_context(_sbt(nc, "pf_fg", [128, 16], F32))
        YB = [st.enter_context(_sbt(nc, f"pf_yb{i}", [128, 16, 512], F32)) for i in range(1)]
        S.dma('sp', FG[:], dr['fg'], w=['FG'])
        for tb in range(NTB):
            S.dma('sp', XB[:], dr['xT'][:, :, tb * 512:(tb + 1) * 512], w=['XB'])
            norm_block(cx, XB, SQ, RS, tb, None)
            yb = YB[0]
            for c in range(KC):
                S.op('dve', lambda e, c=c, yb=yb: e.scalar_tensor_tensor(out=yb[:, c, :], in0=XB[:, c, :],
                                                                        scalar=FG[:, c:c + 1], in1=RS[:],
                                                                        op0=ALU.mult, op1=ALU.mult),
                     r=['XB', 'RS', 'FG'], w=['YB'])
            S.dma('sp', dr['yT'][:, :, tb * 512:(tb + 1) * 512], yb[:], r=['YB'], w=[('DRY', tb)])
        S.barrier()


def _new_nc():
    return bass.Bass("TRN2", target_bir_lowering=False)


def _setup(nc, es):
    cx = Ctx()
    cx.nc = nc
    cx.S = Sched(nc, es)
    cx.ps = [es.enter_context(nc.psum_tensor(f"psb{i}", [128, 512], F32)) for i in range(8)]
    CM = es.enter_context(_sbt(nc, "cmat_sb", [128, 4, 128], BF16))
    cx.CM = CM
    cx.IDENT = CM[:, 0, :]
    cx.ONES = CM[:, 1, :]
    cx.R1 = CM[:, 2, :]
    cx.RC = CM[:, 3, :]
    cx.EPSC = es.enter_context(_sbt(nc, "epsc_sb", [128, 1], F32))
    cx.S.op('dve', lambda e: e.memset(cx.EPSC[:], EPS), w=['EPSC'])
    return cx


def _din(nc, name, shape, dt=F32):
    return nc.dram_tensor(name, list(shape), dt, kind="ExternalInput").ap()


def _dout(nc, name, shape, dt=F32):
    return nc.dram_tensor(name, list(shape), dt, kind="ExternalOutput").ap()


def _dint(nc, name, shape, dt=F32):
    return nc.dram_tensor(name, list(shape), dt, kind="Internal").ap()


def build_L1(stop=9):
    nc = _new_nc()
    dr = {}
    dr['xT'] = _din(nc, 'xT', [128, 16, T])
    dr['cT'] = _din(nc, 'cT', [128, 16])
    dr['wada'] = _din(nc, 'wada', [12, 128, 16, 512])
    dr['bada'] = _din(nc, 'bada', [128, 48])
    dr['ng'] = _din(nc, 'ng', [128, 16])
    dr['wfm'] = _din(nc, 'wfm', [NFM, 128, 16, 128])
    dr['wv'] = _din(nc, 'wv', [3, 128, 16, 512])
    dr['qkn'] = _din(nc, 'qkn', [128, 2])
    dr['rope'] = _din(nc, 'rope', [4, 128, T])
    cmat = _din(nc, 'cmat', [128, 4, 128])
    dr['QS'] = _dout(nc, 'QS', [16, 128, T], BF16)
    dr['KS'] = _dout(nc, 'KS', [12, 128, T], BF16)
    dr['GS'] = _dout(nc, 'GS', [16, 128, T], BF16)
    dr['VS'] = _dout(nc, 'VS', [T, 1536], BF16)
    HTo = _dout(nc, 'HT', [128, 16, T], BF16)
    ADAo = _dout(nc, 'ADA', [128, 48])
    with ExitStack() as es:
        cx = _setup(nc, es)
        S = cx.S
        S.dma('pool', cx.CM[:], cmat, w=['CONST'])
        ADA = es.enter_context(_sbt(nc, "ADA_sb", [128, 48], F32))
        HT = es.enter_context(_sbt(nc, "HT_sb", [128, 16, T], BF16))
        if stop >= 1:
            phase_ada(cx, dr, ADA[:])
        if stop >= 2:
            phase1(cx, dr, ADA, HT, stop)
        S.dma('sp', HTo, HT[:], r=[('HT', i) for i in range(4)], w=['DHT'])
        S.dma('sp', ADAo, ADA[:], r=['ADA'], w=['DADA'])
        S.barrier()
    return nc


def build_L2():
    nc = _new_nc()
    dr = {}
    dr['xT'] = _din(nc, 'xT', [128, 16, T])
    ADAi = _din(nc, 'ADA', [128, 48])
    HTi = _din(nc, 'HT', [128, 16, T], BF16)
    dr['QS'] = _din(nc, 'QS', [16, 128, T], BF16)
    dr['GS'] = _din(nc, 'GS', [16, 128, T], BF16)
    dr['KW'] = _din(nc, 'KW', [10, 128, WIN], BF16)
    dr['KC'] = _din(nc, 'KC', [2, 128, SEQ], BF16)
    dr['VW'] = _din(nc, 'VW', [WIN, 1536], BF16)
    dr['VC'] = _din(nc, 'VC', [SEQ, 256], BF16)
    dr['masks'] = _din(nc, 'masks', [128, 7, 384])
    dr['biasD'] = _din(nc, 'biasD', [4, 128, D_NSLOT * 128])
    sink = _din(nc, 'sink', [128, 4])
    dr['wgm'] = _din(nc, 'wgm', [16, 4, 128, 16, 128])
    dr['wbr'] = _din(nc, 'wbr', [16, 4, 128, 4, 128])
    dr['wout'] = _din(nc, 'wout', [16, 128, 16, 128])
    cmat = _din(nc, 'cmat', [128, 4, 128])
    dr['MIXS'] = _dint(nc, 'MIXS', [16, 128, T], BF16)
    dr['xTo'] = _dout(nc, 'xTo', [128, 16, T])
    with ExitStack() as es:
        cx = _setup(nc, es)
        S = cx.S
        S.dma('pool', cx.CM[:], cmat, w=['CONST'])
        ADA = es.enter_context(_sbt(nc, "ADA_sb", [128, 48], F32))
        S.dma('sp', ADA[:], ADAi, w=['ADA'])
        BR = es.enter_context(_sbt(nc, "BR_sb", [128, 16, T], BF16))
        phase2(cx, dr, BR, sink)
        HT = es.enter_context(_sbt(nc, "HT_sb", [128, 16, T], BF16))
        S.dma('sp', HT[:], HTi, w=['HT'])
        S.dma('sp', ADA[:], ADAi, w=['ADA'])
        phase3(cx, dr, ADA, HT, BR, BR)
    return nc


def build_L3():
    nc = _new_nc()
    dr = {}
    dr['xT'] = _din(nc, 'xT', [128, 16, T])
    dr['fg'] = _din(nc, 'fg', [128, 16])
    cmat = _din(nc, 'cmat', [128, 4, 128])
    dr['yT'] = _dout(nc, 'yT', [128, 16, T])
    with ExitStack() as es:
        cx = _setup(nc, es)
        cx.S.dma('pool', cx.CM[:], cmat, w=['CONST'])
        phase_final(cx, dr)
    return nc


def fm_vec(v, nchunk):
    return np.ascontiguousarray(np.asarray(v, np.float32).reshape(nchunk, 128).T)


def fm_weight(w, cols):
    K = w.shape[0]
    ws = w[:, cols]
    return np.ascontiguousarray(ws.reshape(K // 128, 128, ws.shape[1]).transpose(1, 0, 2))


def rope_consts(t0):
    pos = np.arange(t0, t0 + T)

    def tables(p, dim):
        inv = (np.float32(10000.0) ** (-np.arange(0, dim, 2, dtype=np.float32) / np.float32(dim))).astype(np.float32)
        ang = p.astype(np.float32)[:, None] * inv[None, :]
        ang = np.concatenate([ang, ang], axis=-1)
        return np.cos(ang).astype(np.float32), np.sin(ang).astype(np.float32)

    c1, s1 = tables(pos, 128)
    cr, sr = tables(pos // 64, 64)
    cc, sc = tables(pos % 64, 64)
    cC = np.concatenate([cr, cc], axis=-1)
    sC = np.concatenate([sr, sc], axis=-1)
    return np.ascontiguousarray(np.stack([c1.T, s1.T, cC.T, sC.T], axis=0))


def const_mats():
    cm = np.zeros((128, 4, 128), np.float32)
    cm[:, 0, :] = np.eye(128, dtype=np.float32)
    cm[:, 1, :] = 1.0
    for m in range(128):
        if m < 64:
            cm[m + 64, 2, m] = -1.0
        else:
            cm[m - 64, 2, m] = 1.0
    for half in (0, 64):
        for mm in range(64):
            m = half + mm
            if mm < 32:
                cm[m + 32, 3, m] = -1.0
            else:
                cm[m - 32, 3, m] = 1.0
    return cm


def band_masks(j):
    m = np.zeros((128, 7, 384), np.float32)
    ki = np.arange(128)[:, None]
    qi = np.arange(128)[None, :]
    for base, reach in ((0, 128), (3, 64)):
        for cls in range(3):
            for s in range(3):
                dk = (s - 1) * 128 + ki - qi
                ok = np.abs(dk) <= reach
                if cls == 0 and s == 0 and j == 0:
                    ok = np.zeros_like(ok)
                if cls == 2 and s == 2 and j == 3:
                    ok = np.zeros_like(ok)
                m[:, base + cls, s * 128:(s + 1) * 128] = np.where(ok, 0.0, NEG)
    for s in range(2):
        kiw = s * 128 + ki
        ok = np.abs(kiw - (64 + qi)) <= 64
        if j == 0:
            ok = ok & (kiw >= 64)
        if j == 3:
            ok = ok & (kiw < 192)
        m[:, 6, s * 128:(s + 1) * 128] = np.where(ok, 0.0, NEG)
    return m


def d_bias_tables(rel_bias, j):
    out = np.full((4, 128, D_NSLOT * 128), NEG, np.float32)
    rows_total = SEQ // 64
    kk = np.arange(128)
    qq = np.arange(128)
    kc = kk % 64
    qc = qq % 64
    cs = np.clip(qc - 8, 0, 48)
    colok = (kc[:, None] >= cs[None, :]) & (kc[:, None] < cs[None, :] + 16)
    dc = np.clip(kc[:, None] - qc[None, :], -15, 15) + 15
    for cname, (sl0, nsl, da0) in D_CLASSES.items():
        ul = {'gen': 4, 't0': 0, 't1': 1, 't14': 14, 't15': 15}[cname]
        u = j * 16 + ul
        qrow = 2 * u + qq // 64
        rs = np.clip(qrow - 4, 0, rows_total - 8)
        for s in range(nsl):
            a = u + da0 + s
            krow = 2 * a + kk // 64
            rowok = (krow[:, None] >= rs[None, :]) & (krow[:, None] < rs[None, :] + 8)
            inseq = (krow >= 0) & (krow < rows_total)
            ok = rowok & colok & inseq[:, None]
            drr = np.clip(krow[:, None] - qrow[None, :], -7, 7) + 7
            for h in range(4):
                b = rel_bias[h][drr, dc]
                out[h, :, (sl0 + s) * 128:(sl0 + s + 1) * 128] = np.where(ok, b, NEG)
    return out


_PROGS = {}


def _prog(name):
    if name not in _PROGS:
        _PROGS[name] = {'L1': build_L1, 'L2': build_L2, 'L3': build_L3}[name]()
    return _PROGS[name]


def kernel_unfused(x, c, norm_g, w_ada, b_ada, w_in, a_sink, c_q_norm, c_k_norm, d_rel_bias, w_gate_merge,
                   w_branch, w_out, final_g, _depth=DEPTH):
    x = np.asarray(x, np.float32)
    cores = list(range(NCORES))
    cm = const_mats()
    xT = []
    for core in cores:
        b, j = core // 4, core % 4
        xs = x[b, j * T:(j + 1) * T, :]
        xT.append(np.ascontiguousarray(xs.reshape(T, 16, 128).transpose(2, 1, 0)))
    ropes = [rope_consts(j * T) for j in range(4)]
    masks = [band_masks(j) for j in range(4)]
    cT = [fm_vec(np.asarray(c)[b], 16) for b in range(2)]
    for l in range(_depth):
        wl = np.asarray(w_in[l], np.float32)
        wfm = np.stack([fm_weight(wl, list(range(c0, c0 + 128))) for (_k, _m, _i, c0) in FM_CHUNKS], axis=0)
        wv = np.stack([fm_weight(wl, cols) for cols in V_COLS], axis=0)
        wa = np.asarray(w_ada[l], np.float32)
        wada = np.stack([fm_weight(wa, list(range(s * 512, (s + 1) * 512))) for s in range(12)], axis=0)
        bada = fm_vec(b_ada[l], 48)
        ng = fm_vec(norm_g[l], 16)
        qkn = np.ascontiguousarray(np.stack([np.asarray(c_q_norm[l], np.float32),
                                             np.asarray(c_k_norm[l], np.float32)], axis=1))
        in1 = [dict(xT=xT[core], cT=cT[core // 4], wada=wada, bada=bada, ng=ng, wfm=wfm, wv=wv, qkn=qkn,
                    rope=ropes[core % 4], cmat=cm) for core in cores]
        r1 = run_bass_kernel_spmd(_prog('L1'), in1, core_ids=cores).results
        del wfm, wv, wada
        in2 = []
        wg = np.asarray(w_gate_merge[l], np.float32)
        wgm = np.ascontiguousarray(
            wg.reshape(16, 128, 4, 16, 128).transpose(3, 2, 1, 0, 4))
        wb = np.asarray(w_branch[l], np.float32)
        wbr = np.ascontiguousarray(wb.reshape(4, 4, 128, 16, 128).transpose(3, 0, 2, 1, 4))
        wo = np.asarray(w_out[l], np.float32)
        wout = np.ascontiguousarray(wo.reshape(16, 128, 16, 128).transpose(2, 1, 0, 3))
        sink = np.ascontiguousarray(np.broadcast_to(np.asarray(a_sink[l], np.float32)[None, :], (128, 4)))
        biasD = [d_bias_tables(np.asarray(d_rel_bias[l], np.float32), j) for j in range(4)]
        for b in range(2):
            KSb = np.concatenate([np.asarray(r1[b * 4 + j]['KS']) for j in range(4)], axis=2)
            VSb = np.concatenate([np.asarray(r1[b * 4 + j]['VS']) for j in range(4)], axis=0)
            kwi = [KIDX[('A', 0)], KIDX[('A', 1)]] + [KIDX[('B', h)] for h in range(4)] + \
                  [KIDX[('D', h)] for h in range(4)]
            vcols = np.arange(1536)
            Kpad = np.zeros((10, 128, SEQ + 2048), KSb.dtype)
            Kpad[:, :, 1024:1024 + SEQ] = KSb[kwi]
            Vpad = np.zeros((SEQ + 2048, 1536), VSb.dtype)
            Vpad[1024:1024 + SEQ] = VSb[:, vcols]
            KCb = np.ascontiguousarray(KSb[[KIDX[('C', 0)], KIDX[('C', 1)]]])
            VCb = np.ascontiguousarray(VSb[:, VOFF[('C', 0)] * 128:(VOFF[('C', 1)] + 1) * 128])
            for j in range(4):
                core = b * 4 + j
                t0 = j * T
                in2.append(dict(xT=xT[core], ADA=np.asarray(r1[core]['ADA']), HT=np.asarray(r1[core]['HT']),
                                QS=np.asarray(r1[core]['QS']), GS=np.asarray(r1[core]['GS']),
                                KW=np.ascontiguousarray(Kpad[:, :, t0:t0 + WIN]), KC=KCb,
                                VW=np.ascontiguousarray(Vpad[t0:t0 + WIN]), VC=VCb,
                                masks=masks[j], biasD=biasD[j], sink=sink, wgm=wgm, wbr=wbr, wout=wout, cmat=cm))
        del r1
        r2 = run_bass_kernel_spmd(_prog('L2'), in2, core_ids=cores).results
        xT = [np.asarray(r2[core]['xTo']) for core in cores]
        del in2, r2
    fg = fm_vec(final_g, 16)
    r3 = run_bass_kernel_spmd(_prog('L3'), [dict(xT=xT[core], fg=fg, cmat=cm) for core in cores],
                              core_ids=cores).results
    out = np.empty((2, SEQ, D), np.float32)
    for core in cores:
        b, j = core // 4, core % 4
        yT = np.asarray(r3[core]['yT'])
        out[b, j * T:(j + 1) * T, :] = yT.transpose(2, 1, 0).reshape(T, D)
    return out


GROUPS = [[0, 1, 2, 3], [4, 5, 6, 7]]
I32 = mybir.dt.int32


def emit_exchange(cx, dr, KS_t, VS_t, KG_t, VG_t, KST, VSG):
    S = cx.S
    kheads = {}
    for (mixer, h), kid in KIDX.items():
        kheads.setdefault(kid // 2, []).append(('DR', 'k', mixer, h))
    kcc, vcc = KIDX[('C', 0)] // 2, VOFF[('C', 0)] // 2
    S.allgather(KS_t[kcc], KG_t[kcc], GROUPS, r=kheads[kcc], w=[('KG', kcc)])
    S.allgather(VS_t[vcc], VG_t[vcc], GROUPS, r=[('DRV', vcc)], w=[('VG', vcc)])
    for c in range(6):
        if c != kcc:
            S.allgather(KS_t[c], KG_t[c], GROUPS, r=kheads[c], w=[('KG', c)])
    for c in range(6):
        if c != vcc:
            S.allgather(VS_t[c], VG_t[c], GROUPS, r=[('DRV', c)], w=[('VG', c)])
    items = []
    n = 0
    for (mixer, h), i in KW_IDX.items():
        kid = KIDX[(mixer, h)]
        KG2 = KG_t[kid // 2].ap().rearrange("r (h t) -> (r h) t", h=2)
        for p in range(4):
            items.append((KST[n % 3], ('KST', n % 3), KG2, cx.KIDXT[:, i * 4 + p:i * 4 + p + 1], ('KG', kid // 2),
                          dr['KW'][i][:, p * 1024:(p + 1) * 1024], ('KWd', i, p)))
            n += 1
    n = 0
    for c in (0, 2, 3, 4, 5):
        VG = VG_t[c].ap()
        for wt in range(32):
            items.append((VSG[n % 4], ('VSG', n % 4), VG, cx.VIDXT[:, wt:wt + 1], ('VG', c),
                          dr['VW'][wt * 128:(wt + 1) * 128, c * 256:(c + 1) * 256], ('VWd', wt, c)))
            n += 1
    pend = []
    for it in items:
        buf, tok, src, idx, srctok, dst, dtok = it
        S.idma(buf[:], src, idx, r=[srctok, 'IDX'], w=[tok])
        pend.append(it)
        if len(pend) > 2:
            b2, t2, _s, _i, _st, d2, dt2 = pend.pop(0)
            S.dma('pool', d2, b2[:], r=[t2], w=[dt2])
    for b2, t2, _s, _i, _st, d2, dt2 in pend:
        S.dma('pool', d2, b2[:], r=[t2], w=[dt2])


def build_fused(depth=DEPTH):
    nc = _new_nc()
    xT_in = _din(nc, 'xT', [128, 16, T])
    cT = _din(nc, 'cT', [128, 16])
    wada = _din(nc, 'wada', [depth, 3, 128, 16, 512])
    bada = _din(nc, 'bada', [128, depth * 12])
    AIN_t = nc.dram_tensor('AIN', [128, depth * 12], F32)
    AOUT_t = nc.dram_tensor('AOUT', [4 * 128, depth * 12], F32)
    ng = _din(nc, 'ng', [depth, 128, 16])
    wfm = _din(nc, 'wfm', [depth, NFM, 128, 16, 128])
    wv = _din(nc, 'wv', [depth, 3, 128, 16, 512])
    qkn = _din(nc, 'qkn', [depth, 128, 2])
    rope = _din(nc, 'rope', [4, 128, T])
    cmat = _din(nc, 'cmat', [128, 4, 128])
    masks = _din(nc, 'masks', [128, 7, 384])
    biasD = _din(nc, 'biasD', [depth, 4, 128, D_NSLOT * 128])
    sink = _din(nc, 'sink', [depth, 128, 4])
    wgm = _din(nc, 'wgm', [depth, 16, 4, 128, 16, 128])
    wbr = _din(nc, 'wbr', [depth, 16, 4, 128, 4, 128])
    wout = _din(nc, 'wout', [depth, 16, 128, 16, 128])
    fg = _din(nc, 'fg', [128, 16])
    kidx = _din(nc, 'kidx', [128, 40], I32)
    vidx = _din(nc, 'vidx', [128, 32], I32)
    yT = _dout(nc, 'yT', [128, 16, T])
    XS = [_dint(nc, f'XS{i}', [128, 16, T]) for i in range(2)]
    QS = _dint(nc, 'QS', [16, 128, T], BF16)
    GS = _dint(nc, 'GS', [16, 128, T], BF16)
    HTd = _dint(nc, 'HTd', [128, 16, T], BF16)
    MIXS = _dint(nc, 'MIXS', [16, 128, T], BF16)
    KW = _dint(nc, 'KW', [10, 128, WIN], BF16)
    VW = _dint(nc, 'VW', [WIN, 1536], BF16)
    KS_t = [[nc.dram_tensor(f'KS{i}_{c}', [256, T], BF16) for c in range(6)] for i in range(2)]
    VS_t = [[nc.dram_tensor(f'VS{i}_{c}', [T, 256], BF16) for c in range(6)] for i in range(2)]
    KG_t = [[nc.dram_tensor(f'KG{i}_{c}', [4 * 256, T], BF16) for c in range(6)] for i in range(2)]
    VG_t = [[nc.dram_tensor(f'VG{i}_{c}', [4 * T, 256], BF16) for c in range(6)] for i in range(2)]
    with ExitStack() as es:
        cx = _setup(nc, es)
        S = cx.S
        S.dma('pool', cx.CM[:], cmat, w=['CONST'])
        KIDXT = es.enter_context(_sbt(nc, "kidx_sb", [128, 40], I32))
        VIDXT = es.enter_context(_sbt(nc, "vidx_sb", [128, 32], I32))
        cx.KIDXT, cx.VIDXT = KIDXT, VIDXT
        S.dma('sp', KIDXT[:], kidx, w=['IDX'])
        S.dma('sp', VIDXT[:], vidx, w=['IDX'])
        ADAALL = es.enter_context(_sbt(nc, "ada_all", [128, depth, 48], F32))
        cx.KST = [es.enter_context(_sbt(nc, f"ex_k{i}", [128, 1024], BF16)) for i in range(3)]
        cx.VSG = [es.enter_context(_sbt(nc, f"ex_v{i}", [128, 256], BF16)) for i in range(4)]
        S.barrier()
        phase_ada_dist(cx, cT, wada, bada, ADAALL, depth, AIN_t, AOUT_t)
        for l in range(depth):
            par = l % 2
            ADA = ADAALL[:, l, :]
            x_in = xT_in if l == 0 else XS[(l - 1) % 2]
            x_out = XS[l % 2]
            kcc = KIDX[('C', 0)] // 2
            vcc = VOFF[('C', 0)] // 2
            assert KIDX[('C', 0)] % 2 == 0 and VOFF[('C', 0)] % 2 == 0
            KG4 = KG_t[par][kcc].ap().rearrange("(r h p) t -> r h p t", r=4, h=2)
            KSl = [KS_t[par][k // 2].ap().rearrange("(h p) t -> h p t", h=2)[k % 2] for k in range(12)]
            dr = dict(xT=x_in, xTo=x_out, ng=ng[l], wfm=wfm[l], wv=wv[l], qkn=qkn[l], rope=rope,
                      QS=QS, GS=GS, KS=KSl, VS=[t_.ap() for t_ in VS_t[par]],
                      KW=KW, VW=VW, masks=masks, biasD=biasD[l],
                      KCr=[[KG4[rk][kvh] for kvh in range(2)] for rk in range(4)],
                      VC=VG_t[par][VOFF[('C', 0)] // 2].ap(),
                      wgm=wgm[l], wbr=wbr[l], wout=wout[l], MIXS=MIXS, kcc=kcc, vcc=vcc)
            dr['exch'] = (lambda KST, VSG, dr=dr, par=par: emit_exchange(
                cx, dr, KS_t[par], VS_t[par], KG_t[par], VG_t[par], KST, VSG))
            with ExitStack() as ls:
                HT = ls.enter_context(_sbt(nc, f"HT_sb{l}", [128, 16, T], BF16))
                phase1(cx, dr, ADA, HT)
                S.dma('sp', HTd, HT[:], w=['DHT'])
                S.barrier(partial=True, keep=XKEEP)
            with ExitStack() as ls:
                BR = ls.enter_context(_sbt(nc, f"BR_sb{l}", [128, 16, T], BF16))
                phase2(cx, dr, BR, sink[l])
                HT = ls.enter_context(_sbt(nc, f"HT2_sb{l}", [128, 16, T], BF16))
                for tb in range(NTB):
                    S.dma('sp', HT[:, :, tb * 512:(tb + 1) * 512], HTd[:, :, tb * 512:(tb + 1) * 512],
                          w=[('HT', tb)])
                phase3(cx, dr, ADA, HT, BR, BR)
        phase_final(cx, dict(xT=XS[(depth - 1) % 2], fg=fg, yT=yT))
    return nc


def window_index_tables(j):
    kt = np.zeros((128, 40), np.int32)
    p128 = np.arange(128)
    for (mixer, h), i in KW_IDX.items():
        kid = KIDX[(mixer, h)]
        for p in range(4):
            hs = (2 * j - 1 + p) % 8
            rank, half = hs // 2, hs % 2
            assert half == (p + 1) % 2
            kt[:, i * 4 + p] = (rank * 256 + (kid % 2) * 128 + p128) * 2 + half
    vt = np.zeros((128, 32), np.int32)
    for wt in range(32):
        vt[:, wt] = ((j * T - 1024 + wt * 128) % SEQ) + p128
    return kt, vt


_FUSED = {}


def kernel(x, c, norm_g, w_ada, b_ada, w_in, a_sink, c_q_norm, c_k_norm, d_rel_bias, w_gate_merge, w_branch,
           w_out, final_g, _depth=DEPTH):
    x = np.asarray(x, np.float32)
    cores = list(range(NCORES))
    dp = _depth
    cm = const_mats()
    f32 = lambda a: np.asarray(a, np.float32)
    wfm = np.stack([np.stack([fm_weight(f32(w_in[l]), list(range(c0, c0 + 128))) for (_k, _m, _i, c0) in FM_CHUNKS],
                             axis=0) for l in range(dp)], axis=0)
    wv = np.stack([np.stack([fm_weight(f32(w_in[l]), cols) for cols in V_COLS], axis=0) for l in range(dp)], axis=0)
    wada_j = [np.ascontiguousarray(np.stack([np.stack(
        [fm_weight(f32(w_ada[l]), list(range(j * 1536 + s * 512, j * 1536 + (s + 1) * 512))) for s in range(3)],
        axis=0) for l in range(dp)], axis=0)) for j in range(4)]
    bada_j = [np.ascontiguousarray(np.concatenate([fm_vec(b_ada[l], 48)[:, j * 12:(j + 1) * 12] for l in range(dp)],
                                                  axis=1)) for j in range(4)]
    ng = np.stack([fm_vec(norm_g[l], 16) for l in range(dp)], axis=0)
    qkn = np.stack([np.stack([f32(c_q_norm[l]), f32(c_k_norm[l])], axis=1) for l in range(dp)], axis=0)
    wgm = np.stack([f32(w_gate_merge[l]).reshape(16, 128, 4, 16, 128).transpose(3, 2, 1, 0, 4)
                    for l in range(dp)], axis=0)
    wbr = np.stack([f32(w_branch[l]).reshape(4, 4, 128, 16, 128).transpose(3, 0, 2, 1, 4) for l in range(dp)], axis=0)
    wout = np.stack([f32(w_out[l]).reshape(16, 128, 16, 128).transpose(2, 1, 0, 3) for l in range(dp)], axis=0)
    sink = np.stack([np.broadcast_to(f32(a_sink[l])[None, :], (128, 4)) for l in range(dp)], axis=0)
    fg = fm_vec(final_g, 16)
    shared = dict(ng=ng, wfm=wfm, wv=wv, qkn=np.ascontiguousarray(qkn),
                  cmat=cm, sink=np.ascontiguousarray(sink), wgm=np.ascontiguousarray(wgm),
                  wbr=np.ascontiguousarray(wbr), wout=np.ascontiguousarray(wout), fg=fg)
    per_j = []
    for j in range(4):
        kt, vt = window_index_tables(j)
        per_j.append(dict(rope=rope_consts(j * T), masks=band_masks(j), kidx=kt, vidx=vt, wada=wada_j[j],
                          bada=bada_j[j],
                          biasD=np.stack([d_bias_tables(f32(d_rel_bias[l]), j) for l in range(dp)], axis=0)))
    in_maps = []
    for core in cores:
        b, j = core // 4, core % 4
        xs = x[b, j * T:(j + 1) * T, :]
        m = dict(shared)
        m.update(per_j[j])
        m['xT'] = np.ascontiguousarray(xs.reshape(T, 16, 128).transpose(2, 1, 0))
        m['cT'] = fm_vec(np.asarray(c)[b], 16)
        in_maps.append(m)
    if dp not in _FUSED:
        _FUSED[dp] = build_fused(dp)
    res = run_bass_kernel_spmd(_FUSED[dp], in_maps, core_ids=cores).results
    out = np.empty((2, SEQ, D), np.float32)
    for core in cores:
        b, j = core // 4, core % 4
        yT = np.asarray(res[core]['yT'])
        out[b, j * T:(j + 1) * T, :] = yT.transpose(2, 1, 0).reshape(T, D)
    return out
```

```python
import numpy as np
import ml_dtypes
from contextlib import ExitStack
import concourse.bass as bass
import concourse.mybir as mybir
from concourse.bass_utils import run_bass_kernel_spmd

F32 = mybir.dt.float32
BF16 = mybir.dt.bfloat16
AF = mybir.ActivationFunctionType
ALU = mybir.AluOpType
NPBF = ml_dtypes.bfloat16

D = 2048
KC = 16
T = 2048
SEQ = 8192
NTB = 4
HD = 128
WIN = 4096
EPS = 1e-6
SCALE = HD ** -0.5
NEG = -30000.0
DEPTH = 4
NCORES = 8

_MIX_COL0 = {'A': 0, 'B': 1536, 'C': 3584, 'D': 5120}
_MIX_KV = {'A': 2, 'B': 4, 'C': 2, 'D': 4}
FM_CHUNKS = []
QIDX = {}
KIDX = {}
GIDX = {}
_qi = _ki = _gi = 0
for _m in 'ABCD':
    c0 = _MIX_COL0[_m]
    kv = _MIX_KV[_m]
    for h in range(4):
        FM_CHUNKS.append(('q', _m, h, c0 + h * 128)); QIDX[(_m, h)] = _qi; _qi += 1
    for h in range(kv):
        FM_CHUNKS.append(('k', _m, h, c0 + 512 + h * 128)); KIDX[(_m, h)] = _ki; _ki += 1
    for h in range(4):
        FM_CHUNKS.append(('g', _m, h, c0 + 512 + 2 * kv * 128 + h * 128)); GIDX[(_m, h)] = _gi; _gi += 1
NFM = len(FM_CHUNKS)
V_COLS = (list(range(768, 1024)) + list(range(4352, 4608)),
          list(range(2560, 3072)),
          list(range(6144, 6656)))
VOFF = {}
for h in range(2):
    VOFF[('A', h)] = h
    VOFF[('C', h)] = 2 + h
for h in range(4):
    VOFF[('B', h)] = 4 + h
    VOFF[('D', h)] = 8 + h
KW_IDX = {}
_i = 0
for _m, n in (('A', 2), ('B', 4), ('D', 4)):
    for h in range(n):
        KW_IDX[(_m, h)] = _i; _i += 1

D_CLASSES = {'gen': (0, 5, -2), 't0': (5, 6, -2), 't1': (11, 5, -2), 't14': (16, 5, -2), 't15': (21, 6, -3)}
D_NSLOT = 27


class Sched:
    CENG = ('pe', 'act', 'dve', 'pool')
    ALLQ = ('pe', 'act', 'dve', 'pool', 'sp')

    def __init__(self, nc, es, n_dma_sems=48):
        self.nc = nc
        self.es = es
        self.e = {'pe': nc.tensor, 'act': nc.scalar, 'dve': nc.vector, 'pool': nc.gpsimd, 'sp': nc.sync}
        self.dsems = [es.enter_context(nc.semaphore(f"dq{i}")) for i in range(n_dma_sems)]
        self.dcnt = [0] * n_dma_sems
        self.dnext = 0
        self.NP = 16
        self.psems = [es.enter_context(nc.semaphore(f"pq{i}")) for i in range(self.NP)]
        self.pcnt = [0] * self.NP
        self.pnext = 0
        self.epoch = 0
        self.waited = {q: {} for q in self.ALLQ}
        self.ccsem = None
        self.cccnt = 0
        self._new_epoch()

    def _new_epoch(self):
        self.epoch += 1
        self.sems = {e: self.es.enter_context(self.nc.semaphore(f"s{self.epoch}{e}")) for e in self.CENG}
        self.cnt = {e: 0 for e in self.CENG}
        self.lastw = {}
        self.readers = {}

    def _wait(self, q, ev):
        sem, val, key, _src = ev
        w = self.waited[q]
        if w.get(key, 0) >= val:
            return
        self.e[q].wait_ge(sem, val)
        w[key] = val

    def _deps(self, r, w):
        evs = []
        for t in r:
            ev = self.lastw.get(t)
            if ev is not None:
                evs.append(ev)
        for t in w:
            ev = self.lastw.get(t)
            if ev is not None:
                evs.append(ev)
            rd = self.readers.get(t)
            if rd:
                evs.extend(rd.values())
        return evs

    def _record(self, ev, r, w):
        for t in w:
            self.lastw[t] = ev
            self.readers[t] = {}
        for t in r:
            self.readers.setdefault(t, {})[ev[2]] = ev

    def op(self, eng, fn, r=(), w=()):
        pr = [t for t in r if isinstance(t, tuple) and t[0] == 'ps']
        if pr:
            r = [t for t in r if not (isinstance(t, tuple) and t[0] == 'ps')]
            w = list(w) + pr
        for ev in self._deps(r, w):
            if eng == 'pe' and ev[3] == 'pe':
                continue
            self._wait(eng, ev)
        ins = fn(self.e[eng])
        self.cnt[eng] += 1
        ins.then_inc(self.sems[eng], 1)
        ev = (self.sems[eng], self.cnt[eng], ('e', self.epoch, eng), eng)
        self._record(ev, r, w)

    def _dsem(self, q):
        if q == 'pool':
            i = self.pnext
            self.pnext = (i + 1) % self.NP
            self.pcnt[i] += 16
            return self.psems[i], self.pcnt[i], ('p', i)
        i = self.dnext
        self.dnext = (i + 1) % len(self.dsems)
        self.dcnt[i] += 16
        return self.dsems[i], self.dcnt[i], ('d', i)

    def dma(self, q, out, in_, r=(), w=()):
        for ev in self._deps(r, w):
            self._wait(q, ev)
        sem, val, key = self._dsem(q)
        self.e[q].dma_start(out=out, in_=in_).then_inc(sem, 16)
        ev = (sem, val, key, 'dma')
        self._record(ev, r, w)

    def idma(self, out, in_, idx_ap, r=(), w=()):
        q = 'pool'
        for ev in self._deps(r, w):
            self._wait(q, ev)
        sem, val, key = self._dsem(q)
        self.e[q].indirect_dma_start(out=out, out_offset=None, in_=in_,
                                     in_offset=bass.IndirectOffsetOnAxis(ap=idx_ap, axis=0)
                                     ).then_inc(sem, 16)
        ev = (sem, val, key, 'dma')
        self._record(ev, r, w)

    def allgather(self, src_t, dst_t, groups, r=(), w=()):
        q = 'pool'
        if self.ccsem is None:
            self.ccsem = self.es.enter_context(self.nc.semaphore("ccsem"))
        for ev in self._deps(r, w):
            self._wait(q, ev)
        self.cccnt += 1
        self.e[q].collective_compute("AllGather", ALU.bypass, replica_groups=groups,
                                     ins=[src_t.ap().opt()], outs=[dst_t.ap().opt()]).then_inc(self.ccsem, 1)
        ev = (self.ccsem, self.cccnt, ('cc',), 'cc')
        self._record(ev, r, w)

    def barrier(self, partial=False, keep=()):
        evs = [(self.sems[e], self.cnt[e], ('e', self.epoch, e), e) for e in self.CENG if self.cnt[e] > 0]
        evs += [(self.dsems[i], self.dcnt[i], ('d', i), 'dma') for i in range(len(self.dsems)) if self.dcnt[i] > 0]
        if not partial:
            evs += [(self.psems[i], self.pcnt[i], ('p', i), 'dma') for i in range(self.NP) if self.pcnt[i] > 0]
            if self.cccnt:
                evs.append((self.ccsem, self.cccnt, ('cc',), 'cc'))
        for q in self.ALLQ:
            for ev in evs:
                if q == 'pe' and ev[3] == 'pe':
                    continue
                self._wait(q, ev)
        kept = {t: ev for t, ev in self.lastw.items() if isinstance(t, tuple) and t[0] in keep} if partial else {}
        if max(self.cnt.values()) > 10000:
            self._new_epoch()
        else:
            self.lastw = {}
            self.readers = {}
        self.lastw.update(kept)


class Ctx:
    pass


XKEEP = ('KG', 'VG', 'KWd', 'VWd')

_SBT_N = [0]


def _sbt(nc, name, shape, dt):
    _SBT_N[0] += 1
    return nc.sbuf_tensor(f"{name}_u{_SBT_N[0]}", shape, dt)


def _mm(out, lhsT, rhs, start, stop):
    return lambda e: e.matmul(out, lhsT, rhs, start=start, stop=stop)


def phase_ada(cx, dr, ADA):
    nc, S, ps = cx.nc, cx.S, cx.ps
    with ExitStack() as st:
        CT = st.enter_context(_sbt(nc, "ada_ct", [128, 16], F32))
        CS = st.enter_context(_sbt(nc, "ada_cs", [128, 16], F32))
        BA = st.enter_context(_sbt(nc, "ada_b", [128, 48], F32))
        WA = [st.enter_context(_sbt(nc, f"ada_w{i}", [128, 16, 512], F32)) for i in range(2)]
        S.dma('sp', CT[:], dr['cT'], w=['CT'])
        S.dma('sp', BA[:], dr['bada'], w=['BA'])
        S.op('act', lambda e: e.activation(out=CS[:], in_=CT[:], func=AF.Silu), r=['CT'], w=['CS'])
        for s in range(12):
            wb = WA[s % 2]
            S.dma('sp', wb[:], dr['wada'][s], w=[('WA', s % 2)])
            for mm in range(4):
                m = s * 4 + mm
                for kc in range(KC):
                    S.op('pe', _mm(ps[0][:, m:m + 1], wb[:, kc, mm * 128:(mm + 1) * 128], CS[:, kc:kc + 1],
                                   kc == 0, kc == KC - 1),
                         r=[('WA', s % 2), 'CS'], w=[('ps', 0)])
        S.op('dve', lambda e: e.tensor_tensor(out=ADA, in0=ps[0][:, 0:48], in1=BA[:], op=ALU.add),
             r=[('ps', 0), 'BA'], w=['ADA'])
        S.barrier()


def phase_ada_dist(cx, cT, wada, bada, ADAALL, depth, AIN_t, AOUT_t):
    nc, S, ps = cx.nc, cx.S, cx.ps
    with ExitStack() as st:
        CT = st.enter_context(_sbt(nc, "ada_ct", [128, 16], F32))
        CS = st.enter_context(_sbt(nc, "ada_cs", [128, 16], F32))
        BA = st.enter_context(_sbt(nc, "ada_b", [128, depth * 12], F32))
        AP_ = st.enter_context(_sbt(nc, "ada_p", [128, depth * 12], F32))
        WA = [st.enter_context(_sbt(nc, f"ada_w{i}", [128, 16, 512], F32)) for i in range(2)]
        S.dma('sp', CT[:], cT, w=['CT'])
        S.dma('sp', BA[:], bada, w=['BA'])
        S.op('act', lambda e: e.activation(out=CS[:], in_=CT[:], func=AF.Silu), r=['CT'], w=['CS'])
        n = 0
        for l in range(depth):
            for s3 in range(3):
                wb = WA[n % 2]
                wtok = ('WA', n % 2)
                n += 1
                S.dma('sp', wb[:], wada[l][s3], w=[wtok])
                for mm in range(4):
                    col = l * 12 + s3 * 4 + mm
                    for kc in range(KC):
                        S.op('pe', _mm(ps[0][:, col:col + 1], wb[:, kc, mm * 128:(mm + 1) * 128], CS[:, kc:kc + 1],
                                       kc == 0, kc == KC - 1),
                             r=[wtok, 'CS'], w=[('ps', 0)])
        S.op('dve', lambda e: e.tensor_tensor(out=AP_[:], in0=ps[0][:, 0:depth * 12], in1=BA[:], op=ALU.add),
             r=[('ps', 0), 'BA'], w=['AP'])
        S.dma('sp', AIN_t.ap(), AP_[:], r=['AP'], w=['AIN'])
        S.allgather(AIN_t, AOUT_t, GROUPS, r=['AIN'], w=['AOUT'])
        AO = AOUT_t.ap()
        for rk in range(4):
            S.dma('sp', ADAALL[:, :, rk * 12:(rk + 1) * 12],
                  AO[rk * 128:(rk + 1) * 128, :].rearrange("p (l m) -> p l m", l=depth), r=['AOUT'], w=['ADA'])
        S.barrier()


def norm_block(cx, XB, SQ, RS, tb, xsrc_tokens):
    nc, S, ps = cx.nc, cx.S, cx.ps
    xtok = xsrc_tokens if xsrc_tokens is not None else 'XB'
    S.op('act', lambda e: e.activation(out=SQ[:].rearrange("p c t -> p (c t)"),
                                       in_=XB[:].rearrange("p c t -> p (c t)"), func=AF.Square),
         r=[xtok], w=['SQ'])
    for c in range(KC):
        S.op('pe', _mm(ps[7][:, :], cx.ONES[:], SQ[:, c, :], c == 0, c == KC - 1), r=['SQ', 'CONST'], w=[('ps', 7)])
    S.op('act', lambda e: e.activation(out=RS[:], in_=ps[7][:, :], func=AF.Sqrt, bias=cx.EPSC[:, 0:1],
                                       scale=1.0 / D), r=[('ps', 7), 'EPSC'], w=['RS'])
    S.op('dve', lambda e: e.reciprocal(out=RS[:], in_=RS[:]), r=['RS'], w=['RS'])


def phase1(cx, dr, ADA, HT, stop=9):
    nc, S, ps = cx.nc, cx.S, cx.ps
    with ExitStack() as st:
        XBb = [st.enter_context(_sbt(nc, f"p1_xb{i}", [128, 16, 512], F32)) for i in range(2)]
        SQ = st.enter_context(_sbt(nc, "p1_sq", [128, 16, 512], BF16))
        RS = st.enter_context(_sbt(nc, "p1_rs", [128, 512], F32))
        TM = [st.enter_context(_sbt(nc, f"p1_tm{i}", [128, 512], F32)) for i in range(2)]
        NG = st.enter_context(_sbt(nc, "p1_ng", [128, 16], F32))
        GG = st.enter_context(_sbt(nc, "p1_gg", [128, 16], F32))
        S.dma('sp', NG[:], dr['ng'], w=['NG'])
        S.op('dve', lambda e: e.scalar_tensor_tensor(out=GG[:], in0=ADA[:, 16:32], scalar=1.0, in1=NG[:],
                                                     op0=ALU.add, op1=ALU.mult), r=['ADA', 'NG'], w=['GG'])
        S.dma('sp', XBb[0][:], dr['xT'][:, :, 0:512], w=[('XB', 0)])
        for tb in range(NTB):
            XB = XBb[tb % 2]
            xtok = ('XB', tb % 2)
            if tb + 1 < NTB:
                S.dma('sp', XBb[(tb + 1) % 2][:], dr['xT'][:, :, (tb + 1) * 512:(tb + 2) * 512],
                      w=[('XB', (tb + 1) % 2)])
            norm_block(cx, XB, SQ, RS, tb, xtok)
            for c in range(KC):
                tm = TM[c % 2]
                S.op('dve', lambda e, c=c, tm=tm, XB=XB: e.tensor_tensor(out=tm[:], in0=XB[:, c, :], in1=RS[:],
                                                                        op=ALU.mult),
                     r=[xtok, 'RS'], w=[('TM', c % 2)])
                S.op('act', lambda e, c=c, tm=tm: e.activation(out=HT[:, c, tb * 512:(tb + 1) * 512], in_=tm[:],
                                                              func=AF.Identity, bias=ADA[:, c:c + 1],
                                                              scale=GG[:, c:c + 1]),
                     r=[('TM', c % 2), 'GG', 'ADA'], w=[('HT', tb)])
        S.barrier()
    if stop < 3:
        return
    with ExitStack() as st:
        WB = [st.enter_context(_sbt(nc, f"p1_wb{i}", [128, 16, 128], BF16)) for i in range(3)]
        WS = [st.enter_context(_sbt(nc, f"p1_ws{i}", [128, 2048], F32)) for i in range(2)]
        WVb = [st.enter_context(_sbt(nc, f"p1_wv{i}", [128, 16, 512], BF16)) for i in range(2)]
        COS1 = st.enter_context(_sbt(nc, "p1_cos1", [128, T], F32))
        SIN1 = st.enter_context(_sbt(nc, "p1_sin1", [128, T], F32))
        COSC = st.enter_context(_sbt(nc, "p1_cosc", [128, T], F32))
        SINC = st.enter_context(_sbt(nc, "p1_sinc", [128, T], F32))
        QN = st.enter_context(_sbt(nc, "p1_qn", [128, 2], F32))
        OST = [st.enter_context(_sbt(nc, f"p1_ost{i}", [128, T], BF16)) for i in range(2)]
        VST = st.enter_context(_sbt(nc, "p1_vst", [128, 8, 512], BF16))
        QB = [st.enter_context(_sbt(nc, f"p1_qb{i}", [128, 512], BF16)) for i in range(2)]
        QF = [st.enter_context(_sbt(nc, f"p1_qf{i}", [128, 512], F32)) for i in range(2)]
        T1 = [st.enter_context(_sbt(nc, f"p1_t1{i}", [128, 512], F32)) for i in range(2)]
        T2 = [st.enter_context(_sbt(nc, f"p1_t2{i}", [128, 512], F32)) for i in range(2)]
        RS2 = [st.enter_context(_sbt(nc, f"p1_rs2{i}", [128, 512], F32)) for i in range(2)]
        S.dma('sp', COS1[:], dr['rope'][0], w=['ROPE'])
        S.dma('sp', SIN1[:], dr['rope'][1], w=['ROPE'])
        S.dma('sp', COSC[:], dr['rope'][2], w=['ROPE'])
        S.dma('sp', SINC[:], dr['rope'][3], w=['ROPE'])
        S.dma('sp', QN[:], dr['qkn'], w=['QN'])
        st8 = {'blk': 0, 'seq': 0}

        def w_load(seq, ci):
            S.dma('sp', WS[seq % 2][:], dr['wfm'][ci].rearrange("p k n -> p (k n)"), w=[('WS', seq % 2)])

        def w_cast(seq, on_act=False):
            ws, wb = WS[seq % 2], WB[seq % 3]
            if on_act:
                S.op('act', lambda e: e.activation(out=wb[:].rearrange("p k n -> p (k n)"), in_=ws[:], func=AF.Copy),
                     r=[('WS', seq % 2)], w=[('WB', seq % 3)])
            else:
                S.op('dve', lambda e: e.tensor_copy(out=wb[:].rearrange("p k n -> p (k n)"), in_=ws[:]),
                     r=[('WS', seq % 2)], w=[('WB', seq % 3)])

        def do_chunk(seq, ci):
            kind, mixer, idx, _c0 = FM_CHUNKS[ci]
            wb = WB[seq % 3]
            wbtok = ('WB', seq % 3)
            ost = OST[seq % 2]
            osttok = ('OST', seq % 2)
            pend = []
            for tb in range(NTB):
                blk = st8['blk']
                st8['blk'] += 1
                pb = blk % 3
                b2 = 3 + blk % 2
                b3 = 5 + blk % 2
                k2 = blk % 2
                tsl = slice(tb * 512, (tb + 1) * 512)
                for kc in range(KC):
                    S.op('pe', _mm(ps[pb][:, :], wb[:, kc, :], HT[:, kc, tsl], kc == 0, kc == KC - 1),
                         r=[wbtok, ('HT', tb)], w=[('ps', pb)])
                if kind == 'g':
                    S.op('act', lambda e, pb=pb, tsl=tsl, ost=ost: e.activation(out=ost[:, tsl], in_=ps[pb][:, :],
                                                                               func=AF.Silu),
                         r=[('ps', pb)], w=[osttok])
                elif mixer == 'D':
                    S.op('act', lambda e, pb=pb, tsl=tsl, ost=ost: e.activation(out=ost[:, tsl], in_=ps[pb][:, :],
                                                                               func=AF.Copy),
                         r=[('ps', pb)], w=[osttok])
                elif mixer in 'AB':
                    qb, t1, t2 = QB[k2], T1[k2], T2[k2]
                    S.op('act', lambda e, pb=pb, qb=qb: e.activation(out=qb[:], in_=ps[pb][:, :], func=AF.Copy),
                         r=[('ps', pb)], w=[('QB', k2)])
                    S.op('dve', lambda e, pb=pb, t1=t1, tsl=tsl: e.tensor_tensor(out=t1[:], in0=ps[pb][:, :],
                                                                                in1=COS1[:, tsl], op=ALU.mult),
                         r=[('ps', pb), 'ROPE'], w=[('T1', k2)])

                    def stage1(qb=qb, t1=t1, t2=t2, b2=b2, k2=k2, tsl=tsl):
                        S.op('pe', _mm(ps[b2][:, :], cx.R1[:], qb[:], True, True), r=[('QB', k2), 'CONST'],
                             w=[('ps', b2)])
                        S.op('dve', lambda e: e.tensor_tensor(out=t2[:], in0=ps[b2][:, :], in1=SIN1[:, tsl],
                                                              op=ALU.mult),
                             r=[('ps', b2), 'ROPE'], w=[('T2', k2)])
                        S.op('dve', lambda e: e.tensor_tensor(out=ost[:, tsl], in0=t1[:], in1=t2[:], op=ALU.add),
                             r=[('T1', k2), ('T2', k2)], w=[osttok])
                    pend.append(stage1)
                else:
                    qb, qf, t1, t2, rs2 = QB[k2], QF[k2], T1[k2], T2[k2], RS2[k2]
                    qcol = 0 if kind == 'q' else 1
                    S.op('act', lambda e, pb=pb, qb=qb: e.activation(out=qb[:], in_=ps[pb][:, :], func=AF.Square),
                         r=[('ps', pb)], w=[('QB', k2)])
                    S.op('dve', lambda e, pb=pb, qf=qf, qcol=qcol: e.tensor_scalar(
                        out=qf[:], in0=ps[pb][:, :], scalar1=QN[:, qcol:qcol + 1], scalar2=None, op0=ALU.mult),
                         r=[('ps', pb), 'QN'], w=[('QF', k2)])

                    def stage1(qb=qb, qf=qf, t1=t1, t2=t2, rs2=rs2, b2=b2, b3=b3, k2=k2, tsl=tsl):
                        S.op('pe', _mm(ps[b3][:, :], cx.ONES[:], qb[:], True, True), r=[('QB', k2), 'CONST'],
                             w=[('ps', b3)])
                        S.op('act', lambda e: e.activation(out=rs2[:], in_=ps[b3][:, :], func=AF.Sqrt,
                                                           bias=cx.EPSC[:, 0:1], scale=1.0 / HD),
                             r=[('ps', b3), 'EPSC'], w=[('RS2', k2)])
                        S.op('dve', lambda e: e.reciprocal(out=rs2[:], in_=rs2[:]),
                             r=[('RS2', k2)], w=[('RS2', k2)])
                        S.op('dve', lambda e: e.tensor_tensor(out=qf[:], in0=qf[:], in1=rs2[:], op=ALU.mult),
                             r=[('QF', k2), ('RS2', k2)], w=[('QF', k2)])
                        S.op('act', lambda e: e.activation(out=qb[:], in_=qf[:], func=AF.Copy),
                             r=[('QF', k2)], w=[('QB', k2)])
                        S.op('pe', _mm(ps[b2][:, :], cx.RC[:], qb[:], True, True), r=[('QB', k2), 'CONST'],
                             w=[('ps', b2)])
                        S.op('dve', lambda e: e.tensor_tensor(out=t1[:], in0=qf[:], in1=COSC[:, tsl], op=ALU.mult),
                             r=[('QF', k2), 'ROPE'], w=[('T1', k2)])
                        S.op('dve', lambda e: e.tensor_tensor(out=t2[:], in0=ps[b2][:, :], in1=SINC[:, tsl],
                                                              op=ALU.mult),
                             r=[('ps', b2), 'ROPE'], w=[('T2', k2)])
                        S.op('dve', lambda e: e.tensor_tensor(out=ost[:, tsl], in0=t1[:], in1=t2[:], op=ALU.add),
                             r=[('T1', k2), ('T2', k2)], w=[osttok])
                    pend.append(stage1)
                while len(pend) > 1:
                    pend.pop(0)()
            while pend:
                pend.pop(0)()
            if kind == 'q':
                dst = dr['QS'][QIDX[(mixer, idx)]]
            elif kind == 'k':
                dst = dr['KS'][KIDX[(mixer, idx)]]
            else:
                dst = dr['GS'][GIDX[(mixer, idx)]]
            S.dma('sp', dst, ost[:], r=[osttok], w=[('DR', kind, mixer, idx)])

        def run_chunks(order, hook=None):
            base = st8['seq']
            n = len(order)
            for i in range(min(2, n)):
                w_load(base + i, order[i])
            w_cast(base)
            for i, ci in enumerate(order):
                if i + 1 < n:
                    kind_, mixer_ = FM_CHUNKS[ci][0], FM_CHUNKS[ci][1]
                    w_cast(base + i + 1, on_act=(kind_ != 'g' and mixer_ != 'D'))
                if i + 2 < n:
                    w_load(base + i + 2, order[i + 2])
                if hook is not None:
                    hook(i, n, (base + i + 1) % 2)
                do_chunk(base + i, ci)
            st8['seq'] = base + n

        vst8 = {'n': 0}

        def v_wload(vg, q4, buf=None):
            n = vst8['n']
            vst8['n'] += 1
            if buf is None:
                buf = n % 2
            ws = WS[buf]
            wstok = ('WS', buf)
            wv = WVb[vg % 2]
            S.dma('sp', ws[:], dr['wv'][vg][:, q4 * 4:(q4 + 1) * 4, :].rearrange("p k n -> p (k n)"), w=[wstok])
            S.op('dve', lambda e: e.tensor_copy(out=wv[:, q4 * 4:(q4 + 1) * 4, :].rearrange("p k n -> p (k n)"),
                                                in_=ws[:]), r=[wstok], w=[('WV', vg % 2)])

        def k_hook(i, n, freebuf):
            if stop >= 5 and i >= n - 4:
                v_wload(0, i - (n - 4), freebuf)

        order_k = [ci for ci, ch in enumerate(FM_CHUNKS) if ch[0] == 'k']
        order_qg = [ci for ci, ch in enumerate(FM_CHUNKS) if ch[0] == 'q'] + \
                   [ci for ci, ch in enumerate(FM_CHUNKS) if ch[0] == 'g']
        if stop == 3:
            order_k, order_qg = order_k[:2], []
        run_chunks(order_k, k_hook)
        for vg in range(3):
            if stop < 5:
                break
            WV = WVb[vg % 2]
            wvtok = ('WV', vg % 2)
            for tt in range(16):
                if vg + 1 < 3 and tt % 4 == 0:
                    v_wload(vg + 1, tt // 4)
                blk = st8['blk']
                st8['blk'] += 1
                pb = blk % 3
                for kc in range(KC):
                    S.op('pe', _mm(ps[pb][:, :], HT[:, kc, tt * 128:(tt + 1) * 128], WV[:, kc, :], kc == 0,
                                   kc == KC - 1),
                         r=[wvtok, ('HT', tt // 4)], w=[('ps', pb)])
                if tt % 2 == 0:
                    S.op('act', lambda e, pb=pb, tt=tt: e.activation(out=VST[:, tt % 8, :], in_=ps[pb][:, :],
                                                                    func=AF.Copy),
                         r=[('ps', pb)], w=['VST'])
                else:
                    S.op('dve', lambda e, pb=pb, tt=tt: e.tensor_copy(out=VST[:, tt % 8, :], in_=ps[pb][:, :]),
                         r=[('ps', pb)], w=['VST'])
                if tt % 8 == 7:
                    h8 = tt // 8
                    if isinstance(dr['VS'], list):
                        for hf in range(2):
                            dst = dr['VS'][vg * 2 + hf].rearrange("(tt p) c -> p tt c", p=128)
                            S.dma('sp', dst[:, h8 * 8:(h8 + 1) * 8, :], VST[:, :, hf * 256:(hf + 1) * 256],
                                  r=['VST'], w=[('DRV', vg * 2 + hf)])
                    else:
                        dst = dr['VS'][:, vg * 512:(vg + 1) * 512].rearrange("(tt p) c -> p tt c", p=128)
                        S.dma('sp', dst[:, h8 * 8:(h8 + 1) * 8, :], VST[:], r=['VST'], w=[('DRV', vg)])
        if 'exch' in dr:
            dr['exch'](cx.KST, cx.VSG)
        run_chunks(order_qg)
        S.barrier(partial=('exch' in dr), keep=XKEEP)


def attn_qk(cx, PT, nq, slots, bias_ap, sbanks, toks):
    S, ps = cx.S, cx.ps
    ktoks, qtok, vtoks, btok = toks
    ns = len(slots)
    per_bank = 512 // nq
    nb = (ns + per_bank - 1) // per_bank
    for b in range(nb):
        s0 = b * per_bank
        s1 = min(ns, s0 + per_bank)
        bk = sbanks[b]
        S.op('pe', _mm(ps[bk][:, 0:(s1 - s0) * nq], cx.IDENT[:], bias_ap[:, s0 * nq:s1 * nq], True, False),
             r=['CONST', btok], w=[('ps', bk)])
        for s in range(s0, s1):
            kT, _v, qT = slots[s]
            S.op('pe', _mm(ps[bk][:, (s - s0) * nq:(s - s0 + 1) * nq], kT, qT, False, True),
                 r=ktoks + [qtok], w=[('ps', bk)])
    pt, pttok = PT
    for b in range(nb):
        s0 = b * per_bank
        s1 = min(ns, s0 + per_bank)
        bk = sbanks[b]
        S.op('act', lambda e, bk=bk, s0=s0, s1=s1: e.activation(out=pt[:, s0 * nq:s1 * nq],
                                                               in_=ps[bk][:, 0:(s1 - s0) * nq], func=AF.Exp,
                                                               scale=SCALE),
             r=[('ps', bk)], w=[pttok])


def attn_pv(cx, PT, nq, slots, obank, dbank, ocol, toks):
    S, ps = cx.S, cx.ps
    ktoks, qtok, vtoks, btok = toks
    ns = len(slots)
    pt, pttok = PT
    for s in range(ns):
        _k, v, _q = slots[s]
        S.op('pe', _mm(ps[obank][:, ocol:ocol + nq], v, pt[:, s * nq:(s + 1) * nq], s == 0, s == ns - 1),
             r=[pttok] + vtoks, w=[('ps', obank)])
    for s in range(ns):
        S.op('pe', _mm(ps[dbank][:, ocol:ocol + nq], cx.ONES[:], pt[:, s * nq:(s + 1) * nq], s == 0, s == ns - 1),
             r=[pttok, 'CONST'], w=[('ps', dbank)])


def phase2(cx, dr, BR, l_sink):
    nc, S, ps = cx.nc, cx.S, cx.ps
    with ExitStack() as st:
        KH = st.enter_context(_sbt(nc, "p2_kh", [128, SEQ], BF16))
        VH = st.enter_context(_sbt(nc, "p2_vh", [128, 64, 128], BF16))
        VH4b = [st.enter_context(_sbt(nc, f"p2_vh4{i}", [128, 32, 128], BF16)) for i in range(2)]
        VH16b = [st.enter_context(_sbt(nc, f"p2_vh16{i}", [128, 32, 128], BF16)) for i in range(2)]
        QHb = [st.enter_context(_sbt(nc, f"p2_qh{i}", [128, T], BF16)) for i in range(2)]
        GHb = [st.enter_context(_sbt(nc, f"p2_gh{i}", [128, T], BF16)) for i in range(2)]
        OB = st.enter_context(_sbt(nc, "p2_ob", [128, T], F32))
        DENB = st.enter_context(_sbt(nc, "p2_den", [128, T], F32))
        PTS = [st.enter_context(_sbt(nc, f"p2_pt{i}", [128, 768], BF16)) for i in range(3)]
        MSK = st.enter_context(_sbt(nc, "p2_msk", [128, 7, 384], BF16))
        BDb = [st.enter_context(_sbt(nc, f"p2_bd{i}", [128, D_NSLOT * 128], BF16)) for i in range(2)]
        SK = st.enter_context(_sbt(nc, "p2_sk", [128, 4], F32))
        ES = st.enter_context(_sbt(nc, "p2_es", [128, 4], F32))
        S.dma('pool', MSK[:], dr['masks'], w=['BIAS'])
        S.dma('sp', SK[:], l_sink, w=['SK'])
        S.op('act', lambda e: e.activation(out=ES[:], in_=SK[:], func=AF.Exp), r=['SK'], w=['ES'])
        state = {'pt': 0, 'ob': 0}

        def next_pt():
            i = state['pt']
            state['pt'] = (i + 1) % 3
            return (PTS[i], ('PT', i))

        def obanks():
            i = state['ob']
            state['ob'] = 1 - i
            return 4 + i, 6 + i

        def finalize(job, sink_col=None):
            hidx = QIDX[(job['mixer'], job['h'])]
            GH = GHb[job['qb']]
            if sink_col is not None:
                S.op('dve', lambda e: e.tensor_scalar(out=DENB[:], in0=DENB[:], scalar1=ES[:, sink_col:sink_col + 1],
                                                      scalar2=None, op0=ALU.add), r=['DENB', 'ES'], w=['DENB'])
            S.op('dve', lambda e: e.reciprocal(out=DENB[:], in_=DENB[:]), r=['DENB'], w=['DENB'])
            S.op('dve', lambda e: e.tensor_tensor(out=OB[:], in0=OB[:], in1=DENB[:], op=ALU.mult),
                 r=['OB', 'DENB'], w=['OB'])
            S.op('pool', lambda e: e.tensor_tensor(out=BR[:, hidx, :], in0=OB[:], in1=GH[:], op=ALU.mult),
                 r=['OB', ('GH', job['qb'])], w=[('BR', hidx)])

        def evac(obank, dbank, ov, dv, first):
            pso = ps[obank][:, :]
            psd = ps[dbank][:, :]
            if len(ov.shape) == 3:
                pso = pso.rearrange("p (a b) -> p a b", a=ov.shape[1])
                psd = psd.rearrange("p (a b) -> p a b", a=ov.shape[1])
            if first:
                S.op('dve', lambda e: e.tensor_copy(out=ov, in_=pso), r=[('ps', obank)], w=['OB'])
                S.op('act', lambda e: e.activation(out=dv, in_=psd, func=AF.Copy), r=[('ps', dbank)], w=['DENB'])
            else:
                S.op('dve', lambda e: e.tensor_tensor(out=ov, in0=ov, in1=pso, op=ALU.add),
                     r=[('ps', obank), 'OB'], w=['OB'])
                S.op('dve', lambda e: e.tensor_tensor(out=dv, in0=dv, in1=psd, op=ALU.add),
                     r=[('ps', dbank), 'DENB'], w=['DENB'])

        jobs = []
        for kvh in range(2):
            for g in range(2):
                jobs.append(dict(mixer='C', h=kvh * 2 + g, kvh=kvh, load_kv=(g == 0)))
        for kvh in range(2):
            for g in range(2):
                jobs.append(dict(mixer='A', h=kvh * 2 + g, kvh=kvh, load_kv=(g == 0)))
        for h in range(4):
            jobs.append(dict(mixer='B', h=h, kvh=h, load_kv=True))
        for h in range(4):
            jobs.append(dict(mixer='D', h=h, kvh=h, load_kv=True))
        kvb = 1
        for i, job in enumerate(jobs):
            job['qb'] = i % 2
            if job['load_kv']:
                kvb = 1 - kvb
            job['kvb'] = kvb
            job['kv_late'] = job['mixer'] == 'C' or (i > 0 and jobs[i - 1]['mixer'] == 'C')

        def kv_views(job):
            if job['mixer'] == 'C':
                return KH, VH, [('KH', 0), ('KH', 1)], [('VH', 0), ('VH', 1)]
            b = job['kvb']
            return (KH[:, b * WIN:(b + 1) * WIN], VH[:, b * 32:(b + 1) * 32, :], [('KH', b)], [('VH', b)])

        def emit_load(job, late):
            m, h, q = job['mixer'], job['h'], job['qb']
            is_c = (m == 'C')
            if job['load_kv'] and late == job['kv_late']:
                Kv, Vv, kt, vt = kv_views(job)
                if is_c:
                    kvh = job['kvh']
                    if 'KCr' in dr:
                        for rk in range(4):
                            S.dma('sp', KH[:, rk * T:(rk + 1) * T], dr['KCr'][rk][kvh], r=[('KG', dr['kcc'])], w=kt)
                    else:
                        S.dma('sp', KH[:], dr['KC'][kvh], w=kt)
                    S.dma('sp', VH[:], dr['VC'][:, kvh * 128:(kvh + 1) * 128].rearrange("(t p) c -> p t c", p=128),
                          r=[('VG', dr.get('vcc', 0))], w=vt)
                else:
                    i = KW_IDX[(m, job['kvh'])]
                    vo = VOFF[(m, job['kvh'])]
                    vsrc = dr['VW'][:, vo * 128:(vo + 1) * 128]
                    vwt = [('VWd', wt, vo // 2) for wt in range(32)]
                    S.dma('sp', Kv, dr['KW'][i], r=[('KWd', i, p) for p in range(4)], w=kt)
                    S.dma('sp', Vv, vsrc.rearrange("(t p) c -> p t c", p=128), r=vwt, w=vt)
                    if m == 'B':
                        for r in range(4):
                            S.dma('sp', VH4b[h % 2][:, r * 8:(r + 1) * 8, :],
                                  vsrc.rearrange("(t p r) c -> r p t c", p=128, r=4)[r], r=vwt, w=[('VH4', h % 2)])
                        for r in range(16):
                            S.dma('sp', VH16b[h % 2][:, r * 2:(r + 1) * 2, :],
                                  vsrc.rearrange("(t p r) c -> r p t c", p=128, r=16)[r], r=vwt, w=[('VH16', h % 2)])
                    if m == 'D':
                        S.dma('pool', BDb[h % 2][:], dr['biasD'][h], w=[('BD', h % 2)])
            if not late:
                S.dma('sp', QHb[q][:], dr['QS'][QIDX[(m, h)]], w=[('QH', q)])
                S.dma('sp', GHb[q][:], dr['GS'][GIDX[(m, h)]], w=[('GH', q)])

        def run_tiles(tiles):
            for t in tiles:
                t['pt'] = None
            n = len(tiles)

            def stage1(t):
                t['pt'] = next_pt()
                attn_qk(cx, t['pt'], 128, t['slots'], t['bias'], t['sb'], t['toks'])

            LOOK = 2
            for j in range(min(LOOK, n)):
                stage1(tiles[j])
            for i, t in enumerate(tiles):
                if i + LOOK < n:
                    stage1(tiles[i + LOOK])
                attn_pv(cx, t['pt'], 128, t['slots'], t['ob'], t['db'], t['ocol'], t['toks'])
                if t.get('after') is not None:
                    t['after']()

        def banded_tiles(job, mbase, first_pattern):
            Kv, Vv, kt, vt = kv_views(job)
            QH = QHb[job['qb']]
            toks = (kt, ('QH', job['qb']), vt, 'BIAS')
            tiles = []
            for g4 in range(4):
                ob, db = obanks()
                for qq in range(4):
                    qt = g4 * 4 + qq
                    cls = 0 if qt == 0 else (2 if qt == 15 else 1)
                    slots = []
                    for s in range(3):
                        wt = 8 + qt - 1 + s
                        slots.append((Kv[:, wt * 128:(wt + 1) * 128], Vv[:, wt, :], QH[:, qt * 128:(qt + 1) * 128]))
                    tiles.append(dict(slots=slots, bias=MSK[:, mbase + cls, :], sb=[qt % 3], ob=ob, db=db,
                                      ocol=qq * 128, toks=toks))
                tiles[-1]['after'] = (lambda ob=ob, db=db, g4=g4: evac(
                    ob, db, OB[:, g4 * 512:(g4 + 1) * 512], DENB[:, g4 * 512:(g4 + 1) * 512], first_pattern))
            return tiles

        def compute(job):
            m, h = job['mixer'], job['h']
            Kv, Vv, kt, vt = kv_views(job)
            QH = QHb[job['qb']]
            qtok = ('QH', job['qb'])
            VH4, VH16, BD = VH4b[h % 2], VH16b[h % 2], BDb[h % 2]
            if m == 'A':
                run_tiles(banded_tiles(job, 0, True))
                finalize(job, sink_col=h)
            elif m == 'B':
                tiles = banded_tiles(job, 3, True)
                for r in range(4):
                    ob, db = obanks()
                    for it in range(4):
                        cls = 0 if it == 0 else (2 if it == 3 else 1)
                        q0 = r + 4 * it * 128
                        qT = QH[:, q0:q0 + 4 * 127 + 1:4]
                        slots = []
                        for s in range(3):
                            wt = 2 + it - 1 + s
                            k0 = r + 512 * wt
                            slots.append((Kv[:, k0:k0 + 4 * 127 + 1:4], VH4[:, r * 8 + wt, :], qT))
                        tiles.append(dict(slots=slots, bias=MSK[:, 3 + cls, :], sb=[it % 3], ob=ob, db=db,
                                          ocol=it * 128, toks=(kt, qtok, [('VH4', h % 2)], 'BIAS')))
                    tiles[-1]['after'] = (lambda ob=ob, db=db, r=r: evac(
                        ob, db, OB[:, r:r + 4 * 511 + 1:4], DENB[:, r:r + 4 * 511 + 1:4], False))
                for r0 in range(0, 16, 4):
                    ob, db = obanks()
                    for rr in range(4):
                        r = r0 + rr
                        qT = QH[:, r:r + 16 * 127 + 1:16]
                        slots = []
                        for s in range(2):
                            k0 = r + 16 * 128 * s
                            slots.append((Kv[:, k0:k0 + 16 * 127 + 1:16], VH16[:, r * 2 + s, :], qT))
                        tiles.append(dict(slots=slots, bias=MSK[:, 6, 0:256], sb=[rr % 3], ob=ob, db=db,
                                          ocol=rr * 128, toks=(kt, qtok, [('VH16', h % 2)], 'BIAS')))

                    def after16(ob=ob, db=db, r0=r0):
                        ov = OB[:].rearrange("p (i r) -> p r i", r=16)[:, r0:r0 + 4, :]
                        dv = DENB[:].rearrange("p (i r) -> p r i", r=16)[:, r0:r0 + 4, :]
                        evac(ob, db, ov, dv, False)
                    tiles[-1]['after'] = after16
                run_tiles(tiles)
                finalize(job)
            elif m == 'D':
                tiles = []
                for g4 in range(4):
                    ob, db = obanks()
                    for qq in range(4):
                        ul = g4 * 4 + qq
                        cname = {0: 't0', 1: 't1', 14: 't14', 15: 't15'}.get(ul, 'gen')
                        sl0, nsl, da0 = D_CLASSES[cname]
                        slots = []
                        for s in range(nsl):
                            wt = 8 + ul + da0 + s
                            slots.append((Kv[:, wt * 128:(wt + 1) * 128], Vv[:, wt, :],
                                          QH[:, ul * 128:(ul + 1) * 128]))
                        sb = [0, 2] if ul % 2 == 0 else [1, 3]
                        tiles.append(dict(slots=slots, bias=BD[:, sl0 * 128:(sl0 + nsl) * 128], sb=sb, ob=ob, db=db,
                                          ocol=qq * 128, toks=(kt, qtok, vt, ('BD', h % 2))))
                    tiles[-1]['after'] = (lambda ob=ob, db=db, g4=g4: evac(
                        ob, db, OB[:, g4 * 512:(g4 + 1) * 512], DENB[:, g4 * 512:(g4 + 1) * 512], True))
                run_tiles(tiles)
                finalize(job)
            else:
                for qb in range(4):
                    ob, db = obanks()
                    qT = QH[:, qb * 512:(qb + 1) * 512]
                    NKT = 64
                    pts = {}

                    def qk(kt_):
                        bk = kt_ % 3
                        S.op('pe', _mm(ps[bk][:, :], KH[:, kt_ * 128:(kt_ + 1) * 128], qT, True, True),
                             r=kt + [qtok], w=[('ps', bk)])
                        pt, pttok = next_pt()
                        pts[kt_] = (pt, pttok)
                        S.op('act', lambda e, bk=bk, pt=pt: e.activation(out=pt[:, 0:512], in_=ps[bk][:, :],
                                                                        func=AF.Exp, scale=SCALE),
                             r=[('ps', bk)], w=[pttok])

                    def pv(kt_):
                        pt, pttok = pts.pop(kt_)
                        S.op('pe', _mm(ps[ob][:, :], VH[:, kt_, :], pt[:, 0:512], kt_ == 0, kt_ == NKT - 1),
                             r=[pttok] + vt, w=[('ps', ob)])
                        S.op('pe', _mm(ps[db][:, :], cx.ONES[:], pt[:, 0:512], kt_ == 0, kt_ == NKT - 1),
                             r=[pttok, 'CONST'], w=[('ps', db)])

                    qk(0)
                    qk(1)
                    for kt_ in range(NKT):
                        if kt_ + 2 < NKT:
                            qk(kt_ + 2)
                        pv(kt_)
                    evac(ob, db, OB[:, qb * 512:(qb + 1) * 512], DENB[:, qb * 512:(qb + 1) * 512], True)
                finalize(job)

        emit_load(jobs[0], False)
        for i, job in enumerate(jobs):
            emit_load(job, True)
            if i + 1 < len(jobs):
                emit_load(jobs[i + 1], False)
            compute(job)
        S.barrier()


def phase3(cx, dr, ADA, HT, BR, MX, last_fg=None):
    nc, S, ps = cx.nc, cx.S, cx.ps
    with ExitStack() as st:
        WG = [st.enter_context(_sbt(nc, f"p3_wg{i}", [128, 16, 128], BF16)) for i in range(3)]
        WR = [st.enter_context(_sbt(nc, f"p3_wr{i}", [128, 4, 128], BF16)) for i in range(3)]
        SG = [st.enter_context(_sbt(nc, f"p3_sg{i}", [128, 512], F32)) for i in range(2)]
        TP = [st.enter_context(_sbt(nc, f"p3_tp{i}", [128, 512], F32)) for i in range(2)]
        MIXF = st.enter_context(_sbt(nc, "p3_mixf", [128, 4, 512], F32))
        MIXB = [st.enter_context(_sbt(nc, f"p3_mixb{i}", [128, T], BF16)) for i in range(2)]
        blk = 0
        wi = 0
        for m in range(16):
            mixb = MIXB[m % 2]
            for n in range(4):
                wg, wr = WG[wi % 3], WR[wi % 3]
                wtok = ('W3', wi % 3)
                wi += 1
                S.dma('pool', wg[:], dr['wgm'][m, n], w=[wtok])
                S.dma('pool', wr[:], dr['wbr'][m, n], w=[wtok])
                for tb in range(NTB):
                    pa = blk % 2
                    pg = 2 + blk % 2
                    k2 = blk % 2
                    blk += 1
                    tsl = slice(tb * 512, (tb + 1) * 512)
                    for kc in range(4):
                        S.op('pe', _mm(ps[pa][:, :], wr[:, kc, :], BR[:, n * 4 + kc, tsl], kc == 0, kc == 3),
                             r=[wtok, 'BR'], w=[('ps', pa)])
                    for kc in range(KC):
                        S.op('pe', _mm(ps[pg][:, :], wg[:, kc, :], HT[:, kc, tsl], kc == 0, kc == KC - 1),
                             r=[wtok, 'HT', ('HT', tb)], w=[('ps', pg)])
                    sg, tp = SG[k2], TP[k2]
                    S.op('act', lambda e, pg=pg, sg=sg: e.activation(out=sg[:], in_=ps[pg][:, :], func=AF.Sigmoid),
                         r=[('ps', pg)], w=[('SG', k2)])
                    if n == 0:
                        S.op('dve', lambda e, pa=pa, sg=sg, tb=tb: e.tensor_tensor(out=MIXF[:, tb, :], in0=ps[pa][:, :],
                                                                                  in1=sg[:], op=ALU.mult),
                             r=[('ps', pa), ('SG', k2)], w=[('MIXF', tb)])
                    else:
                        S.op('dve', lambda e, pa=pa, sg=sg, tp=tp: e.tensor_tensor(out=tp[:], in0=ps[pa][:, :],
                                                                                  in1=sg[:], op=ALU.mult),
                             r=[('ps', pa), ('SG', k2)], w=[('TP', k2)])
                        if n < 3:
                            S.op('dve', lambda e, tp=tp, tb=tb: e.tensor_tensor(out=MIXF[:, tb, :], in0=MIXF[:, tb, :],
                                                                               in1=tp[:], op=ALU.add),
                                 r=[('TP', k2), ('MIXF', tb)], w=[('MIXF', tb)])
                        else:
                            S.op('dve', lambda e, tp=tp, tb=tb, tsl=tsl, mixb=mixb: e.tensor_tensor(
                                out=mixb[:, tsl], in0=MIXF[:, tb, :], in1=tp[:], op=ALU.add),
                                 r=[('TP', k2), ('MIXF', tb)], w=[('MIXB', m % 2)])
            S.dma('sp', dr['MIXS'][m], mixb[:], r=[('MIXB', m % 2)], w=[('DRM', m)])
        S.barrier()
    with ExitStack() as st:
        WO = [st.enter_context(_sbt(nc, f"p3_wo{i}", [128, 16, 128], BF16)) for i in range(3)]
        XR = [st.enter_context(_sbt(nc, f"p3_xr{i}", [128, T], F32)) for i in range(2)]
        for tb in range(NTB):
            S.dma('sp', MX[:, :, tb * 512:(tb + 1) * 512],
                  dr['MIXS'][:, :, tb * 512:(tb + 1) * 512].rearrange("m p t -> p m t"), w=[('MX', tb)])
        blk = 0
        for mo in range(16):
            wo = WO[mo % 3]
            xr = XR[mo % 2]
            S.dma('pool', wo[:], dr['wout'][mo], w=[('WO', mo % 3)])
            S.dma('sp', xr[:], dr['xT'][:, mo, :], w=[('XR', mo % 2)])
            for tb in range(NTB):
                pb = blk % 4
                blk += 1
                tsl = slice(tb * 512, (tb + 1) * 512)
                for kc in range(KC):
                    S.op('pe', _mm(ps[pb][:, :], wo[:, kc, :], MX[:, kc, tsl], kc == 0, kc == KC - 1),
                         r=[('WO', mo % 3), ('MX', tb)], w=[('ps', pb)])
                S.op('dve', lambda e, pb=pb, xr=xr, tsl=tsl, mo=mo: e.scalar_tensor_tensor(
                    out=xr[:, tsl], in0=ps[pb][:, :], scalar=ADA[:, 32 + mo:33 + mo], in1=xr[:, tsl],
                    op0=ALU.mult, op1=ALU.add), r=[('ps', pb), ('XR', mo % 2), 'ADA'], w=[('XR', mo % 2)])
            S.dma('sp', dr['xTo'][:, mo, :], xr[:], r=[('XR', mo % 2)], w=[('DRX', mo)])
        S.barrier()


def phase_final(cx, dr):
    nc, S, ps = cx.nc, cx.S, cx.ps
    with ExitStack() as st:
        XB = st.enter_context(_sbt(nc, "pf_xb", [128, 16, 512], F32))
        SQ = st.enter_context(_sbt(nc, "pf_sq", [128, 16, 512], BF16))
        RS = st.enter_context(_sbt(nc, "pf_rs", [128, 512], F32))
        FG = st.enter_context(_sbt(nc, "pf_fg", [128, 16], F32))
        YB = [st.enter_context(_sbt(nc, f"pf_yb{i}", [128, 16, 512], F32)) for i in range(1)]
        S.dma('sp', FG[:], dr['fg'], w=['FG'])
        for tb in range(NTB):
            S.dma('sp', XB[:], dr['xT'][:, :, tb * 512:(tb + 1) * 512], w=['XB'])
            norm_block(cx, XB, SQ, RS, tb, None)
            yb = YB[0]
            for c in range(KC):
                S.op('dve', lambda e, c=c, yb=yb: e.scalar_tensor_tensor(out=yb[:, c, :], in0=XB[:, c, :],
                                                                        scalar=FG[:, c:c + 1], in1=RS[:],
                                                                        op0=ALU.mult, op1=ALU.mult),
                     r=['XB', 'RS', 'FG'], w=['YB'])
            S.dma('sp', dr['yT'][:, :, tb * 512:(tb + 1) * 512], yb[:], r=['YB'], w=[('DRY', tb)])
        S.barrier()


def _new_nc():
    return bass.Bass("TRN2", target_bir_lowering=False)


def _setup(nc, es):
    cx = Ctx()
    cx.nc = nc
    cx.S = Sched(nc, es)
    cx.ps = [es.enter_context(nc.psum_tensor(f"psb{i}", [128, 512], F32)) for i in range(8)]
    CM = es.enter_context(_sbt(nc, "cmat_sb", [128, 4, 128], BF16))
    cx.CM = CM
    cx.IDENT = CM[:, 0, :]
    cx.ONES = CM[:, 1, :]
    cx.R1 = CM[:, 2, :]
    cx.RC = CM[:, 3, :]
    cx.EPSC = es.enter_context(_sbt(nc, "epsc_sb", [128, 1], F32))
    cx.S.op('dve', lambda e: e.memset(cx.EPSC[:], EPS), w=['EPSC'])
    return cx


def _din(nc, name, shape, dt=F32):
    return nc.dram_tensor(name, list(shape), dt, kind="ExternalInput").ap()


def _dout(nc, name, shape, dt=F32):
    return nc.dram_tensor(name, list(shape), dt, kind="ExternalOutput").ap()


def _dint(nc, name, shape, dt=F32):
    return nc.dram_tensor(name, list(shape), dt, kind="Internal").ap()


def build_L1(stop=9):
    nc = _new_nc()
    dr = {}
    dr['xT'] = _din(nc, 'xT', [128, 16, T])
    dr['cT'] = _din(nc, 'cT', [128, 16])
    dr['wada'] = _din(nc, 'wada', [12, 128, 16, 512])
    dr['bada'] = _din(nc, 'bada', [128, 48])
    dr['ng'] = _din(nc, 'ng', [128, 16])
    dr['wfm'] = _din(nc, 'wfm', [NFM, 128, 16, 128])
    dr['wv'] = _din(nc, 'wv', [3, 128, 16, 512])
    dr['qkn'] = _din(nc, 'qkn', [128, 2])
    dr['rope'] = _din(nc, 'rope', [4, 128, T])
    cmat = _din(nc, 'cmat', [128, 4, 128])
    dr['QS'] = _dout(nc, 'QS', [16, 128, T], BF16)
    dr['KS'] = _dout(nc, 'KS', [12, 128, T], BF16)
    dr['GS'] = _dout(nc, 'GS', [16, 128, T], BF16)
    dr['VS'] = _dout(nc, 'VS', [T, 1536], BF16)
    HTo = _dout(nc, 'HT', [128, 16, T], BF16)
    ADAo = _dout(nc, 'ADA', [128, 48])
    with ExitStack() as es:
        cx = _setup(nc, es)
        S = cx.S
        S.dma('pool', cx.CM[:], cmat, w=['CONST'])
        ADA = es.enter_context(_sbt(nc, "ADA_sb", [128, 48], F32))
        HT = es.enter_context(_sbt(nc, "HT_sb", [128, 16, T], BF16))
        if stop >= 1:
            phase_ada(cx, dr, ADA[:])
        if stop >= 2:
            phase1(cx, dr, ADA, HT, stop)
        S.dma('sp', HTo, HT[:], r=[('HT', i) for i in range(4)], w=['DHT'])
        S.dma('sp', ADAo, ADA[:], r=['ADA'], w=['DADA'])
        S.barrier()
    return nc


def build_L2():
    nc = _new_nc()
    dr = {}
    dr['xT'] = _din(nc, 'xT', [128, 16, T])
    ADAi = _din(nc, 'ADA', [128, 48])
    HTi = _din(nc, 'HT', [128, 16, T], BF16)
    dr['QS'] = _din(nc, 'QS', [16, 128, T], BF16)
    dr['GS'] = _din(nc, 'GS', [16, 128, T], BF16)
    dr['KW'] = _din(nc, 'KW', [10, 128, WIN], BF16)
    dr['KC'] = _din(nc, 'KC', [2, 128, SEQ], BF16)
    dr['VW'] = _din(nc, 'VW', [WIN, 1536], BF16)
    dr['VC'] = _din(nc, 'VC', [SEQ, 256], BF16)
    dr['masks'] = _din(nc, 'masks', [128, 7, 384])
    dr['biasD'] = _din(nc, 'biasD', [4, 128, D_NSLOT * 128])
    sink = _din(nc, 'sink', [128, 4])
    dr['wgm'] = _din(nc, 'wgm', [16, 4, 128, 16, 128])
    dr['wbr'] = _din(nc, 'wbr', [16, 4, 128, 4, 128])
    dr['wout'] = _din(nc, 'wout', [16, 128, 16, 128])
    cmat = _din(nc, 'cmat', [128, 4, 128])
    dr['MIXS'] = _dint(nc, 'MIXS', [16, 128, T], BF16)
    dr['xTo'] = _dout(nc, 'xTo', [128, 16, T])
    with ExitStack() as es:
        cx = _setup(nc, es)
        S = cx.S
        S.dma('pool', cx.CM[:], cmat, w=['CONST'])
        ADA = es.enter_context(_sbt(nc, "ADA_sb", [128, 48], F32))
        S.dma('sp', ADA[:], ADAi, w=['ADA'])
        BR = es.enter_context(_sbt(nc, "BR_sb", [128, 16, T], BF16))
        phase2(cx, dr, BR, sink)
        HT = es.enter_context(_sbt(nc, "HT_sb", [128, 16, T], BF16))
        S.dma('sp', HT[:], HTi, w=['HT'])
        S.dma('sp', ADA[:], ADAi, w=['ADA'])
        phase3(cx, dr, ADA, HT, BR, BR)
    return nc


def build_L3():
    nc = _new_nc()
    dr = {}
    dr['xT'] = _din(nc, 'xT', [128, 16, T])
    dr['fg'] = _din(nc, 'fg', [128, 16])
    cmat = _din(nc, 'cmat', [128, 4, 128])
    dr['yT'] = _dout(nc, 'yT', [128, 16, T])
    with ExitStack() as es:
        cx = _setup(nc, es)
        cx.S.dma('pool', cx.CM[:], cmat, w=['CONST'])
        phase_final(cx, dr)
    return nc


def fm_vec(v, nchunk):
    return np.ascontiguousarray(np.asarray(v, np.float32).reshape(nchunk, 128).T)


def fm_weight(w, cols):
    K = w.shape[0]
    ws = w[:, cols]
    return np.ascontiguousarray(ws.reshape(K // 128, 128, ws.shape[1]).transpose(1, 0, 2))


def rope_consts(t0):
    pos = np.arange(t0, t0 + T)

    def tables(p, dim):
        inv = (np.float32(10000.0) ** (-np.arange(0, dim, 2, dtype=np.float32) / np.float32(dim))).astype(np.float32)
        ang = p.astype(np.float32)[:, None] * inv[None, :]
        ang = np.concatenate([ang, ang], axis=-1)
        return np.cos(ang).astype(np.float32), np.sin(ang).astype(np.float32)

    c1, s1 = tables(pos, 128)
    cr, sr = tables(pos // 64, 64)
    cc, sc = tables(pos % 64, 64)
    cC = np.concatenate([cr, cc], axis=-1)
    sC = np.concatenate([sr, sc], axis=-1)
    return np.ascontiguousarray(np.stack([c1.T, s1.T, cC.T, sC.T], axis=0))


def const_mats():
    cm = np.zeros((128, 4, 128), np.float32)
    cm[:, 0, :] = np.eye(128, dtype=np.float32)
    cm[:, 1, :] = 1.0
    for m in range(128):
        if m < 64:
            cm[m + 64, 2, m] = -1.0
        else:
            cm[m - 64, 2, m] = 1.0
    for half in (0, 64):
        for mm in range(64):
            m = half + mm
            if mm < 32:
                cm[m + 32, 3, m] = -1.0
            else:
                cm[m - 32, 3, m] = 1.0
    return cm


def band_masks(j):
    m = np.zeros((128, 7, 384), np.float32)
    ki = np.arange(128)[:, None]
    qi = np.arange(128)[None, :]
    for base, reach in ((0, 128), (3, 64)):
        for cls in range(3):
            for s in range(3):
                dk = (s - 1) * 128 + ki - qi
                ok = np.abs(dk) <= reach
                if cls == 0 and s == 0 and j == 0:
                    ok = np.zeros_like(ok)
                if cls == 2 and s == 2 and j == 3:
                    ok = np.zeros_like(ok)
                m[:, base + cls, s * 128:(s + 1) * 128] = np.where(ok, 0.0, NEG)
    for s in range(2):
        kiw = s * 128 + ki
        ok = np.abs(kiw - (64 + qi)) <= 64
        if j == 0:
            ok = ok & (kiw >= 64)
        if j == 3:
            ok = ok & (kiw < 192)
        m[:, 6, s * 128:(s + 1) * 128] = np.where(ok, 0.0, NEG)
    return m


def d_bias_tables(rel_bias, j):
    out = np.full((4, 128, D_NSLOT * 128), NEG, np.float32)
    rows_total = SEQ // 64
    kk = np.arange(128)
    qq = np.arange(128)
    kc = kk % 64
    qc = qq % 64
    cs = np.clip(qc - 8, 0, 48)
    colok = (kc[:, None] >= cs[None, :]) & (kc[:, None] < cs[None, :] + 16)
    dc = np.clip(kc[:, None] - qc[None, :], -15, 15) + 15
    for cname, (sl0, nsl, da0) in D_CLASSES.items():
        ul = {'gen': 4, 't0': 0, 't1': 1, 't14': 14, 't15': 15}[cname]
        u = j * 16 + ul
        qrow = 2 * u + qq // 64
        rs = np.clip(qrow - 4, 0, rows_total - 8)
        for s in range(nsl):
            a = u + da0 + s
            krow = 2 * a + kk // 64
            rowok = (krow[:, None] >= rs[None, :]) & (krow[:, None] < rs[None, :] + 8)
            inseq = (krow >= 0) & (krow < rows_total)
            ok = rowok & colok & inseq[:, None]
            drr = np.clip(krow[:, None] - qrow[None, :], -7, 7) + 7
            for h in range(4):
                b = rel_bias[h][drr, dc]
                out[h, :, (sl0 + s) * 128:(sl0 + s + 1) * 128] = np.where(ok, b, NEG)
    return out


_PROGS = {}


def _prog(name):
    if name not in _PROGS:
        _PROGS[name] = {'L1': build_L1, 'L2': build_L2, 'L3': build_L3}[name]()
    return _PROGS[name]


def kernel_unfused(x, c, norm_g, w_ada, b_ada, w_in, a_sink, c_q_norm, c_k_norm, d_rel_bias, w_gate_merge,
                   w_branch, w_out, final_g, _depth=DEPTH):
    x = np.asarray(x, np.float32)
    cores = list(range(NCORES))
    cm = const_mats()
    xT = []
    for core in cores:
        b, j = core // 4, core % 4
        xs = x[b, j * T:(j + 1) * T, :]
        xT.append(np.ascontiguousarray(xs.reshape(T, 16, 128).transpose(2, 1, 0)))
    ropes = [rope_consts(j * T) for j in range(4)]
    masks = [band_masks(j) for j in range(4)]
    cT = [fm_vec(np.asarray(c)[b], 16) for b in range(2)]
    for l in range(_depth):
        wl = np.asarray(w_in[l], np.float32)
        wfm = np.stack([fm_weight(wl, list(range(c0, c0 + 128))) for (_k, _m, _i, c0) in FM_CHUNKS], axis=0)
        wv = np.stack([fm_weight(wl, cols) for cols in V_COLS], axis=0)
        wa = np.asarray(w_ada[l], np.float32)
        wada = np.stack([fm_weight(wa, list(range(s * 512, (s + 1) * 512))) for s in range(12)], axis=0)
        bada = fm_vec(b_ada[l], 48)
        ng = fm_vec(norm_g[l], 16)
        qkn = np.ascontiguousarray(np.stack([np.asarray(c_q_norm[l], np.float32),
                                             np.asarray(c_k_norm[l], np.float32)], axis=1))
        in1 = [dict(xT=xT[core], cT=cT[core // 4], wada=wada, bada=bada, ng=ng, wfm=wfm, wv=wv, qkn=qkn,
                    rope=ropes[core % 4], cmat=cm) for core in cores]
        r1 = run_bass_kernel_spmd(_prog('L1'), in1, core_ids=cores).results
        del wfm, wv, wada
        in2 = []
        wg = np.asarray(w_gate_merge[l], np.float32)
        wgm = np.ascontiguousarray(
            wg.reshape(16, 128, 4, 16, 128).transpose(3, 2, 1, 0, 4))
        wb = np.asarray(w_branch[l], np.float32)
        wbr = np.ascontiguousarray(wb.reshape(4, 4, 128, 16, 128).transpose(3, 0, 2, 1, 4))
        wo = np.asarray(w_out[l], np.float32)
        wout = np.ascontiguousarray(wo.reshape(16, 128, 16, 128).transpose(2, 1, 0, 3))
        sink = np.ascontiguousarray(np.broadcast_to(np.asarray(a_sink[l], np.float32)[None, :], (128, 4)))
        biasD = [d_bias_tables(np.asarray(d_rel_bias[l], np.float32), j) for j in range(4)]
        for b in range(2):
            KSb = np.concatenate([np.asarray(r1[b * 4 + j]['KS']) for j in range(4)], axis=2)
            VSb = np.concatenate([np.asarray(r1[b * 4 + j]['VS']) for j in range(4)], axis=0)
            kwi = [KIDX[('A', 0)], KIDX[('A', 1)]] + [KIDX[('B', h)] for h in range(4)] + \
                  [KIDX[('D', h)] for h in range(4)]
            vcols = np.arange(1536)
            Kpad = np.zeros((10, 128, SEQ + 2048), KSb.dtype)
            Kpad[:, :, 1024:1024 + SEQ] = KSb[kwi]
            Vpad = np.zeros((SEQ + 2048, 1536), VSb.dtype)
            Vpad[1024:1024 + SEQ] = VSb[:, vcols]
            KCb = np.ascontiguousarray(KSb[[KIDX[('C', 0)], KIDX[('C', 1)]]])
            VCb = np.ascontiguousarray(VSb[:, VOFF[('C', 0)] * 128:(VOFF[('C', 1)] + 1) * 128])
            for j in range(4):
                core = b * 4 + j
                t0 = j * T
                in2.append(dict(xT=xT[core], ADA=np.asarray(r1[core]['ADA']), HT=np.asarray(r1[core]['HT']),
                                QS=np.asarray(r1[core]['QS']), GS=np.asarray(r1[core]['GS']),
                                KW=np.ascontiguousarray(Kpad[:, :, t0:t0 + WIN]), KC=KCb,
                                VW=np.ascontiguousarray(Vpad[t0:t0 + WIN]), VC=VCb,
                                masks=masks[j], biasD=biasD[j], sink=sink, wgm=wgm, wbr=wbr, wout=wout, cmat=cm))
        del r1
        r2 = run_bass_kernel_spmd(_prog('L2'), in2, core_ids=cores).results
        xT = [np.asarray(r2[core]['xTo']) for core in cores]
        del in2, r2
    fg = fm_vec(final_g, 16)
    r3 = run_bass_kernel_spmd(_prog('L3'), [dict(xT=xT[core], fg=fg, cmat=cm) for core in cores],
                              core_ids=cores).results
    out = np.empty((2, SEQ, D), np.float32)
    for core in cores:
        b, j = core // 4, core % 4
        yT = np.asarray(r3[core]['yT'])
        out[b, j * T:(j + 1) * T, :] = yT.transpose(2, 1, 0).reshape(T, D)
    return out


GROUPS = [[0, 1, 2, 3], [4, 5, 6, 7]]
I32 = mybir.dt.int32


def emit_exchange(cx, dr, KS_t, VS_t, KG_t, VG_t, KST, VSG):
    S = cx.S
    kheads = {}
    for (mixer, h), kid in KIDX.items():
        kheads.setdefault(kid // 2, []).append(('DR', 'k', mixer, h))
    kcc, vcc = KIDX[('C', 0)] // 2, VOFF[('C', 0)] // 2
    S.allgather(KS_t[kcc], KG_t[kcc], GROUPS, r=kheads[kcc], w=[('KG', kcc)])
    S.allgather(VS_t[vcc], VG_t[vcc], GROUPS, r=[('DRV', vcc)], w=[('VG', vcc)])
    for c in range(6):
        if c != kcc:
            S.allgather(KS_t[c], KG_t[c], GROUPS, r=kheads[c], w=[('KG', c)])
    for c in range(6):
        if c != vcc:
            S.allgather(VS_t[c], VG_t[c], GROUPS, r=[('DRV', c)], w=[('VG', c)])
    items = []
    n = 0
    for (mixer, h), i in KW_IDX.items():
        kid = KIDX[(mixer, h)]
        KG2 = KG_t[kid // 2].ap().rearrange("r (h t) -> (r h) t", h=2)
        for p in range(4):
            items.append((KST[n % 3], ('KST', n % 3), KG2, cx.KIDXT[:, i * 4 + p:i * 4 + p + 1], ('KG', kid // 2),
                          dr['KW'][i][:, p * 1024:(p + 1) * 1024], ('KWd', i, p)))
            n += 1
    n = 0
    for c in (0, 2, 3, 4, 5):
        VG = VG_t[c].ap()
        for wt in range(32):
            items.append((VSG[n % 4], ('VSG', n % 4), VG, cx.VIDXT[:, wt:wt + 1], ('VG', c),
                          dr['VW'][wt * 128:(wt + 1) * 128, c * 256:(c + 1) * 256], ('VWd', wt, c)))
            n += 1
    pend = []
    for it in items:
        buf, tok, src, idx, srctok, dst, dtok = it
        S.idma(buf[:], src, idx, r=[srctok, 'IDX'], w=[tok])
        pend.append(it)
        if len(pend) > 2:
            b2, t2, _s, _i, _st, d2, dt2 = pend.pop(0)
            S.dma('pool', d2, b2[:], r=[t2], w=[dt2])
    for b2, t2, _s, _i, _st, d2, dt2 in pend:
        S.dma('pool', d2, b2[:], r=[t2], w=[dt2])


def build_fused(depth=DEPTH):
    nc = _new_nc()
    xT_in = _din(nc, 'xT', [128, 16, T])
    cT = _din(nc, 'cT', [128, 16])
    wada = _din(nc, 'wada', [depth, 3, 128, 16, 512])
    bada = _din(nc, 'bada', [128, depth * 12])
    AIN_t = nc.dram_tensor('AIN', [128, depth * 12], F32)
    AOUT_t = nc.dram_tensor('AOUT', [4 * 128, depth * 12], F32)
    ng = _din(nc, 'ng', [depth, 128, 16])
    wfm = _din(nc, 'wfm', [depth, NFM, 128, 16, 128])
    wv = _din(nc, 'wv', [depth, 3, 128, 16, 512])
    qkn = _din(nc, 'qkn', [depth, 128, 2])
    rope = _din(nc, 'rope', [4, 128, T])
    cmat = _din(nc, 'cmat', [128, 4, 128])
    masks = _din(nc, 'masks', [128, 7, 384])
    biasD = _din(nc, 'biasD', [depth, 4, 128, D_NSLOT * 128])
    sink = _din(nc, 'sink', [depth, 128, 4])
    wgm = _din(nc, 'wgm', [depth, 16, 4, 128, 16, 128])
    wbr = _din(nc, 'wbr', [depth, 16, 4, 128, 4, 128])
    wout = _din(nc, 'wout', [depth, 16, 128, 16, 128])
    fg = _din(nc, 'fg', [128, 16])
    kidx = _din(nc, 'kidx', [128, 40], I32)
    vidx = _din(nc, 'vidx', [128, 32], I32)
    yT = _dout(nc, 'yT', [128, 16, T])
    XS = [_dint(nc, f'XS{i}', [128, 16, T]) for i in range(2)]
    QS = _dint(nc, 'QS', [16, 128, T], BF16)
    GS = _dint(nc, 'GS', [16, 128, T], BF16)
    HTd = _dint(nc, 'HTd', [128, 16, T], BF16)
    MIXS = _dint(nc, 'MIXS', [16, 128, T], BF16)
    KW = _dint(nc, 'KW', [10, 128, WIN], BF16)
    VW = _dint(nc, 'VW', [WIN, 1536], BF16)
    KS_t = [[nc.dram_tensor(f'KS{i}_{c}', [256, T], BF16) for c in range(6)] for i in range(2)]
    VS_t = [[nc.dram_tensor(f'VS{i}_{c}', [T, 256], BF16) for c in range(6)] for i in range(2)]
    KG_t = [[nc.dram_tensor(f'KG{i}_{c}', [4 * 256, T], BF16) for c in range(6)] for i in range(2)]
    VG_t = [[nc.dram_tensor(f'VG{i}_{c}', [4 * T, 256], BF16) for c in range(6)] for i in range(2)]
    with ExitStack() as es:
        cx = _setup(nc, es)
        S = cx.S
        S.dma('pool', cx.CM[:], cmat, w=['CONST'])
        KIDXT = es.enter_context(_sbt(nc, "kidx_sb", [128, 40], I32))
        VIDXT = es.enter_context(_sbt(nc, "vidx_sb", [128, 32], I32))
        cx.KIDXT, cx.VIDXT = KIDXT, VIDXT
        S.dma('sp', KIDXT[:], kidx, w=['IDX'])
        S.dma('sp', VIDXT[:], vidx, w=['IDX'])
        ADAALL = es.enter_context(_sbt(nc, "ada_all", [128, depth, 48], F32))
        cx.KST = [es.enter_context(_sbt(nc, f"ex_k{i}", [128, 1024], BF16)) for i in range(3)]
        cx.VSG = [es.enter_context(_sbt(nc, f"ex_v{i}", [128, 256], BF16)) for i in range(4)]
        S.barrier()
        phase_ada_dist(cx, cT, wada, bada, ADAALL, depth, AIN_t, AOUT_t)
        for l in range(depth):
            par = l % 2
            ADA = ADAALL[:, l, :]
            x_in = xT_in if l == 0 else XS[(l - 1) % 2]
            x_out = XS[l % 2]
            kcc = KIDX[('C', 0)] // 2
            vcc = VOFF[('C', 0)] // 2
            assert KIDX[('C', 0)] % 2 == 0 and VOFF[('C', 0)] % 2 == 0
            KG4 = KG_t[par][kcc].ap().rearrange("(r h p) t -> r h p t", r=4, h=2)
            KSl = [KS_t[par][k // 2].ap().rearrange("(h p) t -> h p t", h=2)[k % 2] for k in range(12)]
            dr = dict(xT=x_in, xTo=x_out, ng=ng[l], wfm=wfm[l], wv=wv[l], qkn=qkn[l], rope=rope,
                      QS=QS, GS=GS, KS=KSl, VS=[t_.ap() for t_ in VS_t[par]],
                      KW=KW, VW=VW, masks=masks, biasD=biasD[l],
                      KCr=[[KG4[rk][kvh] for kvh in range(2)] for rk in range(4)],
                      VC=VG_t[par][VOFF[('C', 0)] // 2].ap(),
                      wgm=wgm[l], wbr=wbr[l], wout=wout[l], MIXS=MIXS, kcc=kcc, vcc=vcc)
            dr['exch'] = (lambda KST, VSG, dr=dr, par=par: emit_exchange(
                cx, dr, KS_t[par], VS_t[par], KG_t[par], VG_t[par], KST, VSG))
            with ExitStack() as ls:
                HT = ls.enter_context(_sbt(nc, f"HT_sb{l}", [128, 16, T], BF16))
                phase1(cx, dr, ADA, HT)
                S.dma('sp', HTd, HT[:], w=['DHT'])
                S.barrier(partial=True, keep=XKEEP)
            with ExitStack() as ls:
                BR = ls.enter_context(_sbt(nc, f"BR_sb{l}", [128, 16, T], BF16))
                phase2(cx, dr, BR, sink[l])
                HT = ls.enter_context(_sbt(nc, f"HT2_sb{l}", [128, 16, T], BF16))
                for tb in range(NTB):
                    S.dma('sp', HT[:, :, tb * 512:(tb + 1) * 512], HTd[:, :, tb * 512:(tb + 1) * 512],
                          w=[('HT', tb)])
                phase3(cx, dr, ADA, HT, BR, BR)
        phase_final(cx, dict(xT=XS[(depth - 1) % 2], fg=fg, yT=yT))
    return nc


def window_index_tables(j):
    kt = np.zeros((128, 40), np.int32)
    p128 = np.arange(128)
    for (mixer, h), i in KW_IDX.items():
        kid = KIDX[(mixer, h)]
        for p in range(4):
            hs = (2 * j - 1 + p) % 8
            rank, half = hs // 2, hs % 2
            assert half == (p + 1) % 2
            kt[:, i * 4 + p] = (rank * 256 + (kid % 2) * 128 + p128) * 2 + half
    vt = np.zeros((128, 32), np.int32)
    for wt in range(32):
        vt[:, wt] = ((j * T - 1024 + wt * 128) % SEQ) + p128
    return kt, vt


_FUSED = {}


def kernel(x, c, norm_g, w_ada, b_ada, w_in, a_sink, c_q_norm, c_k_norm, d_rel_bias, w_gate_merge, w_branch,
           w_out, final_g, _depth=DEPTH):
    x = np.asarray(x, np.float32)
    cores = list(range(NCORES))
    dp = _depth
    cm = const_mats()
    f32 = lambda a: np.asarray(a, np.float32)
    wfm = np.stack([np.stack([fm_weight(f32(w_in[l]), list(range(c0, c0 + 128))) for (_k, _m, _i, c0) in FM_CHUNKS],
                             axis=0) for l in range(dp)], axis=0)
    wv = np.stack([np.stack([fm_weight(f32(w_in[l]), cols) for cols in V_COLS], axis=0) for l in range(dp)], axis=0)
    wada_j = [np.ascontiguousarray(np.stack([np.stack(
        [fm_weight(f32(w_ada[l]), list(range(j * 1536 + s * 512, j * 1536 + (s + 1) * 512))) for s in range(3)],
        axis=0) for l in range(dp)], axis=0)) for j in range(4)]
    bada_j = [np.ascontiguousarray(np.concatenate([fm_vec(b_ada[l], 48)[:, j * 12:(j + 1) * 12] for l in range(dp)],
                                                  axis=1)) for j in range(4)]
    ng = np.stack([fm_vec(norm_g[l], 16) for l in range(dp)], axis=0)
    qkn = np.stack([np.stack([f32(c_q_norm[l]), f32(c_k_norm[l])], axis=1) for l in range(dp)], axis=0)
    wgm = np.stack([f32(w_gate_merge[l]).reshape(16, 128, 4, 16, 128).transpose(3, 2, 1, 0, 4)
                    for l in range(dp)], axis=0)
    wbr = np.stack([f32(w_branch[l]).reshape(4, 4, 128, 16, 128).transpose(3, 0, 2, 1, 4) for l in range(dp)], axis=0)
    wout = np.stack([f32(w_out[l]).reshape(16, 128, 16, 128).transpose(2, 1, 0, 3) for l in range(dp)], axis=0)
    sink = np.stack([np.broadcast_to(f32(a_sink[l])[None, :], (128, 4)) for l in range(dp)], axis=0)
    fg = fm_vec(final_g, 16)
    shared = dict(ng=ng, wfm=wfm, wv=wv, qkn=np.ascontiguousarray(qkn),
                  cmat=cm, sink=np.ascontiguousarray(sink), wgm=np.ascontiguousarray(wgm),
                  wbr=np.ascontiguousarray(wbr), wout=np.ascontiguousarray(wout), fg=fg)
    per_j = []
    for j in range(4):
        kt, vt = window_index_tables(j)
        per_j.append(dict(rope=rope_consts(j * T), masks=band_masks(j), kidx=kt, vidx=vt, wada=wada_j[j],
                          bada=bada_j[j],
                          biasD=np.stack([d_bias_tables(f32(d_rel_bias[l]), j) for l in range(dp)], axis=0)))
    in_maps = []
    for core in cores:
        b, j = core // 4, core % 4
        xs = x[b, j * T:(j + 1) * T, :]
        m = dict(shared)
        m.update(per_j[j])
        m['xT'] = np.ascontiguousarray(xs.reshape(T, 16, 128).transpose(2, 1, 0))
        m['cT'] = fm_vec(np.asarray(c)[b], 16)
        in_maps.append(m)
    if dp not in _FUSED:
        _FUSED[dp] = build_fused(dp)
    res = run_bass_kernel_spmd(_FUSED[dp], in_maps, core_ids=cores).results
    out = np.empty((2, SEQ, D), np.float32)
    for core in cores:
        b, j = core // 4, core % 4
        yT = np.asarray(res[core]['yT'])
        out[b, j * T:(j + 1) * T, :] = yT.transpose(2, 1, 0).reshape(T, D)
    return out
```
